# Optimizing a Trainium2 kernel written in Bass

```python
import jax, jax.numpy as jnp
from jax import lax
import numpy as np

D_MODEL = 1024
BATCH = 4
SEQ = 8192
DEPTH = 1

CHUNK = 64
Q_BLOCK = 128
SB_HEADS = 8
SB_HEAD_DIM = D_MODEL // SB_HEADS
SB_WIDTH = SB_HEADS * SB_HEAD_DIM
RET_HEADS = 4
RET_KEY_DIM = D_MODEL // RET_HEADS
RET_VAL_DIM = D_MODEL // RET_HEADS
RET_QK_WIDTH = RET_HEADS * RET_KEY_DIM
RET_WIDTH = RET_HEADS * RET_VAL_DIM
ROPE_BASE = 10000.0
EPS = 1e-6
SPLIT_SIZES = (SB_WIDTH, SB_WIDTH, SB_WIDTH, SB_WIDTH,
               RET_QK_WIDTH, RET_QK_WIDTH, RET_WIDTH, RET_WIDTH,
               D_MODEL, D_MODEL)
IN_COLS = sum(SPLIT_SIZES)
SPLIT_POINTS = tuple(int(v) for v in np.cumsum(SPLIT_SIZES)[:-1])

kernel_name = "hybrid_stickbreak_retention_gated_block"


def rms_norm(x, gain):
    xf = x.astype(jnp.float32)
    y = xf * lax.rsqrt(jnp.mean(xf * xf, axis=-1, keepdims=True) + EPS)
    return (y * gain.astype(jnp.float32)).astype(x.dtype)


def split_heads(t, n_heads):
    b, s, w = t.shape
    return t.reshape(b, s, n_heads, w // n_heads).transpose(0, 2, 1, 3)


def merge_heads(t):
    b, h, s, d = t.shape
    return t.transpose(0, 2, 1, 3).reshape(b, s, h * d)


def rotary(t):
    s, d = t.shape[2], t.shape[3]
    inv_freq = ROPE_BASE ** (-jnp.arange(0, d, 2, dtype=jnp.float32) / d)
    ang = jnp.arange(s, dtype=jnp.float32)[:, None] * inv_freq[None, :]
    cos, sin = jnp.cos(ang), jnp.sin(ang)
    tf = t.astype(jnp.float32)
    t1, t2 = tf[..., : d // 2], tf[..., d // 2:]
    out = jnp.concatenate([t1 * cos - t2 * sin, t1 * sin + t2 * cos], axis=-1)
    return out.astype(t.dtype)


def stick_breaking_attention(q, k, v):
    b, h, s, d = q.shape
    nb = s // Q_BLOCK
    qb = q.reshape(b, h, nb, Q_BLOCK, d).transpose(2, 0, 1, 3, 4)
    kpos = jnp.arange(s)
    scale = d ** -0.5

    def block(args):
        qi, i = args
        z = jnp.einsum('bhqd,bhkd->bhqk', qi, k).astype(jnp.float32) * scale
        qpos = i * Q_BLOCK + jnp.arange(Q_BLOCK)
        mask = kpos[None, :] < qpos[:, None]
        log_keep = jnp.where(mask, jax.nn.log_sigmoid(-z), 0.0)
        suffix = lax.cumsum(log_keep, axis=3, reverse=True) - log_keep
        w = jnp.where(mask, jnp.exp(jax.nn.log_sigmoid(z) + suffix), 0.0)
        return jnp.einsum('bhqk,bhkd->bhqd', w.astype(v.dtype), v)

    out = lax.map(block, (qb, jnp.arange(nb)))
    return out.transpose(1, 2, 0, 3, 4).reshape(b, h, s, d)


def chunkwise_retention(q, k, v):
    dtype = v.dtype
    q, k, v = (a.astype(jnp.float32) for a in (q, k, v))
    b, h, s, dk = q.shape
    dv = v.shape[-1]
    n = s // CHUNK
    log_gamma = jnp.log1p(-jnp.exp2(-5.0 - jnp.arange(h, dtype=jnp.float32)))
    qc = q.reshape(b, h, n, CHUNK, dk)
    kc = k.reshape(b, h, n, CHUNK, dk)
    vc = v.reshape(b, h, n, CHUNK, dv)
    idx = jnp.arange(CHUNK, dtype=jnp.float32)
    intra_decay = jnp.exp(log_gamma[:, None, None] * jnp.abs(idx[:, None] - idx[None, :]))
    scores = jnp.einsum('bhnqd,bhnkd->bhnqk', qc, kc) * intra_decay[None, :, None]
    intra = jnp.einsum('bhnqk,bhnke->bhnqe', scores, vc)
    q_decay = jnp.exp(log_gamma[:, None] * (idx + 1.0))[None, :, :, None]
    k_decay = jnp.exp(log_gamma[:, None] * (CHUNK - 1.0 - idx))[None, :, :, None]
    chunk_decay = jnp.exp(log_gamma * CHUNK)[None, :, None, None]

    def step(state, inp):
        qi, ki, vi = inp
        out = jnp.einsum('bhqd,bhde->bhqe', qi * q_decay, state)
        state = state * chunk_decay + jnp.einsum('bhkd,bhke->bhde', ki * k_decay, vi)
        return state, out

    init = jnp.zeros((b, h, dk, dv), jnp.float32)
    xs = (qc.transpose(2, 0, 1, 3, 4), kc.transpose(2, 0, 1, 3, 4), vc.transpose(2, 0, 1, 3, 4))
    _, inter = lax.scan(step, init, xs)
    out = intra + inter.transpose(1, 2, 0, 3, 4)
    return out.reshape(b, h, s, dv).astype(dtype)


def setup_inputs(seed: int = 0) -> dict:
    key = jax.random.key(seed)
    ks = jax.random.split(key, 10)
    f32 = jnp.float32
    x = jax.random.normal(ks[0], (BATCH, SEQ, D_MODEL), f32)
    norm_gain = 1.0 + 0.05 * jax.random.normal(ks[1], (DEPTH, D_MODEL), f32)
    w_in = jax.random.normal(ks[2], (DEPTH, D_MODEL, IN_COLS), f32) * D_MODEL ** -0.5
    b_merge = 0.01 * jax.random.normal(ks[3], (DEPTH, 2, D_MODEL), f32)
    sb_q_gain = 1.0 + 0.05 * jax.random.normal(ks[4], (DEPTH, SB_HEAD_DIM), f32)
    sb_k_gain = 1.0 + 0.05 * jax.random.normal(ks[5], (DEPTH, SB_HEAD_DIM), f32)
    ret_out_gain = 1.0 + 0.05 * jax.random.normal(ks[6], (DEPTH, RET_HEADS, RET_VAL_DIM), f32)
    w_branch_sb = jax.random.normal(ks[7], (DEPTH, SB_WIDTH, D_MODEL), f32) * SB_WIDTH ** -0.5
    w_branch_ret = jax.random.normal(ks[8], (DEPTH, RET_WIDTH, D_MODEL), f32) * RET_WIDTH ** -0.5
    w_out = jax.random.normal(ks[9], (DEPTH, D_MODEL, D_MODEL), f32) * D_MODEL ** -0.5
    return {"x": x, "norm_gain": norm_gain, "w_in": w_in, "b_merge": b_merge,
            "sb_q_gain": sb_q_gain, "sb_k_gain": sb_k_gain, "ret_out_gain": ret_out_gain,
            "w_branch_sb": w_branch_sb, "w_branch_ret": w_branch_ret, "w_out": w_out}


def reference(x, norm_gain, w_in, b_merge, sb_q_gain, sb_k_gain, ret_out_gain,
              w_branch_sb, w_branch_ret, w_out):
    for layer in range(DEPTH):
        h = rms_norm(x, norm_gain[layer])
        proj = jnp.einsum('bsd,dc->bsc', h, w_in[layer])
        (sb_q, sb_k, sb_v, sb_g, r_q, r_k, r_v, r_g, m_sb, m_ret) = jnp.split(proj, SPLIT_POINTS, axis=-1)

        qa = rms_norm(split_heads(sb_q, SB_HEADS), sb_q_gain[layer])
        ka = rms_norm(split_heads(sb_k, SB_HEADS), sb_k_gain[layer])
        va = split_heads(sb_v, SB_HEADS)
        out_a = merge_heads(stick_breaking_attention(qa, ka, va)) * jax.nn.silu(sb_g)

        qb = rotary(split_heads(r_q, RET_HEADS))
        kb = rotary(split_heads(r_k, RET_HEADS)) * (RET_KEY_DIM ** -0.5)
        vb = split_heads(r_v, RET_HEADS)
        ret = chunkwise_retention(qb, kb, vb)
        ret = rms_norm(ret, ret_out_gain[layer][None, :, None, :])
        out_b = merge_heads(ret) * jax.nn.silu(r_g)

        p_a = jnp.einsum('bsw,wd->bsd', out_a, w_branch_sb[layer])
        p_b = jnp.einsum('bsw,wd->bsd', out_b, w_branch_ret[layer])
        merged = (jax.nn.sigmoid(m_sb + b_merge[layer, 0]) * p_a
                  + jax.nn.sigmoid(m_ret + b_merge[layer, 1]) * p_b)
        x = x + jnp.einsum('bsd,de->bse', merged, w_out[layer])
    return x
```

```python
import contextlib
import numpy as np
import ml_dtypes
import concourse.bass as bass
import concourse.mybir as mybir
from concourse.bass_utils import run_bass_kernel_spmd

F32 = mybir.dt.float32
BF16 = mybir.dt.bfloat16
AF = mybir.ActivationFunctionType
ALU = mybir.AluOpType
AX = mybir.AxisListType


class Tracker:
    ENG = ("pe", "act", "dve", "pool", "sp")

    def __init__(self, nc, n_dma_sems=6):
        self.nc = nc
        self.n_dma = n_dma_sems
        self.stack = contextlib.ExitStack()
        self.streams = {e: [] for e in self.ENG}
        self.count = {}
        self.known = {e: {} for e in self.ENG}
        self.last_write = {}
        self.readers = {}
        self.n_ops = 0

    def __enter__(self):
        nc = self.nc
        self.stack.__enter__()
        self.sem = {}
        for e in self.ENG:
            self.sem[e] = self.stack.enter_context(nc.semaphore("s_" + e))
            self.count[e] = 0
        self.dma_ring = {}
        self.dma_next = {}
        for q in ("sp", "pool", "act"):
            ring = []
            for k in range(self.n_dma):
                name = "d_%s%d" % (q, k)
                self.sem[name] = self.stack.enter_context(nc.semaphore(name))
                self.count[name] = 0
                ring.append(name)
            self.dma_ring[q] = ring
            self.dma_next[q] = 0
        return self

    def __exit__(self, *a):
        return self.stack.__exit__(*a)

    def _deps(self, reads, writes):
        deps = {}

        def add(s, v):
            if v > deps.get(s, 0):
                deps[s] = v

        for b in reads:
            lw = self.last_write.get(b)
            if lw:
                add(*lw)
        for b in writes:
            lw = self.last_write.get(b)
            if lw:
                add(*lw)
            for s, v in self.readers.get(b, {}).items():
                add(s, v)
        return deps

    def _emit_waits(self, e, deps, skip_self=False):
        for s, v in deps.items():
            if skip_self and s == e:
                continue
            if self.known[e].get(s, 0) >= v:
                continue
            self.known[e][s] = v
            sem = self.sem[s]
            self.streams[e].append(("wait", sem, v))

    def _record(self, key, val, reads, writes):
        for b in reads:
            self.readers.setdefault(b, {})[key] = val
        for b in writes:
            self.last_write[b] = (key, val)
            self.readers[b] = {}

    def op(self, e, fn, reads=(), writes=(), inc=True):
        deps = self._deps(reads, writes)
        self._emit_waits(e, deps, skip_self=(e == "pe"))
        if inc:
            self.count[e] += 1
            val = self.count[e]
            self.streams[e].append(("op", fn, self.sem[e], 1))
        else:
            val = self.count[e] + 1
            self.streams[e].append(("op", fn, None, 0))
        self._record(e, val, reads, writes)
        self.n_ops += 1

    def dma(self, q, out, in_, reads=(), writes=(), **kw):
        deps = self._deps(reads, writes)
        name = self.dma_ring[q][self.dma_next[q]]
        self.dma_next[q] = (self.dma_next[q] + 1) % self.n_dma
        deps[name] = max(deps.get(name, 0), self.count[name])
        self._emit_waits(q, deps)
        self.count[name] += 16
        val = self.count[name]
        self.known[q][name] = max(self.known[q].get(name, 0), 0)
        eng = {"sp": self.nc.sync, "pool": self.nc.gpsimd, "act": self.nc.scalar}[q]
        self.streams[q].append(("op", (lambda: eng.dma_start(out=out, in_=in_, **kw)), self.sem[name], 16))
        self._record(name, val, reads, writes)
        self.n_ops += 1

    def barrier(self):
        for e in self.ENG:
            deps = {s: c for s, c in self.count.items() if c > 0 and s != e}
            self._emit_waits(e, deps)

    def finish(self):
        nc = self.nc
        deps = {s: c for s, c in self.count.items() if c > 0 and s != "sp"}
        self._emit_waits("sp", deps)
        streams = self.streams

        def replay(e, eng):
            for item in streams[e]:
                if item[0] == "wait":
                    eng.wait_ge(item[1], item[2])
                else:
                    ins = item[1]()
                    if item[2] is not None:
                        ins.then_inc(item[2], item[3])

        with nc.Block() as block:
            @block.sync
            def _(eng):
                replay("sp", eng)

            @block.scalar
            def _(eng):
                replay("act", eng)

            @block.vector
            def _(eng):
                replay("dve", eng)

            @block.gpsimd
            def _(eng):
                replay("pool", eng)

            @block.tensor
            def _(eng):
                replay("pe", eng)


S = 8192
DM = 1024
TS = 512
NT = S // TS
NO = NT // 2
SO = NO * TS
NCH = DM // 128
EPS = 1e-6
NEG = -30000.0
C_SBQ, C_SBK, C_SBV, C_SBG, C_RQ, C_RK, C_RV, C_RG, C_MSB, C_MRET = [i * 1024 for i in range(10)]
GAMMA = [1.0 - 2.0 ** (-5.0 - h) for h in range(4)]
G64 = [g ** 64 for g in GAMMA]
DEBUG = False


def _mm(T, nc, out, lhsT, rhs, start, stop, reads, writes, inc):
    T.op("pe", lambda: nc.tensor.matmul(out, lhsT=lhsT, rhs=rhs, start=start, stop=stop),
         reads=reads, writes=writes, inc=inc)


def _tr(T, nc, out, in_, ident, reads, writes, inc):
    T.op("pe", lambda: nc.tensor.transpose(out, in_, ident), reads=reads, writes=writes, inc=inc)


def _act(T, nc, out, in_, func, reads, writes, **kw):
    T.op("act", lambda: nc.scalar.activation(out=out, in_=in_, func=func, **kw), reads=reads, writes=writes)


def _ts(T, nc, eng, out, in0, s1, s2, op0, op1, reads, writes):
    e = nc.vector if eng == "dve" else nc.gpsimd
    if op1 is None:
        T.op(eng, lambda: e.tensor_scalar(out=out, in0=in0, scalar1=s1, scalar2=None, op0=op0), reads=reads, writes=writes)
    else:
        T.op(eng, lambda: e.tensor_scalar(out=out, in0=in0, scalar1=s1, scalar2=s2, op0=op0, op1=op1), reads=reads, writes=writes)


def _tt(T, nc, eng, out, in0, in1, op, reads, writes):
    e = nc.vector if eng == "dve" else nc.gpsimd
    T.op(eng, lambda: e.tensor_tensor(out=out, in0=in0, in1=in1, op=op), reads=reads, writes=writes)


def _stt(T, nc, out, in0, scalar, in1, op0, op1, reads, writes):
    T.op("dve", lambda: nc.vector.scalar_tensor_tensor(out=out, in0=in0, scalar=scalar, in1=in1, op0=op0, op1=op1),
         reads=reads, writes=writes)


def _copy(T, nc, eng, out, in_, reads, writes):
    if eng == "act":
        T.op("act", lambda: nc.scalar.copy(out=out, in_=in_), reads=reads, writes=writes)
    else:
        e = nc.vector if eng == "dve" else nc.gpsimd
        T.op(eng, lambda: e.tensor_copy(out=out, in_=in_), reads=reads, writes=writes)


def build_program(phases="ABCD", nt_a=NT, no_b=NO, heads_c=8, no_c=NO, no_d=NO):
    CFG = dict(phases=phases, nt_a=nt_a, no_b=no_b, heads_c=heads_c, no_c=no_c, no_d=no_d)
    nc = bass.Bass("TRN2", target_bir_lowering=False)
    dt = nc.dram_tensor
    xa = dt("xa", [S, DM], F32, kind="ExternalInput").ap()
    xo = dt("xo", [SO, DM], F32, kind="ExternalInput").ap()
    w_in = dt("w_in", [DM, 10 * 1024], F32, kind="ExternalInput").ap()
    wbs = dt("wbs", [DM, DM], F32, kind="ExternalInput").ap()
    wbr = dt("wbr", [DM, DM], F32, kind="ExternalInput").ap()
    wout = dt("wout", [DM, DM], F32, kind="ExternalInput").ap()
    gain_rep = dt("gain_rep", [128, DM], F32, kind="ExternalInput").ap()
    rgain_rep = dt("rgain_rep", [128, DM], F32, kind="ExternalInput").ap()
    qkg = dt("qkg", [128, 2], F32, kind="ExternalInput").ap()
    bmg = dt("bmg", [128, 16], F32, kind="ExternalInput").ap()
    cosq = dt("cosq", [128, S], F32, kind="ExternalInput").ap()
    sinq = dt("sinq", [128, S], F32, kind="ExternalInput").ap()
    cosk = dt("cosk", [128, S], F32, kind="ExternalInput").ap()
    sink = dt("sink", [128, S], F32, kind="ExternalInput").ap()
    dmask_d = dt("dmask", [128, 4 * 128], F32, kind="ExternalInput").ap()
    qdec_d = dt("qdec", [128, 4 * TS], F32, kind="ExternalInput").ap()
    kdect_d = dt("kdect", [128, 4], F32, kind="ExternalInput").ap()
    cmat_d = dt("cmat", [128, 4 * 128], BF16, kind="ExternalInput").ap()
    amask_d = dt("amask", [128, 8 * TS], BF16, kind="ExternalInput").ap()
    blend_d = dt("blend", [128, 2], F32, kind="ExternalInput").ap()
    y = dt("y", [SO, DM], F32, kind="ExternalOutput").ap()
    sk = "ExternalOutput" if DEBUG else "Internal"
    kT_d = dt("kT_d", [8, 128, S], BF16, kind=sk).ap()
    v_d = dt("v_d", [S, DM], BF16, kind=sk).ap()
    qT_d = dt("qT_d", [8, 128, SO], BF16, kind=sk).ap()
    ga_d = dt("ga_d", [8, 128, SO], BF16, kind=sk).ap()
    gb_d = dt("gb_d", [8, 128, SO], BF16, kind=sk).ap()
    sa_d = dt("sa_d", [8, 128, SO], BF16, kind=sk).ap()
    sb_d = dt("sb_d", [8, 128, SO], BF16, kind=sk).ap()
    ret_d = dt("ret_d", [S, DM], F32, kind=sk).ap()
    oaT_d = dt("oaT_d", [8, 128, SO], F32, kind=sk).ap()
    if DEBUG:
        dbg_bf = dt("dbg_bf", [128, 8192], BF16, kind="ExternalOutput").ap()
        dbg_f = dt("dbg_f", [128, 4096], F32, kind="ExternalOutput").ap()

    es = contextlib.ExitStack()
    with es:
        def sb(name, shape, dtype):
            return es.enter_context(nc.sbuf_tensor("t_" + name, shape, dtype))

        cmat = sb("cmat", [128, 4 * 128], BF16)
        ident = cmat[:, 0:128]
        onesm = cmat[:, 128:256]
        trineg = cmat[:, 256:384]
        carryneg = cmat[:, 384:512]
        qkg_t = sb("qkg_t", [128, 2], F32)
        qg_s = sb("qg_s", [128, 1], F32)
        bmg_t = sb("bmg_t", [128, 16], F32)
        blend_t = sb("blend_t", [128, 2], F32)
        ps = [es.enter_context(nc.psum_tensor("ps%d" % k, [128, 512], F32)) for k in range(6)]
        pb = [es.enter_context(nc.psum_tensor("pb%d" % k, [128, 1024], BF16)) for k in range(2)]
        T = es.enter_context(Tracker(nc))

        T.dma("sp", cmat[:], cmat_d, writes=["cmat"])
        T.dma("sp", qkg_t[:], qkg, writes=["qkg"])
        T.dma("sp", bmg_t[:], bmg, writes=["bmg"])
        T.dma("sp", blend_t[:], blend_d, writes=["blend"])
        _ts(T, nc, "dve", qg_s[:], qkg_t[:, 0:1], float(128 ** -0.5), None, ALU.mult, None, ["qkg"], ["qg_s"])

        def load_w(wt, col0, ncols, key, src=w_in):
            for c0 in range(0, ncols, 512):
                T.dma("pool", wt[:, :, c0:c0 + 512],
                      src[:, col0 + c0: col0 + c0 + 512].rearrange("(c p) n -> p c n", p=128),
                      writes=[key])

        def norm_block(xsrc, r0, xt, junk, ss, gain_t, h, hT, col0, keep_x=None):
            T.dma("sp", xt[:], xsrc[r0:r0 + 128, :], writes=[xt.name])
            _act(T, nc, junk[:], xt[:], AF.Square, [xt.name], ["junk", "ss"], accum_out=ss[:, 0:1])
            _act(T, nc, ss[:, 1:2], ss[:, 0:1], AF.Sqrt, ["ss"], ["ss"], bias=EPS, scale=1.0 / DM)
            T.op("dve", lambda: nc.vector.reciprocal(out=ss[:, 2:3], in_=ss[:, 1:2]), reads=["ss"], writes=["ss"])
            _stt(T, nc, h[:], xt[:], ss[:, 2:3], gain_t[:], ALU.mult, ALU.mult, [xt.name, "ss", "gain"], ["h"])
            for c in range(NCH):
                _tr(T, nc, pb[0][:, c * 128:(c + 1) * 128], h[:, c * 128:(c + 1) * 128], ident,
                    ["h", "cmat"], ["pb0"], inc=(c == NCH - 1))
            _copy(T, nc, "act", hT[:, :, col0:col0 + 128], pb[0][:].rearrange("p (c t) -> p c t", c=NCH),
                  ["pb0"], ["hT"])

        def qk_head(W, wkey, wcol, hT, gcol, pz, pm, sq, rs, outt, outkey):
            for c in range(NCH):
                _mm(T, nc, pz[:], W[:, c, wcol:wcol + 128], hT[:, c, :], c == 0, c == NCH - 1,
                    [wkey, "hT"], [pz.name], inc=(c == NCH - 1))
            _act(T, nc, sq[:], pz[:], AF.Square, [pz.name], ["sq"])
            _mm(T, nc, pm[:], onesm, sq[:], True, True, ["cmat", "sq"], [pm.name], True)
            _act(T, nc, rs[:], pm[:], AF.Sqrt, [pm.name], ["rs"], bias=EPS, scale=1.0)
            T.op("dve", lambda: nc.vector.reciprocal(out=rs[:], in_=rs[:]), reads=["rs"], writes=["rs"])
            _stt(T, nc, outt, pz[:], gcol, rs[:], ALU.mult, ALU.mult, [pz.name, "rs", "qkg", "qg_s"], [outkey])

        with contextlib.ExitStack() as pa:
            def sa_(name, shape, dtype):
                return pa.enter_context(nc.sbuf_tensor("t_" + name, shape, dtype))
            Wa = sa_("Wa", [128, NCH, 5 * 1024], BF16)
            gain_t = sa_("gain_t", [128, DM], F32)
            xt = sa_("xt", [128, DM], F32)
            junk = sa_("junk", [128, DM], F32)
            ss = sa_("ss", [128, 4], F32)
            h = sa_("h", [128, DM], BF16)
            hT = sa_("hT", [128, NCH, TS], BF16)
            sq = sa_("sq", [128, TS], BF16)
            rs = sa_("rs", [128, TS], F32)
            kout = [sa_("kout%d" % k, [128, TS], BF16) for k in range(2)]
            vout = [sa_("vout%d" % k, [128, DM], BF16) for k in range(2)]
            cs = sa_("cs", [128, 4, TS], F32)
            ra = sa_("ra", [128, TS], F32)
            rb = sa_("rb", [128, TS], F32)
            qrT = sa_("qrT", [128, 2, TS], BF16)
            krT = sa_("krT", [128, 2, TS], BF16)
            qdT = sa_("qdT", [128, 2, TS], BF16)
            rof = sa_("rof", [128, TS], F32)
            kdt = sa_("kdt", [128, 4, 256], BF16)
            rv = sa_("rv", [128, 4, DM], BF16)
            sT = sa_("sT", [128, 128], BF16)
            St = sa_("St", [128, 4, 2, 256], F32)
            Sb = sa_("Sb", [128, 4, 2, 256], BF16)
            dmask = sa_("dmask", [128, 4, 128], F32)
            qdec = sa_("qdec", [128, 4, TS], F32)
            kdect = sa_("kdect", [128, 4], F32)
            rout = [sa_("rout%d" % k, [128, 256], F32) for k in range(2)]

            T.dma("sp", gain_t[:], gain_rep, writes=["gain"])
            T.dma("sp", dmask[:].rearrange("p h q -> p (h q)"), dmask_d, writes=["dmask"])
            T.dma("sp", qdec[:].rearrange("p h q -> p (h q)"), qdec_d, writes=["qdec"])
            T.dma("sp", kdect[:], kdect_d, writes=["kdect"])
            load_w(Wa[:, :, 0:1024], C_SBK, 1024, "Wa")
            load_w(Wa[:, :, 1024:2048], C_SBV, 1024, "Wa")
            load_w(Wa[:, :, 2048:3072], C_RQ, 1024, "Wa")
            load_w(Wa[:, :, 3072:4096], C_RK, 1024, "Wa")
            load_w(Wa[:, :, 4096:5120], C_RV, 1024, "Wa")
            T.op("dve", lambda: nc.vector.memset(St[:].rearrange("p a b c -> p (a b c)"), 0.0), writes=["St"])
            T.op("dve", lambda: nc.vector.memset(Sb[:].rearrange("p a b c -> p (a b c)"), 0.0), writes=["Sb"])

            for t in range(nt_a if "A" in phases else 0):
                t0 = t * TS
                for blk in range(4):
                    norm_block(xa, t0 + blk * 128, xt, junk, ss, gain_t, h, hT, blk * 128)
                for hd in range(8):
                    ko = kout[hd % 2]
                    qk_head(Wa, "Wa", hd * 128, hT, qkg_t[:, 1:2], ps[hd % 2], ps[2], sq, rs, ko[:], ko.name)
                    T.dma("sp", kT_d[hd, :, t0:t0 + TS], ko[:], reads=[ko.name], writes=["kT_d"])
                for blk in range(4):
                    vo = vout[blk % 2]
                    for g in range(2):
                        pz = ps[(2 * blk + g) % 2]
                        for c in range(NCH):
                            _mm(T, nc, pz[:], hT[:, c, blk * 128:(blk + 1) * 128], Wa[:, c, 1024 + g * 512:1024 + (g + 1) * 512],
                                c == 0, c == NCH - 1, ["hT", "Wa"], [pz.name], inc=(c == NCH - 1))
                        _copy(T, nc, "act" if g == 0 else "dve", vo[:, g * 512:(g + 1) * 512], pz[:], [pz.name], [vo.name])
                    T.dma("sp", v_d[t0 + blk * 128:t0 + (blk + 1) * 128, :], vo[:], reads=[vo.name], writes=["v_d"])
                    for g in range(2):
                        pz = ps[(2 * blk + g) % 2]
                        for c in range(NCH):
                            _mm(T, nc, pz[:], hT[:, c, blk * 128:(blk + 1) * 128], Wa[:, c, 4096 + g * 512:4096 + (g + 1) * 512],
                                c == 0, c == NCH - 1, ["hT", "Wa"], [pz.name], inc=(c == NCH - 1))
                        _copy(T, nc, "act" if g == 0 else "dve", rv[:, blk, g * 512:(g + 1) * 512], pz[:], [pz.name], ["rv"])
                for k, tab in enumerate((cosq, sinq, cosk, sink)):
                    T.dma("sp", cs[:, k, :], tab[:, t0:t0 + TS], writes=["cs"])
                for hd in range(4):
                    for which in range(2):
                        wc = (2048 if which == 0 else 3072) + hd * 256
                        ct, st_ = (cs[:, 0, :], cs[:, 1, :]) if which == 0 else (cs[:, 2, :], cs[:, 3, :])
                        dst = qrT if which == 0 else krT
                        for half in range(2):
                            pz = ps[half]
                            for c in range(NCH):
                                _mm(T, nc, pz[:], Wa[:, c, wc + half * 128: wc + (half + 1) * 128], hT[:, c, :],
                                    c == 0, c == NCH - 1, ["Wa", "hT"], [pz.name], inc=(c == NCH - 1))
                        _tt(T, nc, "dve", ra[:], ps[0][:], ct, ALU.mult, ["ps0", "cs"], ["ra"])
                        _tt(T, nc, "dve", rb[:], ps[1][:], st_, ALU.mult, ["ps1", "cs"], ["rb"])
                        _tt(T, nc, "dve", rof[:], ra[:], rb[:], ALU.subtract, ["ra", "rb"], ["rof"])
                        _copy(T, nc, "act", dst[:, 0, :], rof[:], ["rof"], [dst.name])
                        if which == 0:
                            _tt(T, nc, "dve", qdT[:, 0, :], rof[:], qdec[:, hd, :], ALU.mult, ["rof", "qdec"], ["qdT"])
                        _tt(T, nc, "dve", ra[:], ps[0][:], st_, ALU.mult, ["ps0", "cs"], ["ra"])
                        _tt(T, nc, "dve", rb[:], ps[1][:], ct, ALU.mult, ["ps1", "cs"], ["rb"])
                        _tt(T, nc, "dve", rof[:], ra[:], rb[:], ALU.add, ["ra", "rb"], ["rof"])
                        _copy(T, nc, "act", dst[:, 1, :], rof[:], ["rof"], [dst.name])
                        if which == 0:
                            _tt(T, nc, "dve", qdT[:, 1, :], rof[:], qdec[:, hd, :], ALU.mult, ["rof", "qdec"], ["qdT"])
                    for blk in range(4):
                        for half in range(2):
                            _tr(T, nc, pb[1][:, half * 128:(half + 1) * 128], krT[:, half, blk * 128:(blk + 1) * 128], ident,
                                [krT.name, "cmat"], ["pb1"], inc=(half == 1))
                        _ts(T, nc, "dve", kdt[:, blk, :], pb[1][:, 0:256], kdect[:, hd:hd + 1], None, ALU.mult, None,
                            ["pb1", "kdect"], ["kdt"])
                    if DEBUG and t == 0 and hd == 0:
                        T.dma("sp", dbg_bf[:, 0:1024], kdt[:].rearrange("p a b -> p (a b)"), reads=["kdt"], writes=["dbg"])
                        T.dma("sp", dbg_bf[:, 1024:2048], qdT[:].rearrange("p a b -> p (a b)"), reads=["qdT"], writes=["dbg"])
                        T.dma("sp", dbg_bf[:, 2048:3072], krT[:].rearrange("p a b -> p (a b)"), reads=[krT.name], writes=["dbg"])
                        T.dma("sp", dbg_bf[:, 3072:4096], qrT[:].rearrange("p a b -> p (a b)"), reads=[qrT.name], writes=["dbg"])
                    for blk in range(4):
                        b0 = blk * 128
                        for half in range(2):
                            _mm(T, nc, ps[3][:, 0:128], krT[:, half, b0:b0 + 128], qrT[:, half, b0:b0 + 128],
                                half == 0, half == 1, [krT.name, qrT.name], ["ps3"], inc=(half == 1))
                        _tt(T, nc, "dve", sT[:], ps[3][:, 0:128], dmask[:, hd, :], ALU.mult, ["ps3", "dmask"], ["sT"])
                        po = ps[4]
                        _mm(T, nc, po[:, 0:256], sT[:], rv[:, blk, hd * 256:(hd + 1) * 256], True, False,
                            ["sT", "rv"], ["ps4"], inc=True)
                        for ch in range(2):
                            r0 = ch * 64
                            for half in range(2):
                                _mm(T, nc, po[r0:r0 + 64, 0:256], qdT[:, half, b0 + r0:b0 + r0 + 64], Sb[:, hd, half, :],
                                    False, (half == 1 and ch == 1), ["qdT", "Sb"], ["ps4"], inc=(half == 1))
                            for half in range(2):
                                _mm(T, nc, ps[5][:, half * 256:(half + 1) * 256], kdt[r0:r0 + 64, blk, half * 128:(half + 1) * 128],
                                    rv[r0:r0 + 64, blk, hd * 256:(hd + 1) * 256], True, True, ["kdt", "rv"], ["ps5"], inc=(half == 1))
                            _stt(T, nc, St[:, hd, :, :].rearrange("p a b -> p (a b)"), St[:, hd, :, :].rearrange("p a b -> p (a b)"),
                                 float(G64[hd]), ps[5][:], ALU.mult, ALU.add, ["St", "ps5"], ["St"])
                            _copy(T, nc, "act", Sb[:, hd, :, :].rearrange("p a b -> p (a b)"),
                                  St[:, hd, :, :].rearrange("p a b -> p (a b)"), ["St"], ["Sb"])
                            if DEBUG and t == 0 and hd == 0 and blk == 0:
                                T.dma("sp", dbg_f[:, ch * 512:(ch + 1) * 512], St[:, hd, :, :].rearrange("p a b -> p (a b)"), reads=["St"], writes=["dbg"])
                                T.dma("sp", dbg_bf[:, 4096 + ch * 512:4096 + (ch + 1) * 512], Sb[:, hd, :, :].rearrange("p a b -> p (a b)"), reads=["Sb"], writes=["dbg"])
                        ro = rout[(blk + hd) % 2]
                        _copy(T, nc, "dve", ro[:], po[:, 0:256], ["ps4"], [ro.name])
                        T.dma("sp", ret_d[t0 + b0:t0 + b0 + 128, hd * 256:(hd + 1) * 256], ro[:], reads=[ro.name], writes=["ret_d"])
            T.barrier()
        build_rest(nc, T, locals())
        T.finish()
    return nc


def build_rest(nc, T, L):
    CFG = L["CFG"]
    phases = CFG["phases"]
    ps, pb = L["ps"], L["pb"]
    ident, onesm, trineg, carryneg = L["ident"], L["onesm"], L["trineg"], L["carryneg"]
    qg_s, bmg_t, blend_t = L["qg_s"], L["bmg_t"], L["blend_t"]
    xo, y = L["xo"], L["y"]
    kT_d, v_d, qT_d, ga_d, gb_d, sa_d, sb_d, ret_d, oaT_d = (L[k] for k in
        ("kT_d", "v_d", "qT_d", "ga_d", "gb_d", "sa_d", "sb_d", "ret_d", "oaT_d"))
    load_w, norm_block, qk_head = L["load_w"], L["norm_block"], L["qk_head"]
    wbs, wbr, wout = L["wbs"], L["wbr"], L["wout"]

    with contextlib.ExitStack() as pbs:
        def sa_(name, shape, dtype):
            return pbs.enter_context(nc.sbuf_tensor("t_" + name, shape, dtype))
        Wb = sa_("Wb", [128, NCH, 5 * 1024], BF16)
        gain_t = sa_("gain_tb", [128, DM], F32)
        xt = sa_("xtb", [128, DM], F32)
        junk = sa_("junkb", [128, DM], F32)
        ss = sa_("ssb", [128, 4], F32)
        h = sa_("hb", [128, DM], BF16)
        hT = sa_("hTb", [128, NCH, TS], BF16)
        sq = sa_("sqb", [128, TS], BF16)
        rs = sa_("rsb", [128, TS], F32)
        qout = [sa_("qout%d" % k, [128, TS], BF16) for k in range(2)]
        gout = [sa_("gout%d" % k, [128, TS], BF16) for k in range(4)]
        T.dma("sp", gain_t[:], L["gain_rep"], writes=["gain"])
        for k, col in enumerate((C_SBQ, C_SBG, C_RG, C_MSB, C_MRET)):
            load_w(Wb[:, :, k * 1024:(k + 1) * 1024], col, 1024, "Wb")

        for i in range(CFG["no_b"] if "B" in phases else 0):
            t0 = i * TS
            for blk in range(4):
                norm_block(xo, t0 + blk * 128, xt, junk, ss, gain_t, h, hT, blk * 128)
            for hd in range(8):
                qo = qout[hd % 2]
                qk_head(Wb, "Wb", hd * 128, hT, qg_s[:, 0:1], ps[hd % 2], ps[2], sq, rs, qo[:], qo.name)
                T.dma("sp", qT_d[hd, :, t0:t0 + TS], qo[:], reads=[qo.name], writes=["qT_d"])
            n = 0
            for k, (dst, func) in enumerate(((ga_d, AF.Silu), (gb_d, AF.Silu), (sa_d, AF.Sigmoid), (sb_d, AF.Sigmoid))):
                for c in range(NCH):
                    pz = ps[3 + (n % 3)]
                    go = gout[n % 4]
                    n += 1
                    wc = (k + 1) * 1024 + c * 128
                    for cc in range(NCH):
                        _mm(T, nc, pz[:], Wb[:, cc, wc:wc + 128], hT[:, cc, :], cc == 0, cc == NCH - 1,
                            ["Wb", "hT"], [pz.name], inc=(cc == NCH - 1))
                    if func == AF.Silu:
                        _act(T, nc, go[:], pz[:], func, [pz.name], [go.name])
                    else:
                        bcol = (k - 2) * 8 + c
                        _act(T, nc, go[:], pz[:], func, [pz.name, "bmg"], [go.name], bias=bmg_t[:, bcol:bcol + 1], scale=1.0)
                    T.dma("sp", dst[c, :, t0:t0 + TS], go[:], reads=[go.name], writes=["gates_d"])
        T.barrier()

    with contextlib.ExitStack() as pcs:
        def sa_(name, shape, dtype):
            return pcs.enter_context(nc.sbuf_tensor("t_" + name, shape, dtype))
        KT = [sa_("KT%d" % k, [128, S], BF16) for k in range(2)]
        VV = [sa_("VV%d" % k, [128, S // 128, 128], BF16) for k in range(2)]
        QT = [sa_("QT%d" % k, [128, SO], BF16) for k in range(2)]
        amask = sa_("amask", [128, 8, TS], BF16)
        E = [sa_("E%d" % k, [128, TS], F32) for k in range(2)]
        Lp = [sa_("Lp%d" % k, [128, TS], BF16) for k in range(2)]
        G = [sa_("G%d" % k, [128, TS], F32) for k in range(2)]
        W = [sa_("W%d" % k, [128, TS], BF16) for k in range(2)]
        osb = [sa_("osb%d" % k, [128, TS], F32) for k in range(2)]
        T.dma("sp", amask[:].rearrange("p r q -> p (r q)"), L["amask_d"], writes=["amask"])
        step = 0
        for hd in range(CFG["heads_c"] if "C" in phases else 0):
            sl = hd % 2
            kt, vv, qt = KT[sl], VV[sl], QT[sl]
            for part in range(4):
                T.dma("sp", kt[:, part * 2048:(part + 1) * 2048], kT_d[hd, :, part * 2048:(part + 1) * 2048],
                      reads=["kT_d"], writes=[kt.name])
            for part in range(4):
                T.dma("sp", vv[:, part * 16:(part + 1) * 16, :],
                      v_d[part * 2048:(part + 1) * 2048, hd * 128:(hd + 1) * 128].rearrange("(b p) d -> p b d", p=128),
                      reads=["v_d"], writes=[vv.name])
            T.dma("sp", qt[:], qT_d[hd, :, :], reads=["qT_d"], writes=[qt.name])
            for i in range(CFG["no_c"]):
                nkb = 8 * i + 8
                qsl = qt[:, i * TS:(i + 1) * TS]
                first = True
                cacc, ot = ps[2], ps[3]
                for kb in range(nkb - 1, -1, -1):
                    par = step % 2
                    step += 1
                    Z = ps[par]
                    masked = kb >= 8 * i
                    _mm(T, nc, Z[:], kt[:, kb * 128:(kb + 1) * 128], qsl, True, not masked,
                        [kt.name, qt.name], [Z.name], inc=(not masked))
                    if masked:
                        _mm(T, nc, Z[:], ident, amask[:, kb - 8 * i, :], False, True, ["cmat", "amask"], [Z.name], True)
                    _act(T, nc, E[par][:], Z[:], AF.Exp, [Z.name], [E[par].name])
                    _act(T, nc, Lp[par][:], E[par][:], AF.Ln, [E[par].name], [Lp[par].name], bias=1.0, scale=1.0)
                    _mm(T, nc, cacc[:], trineg, Lp[par][:], first, False, ["cmat", Lp[par].name], ["cacc"], True)
                    _act(T, nc, G[par][:], cacc[:], AF.Exp, ["cacc"], [G[par].name])
                    _mm(T, nc, cacc[:], carryneg, Lp[par][:], False, kb == 0, ["cmat", Lp[par].name], ["cacc"], True)
                    _tt(T, nc, "dve", W[par][:], E[par][:], G[par][:], ALU.mult, [E[par].name, G[par].name], [W[par].name])
                    _mm(T, nc, ot[:], vv[:, kb, :], W[par][:], first, kb == 0, [vv.name, W[par].name], ["ot"], True)
                    first = False
                ob = osb[i % 2]
                _copy(T, nc, "dve", ob[:], ot[:], ["ot"], [ob.name])
                T.dma("sp", oaT_d[hd, :, i * TS:(i + 1) * TS], ob[:], reads=[ob.name], writes=["oaT_d"])
        T.barrier()

    with contextlib.ExitStack() as pds:
        def sa_(name, shape, dtype):
            return pds.enter_context(nc.sbuf_tensor("t_" + name, shape, dtype))
        Wd = sa_("Wd", [128, NCH, 3 * 1024], BF16)
        rgain_t = sa_("rgain_t", [128, DM], F32)
        rA = [sa_("rA%d" % k, [128, DM], F32) for k in range(2)]
        rB = [sa_("rB%d" % k, [128, DM], F32) for k in range(2)]
        rt = sa_("rt", [128, DM], F32)
        junk = sa_("junkd", [128, 256], F32)
        ss = sa_("ssd", [128, 12], F32)
        rn = sa_("rn", [128, DM], BF16)
        RNT = sa_("RNT", [128, NCH, TS], BF16)
        oa = [sa_("oa%d" % k, [128, TS], F32) for k in range(2)]
        ga = [sa_("ga%d" % k, [128, TS], BF16) for k in range(2)]
        gb = [sa_("gb%d" % k, [128, TS], BF16) for k in range(2)]
        sga = [sa_("sga%d" % k, [128, TS], BF16) for k in range(2)]
        sgb = [sa_("sgb%d" % k, [128, TS], BF16) for k in range(2)]
        OAG = sa_("OAG", [128, NCH, TS], BF16)
        OBG = sa_("OBG", [128, NCH, TS], BF16)
        MG = sa_("MG", [128, NCH, TS], BF16)
        t1 = sa_("t1", [128, TS], F32)
        t2 = sa_("t2", [128, TS], F32)
        xr = [sa_("xr%d" % k, [128, DM], F32) for k in range(2)]
        yt = [sa_("yt%d" % k, [128, DM], F32) for k in range(2)]
        T.dma("sp", rgain_t[:], L["rgain_rep"], writes=["rgain"])
        load_w(Wd[:, :, 0:1024], 0, 1024, "Wd", src=wbs)
        load_w(Wd[:, :, 1024:2048], 0, 1024, "Wd", src=wbr)
        load_w(Wd[:, :, 2048:3072], 0, 1024, "Wd", src=wout)
        n = 0
        for i in range(CFG["no_d"] if "D" in phases else 0):
            t0 = i * TS
            for blk in range(4):
                a_, b_ = rA[blk % 2], rB[blk % 2]
                T.dma("sp", a_[:], ret_d[(2 * i) * TS + blk * 128:(2 * i) * TS + (blk + 1) * 128, :], reads=["ret_d"], writes=[a_.name])
                T.dma("sp", b_[:], ret_d[(2 * i + 1) * TS + blk * 128:(2 * i + 1) * TS + (blk + 1) * 128, :], reads=["ret_d"], writes=[b_.name])
                _ts(T, nc, "dve", rt[:], a_[:], blend_t[:, 0:1], None, ALU.mult, None, [a_.name, "blend"], ["rt"])
                _stt(T, nc, rt[:], b_[:], blend_t[:, 1:2], rt[:], ALU.mult, ALU.add, [b_.name, "blend", "rt"], ["rt"])
                for hd in range(4):
                    _act(T, nc, junk[:], rt[:, hd * 256:(hd + 1) * 256], AF.Square, ["rt"], ["junkd", "ssd"],
                         accum_out=ss[:, hd:hd + 1])
                _act(T, nc, ss[:, 4:8], ss[:, 0:4], AF.Sqrt, ["ssd"], ["ssd"], bias=EPS, scale=1.0 / 256)
                T.op("dve", lambda: nc.vector.reciprocal(out=ss[:, 8:12], in_=ss[:, 4:8]), reads=["ssd"], writes=["ssd"])
                for hd in range(4):
                    _stt(T, nc, rn[:, hd * 256:(hd + 1) * 256], rt[:, hd * 256:(hd + 1) * 256], ss[:, 8 + hd:9 + hd],
                         rgain_t[:, hd * 256:(hd + 1) * 256], ALU.mult, ALU.mult, ["rt", "ssd", "rgain"], ["rn"])
                for c in range(NCH):
                    _tr(T, nc, pb[0][:, c * 128:(c + 1) * 128], rn[:, c * 128:(c + 1) * 128], ident,
                        ["rn", "cmat"], ["pb0"], inc=(c == NCH - 1))
                _copy(T, nc, "act", RNT[:, :, blk * 128:(blk + 1) * 128], pb[0][:].rearrange("p (c t) -> p c t", c=NCH),
                      ["pb0"], ["RNT"])
            for c in range(NCH):
                o_, ga_, gb_ = oa[c % 2], ga[c % 2], gb[c % 2]
                T.dma("sp", o_[:], oaT_d[c, :, t0:t0 + TS], reads=["oaT_d"], writes=[o_.name])
                T.dma("sp", ga_[:], ga_d[c, :, t0:t0 + TS], reads=["gates_d"], writes=[ga_.name])
                T.dma("sp", gb_[:], gb_d[c, :, t0:t0 + TS], reads=["gates_d"], writes=[gb_.name])
                _tt(T, nc, "dve", OAG[:, c, :], o_[:], ga_[:], ALU.mult, [o_.name, ga_.name], ["OAG"])
                _tt(T, nc, "dve", OBG[:, c, :], RNT[:, c, :], gb_[:], ALU.mult, ["RNT", gb_.name], ["OBG"])
            for oc in range(NCH):
                sa_t, sb_t = sga[oc % 2], sgb[oc % 2]
                T.dma("sp", sa_t[:], sa_d[oc, :, t0:t0 + TS], reads=["gates_d"], writes=[sa_t.name])
                T.dma("sp", sb_t[:], sb_d[oc, :, t0:t0 + TS], reads=["gates_d"], writes=[sb_t.name])
                for c in range(NCH):
                    _mm(T, nc, ps[0][:], Wd[:, c, oc * 128:(oc + 1) * 128], OAG[:, c, :], c == 0, c == NCH - 1,
                        ["Wd", "OAG"], ["ps0"], inc=(c == NCH - 1))
                for c in range(NCH):
                    _mm(T, nc, ps[1][:], Wd[:, c, 1024 + oc * 128:1024 + (oc + 1) * 128], OBG[:, c, :], c == 0, c == NCH - 1,
                        ["Wd", "OBG"], ["ps1"], inc=(c == NCH - 1))
                _tt(T, nc, "dve", t1[:], ps[0][:], sa_t[:], ALU.mult, ["ps0", sa_t.name], ["t1"])
                _tt(T, nc, "dve", t2[:], ps[1][:], sb_t[:], ALU.mult, ["ps1", sb_t.name], ["t2"])
                _tt(T, nc, "dve", MG[:, oc, :], t1[:], t2[:], ALU.add, ["t1", "t2"], ["MG"])
            for blk in range(4):
                x_, y_ = xr[blk % 2], yt[blk % 2]
                T.dma("sp", x_[:], xo[t0 + blk * 128:t0 + (blk + 1) * 128, :], writes=[x_.name])
                for g in range(2):
                    pz = ps[2 + g]
                    for oc in range(NCH):
                        _mm(T, nc, pz[:], MG[:, oc, blk * 128:(blk + 1) * 128], Wd[:, oc, 2048 + g * 512:2048 + (g + 1) * 512],
                            oc == 0, oc == NCH - 1, ["MG", "Wd"], [pz.name], inc=(oc == NCH - 1))
                    _tt(T, nc, "dve", y_[:, g * 512:(g + 1) * 512], pz[:], x_[:, g * 512:(g + 1) * 512], ALU.add,
                        [pz.name, x_.name], [y_.name])
                T.dma("sp", y[t0 + blk * 128:t0 + (blk + 1) * 128, :], y_[:], reads=[y_.name], writes=["y"])


_CONST_CACHE = {}


def _const_tables():
    if _CONST_CACHE:
        return _CONST_CACHE
    bf = ml_dtypes.bfloat16
    d = 256
    inv_freq = (np.float32(10000.0) ** (-np.arange(0, d, 2, dtype=np.float32) / np.float32(d))).astype(np.float32)
    pos = np.arange(S, dtype=np.float32)
    ang = (pos[None, :] * inv_freq[:, None]).astype(np.float32)
    c64, s64 = np.cos(ang.astype(np.float64)), np.sin(ang.astype(np.float64))
    _CONST_CACHE["cosq"] = c64.astype(np.float32)
    _CONST_CACHE["sinq"] = s64.astype(np.float32)
    _CONST_CACHE["cosk"] = (c64 / 16.0).astype(np.float32)
    _CONST_CACHE["sink"] = (s64 / 16.0).astype(np.float32)
    lg = np.log1p(-np.exp2(-5.0 - np.arange(4, dtype=np.float64)))
    idx = np.arange(128)
    same = (idx[:, None] // 64) == (idx[None, :] // 64)
    dm = np.zeros((128, 4, 128), np.float64)
    for hh in range(4):
        dm[:, hh, :] = np.where(same, np.exp(lg[hh] * np.abs(idx[:, None] - idx[None, :])), 0.0)
    _CONST_CACHE["dmask"] = dm.reshape(128, 512).astype(np.float32)
    i64 = (np.arange(TS) % 64).astype(np.float64)
    qd = np.zeros((128, 4, TS), np.float64)
    for hh in range(4):
        qd[:, hh, :] = np.exp(lg[hh] * (i64 + 1.0))[None, :]
    _CONST_CACHE["qdec"] = qd.reshape(128, 4 * TS).astype(np.float32)
    kd = np.zeros((128, 4), np.float64)
    for hh in range(4):
        kd[:, hh] = np.exp(lg[hh] * (63.0 - (idx % 64)))
    _CONST_CACHE["kdect"] = kd.astype(np.float32)
    cm = np.zeros((128, 4, 128), np.float32)
    cm[:, 0, :] = np.eye(128)
    cm[:, 1, :] = 1.0 / 128.0
    cm[:, 2, :] = np.where(idx[:, None] >= idx[None, :], -1.0, 0.0)
    cm[:, 3, :] = np.where(idx[:, None] < idx[None, :], -1.0, 0.0)
    _CONST_CACHE["cmat"] = cm.reshape(128, 512).astype(bf)
    q = np.arange(TS)
    diag = np.zeros((4, 128, TS), np.float32)
    for r in range(4):
        diag[r] = np.where((128 * r + idx[:, None]) < q[None, :], 0.0, NEG)
    am0 = np.full((128, 8, TS), NEG, np.float32)
    am1 = np.zeros((128, 8, TS), np.float32)
    for r in range(4):
        am0[:, r, :] = diag[r]
        am1[:, 4 + r, :] = diag[r]
    _CONST_CACHE["amask0"] = am0.reshape(128, 8 * TS).astype(bf)
    _CONST_CACHE["amask1"] = am1.reshape(128, 8 * TS).astype(bf)
    return _CONST_CACHE


_NC_CACHE = {}


def kernel(x, norm_gain, w_in, b_merge, sb_q_gain, sb_k_gain, ret_out_gain, w_branch_sb, w_branch_ret, w_out):
    x = np.asarray(x, np.float32)
    C = _const_tables()
    if "nc" not in _NC_CACHE:
        _NC_CACHE["nc"] = build_program()
    nc = _NC_CACHE["nc"]
    f = lambda a: np.ascontiguousarray(np.asarray(a, np.float32))
    w_in0, wbs0, wbr0, wout0 = f(w_in[0]), f(w_branch_sb[0]), f(w_branch_ret[0]), f(w_out[0])
    gain_rep = np.ascontiguousarray(np.broadcast_to(f(norm_gain[0])[None, :], (128, DM)))
    rgain_rep = np.ascontiguousarray(np.broadcast_to(f(ret_out_gain[0]).reshape(1, DM), (128, DM)))
    qkg = np.ascontiguousarray(np.stack([f(sb_q_gain[0]), f(sb_k_gain[0])], axis=1))
    bmg = np.ascontiguousarray(f(b_merge[0]).reshape(2, 8, 128).transpose(2, 0, 1).reshape(128, 16))
    in_maps = []
    for c in range(8):
        b, p = c // 2, c % 2
        xb = x[b]
        xown = np.ascontiguousarray(xb.reshape(NT, TS, DM)[p::2].reshape(SO, DM))
        blend = np.zeros((128, 2), np.float32)
        blend[:, 0] = 1.0 - p
        blend[:, 1] = float(p)
        in_maps.append({
            "xa": np.ascontiguousarray(xb), "xo": xown, "w_in": w_in0, "wbs": wbs0, "wbr": wbr0, "wout": wout0,
            "gain_rep": gain_rep, "rgain_rep": rgain_rep, "qkg": qkg, "bmg": bmg,
            "cosq": C["cosq"], "sinq": C["sinq"], "cosk": C["cosk"], "sink": C["sink"],
            "dmask": C["dmask"], "qdec": C["qdec"], "kdect": C["kdect"], "cmat": C["cmat"],
            "amask": C["amask%d" % p], "blend": blend,
        })
    res = run_bass_kernel_spmd(nc, in_maps, core_ids=list(range(8)))
    _NC_CACHE["last"] = res
    out = np.empty((4, S, DM), np.float32)
    for c in range(8):
        b, p = c // 2, c % 2
        out[b].reshape(NT, TS, DM)[p::2] = np.asarray(res.results[c]["y"], np.float32).reshape(NO, TS, DM)
    return out
```

```python
import contextlib
import numpy as np
import ml_dtypes
import concourse.bass as bass
import concourse.mybir as mybir
from concourse.bass_utils import run_bass_kernel_spmd

F32 = mybir.dt.float32
BF16 = mybir.dt.bfloat16
AF = mybir.ActivationFunctionType
ALU = mybir.AluOpType
AX = mybir.AxisListType


class Tracker:
    ENG = ("pe", "act", "dve", "pool", "sp")

    def __init__(self, nc, n_dma_sems=6):
        self.nc = nc
        self.n_dma = n_dma_sems
        self.stack = contextlib.ExitStack()
        self.streams = {e: [] for e in self.ENG}
        self.count = {}
        self.known = {e: {} for e in self.ENG}
        self.last_write = {}
        self.readers = {}
        self.n_ops = 0

    def __enter__(self):
        nc = self.nc
        self.stack.__enter__()
        self.sem = {}
        for e in self.ENG:
            self.sem[e] = self.stack.enter_context(nc.semaphore("s_" + e))
            self.count[e] = 0
        self.dma_ring = {}
        self.dma_next = {}
        for q in ("sp", "pool", "act"):
            ring = []
            for k in range(self.n_dma):
                name = "d_%s%d" % (q, k)
                self.sem[name] = self.stack.enter_context(nc.semaphore(name))
                self.count[name] = 0
                ring.append(name)
            self.dma_ring[q] = ring
            self.dma_next[q] = 0
        return self

    def __exit__(self, *a):
        return self.stack.__exit__(*a)

    def _deps(self, reads, writes):
        deps = {}

        def add(s, v):
            if v > deps.get(s, 0):
                deps[s] = v

        for b in reads:
            lw = self.last_write.get(b)
            if lw:
                add(*lw)
        for b in writes:
            lw = self.last_write.get(b)
            if lw:
                add(*lw)
            for s, v in self.readers.get(b, {}).items():
                add(s, v)
        return deps

    def _emit_waits(self, e, deps, skip_self=False):
        for s, v in deps.items():
            if skip_self and s == e:
                continue
            if self.known[e].get(s, 0) >= v:
                continue
            self.known[e][s] = v
            sem = self.sem[s]
            self.streams[e].append(("wait", sem, v))

    def _record(self, key, val, reads, writes):
        for b in reads:
            self.readers.setdefault(b, {})[key] = val
        for b in writes:
            self.last_write[b] = (key, val)
            self.readers[b] = {}

    def op(self, e, fn, reads=(), writes=(), inc=True):
        deps = self._deps(reads, writes)
        self._emit_waits(e, deps, skip_self=(e == "pe"))
        if inc:
            self.count[e] += 1
            val = self.count[e]
            self.streams[e].append(("op", fn, self.sem[e], 1))
        else:
            val = self.count[e] + 1
            self.streams[e].append(("op", fn, None, 0))
        self._record(e, val, reads, writes)
        self.n_ops += 1

    def dma(self, q, out, in_, reads=(), writes=(), **kw):
        deps = self._deps(reads, writes)
        name = self.dma_ring[q][self.dma_next[q]]
        self.dma_next[q] = (self.dma_next[q] + 1) % self.n_dma
        deps[name] = max(deps.get(name, 0), self.count[name])
        self._emit_waits(q, deps)
        self.count[name] += 16
        val = self.count[name]
        self.known[q][name] = max(self.known[q].get(name, 0), 0)
        eng = {"sp": self.nc.sync, "pool": self.nc.gpsimd, "act": self.nc.scalar}[q]
        self.streams[q].append(("op", (lambda: eng.dma_start(out=out, in_=in_, **kw)), self.sem[name], 16))
        self._record(name, val, reads, writes)
        self.n_ops += 1

    def barrier(self):
        for e in self.ENG:
            deps = {s: c for s, c in self.count.items() if c > 0 and s != e}
            self._emit_waits(e, deps)

    def finish(self):
        nc = self.nc
        deps = {s: c for s, c in self.count.items() if c > 0 and s != "sp"}
        self._emit_waits("sp", deps)
        streams = self.streams

        def replay(e, eng):
            for item in streams[e]:
                if item[0] == "wait":
                    eng.wait_ge(item[1], item[2])
                else:
                    ins = item[1]()
                    if item[2] is not None:
                        ins.then_inc(item[2], item[3])

        with nc.Block() as block:
            @block.sync
            def _(eng):
                replay("sp", eng)

            @block.scalar
            def _(eng):
                replay("act", eng)

            @block.vector
            def _(eng):
                replay("dve", eng)

            @block.gpsimd
            def _(eng):
                replay("pool", eng)

            @block.tensor
            def _(eng):
                replay("pe", eng)


S = 8192
DM = 1024
TS = 512
NT = S // TS
NO = NT // 2
SO = NO * TS
NCH = DM // 128
EPS = 1e-6
NEG = -30000.0
C_SBQ, C_SBK, C_SBV, C_SBG, C_RQ, C_RK, C_RV, C_RG, C_MSB, C_MRET = [i * 1024 for i in range(10)]
GAMMA = [1.0 - 2.0 ** (-5.0 - h) for h in range(4)]
G64 = [g ** 64 for g in GAMMA]
DEBUG = False


def _mm(T, nc, out, lhsT, rhs, start, stop, reads, writes, inc):
    T.op("pe", lambda: nc.tensor.matmul(out, lhsT=lhsT, rhs=rhs, start=start, stop=stop),
         reads=reads, writes=writes, inc=inc)


def _tr(T, nc, out, in_, ident, reads, writes, inc):
    T.op("pe", lambda: nc.tensor.transpose(out, in_, ident), reads=reads, writes=writes, inc=inc)


def _act(T, nc, out, in_, func, reads, writes, **kw):
    T.op("act", lambda: nc.scalar.activation(out=out, in_=in_, func=func, **kw), reads=reads, writes=writes)


def _ts(T, nc, eng, out, in0, s1, s2, op0, op1, reads, writes):
    e = nc.vector if eng == "dve" else nc.gpsimd
    if op1 is None:
        T.op(eng, lambda: e.tensor_scalar(out=out, in0=in0, scalar1=s1, scalar2=None, op0=op0), reads=reads, writes=writes)
    else:
        T.op(eng, lambda: e.tensor_scalar(out=out, in0=in0, scalar1=s1, scalar2=s2, op0=op0, op1=op1), reads=reads, writes=writes)


def _tt(T, nc, eng, out, in0, in1, op, reads, writes):
    e = nc.vector if eng == "dve" else nc.gpsimd
    T.op(eng, lambda: e.tensor_tensor(out=out, in0=in0, in1=in1, op=op), reads=reads, writes=writes)


def _stt(T, nc, out, in0, scalar, in1, op0, op1, reads, writes):
    T.op("dve", lambda: nc.vector.scalar_tensor_tensor(out=out, in0=in0, scalar=scalar, in1=in1, op0=op0, op1=op1),
         reads=reads, writes=writes)


def _copy(T, nc, eng, out, in_, reads, writes):
    if eng == "act":
        T.op("act", lambda: nc.scalar.copy(out=out, in_=in_), reads=reads, writes=writes)
    else:
        e = nc.vector if eng == "dve" else nc.gpsimd
        T.op(eng, lambda: e.tensor_copy(out=out, in_=in_), reads=reads, writes=writes)


def build_program(phases="ABCD", nt_a=NT, no_b=NO, heads_c=8, no_c=NO, no_d=NO):
    CFG = dict(phases=phases, nt_a=nt_a, no_b=no_b, heads_c=heads_c, no_c=no_c, no_d=no_d)
    nc = bass.Bass("TRN2", target_bir_lowering=False)
    dt = nc.dram_tensor
    xa = dt("xa", [S, DM], F32, kind="ExternalInput").ap()
    xo = dt("xo", [SO, DM], F32, kind="ExternalInput").ap()
    w_in = dt("w_in", [DM, 10 * 1024], F32, kind="ExternalInput").ap()
    wbs = dt("wbs", [DM, DM], F32, kind="ExternalInput").ap()
    wbr = dt("wbr", [DM, DM], F32, kind="ExternalInput").ap()
    wout = dt("wout", [DM, DM], F32, kind="ExternalInput").ap()
    gain_rep = dt("gain_rep", [128, DM], F32, kind="ExternalInput").ap()
    rgain_rep = dt("rgain_rep", [128, DM], F32, kind="ExternalInput").ap()
    qkg = dt("qkg", [128, 2], F32, kind="ExternalInput").ap()
    bmg = dt("bmg", [128, 16], F32, kind="ExternalInput").ap()
    cosq = dt("cosq", [128, S], F32, kind="ExternalInput").ap()
    sinq = dt("sinq", [128, S], F32, kind="ExternalInput").ap()
    cosk = dt("cosk", [128, S], F32, kind="ExternalInput").ap()
    sink = dt("sink", [128, S], F32, kind="ExternalInput").ap()
    dmask_d = dt("dmask", [128, 4 * 128], F32, kind="ExternalInput").ap()
    qdec_d = dt("qdec", [128, 4 * TS], F32, kind="ExternalInput").ap()
    kdect_d = dt("kdect", [128, 4], F32, kind="ExternalInput").ap()
    cmat_d = dt("cmat", [128, 4 * 128], BF16, kind="ExternalInput").ap()
    amask_d = dt("amask", [128, 8 * TS], BF16, kind="ExternalInput").ap()
    blend_d = dt("blend", [128, 2], F32, kind="ExternalInput").ap()
    y = dt("y", [SO, DM], F32, kind="ExternalOutput").ap()
    sk = "ExternalOutput" if DEBUG else "Internal"
    kT_d = dt("kT_d", [8, 128, S], BF16, kind=sk).ap()
    v_d = dt("v_d", [S, DM], BF16, kind=sk).ap()
    qT_d = dt("qT_d", [8, 128, SO], BF16, kind=sk).ap()
    ga_d = dt("ga_d", [8, 128, SO], BF16, kind=sk).ap()
    gb_d = dt("gb_d", [8, 128, SO], BF16, kind=sk).ap()
    sa_d = dt("sa_d", [8, 128, SO], BF16, kind=sk).ap()
    sb_d = dt("sb_d", [8, 128, SO], BF16, kind=sk).ap()
    ret_d = dt("ret_d", [S, DM], F32, kind=sk).ap()
    oaT_d = dt("oaT_d", [8, 128, SO], F32, kind=sk).ap()
    if DEBUG:
        dbg_bf = dt("dbg_bf", [128, 8192], BF16, kind="ExternalOutput").ap()
        dbg_f = dt("dbg_f", [128, 4096], F32, kind="ExternalOutput").ap()

    es = contextlib.ExitStack()
    with es:
        def sb(name, shape, dtype):
            return es.enter_context(nc.sbuf_tensor("t_" + name, shape, dtype))

        cmat = sb("cmat", [128, 4 * 128], BF16)
        ident = cmat[:, 0:128]
        onesm = cmat[:, 128:256]
        trineg = cmat[:, 256:384]
        carryneg = cmat[:, 384:512]
        qkg_t = sb("qkg_t", [128, 2], F32)
        qg_s = sb("qg_s", [128, 1], F32)
        bmg_t = sb("bmg_t", [128, 16], F32)
        blend_t = sb("blend_t", [128, 2], F32)
        ps = [es.enter_context(nc.psum_tensor("ps%d" % k, [128, 512], F32)) for k in range(6)]
        pb = [es.enter_context(nc.psum_tensor("pb%d" % k, [128, 1024], BF16)) for k in range(2)]
        T = es.enter_context(Tracker(nc))

        T.dma("sp", cmat[:], cmat_d, writes=["cmat"])
        T.dma("sp", qkg_t[:], qkg, writes=["qkg"])
        T.dma("sp", bmg_t[:], bmg, writes=["bmg"])
        T.dma("sp", blend_t[:], blend_d, writes=["blend"])
        _ts(T, nc, "dve", qg_s[:], qkg_t[:, 0:1], float(128 ** -0.5), None, ALU.mult, None, ["qkg"], ["qg_s"])

        def load_w(wt, col0, ncols, key, src=w_in):
            for c0 in range(0, ncols, 512):
                T.dma("pool", wt[:, :, c0:c0 + 512],
                      src[:, col0 + c0: col0 + c0 + 512].rearrange("(c p) n -> p c n", p=128),
                      writes=[key])

        def norm_block(xsrc, r0, xt, junk, ss, gain_t, h, hT, col0, keep_x=None):
            T.dma("sp", xt[:], xsrc[r0:r0 + 128, :], writes=[xt.name])
            _act(T, nc, junk[:], xt[:], AF.Square, [xt.name], ["junk", "ss"], accum_out=ss[:, 0:1])
            _act(T, nc, ss[:, 1:2], ss[:, 0:1], AF.Sqrt, ["ss"], ["ss"], bias=EPS, scale=1.0 / DM)
            T.op("dve", lambda: nc.vector.reciprocal(out=ss[:, 2:3], in_=ss[:, 1:2]), reads=["ss"], writes=["ss"])
            _stt(T, nc, h[:], xt[:], ss[:, 2:3], gain_t[:], ALU.mult, ALU.mult, [xt.name, "ss", "gain"], ["h"])
            for c in range(NCH):
                _tr(T, nc, pb[0][:, c * 128:(c + 1) * 128], h[:, c * 128:(c + 1) * 128], ident,
                    ["h", "cmat"], ["pb0"], inc=(c == NCH - 1))
            _copy(T, nc, "act", hT[:, :, col0:col0 + 128], pb[0][:].rearrange("p (c t) -> p c t", c=NCH),
                  ["pb0"], ["hT"])

        def qk_head(W, wkey, wcol, hT, gcol, pz, pm, sq, rs, outt, outkey):
            for c in range(NCH):
                _mm(T, nc, pz[:], W[:, c, wcol:wcol + 128], hT[:, c, :], c == 0, c == NCH - 1,
                    [wkey, "hT"], [pz.name], inc=(c == NCH - 1))
            _act(T, nc, sq[:], pz[:], AF.Square, [pz.name], ["sq"])
            _mm(T, nc, pm[:], onesm, sq[:], True, True, ["cmat", "sq"], [pm.name], True)
            _act(T, nc, rs[:], pm[:], AF.Sqrt, [pm.name], ["rs"], bias=EPS, scale=1.0)
            T.op("dve", lambda: nc.vector.reciprocal(out=rs[:], in_=rs[:]), reads=["rs"], writes=["rs"])
            _stt(T, nc, outt, pz[:], gcol, rs[:], ALU.mult, ALU.mult, [pz.name, "rs", "qkg", "qg_s"], [outkey])

        with contextlib.ExitStack() as pa:
            def sa_(name, shape, dtype):
                return pa.enter_context(nc.sbuf_tensor("t_" + name, shape, dtype))
            Wa = sa_("Wa", [128, NCH, 5 * 1024], BF16)
            gain_t = sa_("gain_t", [128, DM], F32)
            xt = sa_("xt", [128, DM], F32)
            junk = sa_("junk", [128, DM], F32)
            ss = sa_("ss", [128, 4], F32)
            h = sa_("h", [128, DM], BF16)
            hT = sa_("hT", [128, NCH, TS], BF16)
            sq = sa_("sq", [128, TS], BF16)
            rs = sa_("rs", [128, TS], F32)
            kout = [sa_("kout%d" % k, [128, TS], BF16) for k in range(2)]
            vout = [sa_("vout%d" % k, [128, DM], BF16) for k in range(2)]
            cs = sa_("cs", [128, 4, TS], F32)
            ra = sa_("ra", [128, TS], F32)
            rb = sa_("rb", [128, TS], F32)
            qrT = sa_("qrT", [128, 2, TS], BF16)
            krT = sa_("krT", [128, 2, TS], BF16)
            qdT = sa_("qdT", [128, 2, TS], BF16)
            rof = sa_("rof", [128, TS], F32)
            kdt = sa_("kdt", [128, 4, 256], BF16)
            rv = sa_("rv", [128, 4, DM], BF16)
            sT = sa_("sT", [128, 128], BF16)
            St = sa_("St", [128, 4, 2, 256], F32)
            Sb = sa_("Sb", [128, 4, 2, 256], BF16)
            dmask = sa_("dmask", [128, 4, 128], F32)
            qdec = sa_("qdec", [128, 4, TS], F32)
            kdect = sa_("kdect", [128, 4], F32)
            rout = [sa_("rout%d" % k, [128, 256], F32) for k in range(2)]

            T.dma("sp", gain_t[:], gain_rep, writes=["gain"])
            T.dma("sp", dmask[:].rearrange("p h q -> p (h q)"), dmask_d, writes=["dmask"])
            T.dma("sp", qdec[:].rearrange("p h q -> p (h q)"), qdec_d, writes=["qdec"])
            T.dma("sp", kdect[:], kdect_d, writes=["kdect"])
            load_w(Wa[:, :, 0:1024], C_SBK, 1024, "Wa")
            load_w(Wa[:, :, 1024:2048], C_SBV, 1024, "Wa")
            load_w(Wa[:, :, 2048:3072], C_RQ, 1024, "Wa")
            load_w(Wa[:, :, 3072:4096], C_RK, 1024, "Wa")
            load_w(Wa[:, :, 4096:5120], C_RV, 1024, "Wa")
            T.op("dve", lambda: nc.vector.memset(St[:].rearrange("p a b c -> p (a b c)"), 0.0), writes=["St"])
            T.op("dve", lambda: nc.vector.memset(Sb[:].rearrange("p a b c -> p (a b c)"), 0.0), writes=["Sb"])

            for t in range(nt_a if "A" in phases else 0):
                t0 = t * TS
                for blk in range(4):
                    norm_block(xa, t0 + blk * 128, xt, junk, ss, gain_t, h, hT, blk * 128)
                for hd in range(8):
                    ko = kout[hd % 2]
                    qk_head(Wa, "Wa", hd * 128, hT, qkg_t[:, 1:2], ps[hd % 2], ps[2], sq, rs, ko[:], ko.name)
                    T.dma("sp", kT_d[hd, :, t0:t0 + TS], ko[:], reads=[ko.name], writes=["kT_d"])
                for blk in range(4):
                    vo = vout[blk % 2]
                    for g in range(2):
                        pz = ps[(2 * blk + g) % 2]
                        for c in range(NCH):
                            _mm(T, nc, pz[:], hT[:, c, blk * 128:(blk + 1) * 128], Wa[:, c, 1024 + g * 512:1024 + (g + 1) * 512],
                                c == 0, c == NCH - 1, ["hT", "Wa"], [pz.name], inc=(c == NCH - 1))
                        _copy(T, nc, "act" if g == 0 else "dve", vo[:, g * 512:(g + 1) * 512], pz[:], [pz.name], [vo.name])
                    T.dma("sp", v_d[t0 + blk * 128:t0 + (blk + 1) * 128, :], vo[:], reads=[vo.name], writes=["v_d"])
                    for g in range(2):
                        pz = ps[(2 * blk + g) % 2]
                        for c in range(NCH):
                            _mm(T, nc, pz[:], hT[:, c, blk * 128:(blk + 1) * 128], Wa[:, c, 4096 + g * 512:4096 + (g + 1) * 512],
                                c == 0, c == NCH - 1, ["hT", "Wa"], [pz.name], inc=(c == NCH - 1))
                        _copy(T, nc, "act" if g == 0 else "dve", rv[:, blk, g * 512:(g + 1) * 512], pz[:], [pz.name], ["rv"])
                for k, tab in enumerate((cosq, sinq, cosk, sink)):
                    T.dma("sp", cs[:, k, :], tab[:, t0:t0 + TS], writes=["cs"])
                for hd in range(4):
                    for which in range(2):
                        wc = (2048 if which == 0 else 3072) + hd * 256
                        ct, st_ = (cs[:, 0, :], cs[:, 1, :]) if which == 0 else (cs[:, 2, :], cs[:, 3, :])
                        dst = qrT if which == 0 else krT
                        for half in range(2):
                            pz = ps[half]
                            for c in range(NCH):
                                _mm(T, nc, pz[:], Wa[:, c, wc + half * 128: wc + (half + 1) * 128], hT[:, c, :],
                                    c == 0, c == NCH - 1, ["Wa", "hT"], [pz.name], inc=(c == NCH - 1))
                        _tt(T, nc, "dve", ra[:], ps[0][:], ct, ALU.mult, ["ps0", "cs"], ["ra"])
                        _tt(T, nc, "dve", rb[:], ps[1][:], st_, ALU.mult, ["ps1", "cs"], ["rb"])
                        _tt(T, nc, "dve", rof[:], ra[:], rb[:], ALU.subtract, ["ra", "rb"], ["rof"])
                        _copy(T, nc, "act", dst[:, 0, :], rof[:], ["rof"], [dst.name])
                        if which == 0:
                            _tt(T, nc, "dve", qdT[:, 0, :], rof[:], qdec[:, hd, :], ALU.mult, ["rof", "qdec"], ["qdT"])
                        _tt(T, nc, "dve", ra[:], ps[0][:], st_, ALU.mult, ["ps0", "cs"], ["ra"])
                        _tt(T, nc, "dve", rb[:], ps[1][:], ct, ALU.mult, ["ps1", "cs"], ["rb"])
                        _tt(T, nc, "dve", rof[:], ra[:], rb[:], ALU.add, ["ra", "rb"], ["rof"])
                        _copy(T, nc, "act", dst[:, 1, :], rof[:], ["rof"], [dst.name])
                        if which == 0:
                            _tt(T, nc, "dve", qdT[:, 1, :], rof[:], qdec[:, hd, :], ALU.mult, ["rof", "qdec"], ["qdT"])
                    for blk in range(4):
                        for half in range(2):
                            _tr(T, nc, pb[1][:, half * 128:(half + 1) * 128], krT[:, half, blk * 128:(blk + 1) * 128], ident,
                                [krT.name, "cmat"], ["pb1"], inc=(half == 1))
                        _ts(T, nc, "dve", kdt[:, blk, :], pb[1][:, 0:256], kdect[:, hd:hd + 1], None, ALU.mult, None,
                            ["pb1", "kdect"], ["kdt"])
                    if DEBUG and t == 0 and hd == 0:
                        T.dma("sp", dbg_bf[:, 0:1024], kdt[:].rearrange("p a b -> p (a b)"), reads=["kdt"], writes=["dbg"])
                        T.dma("sp", dbg_bf[:, 1024:2048], qdT[:].rearrange("p a b -> p (a b)"), reads=["qdT"], writes=["dbg"])
                        T.dma("sp", dbg_bf[:, 2048:3072], krT[:].rearrange("p a b -> p (a b)"), reads=[krT.name], writes=["dbg"])
                        T.dma("sp", dbg_bf[:, 3072:4096], qrT[:].rearrange("p a b -> p (a b)"), reads=[qrT.name], writes=["dbg"])
                    for blk in range(4):
                        b0 = blk * 128
                        for half in range(2):
                            _mm(T, nc, ps[3][:, 0:128], krT[:, half, b0:b0 + 128], qrT[:, half, b0:b0 + 128],
                                half == 0, half == 1, [krT.name, qrT.name], ["ps3"], inc=(half == 1))
                        _tt(T, nc, "dve", sT[:], ps[3][:, 0:128], dmask[:, hd, :], ALU.mult, ["ps3", "dmask"], ["sT"])
                        po = ps[4]
                        _mm(T, nc, po[:, 0:256], sT[:], rv[:, blk, hd * 256:(hd + 1) * 256], True, False,
                            ["sT", "rv"], ["ps4"], inc=True)
                        for ch in range(2):
                            r0 = ch * 64
                            for half in range(2):
                                _mm(T, nc, po[r0:r0 + 64, 0:256], qdT[:, half, b0 + r0:b0 + r0 + 64], Sb[:, hd, half, :],
                                    False, (half == 1 and ch == 1), ["qdT", "Sb"], ["ps4"], inc=(half == 1))
                            for half in range(2):
                                _mm(T, nc, ps[5][:, half * 256:(half + 1) * 256], kdt[r0:r0 + 64, blk, half * 128:(half + 1) * 128],
                                    rv[r0:r0 + 64, blk, hd * 256:(hd + 1) * 256], True, True, ["kdt", "rv"], ["ps5"], inc=(half == 1))
                            _stt(T, nc, St[:, hd, :, :].rearrange("p a b -> p (a b)"), St[:, hd, :, :].rearrange("p a b -> p (a b)"),
                                 float(G64[hd]), ps[5][:], ALU.mult, ALU.add, ["St", "ps5"], ["St"])
                            _copy(T, nc, "act", Sb[:, hd, :, :].rearrange("p a b -> p (a b)"),
                                  St[:, hd, :, :].rearrange("p a b -> p (a b)"), ["St"], ["Sb"])
                            if DEBUG and t == 0 and hd == 0 and blk == 0:
                                T.dma("sp", dbg_f[:, ch * 512:(ch + 1) * 512], St[:, hd, :, :].rearrange("p a b -> p (a b)"), reads=["St"], writes=["dbg"])
                                T.dma("sp", dbg_bf[:, 4096 + ch * 512:4096 + (ch + 1) * 512], Sb[:, hd, :, :].rearrange("p a b -> p (a b)"), reads=["Sb"], writes=["dbg"])
                        ro = rout[(blk + hd) % 2]
                        _copy(T, nc, "dve", ro[:], po[:, 0:256], ["ps4"], [ro.name])
                        T.dma("sp", ret_d[t0 + b0:t0 + b0 + 128, hd * 256:(hd + 1) * 256], ro[:], reads=[ro.name], writes=["ret_d"])
            T.barrier()
        build_rest(nc, T, locals())
        T.finish()
    return nc


def build_rest(nc, T, L):
    CFG = L["CFG"]
    phases = CFG["phases"]
    ps, pb = L["ps"], L["pb"]
    ident, onesm, trineg, carryneg = L["ident"], L["onesm"], L["trineg"], L["carryneg"]
    qg_s, bmg_t, blend_t = L["qg_s"], L["bmg_t"], L["blend_t"]
    xo, y = L["xo"], L["y"]
    kT_d, v_d, qT_d, ga_d, gb_d, sa_d, sb_d, ret_d, oaT_d = (L[k] for k in
        ("kT_d", "v_d", "qT_d", "ga_d", "gb_d", "sa_d", "sb_d", "ret_d", "oaT_d"))
    load_w, norm_block, qk_head = L["load_w"], L["norm_block"], L["qk_head"]
    wbs, wbr, wout = L["wbs"], L["wbr"], L["wout"]

    with contextlib.ExitStack() as pbs:
        def sa_(name, shape, dtype):
            return pbs.enter_context(nc.sbuf_tensor("t_" + name, shape, dtype))
        Wb = sa_("Wb", [128, NCH, 5 * 1024], BF16)
        gain_t = sa_("gain_tb", [128, DM], F32)
        xt = sa_("xtb", [128, DM], F32)
        junk = sa_("junkb", [128, DM], F32)
        ss = sa_("ssb", [128, 4], F32)
        h = sa_("hb", [128, DM], BF16)
        hT = sa_("hTb", [128, NCH, TS], BF16)
        sq = sa_("sqb", [128, TS], BF16)
        rs = sa_("rsb", [128, TS], F32)
        qout = [sa_("qout%d" % k, [128, TS], BF16) for k in range(2)]
        gout = [sa_("gout%d" % k, [128, TS], BF16) for k in range(4)]
        T.dma("sp", gain_t[:], L["gain_rep"], writes=["gain"])
        for k, col in enumerate((C_SBQ, C_SBG, C_RG, C_MSB, C_MRET)):
            load_w(Wb[:, :, k * 1024:(k + 1) * 1024], col, 1024, "Wb")

        for i in range(CFG["no_b"] if "B" in phases else 0):
            t0 = i * TS
            for blk in range(4):
                norm_block(xo, t0 + blk * 128, xt, junk, ss, gain_t, h, hT, blk * 128)
            for hd in range(8):
                qo = qout[hd % 2]
                qk_head(Wb, "Wb", hd * 128, hT, qg_s[:, 0:1], ps[hd % 2], ps[2], sq, rs, qo[:], qo.name)
                T.dma("sp", qT_d[hd, :, t0:t0 + TS], qo[:], reads=[qo.name], writes=["qT_d"])
            n = 0
            for k, (dst, func) in enumerate(((ga_d, AF.Silu), (gb_d, AF.Silu), (sa_d, AF.Sigmoid), (sb_d, AF.Sigmoid))):
                for c in range(NCH):
                    pz = ps[3 + (n % 3)]
                    go = gout[n % 4]
                    n += 1
                    wc = (k + 1) * 1024 + c * 128
                    for cc in range(NCH):
                        _mm(T, nc, pz[:], Wb[:, cc, wc:wc + 128], hT[:, cc, :], cc == 0, cc == NCH - 1,
                            ["Wb", "hT"], [pz.name], inc=(cc == NCH - 1))
                    if func == AF.Silu:
                        _act(T, nc, go[:], pz[:], func, [pz.name], [go.name])
                    else:
                        bcol = (k - 2) * 8 + c
                        _act(T, nc, go[:], pz[:], func, [pz.name, "bmg"], [go.name], bias=bmg_t[:, bcol:bcol + 1], scale=1.0)
                    T.dma("sp", dst[c, :, t0:t0 + TS], go[:], reads=[go.name], writes=["gates_d"])
        T.barrier()

    with contextlib.ExitStack() as pcs:
        def sa_(name, shape, dtype):
            return pcs.enter_context(nc.sbuf_tensor("t_" + name, shape, dtype))
        KT = [sa_("KT%d" % k, [128, S], BF16) for k in range(2)]
        VV = [sa_("VV%d" % k, [128, S // 128, 128], BF16) for k in range(2)]
        QT = [sa_("QT%d" % k, [128, SO], BF16) for k in range(2)]
        amask = sa_("amask", [128, 8, TS], BF16)
        E = [[sa_("E%d_%d" % (st, k), [128, TS], F32) for k in range(2)] for st in range(2)]
        Lp = [[sa_("Lp%d_%d" % (st, k), [128, TS], BF16) for k in range(2)] for st in range(2)]
        G = [[sa_("G%d_%d" % (st, k), [128, TS], F32) for k in range(2)] for st in range(2)]
        W = [[sa_("W%d_%d" % (st, k), [128, TS], BF16) for k in range(2)] for st in range(2)]
        osb = [sa_("osb%d" % k, [128, TS], F32) for k in range(2)]
        T.dma("sp", amask[:].rearrange("p r q -> p (r q)"), L["amask_d"], writes=["amask"])
        pz8 = [ps[k][:] for k in range(6)] + [pb[0][:].bitcast(F32), pb[1][:].bitcast(F32)]
        pn8 = [ps[k].name for k in range(6)] + [pb[0].name, pb[1].name]
        zb = [[(pz8[0], pn8[0]), (pz8[1], pn8[1])], [(pz8[2], pn8[2]), (pz8[3], pn8[3])]]
        cacc = [(pz8[4], pn8[4]), (pz8[5], pn8[5])]
        otb = [(pz8[6], pn8[6]), (pz8[7], pn8[7])]
        seqs = ([0, 3, 4, 7], [1, 2, 5, 6])
        steps = [[(i, kb) for i in seq for kb in range(8 * i + 7, -1, -1)] for seq in seqs]
        NS = len(steps[0])
        assert NS == len(steps[1])
        if CFG["no_c"] < NO:
            seqs = ([0], [0]) if CFG["no_c"] == 1 else seqs
            steps = [[(i, kb) for i in seq for kb in range(8 * i + 7, -1, -1)] for seq in seqs]
            NS = len(steps[0])
        nosb = [0]
        for hd in range(CFG["heads_c"] if "C" in phases else 0):
            sl = hd % 2
            kt, vv, qt = KT[sl], VV[sl], QT[sl]
            for part in range(4):
                T.dma("sp", kt[:, part * 2048:(part + 1) * 2048], kT_d[hd, :, part * 2048:(part + 1) * 2048],
                      reads=["kT_d"], writes=[kt.name])
            for part in range(4):
                T.dma("sp", vv[:, part * 16:(part + 1) * 16, :],
                      v_d[part * 2048:(part + 1) * 2048, hd * 128:(hd + 1) * 128].rearrange("(b p) d -> p b d", p=128),
                      reads=["v_d"], writes=[vv.name])
            T.dma("sp", qt[:], qT_d[hd, :, :], reads=["qT_d"], writes=[qt.name])

            def pe1(st, t):
                i, kb = steps[st][t]
                Z, zn = zb[st][t % 2]
                masked = kb >= 8 * i
                _mm(T, nc, Z, kt[:, kb * 128:(kb + 1) * 128], qt[:, i * TS:(i + 1) * TS], True, not masked,
                    [kt.name, qt.name], [zn], inc=(not masked))
                if masked:
                    _mm(T, nc, Z, ident, amask[:, kb - 8 * i, :], False, True, ["cmat", "amask"], [zn], True)

            def a12(st, t):
                Z, zn = zb[st][t % 2]
                e_, l_ = E[st][t % 2], Lp[st][t % 2]
                _act(T, nc, e_[:], Z, AF.Exp, [zn], [e_.name])
                _act(T, nc, l_[:], e_[:], AF.Ln, [e_.name], [l_.name], bias=1.0, scale=1.0)

            def pe2(st, t):
                i, kb = steps[st][t]
                l_ = Lp[st][t % 2]
                _mm(T, nc, cacc[st][0], trineg, l_[:], kb == 8 * i + 7, False, ["cmat", l_.name], [cacc[st][1]], True)

            def a3(st, t):
                g_ = G[st][t % 2]
                _act(T, nc, g_[:], cacc[st][0], AF.Exp, [cacc[st][1]], [g_.name])

            def pe3(st, t):
                i, kb = steps[st][t]
                l_ = Lp[st][t % 2]
                _mm(T, nc, cacc[st][0], carryneg, l_[:], False, kb == 0, ["cmat", l_.name], [cacc[st][1]], True)

            def v1(st, t):
                e_, g_, w_ = E[st][t % 2], G[st][t % 2], W[st][t % 2]
                _tt(T, nc, "dve", w_[:], e_[:], g_[:], ALU.mult, [e_.name, g_.name], [w_.name])

            def pe4(st, t):
                i, kb = steps[st][t]
                w_ = W[st][t % 2]
                _mm(T, nc, otb[st][0], vv[:, kb, :], w_[:], kb == 8 * i + 7, kb == 0, [vv.name, w_.name], [otb[st][1]], True)
                if kb == 0:
                    ob = osb[nosb[0] % 2]
                    nosb[0] += 1
                    _copy(T, nc, "dve", ob[:], otb[st][0], [otb[st][1]], [ob.name])
                    T.dma("sp", oaT_d[hd, :, i * TS:(i + 1) * TS], ob[:], reads=[ob.name], writes=["oaT_d"])

            for st in range(2):
                pe1(st, 0)
            for st in range(2):
                a12(st, 0)
            for t in range(NS):
                for st in range(2):
                    pe2(st, t)
                if t > 0:
                    for st in range(2):
                        pe4(st, t - 1)
                if t + 1 < NS:
                    for st in range(2):
                        pe1(st, t + 1)
                for st in range(2):
                    a3(st, t)
                for st in range(2):
                    pe3(st, t)
                for st in range(2):
                    v1(st, t)
                if t + 1 < NS:
                    for st in range(2):
                        a12(st, t + 1)
            for st in range(2):
                pe4(st, NS - 1)
        T.barrier()

    with contextlib.ExitStack() as pds:
        def sa_(name, shape, dtype):
            return pds.enter_context(nc.sbuf_tensor("t_" + name, shape, dtype))
        Wd = sa_("Wd", [128, NCH, 3 * 1024], BF16)
        rgain_t = sa_("rgain_t", [128, DM], F32)
        rA = [sa_("rA%d" % k, [128, DM], F32) for k in range(2)]
        rB = [sa_("rB%d" % k, [128, DM], F32) for k in range(2)]
        rt = sa_("rt", [128, DM], F32)
        junk = sa_("junkd", [128, 256], F32)
        ss = sa_("ssd", [128, 12], F32)
        rn = sa_("rn", [128, DM], BF16)
        RNT = sa_("RNT", [128, NCH, TS], BF16)
        oa = [sa_("oa%d" % k, [128, TS], F32) for k in range(2)]
        ga = [sa_("ga%d" % k, [128, TS], BF16) for k in range(2)]
        gb = [sa_("gb%d" % k, [128, TS], BF16) for k in range(2)]
        sga = [sa_("sga%d" % k, [128, TS], BF16) for k in range(2)]
        sgb = [sa_("sgb%d" % k, [128, TS], BF16) for k in range(2)]
        OAG = sa_("OAG", [128, NCH, TS], BF16)
        OBG = sa_("OBG", [128, NCH, TS], BF16)
        MG = sa_("MG", [128, NCH, TS], BF16)
        t1 = sa_("t1", [128, TS], F32)
        t2 = sa_("t2", [128, TS], F32)
        xr = [sa_("xr%d" % k, [128, DM], F32) for k in range(2)]
        yt = [sa_("yt%d" % k, [128, DM], F32) for k in range(2)]
        T.dma("sp", rgain_t[:], L["rgain_rep"], writes=["rgain"])
        load_w(Wd[:, :, 0:1024], 0, 1024, "Wd", src=wbs)
        load_w(Wd[:, :, 1024:2048], 0, 1024, "Wd", src=wbr)
        load_w(Wd[:, :, 2048:3072], 0, 1024, "Wd", src=wout)
        n = 0
        for i in range(CFG["no_d"] if "D" in phases else 0):
            t0 = i * TS
            for blk in range(4):
                a_, b_ = rA[blk % 2], rB[blk % 2]
                T.dma("sp", a_[:], ret_d[(2 * i) * TS + blk * 128:(2 * i) * TS + (blk + 1) * 128, :], reads=["ret_d"], writes=[a_.name])
                T.dma("sp", b_[:], ret_d[(2 * i + 1) * TS + blk * 128:(2 * i + 1) * TS + (blk + 1) * 128, :], reads=["ret_d"], writes=[b_.name])
                _ts(T, nc, "dve", rt[:], a_[:], blend_t[:, 0:1], None, ALU.mult, None, [a_.name, "blend"], ["rt"])
                _stt(T, nc, rt[:], b_[:], blend_t[:, 1:2], rt[:], ALU.mult, ALU.add, [b_.name, "blend", "rt"], ["rt"])
                for hd in range(4):
                    _act(T, nc, junk[:], rt[:, hd * 256:(hd + 1) * 256], AF.Square, ["rt"], ["junkd", "ssd"],
                         accum_out=ss[:, hd:hd + 1])
                _act(T, nc, ss[:, 4:8], ss[:, 0:4], AF.Sqrt, ["ssd"], ["ssd"], bias=EPS, scale=1.0 / 256)
                T.op("dve", lambda: nc.vector.reciprocal(out=ss[:, 8:12], in_=ss[:, 4:8]), reads=["ssd"], writes=["ssd"])
                for hd in range(4):
                    _stt(T, nc, rn[:, hd * 256:(hd + 1) * 256], rt[:, hd * 256:(hd + 1) * 256], ss[:, 8 + hd:9 + hd],
                         rgain_t[:, hd * 256:(hd + 1) * 256], ALU.mult, ALU.mult, ["rt", "ssd", "rgain"], ["rn"])
                for c in range(NCH):
                    _tr(T, nc, pb[0][:, c * 128:(c + 1) * 128], rn[:, c * 128:(c + 1) * 128], ident,
                        ["rn", "cmat"], ["pb0"], inc=(c == NCH - 1))
                _copy(T, nc, "act", RNT[:, :, blk * 128:(blk + 1) * 128], pb[0][:].rearrange("p (c t) -> p c t", c=NCH),
                      ["pb0"], ["RNT"])
            for c in range(NCH):
                o_, ga_, gb_ = oa[c % 2], ga[c % 2], gb[c % 2]
                T.dma("sp", o_[:], oaT_d[c, :, t0:t0 + TS], reads=["oaT_d"], writes=[o_.name])
                T.dma("sp", ga_[:], ga_d[c, :, t0:t0 + TS], reads=["gates_d"], writes=[ga_.name])
                T.dma("sp", gb_[:], gb_d[c, :, t0:t0 + TS], reads=["gates_d"], writes=[gb_.name])
                _tt(T, nc, "dve", OAG[:, c, :], o_[:], ga_[:], ALU.mult, [o_.name, ga_.name], ["OAG"])
                _tt(T, nc, "dve", OBG[:, c, :], RNT[:, c, :], gb_[:], ALU.mult, ["RNT", gb_.name], ["OBG"])
            for oc in range(NCH):
                sa_t, sb_t = sga[oc % 2], sgb[oc % 2]
                T.dma("sp", sa_t[:], sa_d[oc, :, t0:t0 + TS], reads=["gates_d"], writes=[sa_t.name])
                T.dma("sp", sb_t[:], sb_d[oc, :, t0:t0 + TS], reads=["gates_d"], writes=[sb_t.name])
                for c in range(NCH):
                    _mm(T, nc, ps[0][:], Wd[:, c, oc * 128:(oc + 1) * 128], OAG[:, c, :], c == 0, c == NCH - 1,
                        ["Wd", "OAG"], ["ps0"], inc=(c == NCH - 1))
                for c in range(NCH):
                    _mm(T, nc, ps[1][:], Wd[:, c, 1024 + oc * 128:1024 + (oc + 1) * 128], OBG[:, c, :], c == 0, c == NCH - 1,
                        ["Wd", "OBG"], ["ps1"], inc=(c == NCH - 1))
                _tt(T, nc, "dve", t1[:], ps[0][:], sa_t[:], ALU.mult, ["ps0", sa_t.name], ["t1"])
                _tt(T, nc, "dve", t2[:], ps[1][:], sb_t[:], ALU.mult, ["ps1", sb_t.name], ["t2"])
                _tt(T, nc, "dve", MG[:, oc, :], t1[:], t2[:], ALU.add, ["t1", "t2"], ["MG"])
            for blk in range(4):
                x_, y_ = xr[blk % 2], yt[blk % 2]
                T.dma("sp", x_[:], xo[t0 + blk * 128:t0 + (blk + 1) * 128, :], writes=[x_.name])
                for g in range(2):
                    pz = ps[2 + g]
                    for oc in range(NCH):
                        _mm(T, nc, pz[:], MG[:, oc, blk * 128:(blk + 1) * 128], Wd[:, oc, 2048 + g * 512:2048 + (g + 1) * 512],
                            oc == 0, oc == NCH - 1, ["MG", "Wd"], [pz.name], inc=(oc == NCH - 1))
                    _tt(T, nc, "dve", y_[:, g * 512:(g + 1) * 512], pz[:], x_[:, g * 512:(g + 1) * 512], ALU.add,
                        [pz.name, x_.name], [y_.name])
                T.dma("sp", y[t0 + blk * 128:t0 + (blk + 1) * 128, :], y_[:], reads=[y_.name], writes=["y"])


_CONST_CACHE = {}


def _const_tables():
    if _CONST_CACHE:
        return _CONST_CACHE
    bf = ml_dtypes.bfloat16
    d = 256
    inv_freq = (np.float32(10000.0) ** (-np.arange(0, d, 2, dtype=np.float32) / np.float32(d))).astype(np.float32)
    pos = np.arange(S, dtype=np.float32)
    ang = (pos[None, :] * inv_freq[:, None]).astype(np.float32)
    c64, s64 = np.cos(ang.astype(np.float64)), np.sin(ang.astype(np.float64))
    _CONST_CACHE["cosq"] = c64.astype(np.float32)
    _CONST_CACHE["sinq"] = s64.astype(np.float32)
    _CONST_CACHE["cosk"] = (c64 / 16.0).astype(np.float32)
    _CONST_CACHE["sink"] = (s64 / 16.0).astype(np.float32)
    lg = np.log1p(-np.exp2(-5.0 - np.arange(4, dtype=np.float64)))
    idx = np.arange(128)
    same = (idx[:, None] // 64) == (idx[None, :] // 64)
    dm = np.zeros((128, 4, 128), np.float64)
    for hh in range(4):
        dm[:, hh, :] = np.where(same, np.exp(lg[hh] * np.abs(idx[:, None] - idx[None, :])), 0.0)
    _CONST_CACHE["dmask"] = dm.reshape(128, 512).astype(np.float32)
    i64 = (np.arange(TS) % 64).astype(np.float64)
    qd = np.zeros((128, 4, TS), np.float64)
    for hh in range(4):
        qd[:, hh, :] = np.exp(lg[hh] * (i64 + 1.0))[None, :]
    _CONST_CACHE["qdec"] = qd.reshape(128, 4 * TS).astype(np.float32)
    kd = np.zeros((128, 4), np.float64)
    for hh in range(4):
        kd[:, hh] = np.exp(lg[hh] * (63.0 - (idx % 64)))
    _CONST_CACHE["kdect"] = kd.astype(np.float32)
    cm = np.zeros((128, 4, 128), np.float32)
    cm[:, 0, :] = np.eye(128)
    cm[:, 1, :] = 1.0 / 128.0
    cm[:, 2, :] = np.where(idx[:, None] >= idx[None, :], -1.0, 0.0)
    cm[:, 3, :] = np.where(idx[:, None] < idx[None, :], -1.0, 0.0)
    _CONST_CACHE["cmat"] = cm.reshape(128, 512).astype(bf)
    q = np.arange(TS)
    diag = np.zeros((4, 128, TS), np.float32)
    for r in range(4):
        diag[r] = np.where((128 * r + idx[:, None]) < q[None, :], 0.0, NEG)
    am0 = np.full((128, 8, TS), NEG, np.float32)
    am1 = np.zeros((128, 8, TS), np.float32)
    for r in range(4):
        am0[:, r, :] = diag[r]
        am1[:, 4 + r, :] = diag[r]
    _CONST_CACHE["amask0"] = am0.reshape(128, 8 * TS).astype(bf)
    _CONST_CACHE["amask1"] = am1.reshape(128, 8 * TS).astype(bf)
    return _CONST_CACHE


_NC_CACHE = {}


def kernel(x, norm_gain, w_in, b_merge, sb_q_gain, sb_k_gain, ret_out_gain, w_branch_sb, w_branch_ret, w_out):
    x = np.asarray(x, np.float32)
    C = _const_tables()
    if "nc" not in _NC_CACHE:
        _NC_CACHE["nc"] = build_program()
    nc = _NC_CACHE["nc"]
    f = lambda a: np.ascontiguousarray(np.asarray(a, np.float32))
    w_in0, wbs0, wbr0, wout0 = f(w_in[0]), f(w_branch_sb[0]), f(w_branch_ret[0]), f(w_out[0])
    gain_rep = np.ascontiguousarray(np.broadcast_to(f(norm_gain[0])[None, :], (128, DM)))
    rgain_rep = np.ascontiguousarray(np.broadcast_to(f(ret_out_gain[0]).reshape(1, DM), (128, DM)))
    qkg = np.ascontiguousarray(np.stack([f(sb_q_gain[0]), f(sb_k_gain[0])], axis=1))
    bmg = np.ascontiguousarray(f(b_merge[0]).reshape(2, 8, 128).transpose(2, 0, 1).reshape(128, 16))
    in_maps = []
    for c in range(8):
        b, p = c // 2, c % 2
        xb = x[b]
        xown = np.ascontiguousarray(xb.reshape(NT, TS, DM)[p::2].reshape(SO, DM))
        blend = np.zeros((128, 2), np.float32)
        blend[:, 0] = 1.0 - p
        blend[:, 1] = float(p)
        in_maps.append({
            "xa": np.ascontiguousarray(xb), "xo": xown, "w_in": w_in0, "wbs": wbs0, "wbr": wbr0, "wout": wout0,
            "gain_rep": gain_rep, "rgain_rep": rgain_rep, "qkg": qkg, "bmg": bmg,
            "cosq": C["cosq"], "sinq": C["sinq"], "cosk": C["cosk"], "sink": C["sink"],
            "dmask": C["dmask"], "qdec": C["qdec"], "kdect": C["kdect"], "cmat": C["cmat"],
            "amask": C["amask%d" % p], "blend": blend,
        })
    res = run_bass_kernel_spmd(nc, in_maps, core_ids=list(range(8)))
    _NC_CACHE["last"] = res
    out = np.empty((4, S, DM), np.float32)
    for c in range(8):
        b, p = c // 2, c % 2
        out[b].reshape(NT, TS, DM)[p::2] = np.asarray(res.results[c]["y"], np.float32).reshape(NO, TS, DM)
    return out
```

```python
import contextlib
import numpy as np
import ml_dtypes
import concourse.bass as bass
import concourse.mybir as mybir
from concourse.bass_utils import run_bass_kernel_spmd

F32 = mybir.dt.float32
BF16 = mybir.dt.bfloat16
AF = mybir.ActivationFunctionType
ALU = mybir.AluOpType
AX = mybir.AxisListType


class Tracker:
    ENG = ("pe", "act", "dve", "pool", "sp")

    def __init__(self, nc, n_dma_sems=6):
        self.nc = nc
        self.n_dma = n_dma_sems
        self.stack = contextlib.ExitStack()
        self.streams = {e: [] for e in self.ENG}
        self.count = {}
        self.known = {e: {} for e in self.ENG}
        self.last_write = {}
        self.readers = {}
        self.n_ops = 0

    def __enter__(self):
        nc = self.nc
        self.stack.__enter__()
        self.sem = {}
        for e in self.ENG:
            self.sem[e] = self.stack.enter_context(nc.semaphore("s_" + e))
            self.count[e] = 0
        self.dma_ring = {}
        self.dma_next = {}
        for q in ("sp", "pool", "act"):
            ring = []
            for k in range(self.n_dma):
                name = "d_%s%d" % (q, k)
                self.sem[name] = self.stack.enter_context(nc.semaphore(name))
                self.count[name] = 0
                ring.append(name)
            self.dma_ring[q] = ring
            self.dma_next[q] = 0
        return self

    def __exit__(self, *a):
        return self.stack.__exit__(*a)

    def _deps(self, reads, writes):
        deps = {}

        def add(s, v):
            if v > deps.get(s, 0):
                deps[s] = v

        for b in reads:
            lw = self.last_write.get(b)
            if lw:
                add(*lw)
        for b in writes:
            lw = self.last_write.get(b)
            if lw:
                add(*lw)
            for s, v in self.readers.get(b, {}).items():
                add(s, v)
        return deps

    def _emit_waits(self, e, deps, skip_self=False):
        for s, v in deps.items():
            if skip_self and s == e:
                continue
            if self.known[e].get(s, 0) >= v:
                continue
            self.known[e][s] = v
            sem = self.sem[s]
            self.streams[e].append(("wait", sem, v))

    def _record(self, key, val, reads, writes):
        for b in reads:
            self.readers.setdefault(b, {})[key] = val
        for b in writes:
            self.last_write[b] = (key, val)
            self.readers[b] = {}

    def op(self, e, fn, reads=(), writes=(), inc=True):
        deps = self._deps(reads, writes)
        self._emit_waits(e, deps, skip_self=(e == "pe"))
        if inc:
            self.count[e] += 1
            val = self.count[e]
            self.streams[e].append(("op", fn, self.sem[e], 1))
        else:
            val = self.count[e] + 1
            self.streams[e].append(("op", fn, None, 0))
        self._record(e, val, reads, writes)
        self.n_ops += 1

    def dma(self, q, out, in_, reads=(), writes=(), **kw):
        deps = self._deps(reads, writes)
        name = self.dma_ring[q][self.dma_next[q]]
        self.dma_next[q] = (self.dma_next[q] + 1) % self.n_dma
        deps[name] = max(deps.get(name, 0), self.count[name])
        self._emit_waits(q, deps)
        self.count[name] += 16
        val = self.count[name]
        self.known[q][name] = max(self.known[q].get(name, 0), 0)
        eng = {"sp": self.nc.sync, "pool": self.nc.gpsimd, "act": self.nc.scalar}[q]
        self.streams[q].append(("op", (lambda: eng.dma_start(out=out, in_=in_, **kw)), self.sem[name], 16))
        self._record(name, val, reads, writes)
        self.n_ops += 1

    def barrier(self):
        for e in self.ENG:
            deps = {s: c for s, c in self.count.items() if c > 0 and s != e}
            self._emit_waits(e, deps)

    def finish(self):
        nc = self.nc
        deps = {s: c for s, c in self.count.items() if c > 0 and s != "sp"}
        self._emit_waits("sp", deps)
        streams = self.streams

        def replay(e, eng):
            for item in streams[e]:
                if item[0] == "wait":
                    eng.wait_ge(item[1], item[2])
                else:
                    ins = item[1]()
                    if item[2] is not None:
                        ins.then_inc(item[2], item[3])

        with nc.Block() as block:
            @block.sync
            def _(eng):
                replay("sp", eng)

            @block.scalar
            def _(eng):
                replay("act", eng)

            @block.vector
            def _(eng):
                replay("dve", eng)

            @block.gpsimd
            def _(eng):
                replay("pool", eng)

            @block.tensor
            def _(eng):
                replay("pe", eng)


S = 8192
DM = 1024
TS = 512
NT = S // TS
NO = NT // 2
SO = NO * TS
NCH = DM // 128
EPS = 1e-6
NEG = -30000.0
C_SBQ, C_SBK, C_SBV, C_SBG, C_RQ, C_RK, C_RV, C_RG, C_MSB, C_MRET = [i * 1024 for i in range(10)]
GAMMA = [1.0 - 2.0 ** (-5.0 - h) for h in range(4)]
G64 = [g ** 64 for g in GAMMA]
DEBUG = False


def _mm(T, nc, out, lhsT, rhs, start, stop, reads, writes, inc):
    T.op("pe", lambda: nc.tensor.matmul(out, lhsT=lhsT, rhs=rhs, start=start, stop=stop),
         reads=reads, writes=writes, inc=inc)


def _tr(T, nc, out, in_, ident, reads, writes, inc):
    T.op("pe", lambda: nc.tensor.transpose(out, in_, ident), reads=reads, writes=writes, inc=inc)


def _act(T, nc, out, in_, func, reads, writes, **kw):
    T.op("act", lambda: nc.scalar.activation(out=out, in_=in_, func=func, **kw), reads=reads, writes=writes)


def _ts(T, nc, eng, out, in0, s1, s2, op0, op1, reads, writes):
    e = nc.vector if eng == "dve" else nc.gpsimd
    if op1 is None:
        T.op(eng, lambda: e.tensor_scalar(out=out, in0=in0, scalar1=s1, scalar2=None, op0=op0), reads=reads, writes=writes)
    else:
        T.op(eng, lambda: e.tensor_scalar(out=out, in0=in0, scalar1=s1, scalar2=s2, op0=op0, op1=op1), reads=reads, writes=writes)


def _tt(T, nc, eng, out, in0, in1, op, reads, writes):
    e = nc.vector if eng == "dve" else nc.gpsimd
    T.op(eng, lambda: e.tensor_tensor(out=out, in0=in0, in1=in1, op=op), reads=reads, writes=writes)


def _stt(T, nc, out, in0, scalar, in1, op0, op1, reads, writes):
    T.op("dve", lambda: nc.vector.scalar_tensor_tensor(out=out, in0=in0, scalar=scalar, in1=in1, op0=op0, op1=op1),
         reads=reads, writes=writes)


def _copy(T, nc, eng, out, in_, reads, writes):
    if eng == "act":
        T.op("act", lambda: nc.scalar.copy(out=out, in_=in_), reads=reads, writes=writes)
    else:
        e = nc.vector if eng == "dve" else nc.gpsimd
        T.op(eng, lambda: e.tensor_copy(out=out, in_=in_), reads=reads, writes=writes)


def build_program(phases="ABCD", nt_a=NT, no_b=NO, heads_c=8, no_c=NO, no_d=NO):
    CFG = dict(phases=phases, nt_a=nt_a, no_b=no_b, heads_c=heads_c, no_c=no_c, no_d=no_d)
    nc = bass.Bass("TRN2", target_bir_lowering=False)
    dt = nc.dram_tensor
    xa = dt("xa", [S, DM], F32, kind="ExternalInput").ap()
    xo = dt("xo", [SO, DM], F32, kind="ExternalInput").ap()
    w_in = dt("w_in", [DM, 10 * 1024], F32, kind="ExternalInput").ap()
    wbs = dt("wbs", [DM, DM], F32, kind="ExternalInput").ap()
    wbr = dt("wbr", [DM, DM], F32, kind="ExternalInput").ap()
    wout = dt("wout", [DM, DM], F32, kind="ExternalInput").ap()
    gain_rep = dt("gain_rep", [128, DM], F32, kind="ExternalInput").ap()
    rgain_rep = dt("rgain_rep", [128, DM], F32, kind="ExternalInput").ap()
    qkg = dt("qkg", [128, 2], F32, kind="ExternalInput").ap()
    bmg = dt("bmg", [128, 16], F32, kind="ExternalInput").ap()
    cosq = dt("cosq", [128, S], F32, kind="ExternalInput").ap()
    sinq = dt("sinq", [128, S], F32, kind="ExternalInput").ap()
    dmask_d = dt("dmask", [128, 4 * 128], F32, kind="ExternalInput").ap()
    qdec_d = dt("qdec", [128, 4 * 128], F32, kind="ExternalInput").ap()
    kdect_d = dt("kdect", [128, 4], F32, kind="ExternalInput").ap()
    cmat_d = dt("cmat", [128, 4 * 128], BF16, kind="ExternalInput").ap()
    amask_d = dt("amask", [128, 8 * TS], BF16, kind="ExternalInput").ap()
    blend_d = dt("blend", [128, 2], F32, kind="ExternalInput").ap()
    y = dt("y", [SO, DM], F32, kind="ExternalOutput").ap()
    sk = "ExternalOutput" if DEBUG else "Internal"
    kT_d = dt("kT_d", [8, 128, S], BF16, kind=sk).ap()
    v_d = dt("v_d", [S, DM], BF16, kind=sk).ap()
    qT_d = dt("qT_d", [8, 128, SO], BF16, kind=sk).ap()
    ga_d = dt("ga_d", [8, 128, SO], BF16, kind=sk).ap()
    gb_d = dt("gb_d", [8, 128, SO], BF16, kind=sk).ap()
    sa_d = dt("sa_d", [8, 128, SO], BF16, kind=sk).ap()
    sb_d = dt("sb_d", [8, 128, SO], BF16, kind=sk).ap()
    ret_d = dt("ret_d", [S, DM], F32, kind=sk).ap()
    oaT_d = dt("oaT_d", [8, 128, SO], F32, kind=sk).ap()
    if DEBUG:
        dbg_bf = dt("dbg_bf", [128, 8192], BF16, kind="ExternalOutput").ap()
        dbg_f = dt("dbg_f", [128, 4096], F32, kind="ExternalOutput").ap()

    es = contextlib.ExitStack()
    with es:
        def sb(name, shape, dtype):
            return es.enter_context(nc.sbuf_tensor("t_" + name, shape, dtype))

        cmat = sb("cmat", [128, 4 * 128], BF16)
        ident = cmat[:, 0:128]
        onesm = cmat[:, 128:256]
        trineg = cmat[:, 256:384]
        carryneg = cmat[:, 384:512]
        qkg_t = sb("qkg_t", [128, 2], F32)
        qg_s = sb("qg_s", [128, 1], F32)
        bmg_t = sb("bmg_t", [128, 16], F32)
        blend_t = sb("blend_t", [128, 2], F32)
        ps = [es.enter_context(nc.psum_tensor("ps%d" % k, [128, 512], F32)) for k in range(6)]
        pb = [es.enter_context(nc.psum_tensor("pb%d" % k, [128, 1024], BF16)) for k in range(2)]
        T = es.enter_context(Tracker(nc))

        T.dma("sp", cmat[:], cmat_d, writes=["cmat"])
        T.dma("sp", qkg_t[:], qkg, writes=["qkg"])
        T.dma("sp", bmg_t[:], bmg, writes=["bmg"])
        T.dma("sp", blend_t[:], blend_d, writes=["blend"])
        _ts(T, nc, "dve", qg_s[:], qkg_t[:, 0:1], float(128 ** -0.5), None, ALU.mult, None, ["qkg"], ["qg_s"])

        def load_w(wt, col0, ncols, key, src=w_in):
            for c0 in range(0, ncols, 512):
                T.dma("pool", wt[:, :, c0:c0 + 512],
                      src[:, col0 + c0: col0 + c0 + 512].rearrange("(c p) n -> p c n", p=128),
                      writes=[key])

        def norm_block(xsrc, r0, xt, junk, ss, gain_t, h, hT, col0, pbt=None):
            pbt = pbt if pbt is not None else pb[0]
            T.dma("sp", xt[:], xsrc[r0:r0 + 128, :], writes=[xt.name])
            _act(T, nc, junk[:], xt[:], AF.Square, [xt.name], [junk.name, ss.name], accum_out=ss[:, 0:1])
            _act(T, nc, ss[:, 1:2], ss[:, 0:1], AF.Sqrt, [ss.name], [ss.name], bias=EPS, scale=1.0 / DM)
            T.op("dve", lambda: nc.vector.reciprocal(out=ss[:, 2:3], in_=ss[:, 1:2]), reads=[ss.name], writes=[ss.name])
            _stt(T, nc, h[:], xt[:], ss[:, 2:3], gain_t[:], ALU.mult, ALU.mult, [xt.name, ss.name, "gain"], [h.name])
            for c in range(NCH):
                _tr(T, nc, pbt[:, c * 128:(c + 1) * 128], h[:, c * 128:(c + 1) * 128], ident,
                    [h.name, "cmat"], [pbt.name], inc=(c == NCH - 1))
            _copy(T, nc, "act", hT[:, :, col0:col0 + 128], pbt[:].rearrange("p (c t) -> p c t", c=NCH),
                  [pbt.name], ["hT"])

        def qk_proj(W, wkey, wcol, hT, pz):
            for c in range(NCH):
                _mm(T, nc, pz[:], W[:, c, wcol:wcol + 128], hT[:, c, :], c == 0, c == NCH - 1,
                    [wkey, "hT"], [pz.name], inc=(c == NCH - 1))

        def qk_norm(gcol, pz, pm, sq, rs, outt, outkey):
            _act(T, nc, sq[:], pz[:], AF.Square, [pz.name], [sq.name])
            _mm(T, nc, pm[:], onesm, sq[:], True, True, ["cmat", sq.name], [pm.name], True)
            _act(T, nc, rs[:], pm[:], AF.Sqrt, [pm.name], [rs.name], bias=EPS, scale=1.0)
            T.op("dve", lambda: nc.vector.reciprocal(out=rs[:], in_=rs[:]), reads=[rs.name], writes=[rs.name])
            _stt(T, nc, outt, pz[:], gcol, rs[:], ALU.mult, ALU.mult, [pz.name, rs.name, "qkg", "qg_s"], [outkey])

        def qk_head(W, wkey, wcol, hT, gcol, pz, pm, sq, rs, outt, outkey):
            qk_proj(W, wkey, wcol, hT, pz)
            qk_norm(gcol, pz, pm, sq, rs, outt, outkey)

        with contextlib.ExitStack() as pa:
            def sa_(name, shape, dtype):
                return pa.enter_context(nc.sbuf_tensor("t_" + name, shape, dtype))
            Wa = sa_("Wa", [128, NCH, 5 * 1024], BF16)
            gain_t = sa_("gain_t", [128, DM], F32)
            xts = [sa_("xt%d" % k, [128, DM], F32) for k in range(2)]
            junk = sa_("junk", [128, DM], BF16)
            sss = [sa_("ss%d" % k, [128, 4], F32) for k in range(2)]
            hs = [sa_("h%d" % k, [128, DM], BF16) for k in range(2)]
            hT = sa_("hT", [128, NCH, TS], BF16)
            sqs = [sa_("sq%d" % k, [128, TS], BF16) for k in range(2)]
            rss = [sa_("rs%d" % k, [128, TS], F32) for k in range(2)]
            kout = [sa_("kout%d" % k, [128, TS], BF16) for k in range(2)]
            vout = [sa_("vout%d" % k, [128, DM], BF16) for k in range(2)]
            cs = sa_("cs", [128, 2, TS], F32)
            ra = [sa_("ra%d" % k, [128, TS], F32) for k in range(2)]
            rb = [sa_("rb%d" % k, [128, TS], F32) for k in range(2)]
            qrT = [sa_("qrT%d" % k, [128, 2, TS], BF16) for k in range(2)]
            krT = [sa_("krT%d" % k, [128, 2, TS], BF16) for k in range(2)]
            qdT = [sa_("qdT%d" % k, [128, 2, TS], BF16) for k in range(2)]
            kdt = [sa_("kdt%d" % k, [128, 4, 256], BF16) for k in range(2)]
            rv = sa_("rv", [128, 4, DM], BF16)
            sT = [sa_("sT%d" % k, [128, 4, 128], BF16) for k in range(2)]
            St = sa_("St", [128, 4, 2, 256], F32)
            Sb0 = sa_("Sb0", [128, 4, 2, 256], BF16)
            Sbt = [sa_("Sbt%d" % k, [128, 3, 2, 256], BF16) for k in range(2)]
            dmask = sa_("dmask", [128, 4, 128], F32)
            qdec = sa_("qdec", [128, 4, 128], F32)
            kdect = sa_("kdect", [128, 4], F32)
            rout = [sa_("rout%d" % k, [128, 256], F32) for k in range(2)]
            pb0f, pb1f = pb[0][:].bitcast(F32), pb[1][:].bitcast(F32)

            T.dma("sp", gain_t[:], gain_rep, writes=["gain"])
            T.dma("sp", dmask[:].rearrange("p h q -> p (h q)"), dmask_d, writes=["dmask"])
            T.dma("sp", qdec[:].rearrange("p h q -> p (h q)"), qdec_d, writes=["qdec"])
            T.dma("sp", kdect[:], kdect_d, writes=["kdect"])
            load_w(Wa[:, :, 0:1024], C_SBK, 1024, "Wa")
            load_w(Wa[:, :, 1024:2048], C_SBV, 1024, "Wa")
            load_w(Wa[:, :, 2048:3072], C_RQ, 1024, "Wa")
            load_w(Wa[:, :, 3072:4096], C_RK, 1024, "Wa")
            load_w(Wa[:, :, 4096:5120], C_RV, 1024, "Wa")
            T.op("dve", lambda: nc.vector.memset(St[:].rearrange("p a b c -> p (a b c)"), 0.0), writes=["St"])
            T.op("dve", lambda: nc.vector.memset(Sb0[:].rearrange("p a b c -> p (a b c)"), 0.0), writes=["Sb0"])
            G128 = [g ** 128 for g in GAMMA]

            def ret_proj(hd, t0):
                par = hd % 2
                for which in range(2):
                    wc = (2048 if which == 0 else 3072) + hd * 256
                    p1, p2 = (ps[0], ps[1]) if which == 0 else (ps[2], ps[3])
                    for half, pz in enumerate((p1, p2)):
                        for c in range(NCH):
                            _mm(T, nc, pz[:], Wa[:, c, wc + half * 128: wc + (half + 1) * 128], hT[:, c, :],
                                c == 0, c == NCH - 1, ["Wa", "hT"], [pz.name], inc=(c == NCH - 1))
                    dst = qrT[par] if which == 0 else krT[par]
                    ct, st_ = cs[:, 0, :], cs[:, 1, :]
                    _tt(T, nc, "dve", ra[0][:], p1[:], ct, ALU.mult, [p1.name, "cs"], [ra[0].name])
                    _tt(T, nc, "dve", rb[0][:], p2[:], st_, ALU.mult, [p2.name, "cs"], [rb[0].name])
                    _tt(T, nc, "dve", ra[1][:], p1[:], st_, ALU.mult, [p1.name, "cs"], [ra[1].name])
                    _tt(T, nc, "dve", rb[1][:], p2[:], ct, ALU.mult, [p2.name, "cs"], [rb[1].name])
                    for half in range(2):
                        _tt(T, nc, "pool", ra[half][:], ra[half][:], rb[half][:], ALU.subtract if half == 0 else ALU.add,
                            [ra[half].name, rb[half].name], [ra[half].name])
                        if which == 0:
                            _copy(T, nc, "act", dst[:, half, :], ra[half][:], [ra[half].name], [dst.name])
                            for blk in range(4):
                                _tt(T, nc, "pool", qdT[par][:, half, blk * 128:(blk + 1) * 128], ra[half][:, blk * 128:(blk + 1) * 128],
                                    qdec[:, hd, :], ALU.mult, [ra[half].name, "qdec"], [qdT[par].name])
                        else:
                            T.op("act", (lambda o=dst[:, half, :], i_=ra[half][:]: nc.scalar.mul(out=o, in_=i_, mul=1.0 / 16.0)),
                                 reads=[ra[half].name], writes=[dst.name])

            def ret_small(hd, t0):
                par = hd % 2
                kr, qr, qd, kd, st = krT[par], qrT[par], qdT[par], kdt[par], sT[par]
                for blk in range(4):
                    for half in range(2):
                        _tr(T, nc, pb[1][:, half * 128:(half + 1) * 128], kr[:, half, blk * 128:(blk + 1) * 128], ident,
                            [kr.name, "cmat"], [pb[1].name], inc=(half == 1))
                    _ts(T, nc, "dve", kd[:, blk, :], pb[1][:, 0:256], kdect[:, hd:hd + 1], None, ALU.mult, None,
                        [pb[1].name, "kdect"], [kd.name])
                for blk in range(4):
                    kv = ps[4 + blk % 2]
                    for half in range(2):
                        _mm(T, nc, kv[:, half * 256:(half + 1) * 256], kd[:, blk, half * 128:(half + 1) * 128],
                            rv[:, blk, hd * 256:(hd + 1) * 256], True, True, [kd.name, "rv"], [kv.name], inc=(half == 1))
                    sflat = St[:, hd, :, :].rearrange("p a b -> p (a b)")
                    _stt(T, nc, sflat, sflat, float(G128[hd]), kv[:], ALU.mult, ALU.add, ["St", kv.name], ["St"])
                    if blk < 3:
                        _copy(T, nc, "act", Sbt[par][:, blk, :, :].rearrange("p a b -> p (a b)"), sflat, ["St"], [Sbt[par].name + str(blk)])
                for blk in range(4):
                    b0 = blk * 128
                    for half in range(2):
                        _mm(T, nc, pb0f[:, blk * 128:(blk + 1) * 128], kr[:, half, b0:b0 + 128], qr[:, half, b0:b0 + 128],
                            half == 0, half == 1, [kr.name, qr.name], [pb[0].name], inc=(half == 1))
                for blk in range(4):
                    _tt(T, nc, "dve", st[:, blk, :], pb0f[:, blk * 128:(blk + 1) * 128], dmask[:, hd, :], ALU.mult,
                        [pb[0].name, "dmask"], [st.name])
                for blk in range(4):
                    b0 = blk * 128
                    po = pb1f
                    _mm(T, nc, po[:, 0:256], st[:, blk, :], rv[:, blk, hd * 256:(hd + 1) * 256], True, False,
                        [st.name, "rv"], [pb[1].name], inc=True)
                    for half in range(2):
                        sb_ap = Sb0[:, hd, half, :] if blk == 0 else Sbt[par][:, blk - 1, half, :]
                        skey = "Sb0" if blk == 0 else Sbt[par].name + str(blk - 1)
                        _mm(T, nc, po[:, 0:256], qd[:, half, b0:b0 + 128], sb_ap, False, half == 1,
                            [qd.name, skey], [pb[1].name], inc=(half == 1))
                    ro = rout[blk % 2]
                    _copy(T, nc, "act" if blk % 2 == 0 else "dve", ro[:], po[:, 0:256], [pb[1].name], [ro.name])
                    T.dma("sp", ret_d[t0 + b0:t0 + b0 + 128, hd * 256:(hd + 1) * 256], ro[:], reads=[ro.name], writes=["ret_d"])
                _copy(T, nc, "act", Sb0[:, hd, :, :].rearrange("p a b -> p (a b)"),
                      St[:, hd, :, :].rearrange("p a b -> p (a b)"), ["St"], ["Sb0"])

            for t in range(nt_a if "A" in phases else 0):
                t0 = t * TS
                for k, tab in enumerate((cosq, sinq)):
                    T.dma("sp", cs[:, k, :], tab[:, t0:t0 + TS], writes=["cs"])
                for blk in range(4):
                    norm_block(xa, t0 + blk * 128, xts[blk % 2], junk, sss[blk % 2], gain_t, hs[blk % 2], hT, blk * 128,
                               pbt=pb[blk % 2])
                qk_proj(Wa, "Wa", 0, hT, ps[0])
                for hd in range(8):
                    if hd + 1 < 8:
                        qk_proj(Wa, "Wa", (hd + 1) * 128, hT, ps[(hd + 1) % 2])
                    ko = kout[hd % 2]
                    qk_norm(qkg_t[:, 1:2], ps[hd % 2], ps[2 + hd % 2], sqs[hd % 2], rss[hd % 2], ko[:], ko.name)
                    T.dma("sp", kT_d[hd, :, t0:t0 + TS], ko[:], reads=[ko.name], writes=["kT_d"])
                nb = 0
                for blk in range(4):
                    vo = vout[blk % 2]
                    for g in range(2):
                        pz = ps[(4 + nb) % 6]
                        nb += 1
                        for c in range(NCH):
                            _mm(T, nc, pz[:], hT[:, c, blk * 128:(blk + 1) * 128], Wa[:, c, 1024 + g * 512:1024 + (g + 1) * 512],
                                c == 0, c == NCH - 1, ["hT", "Wa"], [pz.name], inc=(c == NCH - 1))
                        _copy(T, nc, "act" if g == 0 else "dve", vo[:, g * 512:(g + 1) * 512], pz[:], [pz.name], [vo.name])
                    T.dma("sp", v_d[t0 + blk * 128:t0 + (blk + 1) * 128, :], vo[:], reads=[vo.name], writes=["v_d"])
                    for g in range(2):
                        pz = ps[(4 + nb) % 6]
                        nb += 1
                        for c in range(NCH):
                            _mm(T, nc, pz[:], hT[:, c, blk * 128:(blk + 1) * 128], Wa[:, c, 4096 + g * 512:4096 + (g + 1) * 512],
                                c == 0, c == NCH - 1, ["hT", "Wa"], [pz.name], inc=(c == NCH - 1))
                        _copy(T, nc, "act" if g == 0 else "dve", rv[:, blk, g * 512:(g + 1) * 512], pz[:], [pz.name], ["rv"])
                ret_proj(0, t0)
                for hd in range(4):
                    if hd + 1 < 4:
                        ret_proj(hd + 1, t0)
                    ret_small(hd, t0)
            T.barrier()
        build_rest(nc, T, locals())
        T.finish()
    return nc


def build_rest(nc, T, L):
    CFG = L["CFG"]
    phases = CFG["phases"]
    ps, pb = L["ps"], L["pb"]
    ident, onesm, trineg, carryneg = L["ident"], L["onesm"], L["trineg"], L["carryneg"]
    qg_s, bmg_t, blend_t = L["qg_s"], L["bmg_t"], L["blend_t"]
    xo, y = L["xo"], L["y"]
    kT_d, v_d, qT_d, ga_d, gb_d, sa_d, sb_d, ret_d, oaT_d = (L[k] for k in
        ("kT_d", "v_d", "qT_d", "ga_d", "gb_d", "sa_d", "sb_d", "ret_d", "oaT_d"))
    load_w, norm_block, qk_head = L["load_w"], L["norm_block"], L["qk_head"]
    wbs, wbr, wout = L["wbs"], L["wbr"], L["wout"]

    with contextlib.ExitStack() as pbs:
        def sa_(name, shape, dtype):
            return pbs.enter_context(nc.sbuf_tensor("t_" + name, shape, dtype))
        Wb = sa_("Wb", [128, NCH, 5 * 1024], BF16)
        gain_t = sa_("gain_tb", [128, DM], F32)
        xt = sa_("xtb", [128, DM], F32)
        junk = sa_("junkb", [128, DM], F32)
        ss = sa_("ssb", [128, 4], F32)
        h = sa_("hb", [128, DM], BF16)
        hT = sa_("hTb", [128, NCH, TS], BF16)
        sq = sa_("sqb", [128, TS], BF16)
        rs = sa_("rsb", [128, TS], F32)
        qout = [sa_("qout%d" % k, [128, TS], BF16) for k in range(2)]
        gout = [sa_("gout%d" % k, [128, TS], BF16) for k in range(4)]
        T.dma("sp", gain_t[:], L["gain_rep"], writes=["gain"])
        for k, col in enumerate((C_SBQ, C_SBG, C_RG, C_MSB, C_MRET)):
            load_w(Wb[:, :, k * 1024:(k + 1) * 1024], col, 1024, "Wb")

        for i in range(CFG["no_b"] if "B" in phases else 0):
            t0 = i * TS
            for blk in range(4):
                norm_block(xo, t0 + blk * 128, xt, junk, ss, gain_t, h, hT, blk * 128)
            for hd in range(8):
                qo = qout[hd % 2]
                qk_head(Wb, "Wb", hd * 128, hT, qg_s[:, 0:1], ps[hd % 2], ps[2], sq, rs, qo[:], qo.name)
                T.dma("sp", qT_d[hd, :, t0:t0 + TS], qo[:], reads=[qo.name], writes=["qT_d"])
            n = 0
            for k, (dst, func) in enumerate(((ga_d, AF.Silu), (gb_d, AF.Silu), (sa_d, AF.Sigmoid), (sb_d, AF.Sigmoid))):
                for c in range(NCH):
                    pz = ps[3 + (n % 3)]
                    go = gout[n % 4]
                    n += 1
                    wc = (k + 1) * 1024 + c * 128
                    for cc in range(NCH):
                        _mm(T, nc, pz[:], Wb[:, cc, wc:wc + 128], hT[:, cc, :], cc == 0, cc == NCH - 1,
                            ["Wb", "hT"], [pz.name], inc=(cc == NCH - 1))
                    if func == AF.Silu:
                        _act(T, nc, go[:], pz[:], func, [pz.name], [go.name])
                    else:
                        bcol = (k - 2) * 8 + c
                        _act(T, nc, go[:], pz[:], func, [pz.name, "bmg"], [go.name], bias=bmg_t[:, bcol:bcol + 1], scale=1.0)
                    T.dma("sp", dst[c, :, t0:t0 + TS], go[:], reads=[go.name], writes=["gates_d"])
        T.barrier()

    with contextlib.ExitStack() as pcs:
        def sa_(name, shape, dtype):
            return pcs.enter_context(nc.sbuf_tensor("t_" + name, shape, dtype))
        KT = [sa_("KT%d" % k, [128, S], BF16) for k in range(2)]
        VV = [sa_("VV%d" % k, [128, S // 128, 128], BF16) for k in range(2)]
        QT = [sa_("QT%d" % k, [128, SO], BF16) for k in range(2)]
        amask = sa_("amask", [128, 8, TS], BF16)
        E = [[sa_("E%d_%d" % (st, k), [128, TS], F32) for k in range(2)] for st in range(2)]
        Lp = [[sa_("Lp%d_%d" % (st, k), [128, TS], BF16) for k in range(2)] for st in range(2)]
        G = [[sa_("G%d_%d" % (st, k), [128, TS], F32) for k in range(2)] for st in range(2)]
        W = [[sa_("W%d_%d" % (st, k), [128, TS], BF16) for k in range(2)] for st in range(2)]
        osb = [sa_("osb%d" % k, [128, TS], F32) for k in range(2)]
        T.dma("sp", amask[:].rearrange("p r q -> p (r q)"), L["amask_d"], writes=["amask"])
        pz8 = [ps[k][:] for k in range(6)] + [pb[0][:].bitcast(F32), pb[1][:].bitcast(F32)]
        pn8 = [ps[k].name for k in range(6)] + [pb[0].name, pb[1].name]
        zb = [[(pz8[0], pn8[0]), (pz8[1], pn8[1])], [(pz8[2], pn8[2]), (pz8[3], pn8[3])]]
        cacc = [(pz8[4], pn8[4]), (pz8[5], pn8[5])]
        otb = [(pz8[6], pn8[6]), (pz8[7], pn8[7])]
        seqs = ([0, 3, 4, 7], [1, 2, 5, 6])
        steps = [[(i, kb) for i in seq for kb in range(8 * i + 7, -1, -1)] for seq in seqs]
        NS = len(steps[0])
        assert NS == len(steps[1])
        if CFG["no_c"] < NO:
            seqs = ([0], [0]) if CFG["no_c"] == 1 else seqs
            steps = [[(i, kb) for i in seq for kb in range(8 * i + 7, -1, -1)] for seq in seqs]
            NS = len(steps[0])
        nosb = [0]
        for hd in range(CFG["heads_c"] if "C" in phases else 0):
            sl = hd % 2
            kt, vv, qt = KT[sl], VV[sl], QT[sl]
            for part in range(4):
                T.dma("sp", kt[:, part * 2048:(part + 1) * 2048], kT_d[hd, :, part * 2048:(part + 1) * 2048],
                      reads=["kT_d"], writes=[kt.name])
            for part in range(4):
                T.dma("sp", vv[:, part * 16:(part + 1) * 16, :],
                      v_d[part * 2048:(part + 1) * 2048, hd * 128:(hd + 1) * 128].rearrange("(b p) d -> p b d", p=128),
                      reads=["v_d"], writes=[vv.name])
            T.dma("sp", qt[:], qT_d[hd, :, :], reads=["qT_d"], writes=[qt.name])

            def pe1(st, t):
                i, kb = steps[st][t]
                Z, zn = zb[st][t % 2]
                masked = kb >= 8 * i
                _mm(T, nc, Z, kt[:, kb * 128:(kb + 1) * 128], qt[:, i * TS:(i + 1) * TS], True, not masked,
                    [kt.name, qt.name], [zn], inc=(not masked))
                if masked:
                    _mm(T, nc, Z, ident, amask[:, kb - 8 * i, :], False, True, ["cmat", "amask"], [zn], True)

            def a12(st, t):
                Z, zn = zb[st][t % 2]
                e_, l_ = E[st][t % 2], Lp[st][t % 2]
                _act(T, nc, e_[:], Z, AF.Exp, [zn], [e_.name])
                _act(T, nc, l_[:], e_[:], AF.Ln, [e_.name], [l_.name], bias=1.0, scale=1.0)

            def pe2(st, t):
                i, kb = steps[st][t]
                l_ = Lp[st][t % 2]
                fst = (kb == 8 * i + 7)
                T.op("pe", (lambda o=cacc[st][0], r_=l_[:], f_=fst: nc.tensor.matmul(o, lhsT=trineg, rhs=r_, start=f_, stop=True, skip_group_check=(not f_))),
                     reads=["cmat", l_.name], writes=[cacc[st][1]], inc=True)

            def a3(st, t):
                g_ = G[st][t % 2]
                _act(T, nc, g_[:], cacc[st][0], AF.Exp, [cacc[st][1]], [g_.name])

            def pe3(st, t):
                i, kb = steps[st][t]
                l_ = Lp[st][t % 2]
                T.op("pe", (lambda o=cacc[st][0], r_=l_[:]: nc.tensor.matmul(o, lhsT=carryneg, rhs=r_, start=False, stop=True, skip_group_check=True)),
                     reads=["cmat", l_.name], writes=[cacc[st][1]], inc=True)

            def v1(st, t):
                e_, g_, w_ = E[st][t % 2], G[st][t % 2], W[st][t % 2]
                _tt(T, nc, "dve", w_[:], e_[:], g_[:], ALU.mult, [e_.name, g_.name], [w_.name])

            def pe4(st, t):
                i, kb = steps[st][t]
                w_ = W[st][t % 2]
                _mm(T, nc, otb[st][0], vv[:, kb, :], w_[:], kb == 8 * i + 7, kb == 0, [vv.name, w_.name], [otb[st][1]], True)
                if kb == 0:
                    ob = osb[nosb[0] % 2]
                    nosb[0] += 1
                    _copy(T, nc, "dve", ob[:], otb[st][0], [otb[st][1]], [ob.name])
                    T.dma("sp", oaT_d[hd, :, i * TS:(i + 1) * TS], ob[:], reads=[ob.name], writes=["oaT_d"])

            for st in range(2):
                pe1(st, 0)
            for st in range(2):
                a12(st, 0)
            for t in range(NS):
                for st in range(2):
                    pe2(st, t)
                if t > 0:
                    for st in range(2):
                        pe4(st, t - 1)
                if t + 1 < NS:
                    for st in range(2):
                        pe1(st, t + 1)
                for st in range(2):
                    a3(st, t)
                for st in range(2):
                    pe3(st, t)
                for st in range(2):
                    v1(st, t)
                if t + 1 < NS:
                    for st in range(2):
                        a12(st, t + 1)
            for st in range(2):
                pe4(st, NS - 1)
        T.barrier()

    with contextlib.ExitStack() as pds:
        def sa_(name, shape, dtype):
            return pds.enter_context(nc.sbuf_tensor("t_" + name, shape, dtype))
        Wd = sa_("Wd", [128, NCH, 3 * 1024], BF16)
        rgain_t = sa_("rgain_t", [128, DM], F32)
        rA = [sa_("rA%d" % k, [128, DM], F32) for k in range(2)]
        rB = [sa_("rB%d" % k, [128, DM], F32) for k in range(2)]
        rt = sa_("rt", [128, DM], F32)
        junk = sa_("junkd", [128, 256], F32)
        ss = sa_("ssd", [128, 12], F32)
        rn = sa_("rn", [128, DM], BF16)
        RNT = sa_("RNT", [128, NCH, TS], BF16)
        oa = [sa_("oa%d" % k, [128, TS], F32) for k in range(2)]
        ga = [sa_("ga%d" % k, [128, TS], BF16) for k in range(2)]
        gb = [sa_("gb%d" % k, [128, TS], BF16) for k in range(2)]
        sga = [sa_("sga%d" % k, [128, TS], BF16) for k in range(2)]
        sgb = [sa_("sgb%d" % k, [128, TS], BF16) for k in range(2)]
        OAG = sa_("OAG", [128, NCH, TS], BF16)
        OBG = sa_("OBG", [128, NCH, TS], BF16)
        MG = sa_("MG", [128, NCH, TS], BF16)
        t1 = sa_("t1", [128, TS], F32)
        t2 = sa_("t2", [128, TS], F32)
        xr = [sa_("xr%d" % k, [128, DM], F32) for k in range(2)]
        yt = [sa_("yt%d" % k, [128, DM], F32) for k in range(2)]
        T.dma("sp", rgain_t[:], L["rgain_rep"], writes=["rgain"])
        load_w(Wd[:, :, 0:1024], 0, 1024, "Wd", src=wbs)
        load_w(Wd[:, :, 1024:2048], 0, 1024, "Wd", src=wbr)
        load_w(Wd[:, :, 2048:3072], 0, 1024, "Wd", src=wout)
        n = 0
        for i in range(CFG["no_d"] if "D" in phases else 0):
            t0 = i * TS
            for blk in range(4):
                a_, b_ = rA[blk % 2], rB[blk % 2]
                T.dma("sp", a_[:], ret_d[(2 * i) * TS + blk * 128:(2 * i) * TS + (blk + 1) * 128, :], reads=["ret_d"], writes=[a_.name])
                T.dma("sp", b_[:], ret_d[(2 * i + 1) * TS + blk * 128:(2 * i + 1) * TS + (blk + 1) * 128, :], reads=["ret_d"], writes=[b_.name])
                _ts(T, nc, "dve", rt[:], a_[:], blend_t[:, 0:1], None, ALU.mult, None, [a_.name, "blend"], ["rt"])
                _stt(T, nc, rt[:], b_[:], blend_t[:, 1:2], rt[:], ALU.mult, ALU.add, [b_.name, "blend", "rt"], ["rt"])
                for hd in range(4):
                    _act(T, nc, junk[:], rt[:, hd * 256:(hd + 1) * 256], AF.Square, ["rt"], ["junkd", "ssd"],
                         accum_out=ss[:, hd:hd + 1])
                _act(T, nc, ss[:, 4:8], ss[:, 0:4], AF.Sqrt, ["ssd"], ["ssd"], bias=EPS, scale=1.0 / 256)
                T.op("dve", lambda: nc.vector.reciprocal(out=ss[:, 8:12], in_=ss[:, 4:8]), reads=["ssd"], writes=["ssd"])
                for hd in range(4):
                    _stt(T, nc, rn[:, hd * 256:(hd + 1) * 256], rt[:, hd * 256:(hd + 1) * 256], ss[:, 8 + hd:9 + hd],
                         rgain_t[:, hd * 256:(hd + 1) * 256], ALU.mult, ALU.mult, ["rt", "ssd", "rgain"], ["rn"])
                for c in range(NCH):
                    _tr(T, nc, pb[0][:, c * 128:(c + 1) * 128], rn[:, c * 128:(c + 1) * 128], ident,
                        ["rn", "cmat"], ["pb0"], inc=(c == NCH - 1))
                _copy(T, nc, "act", RNT[:, :, blk * 128:(blk + 1) * 128], pb[0][:].rearrange("p (c t) -> p c t", c=NCH),
                      ["pb0"], ["RNT"])
            for c in range(NCH):
                o_, ga_, gb_ = oa[c % 2], ga[c % 2], gb[c % 2]
                T.dma("sp", o_[:], oaT_d[c, :, t0:t0 + TS], reads=["oaT_d"], writes=[o_.name])
                T.dma("sp", ga_[:], ga_d[c, :, t0:t0 + TS], reads=["gates_d"], writes=[ga_.name])
                T.dma("sp", gb_[:], gb_d[c, :, t0:t0 + TS], reads=["gates_d"], writes=[gb_.name])
                _tt(T, nc, "dve", OAG[:, c, :], o_[:], ga_[:], ALU.mult, [o_.name, ga_.name], ["OAG"])
                _tt(T, nc, "dve", OBG[:, c, :], RNT[:, c, :], gb_[:], ALU.mult, ["RNT", gb_.name], ["OBG"])
            for oc in range(NCH):
                sa_t, sb_t = sga[oc % 2], sgb[oc % 2]
                T.dma("sp", sa_t[:], sa_d[oc, :, t0:t0 + TS], reads=["gates_d"], writes=[sa_t.name])
                T.dma("sp", sb_t[:], sb_d[oc, :, t0:t0 + TS], reads=["gates_d"], writes=[sb_t.name])
                for c in range(NCH):
                    _mm(T, nc, ps[0][:], Wd[:, c, oc * 128:(oc + 1) * 128], OAG[:, c, :], c == 0, c == NCH - 1,
                        ["Wd", "OAG"], ["ps0"], inc=(c == NCH - 1))
                for c in range(NCH):
                    _mm(T, nc, ps[1][:], Wd[:, c, 1024 + oc * 128:1024 + (oc + 1) * 128], OBG[:, c, :], c == 0, c == NCH - 1,
                        ["Wd", "OBG"], ["ps1"], inc=(c == NCH - 1))
                _tt(T, nc, "dve", t1[:], ps[0][:], sa_t[:], ALU.mult, ["ps0", sa_t.name], ["t1"])
                _tt(T, nc, "dve", t2[:], ps[1][:], sb_t[:], ALU.mult, ["ps1", sb_t.name], ["t2"])
                _tt(T, nc, "dve", MG[:, oc, :], t1[:], t2[:], ALU.add, ["t1", "t2"], ["MG"])
            for blk in range(4):
                x_, y_ = xr[blk % 2], yt[blk % 2]
                T.dma("sp", x_[:], xo[t0 + blk * 128:t0 + (blk + 1) * 128, :], writes=[x_.name])
                for g in range(2):
                    pz = ps[2 + g]
                    for oc in range(NCH):
                        _mm(T, nc, pz[:], MG[:, oc, blk * 128:(blk + 1) * 128], Wd[:, oc, 2048 + g * 512:2048 + (g + 1) * 512],
                            oc == 0, oc == NCH - 1, ["MG", "Wd"], [pz.name], inc=(oc == NCH - 1))
                    _tt(T, nc, "dve", y_[:, g * 512:(g + 1) * 512], pz[:], x_[:, g * 512:(g + 1) * 512], ALU.add,
                        [pz.name, x_.name], [y_.name])
                T.dma("sp", y[t0 + blk * 128:t0 + (blk + 1) * 128, :], y_[:], reads=[y_.name], writes=["y"])


_CONST_CACHE = {}


def _const_tables():
    if _CONST_CACHE:
        return _CONST_CACHE
    bf = ml_dtypes.bfloat16
    d = 256
    inv_freq = (np.float32(10000.0) ** (-np.arange(0, d, 2, dtype=np.float32) / np.float32(d))).astype(np.float32)
    pos = np.arange(S, dtype=np.float32)
    ang = (pos[None, :] * inv_freq[:, None]).astype(np.float32)
    c64, s64 = np.cos(ang.astype(np.float64)), np.sin(ang.astype(np.float64))
    _CONST_CACHE["cosq"] = c64.astype(np.float32)
    _CONST_CACHE["sinq"] = s64.astype(np.float32)
    lg = np.log1p(-np.exp2(-5.0 - np.arange(4, dtype=np.float64)))
    idx = np.arange(128)
    same = (idx[:, None] // 64) == (idx[None, :] // 64)
    k_first = (idx[:, None] // 64) < (idx[None, :] // 64)
    dm = np.zeros((128, 4, 128), np.float64)
    for hh in range(4):
        dm[:, hh, :] = np.where(same | k_first, np.exp(lg[hh] * np.abs(idx[:, None] - idx[None, :])), 0.0)
    _CONST_CACHE["dmask"] = dm.reshape(128, 512).astype(np.float32)
    qd = np.zeros((128, 4, 128), np.float64)
    for hh in range(4):
        qd[:, hh, :] = np.exp(lg[hh] * (idx + 1.0))[None, :]
    _CONST_CACHE["qdec"] = qd.reshape(128, 4 * 128).astype(np.float32)
    kd = np.zeros((128, 4), np.float64)
    for hh in range(4):
        kd[:, hh] = np.exp(lg[hh] * (127.0 - idx))
    _CONST_CACHE["kdect"] = kd.astype(np.float32)
    cm = np.zeros((128, 4, 128), np.float32)
    cm[:, 0, :] = np.eye(128)
    cm[:, 1, :] = 1.0 / 128.0
    cm[:, 2, :] = np.where(idx[:, None] >= idx[None, :], -1.0, 0.0)
    cm[:, 3, :] = np.where(idx[:, None] < idx[None, :], -1.0, 0.0)
    _CONST_CACHE["cmat"] = cm.reshape(128, 512).astype(bf)
    q = np.arange(TS)
    diag = np.zeros((4, 128, TS), np.float32)
    for r in range(4):
        diag[r] = np.where((128 * r + idx[:, None]) < q[None, :], 0.0, NEG)
    am0 = np.full((128, 8, TS), NEG, np.float32)
    am1 = np.zeros((128, 8, TS), np.float32)
    for r in range(4):
        am0[:, r, :] = diag[r]
        am1[:, 4 + r, :] = diag[r]
    _CONST_CACHE["amask0"] = am0.reshape(128, 8 * TS).astype(bf)
    _CONST_CACHE["amask1"] = am1.reshape(128, 8 * TS).astype(bf)
    return _CONST_CACHE


_NC_CACHE = {}


def kernel(x, norm_gain, w_in, b_merge, sb_q_gain, sb_k_gain, ret_out_gain, w_branch_sb, w_branch_ret, w_out):
    x = np.asarray(x, np.float32)
    C = _const_tables()
    if "nc" not in _NC_CACHE:
        _NC_CACHE["nc"] = build_program()
    nc = _NC_CACHE["nc"]
    f = lambda a: np.ascontiguousarray(np.asarray(a, np.float32))
    w_in0, wbs0, wbr0, wout0 = f(w_in[0]), f(w_branch_sb[0]), f(w_branch_ret[0]), f(w_out[0])
    gain_rep = np.ascontiguousarray(np.broadcast_to(f(norm_gain[0])[None, :], (128, DM)))
    rgain_rep = np.ascontiguousarray(np.broadcast_to(f(ret_out_gain[0]).reshape(1, DM), (128, DM)))
    qkg = np.ascontiguousarray(np.stack([f(sb_q_gain[0]), f(sb_k_gain[0])], axis=1))
    bmg = np.ascontiguousarray(f(b_merge[0]).reshape(2, 8, 128).transpose(2, 0, 1).reshape(128, 16))
    in_maps = []
    for c in range(8):
        b, p = c // 2, c % 2
        xb = x[b]
        xown = np.ascontiguousarray(xb.reshape(NT, TS, DM)[p::2].reshape(SO, DM))
        blend = np.zeros((128, 2), np.float32)
        blend[:, 0] = 1.0 - p
        blend[:, 1] = float(p)
        in_maps.append({
            "xa": np.ascontiguousarray(xb), "xo": xown, "w_in": w_in0, "wbs": wbs0, "wbr": wbr0, "wout": wout0,
            "gain_rep": gain_rep, "rgain_rep": rgain_rep, "qkg": qkg, "bmg": bmg,
            "cosq": C["cosq"], "sinq": C["sinq"],
            "dmask": C["dmask"], "qdec": C["qdec"], "kdect": C["kdect"], "cmat": C["cmat"],
            "amask": C["amask%d" % p], "blend": blend,
        })
    res = run_bass_kernel_spmd(nc, in_maps, core_ids=list(range(8)))
    _NC_CACHE["last"] = res
    out = np.empty((4, S, DM), np.float32)
    for c in range(8):
        b, p = c // 2, c % 2
        out[b].reshape(NT, TS, DM)[p::2] = np.asarray(res.results[c]["y"], np.float32).reshape(NO, TS, DM)
    return out
```

```python
import contextlib
import numpy as np
import ml_dtypes
import concourse.bass as bass
import concourse.mybir as mybir
from concourse.bass_utils import run_bass_kernel_spmd

F32 = mybir.dt.float32
BF16 = mybir.dt.bfloat16
AF = mybir.ActivationFunctionType
ALU = mybir.AluOpType
AX = mybir.AxisListType


class Tracker:
    ENG = ("pe", "act", "dve", "pool", "sp")

    def __init__(self, nc, n_dma_sems=6):
        self.nc = nc
        self.n_dma = n_dma_sems
        self.stack = contextlib.ExitStack()
        self.streams = {e: [] for e in self.ENG}
        self.count = {}
        self.known = {e: {} for e in self.ENG}
        self.last_write = {}
        self.readers = {}
        self.n_ops = 0

    def __enter__(self):
        nc = self.nc
        self.stack.__enter__()
        self.sem = {}
        for e in self.ENG:
            self.sem[e] = self.stack.enter_context(nc.semaphore("s_" + e))
            self.count[e] = 0
        self.dma_ring = {}
        self.dma_next = {}
        for q in ("sp", "pool", "act"):
            ring = []
            for k in range(self.n_dma):
                name = "d_%s%d" % (q, k)
                self.sem[name] = self.stack.enter_context(nc.semaphore(name))
                self.count[name] = 0
                ring.append(name)
            self.dma_ring[q] = ring
            self.dma_next[q] = 0
        return self

    def __exit__(self, *a):
        return self.stack.__exit__(*a)

    def _deps(self, reads, writes):
        deps = {}

        def add(s, v):
            if v > deps.get(s, 0):
                deps[s] = v

        for b in reads:
            lw = self.last_write.get(b)
            if lw:
                add(*lw)
        for b in writes:
            lw = self.last_write.get(b)
            if lw:
                add(*lw)
            for s, v in self.readers.get(b, {}).items():
                add(s, v)
        return deps

    def _emit_waits(self, e, deps, skip_self=False):
        for s, v in deps.items():
            if skip_self and s == e:
                continue
            if self.known[e].get(s, 0) >= v:
                continue
            self.known[e][s] = v
            sem = self.sem[s]
            self.streams[e].append(("wait", sem, v))

    def _record(self, key, val, reads, writes):
        for b in reads:
            self.readers.setdefault(b, {})[key] = val
        for b in writes:
            self.last_write[b] = (key, val)
            self.readers[b] = {}

    def op(self, e, fn, reads=(), writes=(), inc=True):
        deps = self._deps(reads, writes)
        self._emit_waits(e, deps, skip_self=(e == "pe"))
        if inc:
            self.count[e] += 1
            val = self.count[e]
            self.streams[e].append(("op", fn, self.sem[e], 1))
        else:
            val = self.count[e] + 1
            self.streams[e].append(("op", fn, None, 0))
        self._record(e, val, reads, writes)
        self.n_ops += 1

    def dma(self, q, out, in_, reads=(), writes=(), **kw):
        deps = self._deps(reads, writes)
        name = self.dma_ring[q][self.dma_next[q]]
        self.dma_next[q] = (self.dma_next[q] + 1) % self.n_dma
        deps[name] = max(deps.get(name, 0), self.count[name])
        self._emit_waits(q, deps)
        self.count[name] += 16
        val = self.count[name]
        self.known[q][name] = max(self.known[q].get(name, 0), 0)
        eng = {"sp": self.nc.sync, "pool": self.nc.gpsimd, "act": self.nc.scalar}[q]
        self.streams[q].append(("op", (lambda: eng.dma_start(out=out, in_=in_, **kw)), self.sem[name], 16))
        self._record(name, val, reads, writes)
        self.n_ops += 1

    def barrier(self):
        for e in self.ENG:
            deps = {s: c for s, c in self.count.items() if c > 0 and s != e}
            self._emit_waits(e, deps)

    def finish(self):
        nc = self.nc
        deps = {s: c for s, c in self.count.items() if c > 0 and s != "sp"}
        self._emit_waits("sp", deps)
        streams = self.streams

        def replay(e, eng):
            for item in streams[e]:
                if item[0] == "wait":
                    eng.wait_ge(item[1], item[2])
                else:
                    ins = item[1]()
                    if item[2] is not None:
                        ins.then_inc(item[2], item[3])

        with nc.Block() as block:
            @block.sync
            def _(eng):
                replay("sp", eng)

            @block.scalar
            def _(eng):
                replay("act", eng)

            @block.vector
            def _(eng):
                replay("dve", eng)

            @block.gpsimd
            def _(eng):
                replay("pool", eng)

            @block.tensor
            def _(eng):
                replay("pe", eng)


S = 8192
DM = 1024
TS = 512
NT = S // TS
NO = NT // 2
SO = NO * TS
NCH = DM // 128
EPS = 1e-6
NEG = -30000.0
C_SBQ, C_SBK, C_SBV, C_SBG, C_RQ, C_RK, C_RV, C_RG, C_MSB, C_MRET = [i * 1024 for i in range(10)]
GAMMA = [1.0 - 2.0 ** (-5.0 - h) for h in range(4)]
G64 = [g ** 64 for g in GAMMA]
DEBUG = False


def _mm(T, nc, out, lhsT, rhs, start, stop, reads, writes, inc):
    T.op("pe", lambda: nc.tensor.matmul(out, lhsT=lhsT, rhs=rhs, start=start, stop=stop),
         reads=reads, writes=writes, inc=inc)


def _tr(T, nc, out, in_, ident, reads, writes, inc):
    T.op("pe", lambda: nc.tensor.transpose(out, in_, ident), reads=reads, writes=writes, inc=inc)


def _act(T, nc, out, in_, func, reads, writes, **kw):
    T.op("act", lambda: nc.scalar.activation(out=out, in_=in_, func=func, **kw), reads=reads, writes=writes)


def _ts(T, nc, eng, out, in0, s1, s2, op0, op1, reads, writes):
    e = nc.vector if eng == "dve" else nc.gpsimd
    if op1 is None:
        T.op(eng, lambda: e.tensor_scalar(out=out, in0=in0, scalar1=s1, scalar2=None, op0=op0), reads=reads, writes=writes)
    else:
        T.op(eng, lambda: e.tensor_scalar(out=out, in0=in0, scalar1=s1, scalar2=s2, op0=op0, op1=op1), reads=reads, writes=writes)


def _tt(T, nc, eng, out, in0, in1, op, reads, writes):
    e = nc.vector if eng == "dve" else nc.gpsimd
    T.op(eng, lambda: e.tensor_tensor(out=out, in0=in0, in1=in1, op=op), reads=reads, writes=writes)


def _stt(T, nc, out, in0, scalar, in1, op0, op1, reads, writes):
    T.op("dve", lambda: nc.vector.scalar_tensor_tensor(out=out, in0=in0, scalar=scalar, in1=in1, op0=op0, op1=op1),
         reads=reads, writes=writes)


def _copy(T, nc, eng, out, in_, reads, writes):
    if eng == "act":
        T.op("act", lambda: nc.scalar.copy(out=out, in_=in_), reads=reads, writes=writes)
    else:
        e = nc.vector if eng == "dve" else nc.gpsimd
        T.op(eng, lambda: e.tensor_copy(out=out, in_=in_), reads=reads, writes=writes)


def build_program(phases="ABCD", nt_a=NT, no_b=NO, heads_c=8, no_c=NO, no_d=NO):
    CFG = dict(phases=phases, nt_a=nt_a, no_b=no_b, heads_c=heads_c, no_c=no_c, no_d=no_d)
    nc = bass.Bass("TRN2", target_bir_lowering=False)
    dt = nc.dram_tensor
    xa = dt("xa", [S, DM], F32, kind="ExternalInput").ap()
    xo = dt("xo", [SO, DM], F32, kind="ExternalInput").ap()
    w_in = dt("w_in", [DM, 10 * 1024], F32, kind="ExternalInput").ap()
    wbs = dt("wbs", [DM, DM], F32, kind="ExternalInput").ap()
    wbr = dt("wbr", [DM, DM], F32, kind="ExternalInput").ap()
    wout = dt("wout", [DM, DM], F32, kind="ExternalInput").ap()
    gain_rep = dt("gain_rep", [128, DM], F32, kind="ExternalInput").ap()
    rgain_rep = dt("rgain_rep", [128, DM], F32, kind="ExternalInput").ap()
    qkg = dt("qkg", [128, 2], F32, kind="ExternalInput").ap()
    bmg = dt("bmg", [128, 16], F32, kind="ExternalInput").ap()
    cosq = dt("cosq", [128, S], F32, kind="ExternalInput").ap()
    sinq = dt("sinq", [128, S], F32, kind="ExternalInput").ap()
    dmask_d = dt("dmask", [128, 4 * 128], F32, kind="ExternalInput").ap()
    qdec_d = dt("qdec", [128, 4 * 128], F32, kind="ExternalInput").ap()
    kdect_d = dt("kdect", [128, 4], F32, kind="ExternalInput").ap()
    cmat_d = dt("cmat", [128, 4 * 128], BF16, kind="ExternalInput").ap()
    amask_d = dt("amask", [128, 8 * TS], BF16, kind="ExternalInput").ap()
    blend_d = dt("blend", [128, 2], F32, kind="ExternalInput").ap()
    y = dt("y", [SO, DM], F32, kind="ExternalOutput").ap()
    sk = "ExternalOutput" if DEBUG else "Internal"
    kT_d = dt("kT_d", [8, 128, S], BF16, kind=sk).ap()
    v_d = dt("v_d", [S, DM], BF16, kind=sk).ap()
    qT_d = dt("qT_d", [8, 128, SO], BF16, kind=sk).ap()
    ga_d = dt("ga_d", [8, 128, SO], BF16, kind=sk).ap()
    gb_d = dt("gb_d", [8, 128, SO], BF16, kind=sk).ap()
    sa_d = dt("sa_d", [8, 128, SO], BF16, kind=sk).ap()
    sb_d = dt("sb_d", [8, 128, SO], BF16, kind=sk).ap()
    ret_d = dt("ret_d", [S, DM], F32, kind=sk).ap()
    oaT_d = dt("oaT_d", [8, 128, SO], F32, kind=sk).ap()
    if DEBUG:
        dbg_bf = dt("dbg_bf", [128, 8192], BF16, kind="ExternalOutput").ap()
        dbg_f = dt("dbg_f", [128, 4096], F32, kind="ExternalOutput").ap()

    es = contextlib.ExitStack()
    with es:
        def sb(name, shape, dtype):
            return es.enter_context(nc.sbuf_tensor("t_" + name, shape, dtype))

        cmat = sb("cmat", [128, 4 * 128], BF16)
        ident = cmat[:, 0:128]
        onesm = cmat[:, 128:256]
        trineg = cmat[:, 256:384]
        carryneg = cmat[:, 384:512]
        qkg_t = sb("qkg_t", [128, 2], F32)
        qg_s = sb("qg_s", [128, 1], F32)
        bmg_t = sb("bmg_t", [128, 16], F32)
        blend_t = sb("blend_t", [128, 2], F32)
        ps, pb = [], []
        psum_ctr = [0]

        def alloc_psum(stack):
            tag = "abcdefgh"[psum_ctr[0]]
            psum_ctr[0] += 1
            ps[:] = [stack.enter_context(nc.psum_tensor("ps%d%s" % (k, tag), [128, 512], F32)) for k in range(6)]
            pb[:] = [stack.enter_context(nc.psum_tensor("pb%d%s" % (k, tag), [128, 1024], BF16)) for k in range(2)]
        T = es.enter_context(Tracker(nc))

        T.dma("sp", cmat[:], cmat_d, writes=["cmat"])
        T.dma("sp", qkg_t[:], qkg, writes=["qkg"])
        T.dma("sp", bmg_t[:], bmg, writes=["bmg"])
        T.dma("sp", blend_t[:], blend_d, writes=["blend"])
        _ts(T, nc, "dve", qg_s[:], qkg_t[:, 0:1], float(128 ** -0.5), None, ALU.mult, None, ["qkg"], ["qg_s"])

        def load_w(wt, col0, ncols, key, src=w_in):
            for c0 in range(0, ncols, 512):
                T.dma("pool", wt[:, :, c0:c0 + 512],
                      src[:, col0 + c0: col0 + c0 + 512].rearrange("(c p) n -> p c n", p=128),
                      writes=[key])

        def norm_block(xsrc, r0, xt, junk, ss, gain_t, h, hT, col0, pbt=None):
            pbt = pbt if pbt is not None else pb[0]
            T.dma("sp", xt[:], xsrc[r0:r0 + 128, :], writes=[xt.name])
            _act(T, nc, junk[:], xt[:], AF.Square, [xt.name], [junk.name, ss.name], accum_out=ss[:, 0:1])
            _act(T, nc, ss[:, 1:2], ss[:, 0:1], AF.Sqrt, [ss.name], [ss.name], bias=EPS, scale=1.0 / DM)
            T.op("dve", lambda: nc.vector.reciprocal(out=ss[:, 2:3], in_=ss[:, 1:2]), reads=[ss.name], writes=[ss.name])
            _stt(T, nc, h[:], xt[:], ss[:, 2:3], gain_t[:], ALU.mult, ALU.mult, [xt.name, ss.name, "gain"], [h.name])
            for c in range(NCH):
                _tr(T, nc, pbt[:, c * 128:(c + 1) * 128], h[:, c * 128:(c + 1) * 128], ident,
                    [h.name, "cmat"], [pbt.name], inc=(c == NCH - 1))
            _copy(T, nc, "act", hT[:, :, col0:col0 + 128], pbt[:].rearrange("p (c t) -> p c t", c=NCH),
                  [pbt.name], ["hT"])

        def qk_proj(W, wkey, wcol, hT, pz):
            for c in range(NCH):
                _mm(T, nc, pz[:], W[:, c, wcol:wcol + 128], hT[:, c, :], c == 0, c == NCH - 1,
                    [wkey, "hT"], [pz.name], inc=(c == NCH - 1))

        def qk_norm(gcol, pz, pm, sq, rs, outt, outkey):
            _act(T, nc, sq[:], pz[:], AF.Square, [pz.name], [sq.name])
            _mm(T, nc, pm[:], onesm, sq[:], True, True, ["cmat", sq.name], [pm.name], True)
            _act(T, nc, rs[:], pm[:], AF.Ln, [pm.name], [rs.name], bias=EPS, scale=1.0)
            _act(T, nc, rs[:], rs[:], AF.Exp, [rs.name], [rs.name], scale=-0.5)
            _stt(T, nc, outt, pz[:], gcol, rs[:], ALU.mult, ALU.mult, [pz.name, rs.name, "qkg", "qg_s"], [outkey])

        def qk_head(W, wkey, wcol, hT, gcol, pz, pm, sq, rs, outt, outkey):
            qk_proj(W, wkey, wcol, hT, pz)
            qk_norm(gcol, pz, pm, sq, rs, outt, outkey)

        with contextlib.ExitStack() as pa:
            def sa_(name, shape, dtype):
                return pa.enter_context(nc.sbuf_tensor("t_" + name, shape, dtype))
            alloc_psum(pa)
            Wa = sa_("Wa", [128, NCH, 5 * 1024], BF16)
            gain_t = sa_("gain_t", [128, DM], F32)
            xts = [sa_("xt%d" % k, [128, DM], F32) for k in range(2)]
            junk = sa_("junk", [128, DM], BF16)
            sss = [sa_("ss%d" % k, [128, 4], F32) for k in range(2)]
            hs = [sa_("h%d" % k, [128, DM], BF16) for k in range(2)]
            hT = sa_("hT", [128, NCH, TS], BF16)
            sqs = [sa_("sq%d" % k, [128, TS], BF16) for k in range(2)]
            rss = [sa_("rs%d" % k, [128, TS], F32) for k in range(2)]
            kout = [sa_("kout%d" % k, [128, TS], BF16) for k in range(2)]
            vout = [sa_("vout%d" % k, [128, DM], BF16) for k in range(2)]
            cs = sa_("cs", [128, 2, TS], F32)
            ra = [sa_("ra%d" % k, [128, TS], F32) for k in range(2)]
            rb = [sa_("rb%d" % k, [128, TS], F32) for k in range(2)]
            qrT = [sa_("qrT%d" % k, [128, 2, TS], BF16) for k in range(2)]
            krT = [sa_("krT%d" % k, [128, 2, TS], BF16) for k in range(2)]
            qdT = [sa_("qdT%d" % k, [128, 2, TS], BF16) for k in range(2)]
            kdt = [sa_("kdt%d" % k, [128, 4, 256], BF16) for k in range(2)]
            rv = sa_("rv", [128, 4, DM], BF16)
            sT = [sa_("sT%d" % k, [128, 4, 128], BF16) for k in range(2)]
            St = sa_("St", [128, 4, 2, 256], F32)
            Sb0 = sa_("Sb0", [128, 4, 2, 256], BF16)
            Sbt = [sa_("Sbt%d" % k, [128, 3, 2, 256], BF16) for k in range(2)]
            dmask = sa_("dmask", [128, 4, 128], F32)
            qdec = sa_("qdec", [128, 4, 128], F32)
            kdect = sa_("kdect", [128, 4], F32)
            rout = [sa_("rout%d" % k, [128, 256], F32) for k in range(2)]
            pb0f, pb1f = pb[0][:].bitcast(F32), pb[1][:].bitcast(F32)

            T.dma("sp", gain_t[:], gain_rep, writes=["gain"])
            T.dma("sp", dmask[:].rearrange("p h q -> p (h q)"), dmask_d, writes=["dmask"])
            T.dma("sp", qdec[:].rearrange("p h q -> p (h q)"), qdec_d, writes=["qdec"])
            T.dma("sp", kdect[:], kdect_d, writes=["kdect"])
            load_w(Wa[:, :, 0:1024], C_SBK, 1024, "Wa")
            load_w(Wa[:, :, 1024:2048], C_SBV, 1024, "Wa")
            load_w(Wa[:, :, 2048:3072], C_RQ, 1024, "Wa")
            load_w(Wa[:, :, 3072:4096], C_RK, 1024, "Wa")
            load_w(Wa[:, :, 4096:5120], C_RV, 1024, "Wa")
            T.op("dve", lambda: nc.vector.memset(St[:].rearrange("p a b c -> p (a b c)"), 0.0), writes=["St"])
            T.op("dve", lambda: nc.vector.memset(Sb0[:].rearrange("p a b c -> p (a b c)"), 0.0), writes=["Sb0"])
            G128 = [g ** 128 for g in GAMMA]

            def ret_proj(hd, t0):
                par = hd % 2
                for which in range(2):
                    wc = (2048 if which == 0 else 3072) + hd * 256
                    p1, p2 = (ps[0], ps[1]) if which == 0 else (ps[2], ps[3])
                    for half, pz in enumerate((p1, p2)):
                        for c in range(NCH):
                            _mm(T, nc, pz[:], Wa[:, c, wc + half * 128: wc + (half + 1) * 128], hT[:, c, :],
                                c == 0, c == NCH - 1, ["Wa", "hT"], [pz.name], inc=(c == NCH - 1))
                    dst = qrT[par] if which == 0 else krT[par]
                    ct, st_ = cs[:, 0, :], cs[:, 1, :]
                    _tt(T, nc, "dve", ra[0][:], p1[:], ct, ALU.mult, [p1.name, "cs"], [ra[0].name])
                    _tt(T, nc, "dve", rb[0][:], p2[:], st_, ALU.mult, [p2.name, "cs"], [rb[0].name])
                    _tt(T, nc, "dve", ra[1][:], p1[:], st_, ALU.mult, [p1.name, "cs"], [ra[1].name])
                    _tt(T, nc, "dve", rb[1][:], p2[:], ct, ALU.mult, [p2.name, "cs"], [rb[1].name])
                    for half in range(2):
                        _tt(T, nc, "pool", ra[half][:], ra[half][:], rb[half][:], ALU.subtract if half == 0 else ALU.add,
                            [ra[half].name, rb[half].name], [ra[half].name])
                        if which == 0:
                            _copy(T, nc, "act", dst[:, half, :], ra[half][:], [ra[half].name], [dst.name])
                            for blk in range(4):
                                _tt(T, nc, "pool", qdT[par][:, half, blk * 128:(blk + 1) * 128], ra[half][:, blk * 128:(blk + 1) * 128],
                                    qdec[:, hd, :], ALU.mult, [ra[half].name, "qdec"], [qdT[par].name])
                        else:
                            T.op("act", (lambda o=dst[:, half, :], i_=ra[half][:]: nc.scalar.mul(out=o, in_=i_, mul=1.0 / 16.0)),
                                 reads=[ra[half].name], writes=[dst.name])

            def ret_small(hd, t0):
                par = hd % 2
                kr, qr, qd, kd, st = krT[par], qrT[par], qdT[par], kdt[par], sT[par]
                for blk in range(4):
                    for half in range(2):
                        _tr(T, nc, pb[1][:, blk * 256 + half * 128:blk * 256 + (half + 1) * 128], kr[:, half, blk * 128:(blk + 1) * 128], ident,
                            [kr.name, "cmat"], [pb[1].name], inc=(blk == 3 and half == 1))
                _ts(T, nc, "dve", kd[:].rearrange("p a b -> p (a b)"), pb[1][:], kdect[:, hd:hd + 1], None, ALU.mult, None,
                    [pb[1].name, "kdect"], [kd.name])
                for blk in range(4):
                    kv = ps[4 + blk % 2]
                    for half in range(2):
                        _mm(T, nc, kv[:, half * 256:(half + 1) * 256], kd[:, blk, half * 128:(half + 1) * 128],
                            rv[:, blk, hd * 256:(hd + 1) * 256], True, True, [kd.name, "rv"], [kv.name], inc=(half == 1))
                    sflat = St[:, hd, :, :].rearrange("p a b -> p (a b)")
                    _stt(T, nc, sflat, sflat, float(G128[hd]), kv[:], ALU.mult, ALU.add, ["St", kv.name], ["St"])
                    if blk < 3:
                        _copy(T, nc, "act", Sbt[par][:, blk, :, :].rearrange("p a b -> p (a b)"), sflat, ["St"], [Sbt[par].name + str(blk)])
                for blk in range(4):
                    b0 = blk * 128
                    for half in range(2):
                        _mm(T, nc, pb0f[:, blk * 128:(blk + 1) * 128], kr[:, half, b0:b0 + 128], qr[:, half, b0:b0 + 128],
                            half == 0, half == 1, [kr.name, qr.name], [pb[0].name], inc=(half == 1))
                for blk in range(4):
                    _tt(T, nc, "dve", st[:, blk, :], pb0f[:, blk * 128:(blk + 1) * 128], dmask[:, hd, :], ALU.mult,
                        [pb[0].name, "dmask"], [st.name])
                for blk in range(4):
                    b0 = blk * 128
                    pot = ps[blk % 2]
                    po = pot[:]
                    _mm(T, nc, po[:, 0:256], st[:, blk, :], rv[:, blk, hd * 256:(hd + 1) * 256], True, False,
                        [st.name, "rv"], [pot.name], inc=True)
                    for half in range(2):
                        sb_ap = Sb0[:, hd, half, :] if blk == 0 else Sbt[par][:, blk - 1, half, :]
                        skey = "Sb0" if blk == 0 else Sbt[par].name + str(blk - 1)
                        _mm(T, nc, po[:, 0:256], qd[:, half, b0:b0 + 128], sb_ap, False, half == 1,
                            [qd.name, skey], [pot.name], inc=(half == 1))
                    ro = rout[blk % 2]
                    _copy(T, nc, "act" if blk % 2 == 0 else "dve", ro[:], po[:, 0:256], [pot.name], [ro.name])
                    T.dma("sp", ret_d[t0 + b0:t0 + b0 + 128, hd * 256:(hd + 1) * 256], ro[:], reads=[ro.name], writes=["ret_d"])
                _copy(T, nc, "act", Sb0[:, hd, :, :].rearrange("p a b -> p (a b)"),
                      St[:, hd, :, :].rearrange("p a b -> p (a b)"), ["St"], ["Sb0"])

            for t in range(nt_a if "A" in phases else 0):
                t0 = t * TS
                for k, tab in enumerate((cosq, sinq)):
                    T.dma("sp", cs[:, k, :], tab[:, t0:t0 + TS], writes=["cs"])
                for blk in range(4):
                    norm_block(xa, t0 + blk * 128, xts[blk % 2], junk, sss[blk % 2], gain_t, hs[blk % 2], hT, blk * 128,
                               pbt=pb[blk % 2])
                pzk = [ps[0], ps[1], ps[4]]
                qk_proj(Wa, "Wa", 0, hT, pzk[0])
                qk_proj(Wa, "Wa", 128, hT, pzk[1])
                for hd in range(8):
                    if hd + 2 < 8:
                        qk_proj(Wa, "Wa", (hd + 2) * 128, hT, pzk[(hd + 2) % 3])
                    ko = kout[hd % 2]
                    qk_norm(qkg_t[:, 1:2], pzk[hd % 3], ps[2 + hd % 2], sqs[hd % 2], rss[hd % 2], ko[:], ko.name)
                    T.dma("sp", kT_d[hd, :, t0:t0 + TS], ko[:], reads=[ko.name], writes=["kT_d"])
                nb = 0
                for blk in range(4):
                    vo = vout[blk % 2]
                    for g in range(2):
                        pz = ps[(4 + nb) % 6]
                        nb += 1
                        for c in range(NCH):
                            _mm(T, nc, pz[:], hT[:, c, blk * 128:(blk + 1) * 128], Wa[:, c, 1024 + g * 512:1024 + (g + 1) * 512],
                                c == 0, c == NCH - 1, ["hT", "Wa"], [pz.name], inc=(c == NCH - 1))
                        _copy(T, nc, "act" if g == 0 else "dve", vo[:, g * 512:(g + 1) * 512], pz[:], [pz.name], [vo.name])
                    T.dma("sp", v_d[t0 + blk * 128:t0 + (blk + 1) * 128, :], vo[:], reads=[vo.name], writes=["v_d"])
                    for g in range(2):
                        pz = ps[(4 + nb) % 6]
                        nb += 1
                        for c in range(NCH):
                            _mm(T, nc, pz[:], hT[:, c, blk * 128:(blk + 1) * 128], Wa[:, c, 4096 + g * 512:4096 + (g + 1) * 512],
                                c == 0, c == NCH - 1, ["hT", "Wa"], [pz.name], inc=(c == NCH - 1))
                        _copy(T, nc, "act" if g == 0 else "dve", rv[:, blk, g * 512:(g + 1) * 512], pz[:], [pz.name], ["rv"])
                ret_proj(0, t0)
                for hd in range(4):
                    if hd + 1 < 4:
                        ret_proj(hd + 1, t0)
                    ret_small(hd, t0)
            T.barrier()
        build_rest(nc, T, locals())
        T.finish()
    return nc


def build_rest(nc, T, L):
    CFG = L["CFG"]
    phases = CFG["phases"]
    ps, pb = L["ps"], L["pb"]
    ident, onesm, trineg, carryneg = L["ident"], L["onesm"], L["trineg"], L["carryneg"]
    qg_s, bmg_t, blend_t = L["qg_s"], L["bmg_t"], L["blend_t"]
    xo, y = L["xo"], L["y"]
    kT_d, v_d, qT_d, ga_d, gb_d, sa_d, sb_d, ret_d, oaT_d = (L[k] for k in
        ("kT_d", "v_d", "qT_d", "ga_d", "gb_d", "sa_d", "sb_d", "ret_d", "oaT_d"))
    load_w, norm_block, qk_head = L["load_w"], L["norm_block"], L["qk_head"]
    wbs, wbr, wout = L["wbs"], L["wbr"], L["wout"]

    with contextlib.ExitStack() as pbs:
        def sa_(name, shape, dtype):
            return pbs.enter_context(nc.sbuf_tensor("t_" + name, shape, dtype))
        L["alloc_psum"](pbs)
        Wb = sa_("Wb", [128, NCH, 5 * 1024], BF16)
        gain_t = sa_("gain_tb", [128, DM], F32)
        xt = sa_("xtb", [128, DM], F32)
        junk = sa_("junkb", [128, DM], F32)
        ss = sa_("ssb", [128, 4], F32)
        h = sa_("hb", [128, DM], BF16)
        hT = sa_("hTb", [128, NCH, TS], BF16)
        sq = sa_("sqb", [128, TS], BF16)
        rs = sa_("rsb", [128, TS], F32)
        qout = [sa_("qout%d" % k, [128, TS], BF16) for k in range(2)]
        gout = [sa_("gout%d" % k, [128, TS], BF16) for k in range(4)]
        T.dma("sp", gain_t[:], L["gain_rep"], writes=["gain"])
        for k, col in enumerate((C_SBQ, C_SBG, C_RG, C_MSB, C_MRET)):
            load_w(Wb[:, :, k * 1024:(k + 1) * 1024], col, 1024, "Wb")

        for i in range(CFG["no_b"] if "B" in phases else 0):
            t0 = i * TS
            for blk in range(4):
                norm_block(xo, t0 + blk * 128, xt, junk, ss, gain_t, h, hT, blk * 128)
            for hd in range(8):
                qo = qout[hd % 2]
                qk_head(Wb, "Wb", hd * 128, hT, qg_s[:, 0:1], ps[hd % 2], ps[2], sq, rs, qo[:], qo.name)
                T.dma("sp", qT_d[hd, :, t0:t0 + TS], qo[:], reads=[qo.name], writes=["qT_d"])
            n = 0
            for k, (dst, func) in enumerate(((ga_d, AF.Silu), (gb_d, AF.Silu), (sa_d, AF.Sigmoid), (sb_d, AF.Sigmoid))):
                for c in range(NCH):
                    pz = ps[3 + (n % 3)]
                    go = gout[n % 4]
                    n += 1
                    wc = (k + 1) * 1024 + c * 128
                    for cc in range(NCH):
                        _mm(T, nc, pz[:], Wb[:, cc, wc:wc + 128], hT[:, cc, :], cc == 0, cc == NCH - 1,
                            ["Wb", "hT"], [pz.name], inc=(cc == NCH - 1))
                    if func == AF.Silu:
                        _act(T, nc, go[:], pz[:], func, [pz.name], [go.name])
                    else:
                        bcol = (k - 2) * 8 + c
                        _act(T, nc, go[:], pz[:], func, [pz.name, "bmg"], [go.name], bias=bmg_t[:, bcol:bcol + 1], scale=1.0)
                    T.dma("sp", dst[c, :, t0:t0 + TS], go[:], reads=[go.name], writes=["gates_d"])
        T.barrier()

    with contextlib.ExitStack() as pcs:
        def sa_(name, shape, dtype):
            return pcs.enter_context(nc.sbuf_tensor("t_" + name, shape, dtype))
        KT = [sa_("KT%d" % k, [128, S], BF16) for k in range(2)]
        VV = [sa_("VV%d" % k, [128, S // 128, 128], BF16) for k in range(2)]
        QT = [sa_("QT%d" % k, [128, SO], BF16) for k in range(2)]
        amask = sa_("amask", [128, 8, TS], BF16)
        E = [sa_("E%d" % k, [128, 2, TS], F32) for k in range(2)]
        Lp = [sa_("Lp%d" % k, [128, 2, TS], BF16) for k in range(2)]
        G = [sa_("G%d" % k, [128, 2, TS], F32) for k in range(2)]
        W = [sa_("W%d" % k, [128, 2, TS], BF16) for k in range(2)]
        osb = [sa_("osb%d" % k, [128, TS], F32) for k in range(2)]
        T.dma("sp", amask[:].rearrange("p r q -> p (r q)"), L["amask_d"], writes=["amask"])
        Zall = pcs.enter_context(nc.psum_tensor("Zall", [128, 2, 2, TS], F32))
        Call = pcs.enter_context(nc.psum_tensor("Call", [128, 2, TS], F32))
        Oall = pcs.enter_context(nc.psum_tensor("Oall", [128, 2, TS], F32))
        seqs = ([0, 3, 4, 7], [1, 2, 5, 6])
        steps = [[(i, kb) for i in seq for kb in range(8 * i + 7, -1, -1)] for seq in seqs]
        NS = len(steps[0])
        assert NS == len(steps[1])
        if CFG["no_c"] < NO:
            seqs = ([0], [0]) if CFG["no_c"] == 1 else seqs
            steps = [[(i, kb) for i in seq for kb in range(8 * i + 7, -1, -1)] for seq in seqs]
            NS = len(steps[0])
        nosb = [0]
        for hd in range(CFG["heads_c"] if "C" in phases else 0):
            sl = hd % 2
            kt, vv, qt = KT[sl], VV[sl], QT[sl]
            for part in range(4):
                T.dma("sp", kt[:, part * 2048:(part + 1) * 2048], kT_d[hd, :, part * 2048:(part + 1) * 2048],
                      reads=["kT_d"], writes=[kt.name])
            for part in range(4):
                T.dma("sp", vv[:, part * 16:(part + 1) * 16, :],
                      v_d[part * 2048:(part + 1) * 2048, hd * 128:(hd + 1) * 128].rearrange("(b p) d -> p b d", p=128),
                      reads=["v_d"], writes=[vv.name])
            T.dma("sp", qt[:], qT_d[hd, :, :], reads=["qT_d"], writes=[qt.name])

            def pe1(st, t):
                i, kb = steps[st][t]
                par = t % 2
                Z = Zall[:, par, st, :]
                masked = kb >= 8 * i
                _mm(T, nc, Z, kt[:, kb * 128:(kb + 1) * 128], qt[:, i * TS:(i + 1) * TS], True, not masked,
                    [kt.name, qt.name], ["Z%d" % par], inc=(not masked))
                if masked:
                    _mm(T, nc, Z, ident, amask[:, kb - 8 * i, :], False, True, ["cmat", "amask"], ["Z%d" % par], True)

            def a12(t):
                par = t % 2
                _act(T, nc, E[par][:], Zall[:, par, :, :], AF.Exp, ["Z%d" % par], [E[par].name])
                _act(T, nc, Lp[par][:], E[par][:], AF.Ln, [E[par].name], [Lp[par].name], bias=1.0, scale=1.0)

            def pe2(st, t):
                i, kb = steps[st][t]
                l_ = Lp[t % 2]
                fst = (kb == 8 * i + 7)
                T.op("pe", (lambda o=Call[:, st, :], r_=l_[:, st, :], f_=fst: nc.tensor.matmul(o, lhsT=trineg, rhs=r_, start=f_, stop=True, skip_group_check=(not f_))),
                     reads=["cmat", l_.name], writes=["Call"], inc=True)

            def a3(t):
                par = t % 2
                _act(T, nc, G[par][:], Call[:], AF.Exp, ["Call"], [G[par].name])

            def pe3(st, t):
                l_ = Lp[t % 2]
                T.op("pe", (lambda o=Call[:, st, :], r_=l_[:, st, :]: nc.tensor.matmul(o, lhsT=carryneg, rhs=r_, start=False, stop=True, skip_group_check=True)),
                     reads=["cmat", l_.name], writes=["Call"], inc=True)

            def v1(t):
                par = t % 2
                _tt(T, nc, "dve", W[par][:], E[par][:], G[par][:], ALU.mult, [E[par].name, G[par].name], [W[par].name])

            def pe4(st, t):
                i, kb = steps[st][t]
                w_ = W[t % 2]
                _mm(T, nc, Oall[:, st, :], vv[:, kb, :], w_[:, st, :], kb == 8 * i + 7, kb == 0, [vv.name, w_.name], ["O%d" % st], True)
                if kb == 0:
                    ob = osb[nosb[0] % 2]
                    nosb[0] += 1
                    _copy(T, nc, "dve", ob[:], Oall[:, st, :], ["O%d" % st], [ob.name])
                    T.dma("sp", oaT_d[hd, :, i * TS:(i + 1) * TS], ob[:], reads=[ob.name], writes=["oaT_d"])

            for st in range(2):
                pe1(st, 0)
            a12(0)
            for t in range(NS):
                for st in range(2):
                    pe2(st, t)
                if t > 0:
                    for st in range(2):
                        pe4(st, t - 1)
                if t + 1 < NS:
                    for st in range(2):
                        pe1(st, t + 1)
                a3(t)
                for st in range(2):
                    pe3(st, t)
                v1(t)
                if t + 1 < NS:
                    a12(t + 1)
            for st in range(2):
                pe4(st, NS - 1)
        T.barrier()

    with contextlib.ExitStack() as pds:
        def sa_(name, shape, dtype):
            return pds.enter_context(nc.sbuf_tensor("t_" + name, shape, dtype))
        L["alloc_psum"](pds)
        Wd = sa_("Wd", [128, NCH, 3 * 1024], BF16)
        rgain_t = sa_("rgain_t", [128, DM], F32)
        rA = [sa_("rA%d" % k, [128, DM], F32) for k in range(2)]
        rB = [sa_("rB%d" % k, [128, DM], F32) for k in range(2)]
        rt = sa_("rt", [128, DM], F32)
        junk = sa_("junkd", [128, 256], F32)
        ss = sa_("ssd", [128, 12], F32)
        rn = sa_("rn", [128, DM], BF16)
        RNT = sa_("RNT", [128, NCH, TS], BF16)
        oa = [sa_("oa%d" % k, [128, TS], F32) for k in range(2)]
        ga = [sa_("ga%d" % k, [128, TS], BF16) for k in range(2)]
        gb = [sa_("gb%d" % k, [128, TS], BF16) for k in range(2)]
        sga = [sa_("sga%d" % k, [128, TS], BF16) for k in range(2)]
        sgb = [sa_("sgb%d" % k, [128, TS], BF16) for k in range(2)]
        OAG = sa_("OAG", [128, NCH, TS], BF16)
        OBG = sa_("OBG", [128, NCH, TS], BF16)
        MG = sa_("MG", [128, NCH, TS], BF16)
        t1 = sa_("t1", [128, TS], F32)
        t2 = sa_("t2", [128, TS], F32)
        xr = [sa_("xr%d" % k, [128, DM], F32) for k in range(2)]
        yt = [sa_("yt%d" % k, [128, DM], F32) for k in range(2)]
        T.dma("sp", rgain_t[:], L["rgain_rep"], writes=["rgain"])
        load_w(Wd[:, :, 0:1024], 0, 1024, "Wd", src=wbs)
        load_w(Wd[:, :, 1024:2048], 0, 1024, "Wd", src=wbr)
        load_w(Wd[:, :, 2048:3072], 0, 1024, "Wd", src=wout)
        n = 0
        for i in range(CFG["no_d"] if "D" in phases else 0):
            t0 = i * TS
            for blk in range(4):
                a_, b_ = rA[blk % 2], rB[blk % 2]
                T.dma("sp", a_[:], ret_d[(2 * i) * TS + blk * 128:(2 * i) * TS + (blk + 1) * 128, :], reads=["ret_d"], writes=[a_.name])
                T.dma("sp", b_[:], ret_d[(2 * i + 1) * TS + blk * 128:(2 * i + 1) * TS + (blk + 1) * 128, :], reads=["ret_d"], writes=[b_.name])
                _ts(T, nc, "dve", rt[:], a_[:], blend_t[:, 0:1], None, ALU.mult, None, [a_.name, "blend"], ["rt"])
                _stt(T, nc, rt[:], b_[:], blend_t[:, 1:2], rt[:], ALU.mult, ALU.add, [b_.name, "blend", "rt"], ["rt"])
                for hd in range(4):
                    _act(T, nc, junk[:], rt[:, hd * 256:(hd + 1) * 256], AF.Square, ["rt"], ["junkd", "ssd"],
                         accum_out=ss[:, hd:hd + 1])
                _act(T, nc, ss[:, 4:8], ss[:, 0:4], AF.Sqrt, ["ssd"], ["ssd"], bias=EPS, scale=1.0 / 256)
                T.op("dve", lambda: nc.vector.reciprocal(out=ss[:, 8:12], in_=ss[:, 4:8]), reads=["ssd"], writes=["ssd"])
                for hd in range(4):
                    _stt(T, nc, rn[:, hd * 256:(hd + 1) * 256], rt[:, hd * 256:(hd + 1) * 256], ss[:, 8 + hd:9 + hd],
                         rgain_t[:, hd * 256:(hd + 1) * 256], ALU.mult, ALU.mult, ["rt", "ssd", "rgain"], ["rn"])
                for c in range(NCH):
                    _tr(T, nc, pb[0][:, c * 128:(c + 1) * 128], rn[:, c * 128:(c + 1) * 128], ident,
                        ["rn", "cmat"], ["pb0"], inc=(c == NCH - 1))
                _copy(T, nc, "act", RNT[:, :, blk * 128:(blk + 1) * 128], pb[0][:].rearrange("p (c t) -> p c t", c=NCH),
                      ["pb0"], ["RNT"])
            for c in range(NCH):
                o_, ga_, gb_ = oa[c % 2], ga[c % 2], gb[c % 2]
                T.dma("sp", o_[:], oaT_d[c, :, t0:t0 + TS], reads=["oaT_d"], writes=[o_.name])
                T.dma("sp", ga_[:], ga_d[c, :, t0:t0 + TS], reads=["gates_d"], writes=[ga_.name])
                T.dma("sp", gb_[:], gb_d[c, :, t0:t0 + TS], reads=["gates_d"], writes=[gb_.name])
                _tt(T, nc, "dve", OAG[:, c, :], o_[:], ga_[:], ALU.mult, [o_.name, ga_.name], ["OAG"])
                _tt(T, nc, "dve", OBG[:, c, :], RNT[:, c, :], gb_[:], ALU.mult, ["RNT", gb_.name], ["OBG"])
            for oc in range(NCH):
                sa_t, sb_t = sga[oc % 2], sgb[oc % 2]
                T.dma("sp", sa_t[:], sa_d[oc, :, t0:t0 + TS], reads=["gates_d"], writes=[sa_t.name])
                T.dma("sp", sb_t[:], sb_d[oc, :, t0:t0 + TS], reads=["gates_d"], writes=[sb_t.name])
                for c in range(NCH):
                    _mm(T, nc, ps[0][:], Wd[:, c, oc * 128:(oc + 1) * 128], OAG[:, c, :], c == 0, c == NCH - 1,
                        ["Wd", "OAG"], ["ps0"], inc=(c == NCH - 1))
                for c in range(NCH):
                    _mm(T, nc, ps[1][:], Wd[:, c, 1024 + oc * 128:1024 + (oc + 1) * 128], OBG[:, c, :], c == 0, c == NCH - 1,
                        ["Wd", "OBG"], ["ps1"], inc=(c == NCH - 1))
                _tt(T, nc, "dve", t1[:], ps[0][:], sa_t[:], ALU.mult, ["ps0", sa_t.name], ["t1"])
                _tt(T, nc, "dve", t2[:], ps[1][:], sb_t[:], ALU.mult, ["ps1", sb_t.name], ["t2"])
                _tt(T, nc, "dve", MG[:, oc, :], t1[:], t2[:], ALU.add, ["t1", "t2"], ["MG"])
            for blk in range(4):
                x_, y_ = xr[blk % 2], yt[blk % 2]
                T.dma("sp", x_[:], xo[t0 + blk * 128:t0 + (blk + 1) * 128, :], writes=[x_.name])
                for g in range(2):
                    pz = ps[2 + g]
                    for oc in range(NCH):
                        _mm(T, nc, pz[:], MG[:, oc, blk * 128:(blk + 1) * 128], Wd[:, oc, 2048 + g * 512:2048 + (g + 1) * 512],
                            oc == 0, oc == NCH - 1, ["MG", "Wd"], [pz.name], inc=(oc == NCH - 1))
                    _tt(T, nc, "dve", y_[:, g * 512:(g + 1) * 512], pz[:], x_[:, g * 512:(g + 1) * 512], ALU.add,
                        [pz.name, x_.name], [y_.name])
                T.dma("sp", y[t0 + blk * 128:t0 + (blk + 1) * 128, :], y_[:], reads=[y_.name], writes=["y"])


_CONST_CACHE = {}


def _const_tables():
    if _CONST_CACHE:
        return _CONST_CACHE
    bf = ml_dtypes.bfloat16
    d = 256
    inv_freq = (np.float32(10000.0) ** (-np.arange(0, d, 2, dtype=np.float32) / np.float32(d))).astype(np.float32)
    pos = np.arange(S, dtype=np.float32)
    ang = (pos[None, :] * inv_freq[:, None]).astype(np.float32)
    c64, s64 = np.cos(ang.astype(np.float64)), np.sin(ang.astype(np.float64))
    _CONST_CACHE["cosq"] = c64.astype(np.float32)
    _CONST_CACHE["sinq"] = s64.astype(np.float32)
    lg = np.log1p(-np.exp2(-5.0 - np.arange(4, dtype=np.float64)))
    idx = np.arange(128)
    same = (idx[:, None] // 64) == (idx[None, :] // 64)
    k_first = (idx[:, None] // 64) < (idx[None, :] // 64)
    dm = np.zeros((128, 4, 128), np.float64)
    for hh in range(4):
        dm[:, hh, :] = np.where(same | k_first, np.exp(lg[hh] * np.abs(idx[:, None] - idx[None, :])), 0.0)
    _CONST_CACHE["dmask"] = dm.reshape(128, 512).astype(np.float32)
    qd = np.zeros((128, 4, 128), np.float64)
    for hh in range(4):
        qd[:, hh, :] = np.exp(lg[hh] * (idx + 1.0))[None, :]
    _CONST_CACHE["qdec"] = qd.reshape(128, 4 * 128).astype(np.float32)
    kd = np.zeros((128, 4), np.float64)
    for hh in range(4):
        kd[:, hh] = np.exp(lg[hh] * (127.0 - idx))
    _CONST_CACHE["kdect"] = kd.astype(np.float32)
    cm = np.zeros((128, 4, 128), np.float32)
    cm[:, 0, :] = np.eye(128)
    cm[:, 1, :] = 1.0 / 128.0
    cm[:, 2, :] = np.where(idx[:, None] >= idx[None, :], -1.0, 0.0)
    cm[:, 3, :] = np.where(idx[:, None] < idx[None, :], -1.0, 0.0)
    _CONST_CACHE["cmat"] = cm.reshape(128, 512).astype(bf)
    q = np.arange(TS)
    diag = np.zeros((4, 128, TS), np.float32)
    for r in range(4):
        diag[r] = np.where((128 * r + idx[:, None]) < q[None, :], 0.0, NEG)
    am0 = np.full((128, 8, TS), NEG, np.float32)
    am1 = np.zeros((128, 8, TS), np.float32)
    for r in range(4):
        am0[:, r, :] = diag[r]
        am1[:, 4 + r, :] = diag[r]
    _CONST_CACHE["amask0"] = am0.reshape(128, 8 * TS).astype(bf)
    _CONST_CACHE["amask1"] = am1.reshape(128, 8 * TS).astype(bf)
    return _CONST_CACHE


_NC_CACHE = {}


def kernel(x, norm_gain, w_in, b_merge, sb_q_gain, sb_k_gain, ret_out_gain, w_branch_sb, w_branch_ret, w_out):
    x = np.asarray(x, np.float32)
    C = _const_tables()
    if "nc" not in _NC_CACHE:
        _NC_CACHE["nc"] = build_program()
    nc = _NC_CACHE["nc"]
    f = lambda a: np.ascontiguousarray(np.asarray(a, np.float32))
    w_in0, wbs0, wbr0, wout0 = f(w_in[0]), f(w_branch_sb[0]), f(w_branch_ret[0]), f(w_out[0])
    gain_rep = np.ascontiguousarray(np.broadcast_to(f(norm_gain[0])[None, :], (128, DM)))
    rgain_rep = np.ascontiguousarray(np.broadcast_to(f(ret_out_gain[0]).reshape(1, DM), (128, DM)))
    qkg = np.ascontiguousarray(np.stack([f(sb_q_gain[0]), f(sb_k_gain[0])], axis=1))
    bmg = np.ascontiguousarray(f(b_merge[0]).reshape(2, 8, 128).transpose(2, 0, 1).reshape(128, 16))
    in_maps = []
    for c in range(8):
        b, p = c // 2, c % 2
        xb = x[b]
        xown = np.ascontiguousarray(xb.reshape(NT, TS, DM)[p::2].reshape(SO, DM))
        blend = np.zeros((128, 2), np.float32)
        blend[:, 0] = 1.0 - p
        blend[:, 1] = float(p)
        in_maps.append({
            "xa": np.ascontiguousarray(xb), "xo": xown, "w_in": w_in0, "wbs": wbs0, "wbr": wbr0, "wout": wout0,
            "gain_rep": gain_rep, "rgain_rep": rgain_rep, "qkg": qkg, "bmg": bmg,
            "cosq": C["cosq"], "sinq": C["sinq"],
            "dmask": C["dmask"], "qdec": C["qdec"], "kdect": C["kdect"], "cmat": C["cmat"],
            "amask": C["amask%d" % p], "blend": blend,
        })
    res = run_bass_kernel_spmd(nc, in_maps, core_ids=list(range(8)))
    _NC_CACHE["last"] = res
    out = np.empty((4, S, DM), np.float32)
    for c in range(8):
        b, p = c // 2, c % 2
        out[b].reshape(NT, TS, DM)[p::2] = np.asarray(res.results[c]["y"], np.float32).reshape(NO, TS, DM)
    return out
```

```python
import contextlib
import numpy as np
import ml_dtypes
import concourse.bass as bass
import concourse.mybir as mybir
from concourse.bass_utils import run_bass_kernel_spmd

F32 = mybir.dt.float32
BF16 = mybir.dt.bfloat16
AF = mybir.ActivationFunctionType
ALU = mybir.AluOpType
AX = mybir.AxisListType


class Tracker:
    ENG = ("pe", "act", "dve", "pool", "sp")

    def __init__(self, nc, n_dma_sems=6):
        self.nc = nc
        self.n_dma = n_dma_sems
        self.stack = contextlib.ExitStack()
        self.streams = {e: [] for e in self.ENG}
        self.count = {}
        self.known = {e: {} for e in self.ENG}
        self.last_write = {}
        self.readers = {}
        self.n_ops = 0

    def __enter__(self):
        nc = self.nc
        self.stack.__enter__()
        self.sem = {}
        for e in self.ENG:
            self.sem[e] = self.stack.enter_context(nc.semaphore("s_" + e))
            self.count[e] = 0
        self.dma_ring = {}
        self.dma_next = {}
        for q in ("sp", "pool", "act"):
            ring = []
            for k in range(self.n_dma):
                name = "d_%s%d" % (q, k)
                self.sem[name] = self.stack.enter_context(nc.semaphore(name))
                self.count[name] = 0
                ring.append(name)
            self.dma_ring[q] = ring
            self.dma_next[q] = 0
        return self

    def __exit__(self, *a):
        return self.stack.__exit__(*a)

    def _deps(self, reads, writes):
        deps = {}

        def add(s, v):
            if v > deps.get(s, 0):
                deps[s] = v

        for b in reads:
            lw = self.last_write.get(b)
            if lw:
                add(*lw)
        for b in writes:
            lw = self.last_write.get(b)
            if lw:
                add(*lw)
            for s, v in self.readers.get(b, {}).items():
                add(s, v)
        return deps

    def _emit_waits(self, e, deps, skip_self=False):
        for s, v in deps.items():
            if skip_self and s == e:
                continue
            if self.known[e].get(s, 0) >= v:
                continue
            self.known[e][s] = v
            sem = self.sem[s]
            self.streams[e].append(("wait", sem, v))

    def _record(self, key, val, reads, writes):
        for b in reads:
            self.readers.setdefault(b, {})[key] = val
        for b in writes:
            self.last_write[b] = (key, val)
            self.readers[b] = {}

    def op(self, e, fn, reads=(), writes=(), inc=True):
        deps = self._deps(reads, writes)
        self._emit_waits(e, deps, skip_self=(e == "pe"))
        if inc:
            self.count[e] += 1
            val = self.count[e]
            self.streams[e].append(("op", fn, self.sem[e], 1))
        else:
            val = self.count[e] + 1
            self.streams[e].append(("op", fn, None, 0))
        self._record(e, val, reads, writes)
        self.n_ops += 1

    def dma(self, q, out, in_, reads=(), writes=(), **kw):
        deps = self._deps(reads, writes)
        name = self.dma_ring[q][self.dma_next[q]]
        self.dma_next[q] = (self.dma_next[q] + 1) % self.n_dma
        deps[name] = max(deps.get(name, 0), self.count[name])
        self._emit_waits(q, deps)
        self.count[name] += 16
        val = self.count[name]
        self.known[q][name] = max(self.known[q].get(name, 0), 0)
        eng = {"sp": self.nc.sync, "pool": self.nc.gpsimd, "act": self.nc.scalar}[q]
        self.streams[q].append(("op", (lambda: eng.dma_start(out=out, in_=in_, **kw)), self.sem[name], 16))
        self._record(name, val, reads, writes)
        self.n_ops += 1

    def barrier(self):
        for e in self.ENG:
            deps = {s: c for s, c in self.count.items() if c > 0 and s != e}
            self._emit_waits(e, deps)

    def finish(self):
        nc = self.nc
        deps = {s: c for s, c in self.count.items() if c > 0 and s != "sp"}
        self._emit_waits("sp", deps)
        streams = self.streams

        def replay(e, eng):
            for item in streams[e]:
                if item[0] == "wait":
                    eng.wait_ge(item[1], item[2])
                else:
                    ins = item[1]()
                    if item[2] is not None:
                        ins.then_inc(item[2], item[3])

        with nc.Block() as block:
            @block.sync
            def _(eng):
                replay("sp", eng)

            @block.scalar
            def _(eng):
                replay("act", eng)

            @block.vector
            def _(eng):
                replay("dve", eng)

            @block.gpsimd
            def _(eng):
                replay("pool", eng)

            @block.tensor
            def _(eng):
                replay("pe", eng)


S = 8192
DM = 1024
TS = 512
NT = S // TS
NO = NT // 2
SO = NO * TS
NCH = DM // 128
EPS = 1e-6
NEG = -30000.0
C_SBQ, C_SBK, C_SBV, C_SBG, C_RQ, C_RK, C_RV, C_RG, C_MSB, C_MRET = [i * 1024 for i in range(10)]
GAMMA = [1.0 - 2.0 ** (-5.0 - h) for h in range(4)]
G64 = [g ** 64 for g in GAMMA]
DEBUG = False


def _mm(T, nc, out, lhsT, rhs, start, stop, reads, writes, inc):
    T.op("pe", lambda: nc.tensor.matmul(out, lhsT=lhsT, rhs=rhs, start=start, stop=stop),
         reads=reads, writes=writes, inc=inc)


def _tr(T, nc, out, in_, ident, reads, writes, inc):
    T.op("pe", lambda: nc.tensor.transpose(out, in_, ident), reads=reads, writes=writes, inc=inc)


def _act(T, nc, out, in_, func, reads, writes, **kw):
    T.op("act", lambda: nc.scalar.activation(out=out, in_=in_, func=func, **kw), reads=reads, writes=writes)


def _ts(T, nc, eng, out, in0, s1, s2, op0, op1, reads, writes):
    e = nc.vector if eng == "dve" else nc.gpsimd
    if op1 is None:
        T.op(eng, lambda: e.tensor_scalar(out=out, in0=in0, scalar1=s1, scalar2=None, op0=op0), reads=reads, writes=writes)
    else:
        T.op(eng, lambda: e.tensor_scalar(out=out, in0=in0, scalar1=s1, scalar2=s2, op0=op0, op1=op1), reads=reads, writes=writes)


def _tt(T, nc, eng, out, in0, in1, op, reads, writes):
    e = nc.vector if eng == "dve" else nc.gpsimd
    T.op(eng, lambda: e.tensor_tensor(out=out, in0=in0, in1=in1, op=op), reads=reads, writes=writes)


def _stt(T, nc, out, in0, scalar, in1, op0, op1, reads, writes):
    T.op("dve", lambda: nc.vector.scalar_tensor_tensor(out=out, in0=in0, scalar=scalar, in1=in1, op0=op0, op1=op1),
         reads=reads, writes=writes)


def _copy(T, nc, eng, out, in_, reads, writes):
    if eng == "act":
        T.op("act", lambda: nc.scalar.copy(out=out, in_=in_), reads=reads, writes=writes)
    else:
        e = nc.vector if eng == "dve" else nc.gpsimd
        T.op(eng, lambda: e.tensor_copy(out=out, in_=in_), reads=reads, writes=writes)


def build_program(phases="ABCD", nt_a=NT, no_b=NO, heads_c=8, no_c=NO, no_d=NO):
    CFG = dict(phases=phases, nt_a=nt_a, no_b=no_b, heads_c=heads_c, no_c=no_c, no_d=no_d)
    nc = bass.Bass("TRN2", target_bir_lowering=False)
    dt = nc.dram_tensor
    xa = dt("xa", [S, DM], F32, kind="ExternalInput").ap()
    xo = dt("xo", [SO, DM], F32, kind="ExternalInput").ap()
    w_in = dt("w_in", [DM, 10 * 1024], F32, kind="ExternalInput").ap()
    wbs = dt("wbs", [DM, DM], F32, kind="ExternalInput").ap()
    wbr = dt("wbr", [DM, DM], F32, kind="ExternalInput").ap()
    wout = dt("wout", [DM, DM], F32, kind="ExternalInput").ap()
    gain_rep = dt("gain_rep", [128, DM], F32, kind="ExternalInput").ap()
    rgain_rep = dt("rgain_rep", [128, DM], F32, kind="ExternalInput").ap()
    qkg = dt("qkg", [128, 2], F32, kind="ExternalInput").ap()
    bmg = dt("bmg", [128, 16], F32, kind="ExternalInput").ap()
    cosq = dt("cosq", [128, S], F32, kind="ExternalInput").ap()
    sinq = dt("sinq", [128, S], F32, kind="ExternalInput").ap()
    dmask_d = dt("dmask", [128, 4 * 128], F32, kind="ExternalInput").ap()
    qdec_d = dt("qdec", [128, 4 * 128], F32, kind="ExternalInput").ap()
    kdect_d = dt("kdect", [128, 4], F32, kind="ExternalInput").ap()
    cmat_d = dt("cmat", [128, 4 * 128], BF16, kind="ExternalInput").ap()
    amask_d = dt("amask", [128, 8 * TS], BF16, kind="ExternalInput").ap()
    blend_d = dt("blend", [128, 2], F32, kind="ExternalInput").ap()
    y = dt("y", [SO, DM], F32, kind="ExternalOutput").ap()
    sk = "ExternalOutput" if DEBUG else "Internal"
    kT_d = dt("kT_d", [8, 128, S], BF16, kind=sk).ap()
    v_d = dt("v_d", [S, DM], BF16, kind=sk).ap()
    qT_d = dt("qT_d", [8, 128, SO], BF16, kind=sk).ap()
    ga_d = dt("ga_d", [8, 128, SO], BF16, kind=sk).ap()
    gb_d = dt("gb_d", [8, 128, SO], BF16, kind=sk).ap()
    sa_d = dt("sa_d", [8, 128, SO], BF16, kind=sk).ap()
    sb_d = dt("sb_d", [8, 128, SO], BF16, kind=sk).ap()
    ret_d = dt("ret_d", [S, DM], F32, kind=sk).ap()
    oaT_d = dt("oaT_d", [8, 128, SO], F32, kind=sk).ap()
    if DEBUG:
        dbg_bf = dt("dbg_bf", [128, 8192], BF16, kind="ExternalOutput").ap()
        dbg_f = dt("dbg_f", [128, 4096], F32, kind="ExternalOutput").ap()

    es = contextlib.ExitStack()
    with es:
        def sb(name, shape, dtype):
            return es.enter_context(nc.sbuf_tensor("t_" + name, shape, dtype))

        cmat = sb("cmat", [128, 4 * 128], BF16)
        ident = cmat[:, 0:128]
        onesm = cmat[:, 128:256]
        trineg = cmat[:, 256:384]
        carryneg = cmat[:, 384:512]
        qkg_t = sb("qkg_t", [128, 2], F32)
        qg_s = sb("qg_s", [128, 1], F32)
        bmg_t = sb("bmg_t", [128, 16], F32)
        blend_t = sb("blend_t", [128, 2], F32)
        ps, pb = [], []
        psum_ctr = [0]

        def alloc_psum(stack):
            tag = "abcdefgh"[psum_ctr[0]]
            psum_ctr[0] += 1
            ps[:] = [stack.enter_context(nc.psum_tensor("ps%d%s" % (k, tag), [128, 512], F32)) for k in range(6)]
            pb[:] = [stack.enter_context(nc.psum_tensor("pb%d%s" % (k, tag), [128, 1024], BF16)) for k in range(2)]
        T = es.enter_context(Tracker(nc))

        T.dma("sp", cmat[:], cmat_d, writes=["cmat"])
        T.dma("sp", qkg_t[:], qkg, writes=["qkg"])
        T.dma("sp", bmg_t[:], bmg, writes=["bmg"])
        T.dma("sp", blend_t[:], blend_d, writes=["blend"])
        _ts(T, nc, "dve", qg_s[:], qkg_t[:, 0:1], float(128 ** -0.5), None, ALU.mult, None, ["qkg"], ["qg_s"])

        def load_w(wt, col0, ncols, key, src=w_in):
            for c0 in range(0, ncols, 512):
                T.dma("pool", wt[:, :, c0:c0 + 512],
                      src[:, col0 + c0: col0 + c0 + 512].rearrange("(c p) n -> p c n", p=128),
                      writes=[key])

        def norm_block(xsrc, r0, xt, junk, ss, gain_t, h, hT, col0, pbt=None):
            pbt = pbt if pbt is not None else pb[0]
            T.dma("sp", xt[:], xsrc[r0:r0 + 128, :], writes=[xt.name])
            _act(T, nc, junk[:], xt[:], AF.Square, [xt.name], [junk.name, ss.name], accum_out=ss[:, 0:1])
            _act(T, nc, ss[:, 1:2], ss[:, 0:1], AF.Sqrt, [ss.name], [ss.name], bias=EPS, scale=1.0 / DM)
            T.op("dve", lambda: nc.vector.reciprocal(out=ss[:, 2:3], in_=ss[:, 1:2]), reads=[ss.name], writes=[ss.name])
            _stt(T, nc, h[:], xt[:], ss[:, 2:3], gain_t[:], ALU.mult, ALU.mult, [xt.name, ss.name, "gain"], [h.name])
            for c in range(NCH):
                _tr(T, nc, pbt[:, c * 128:(c + 1) * 128], h[:, c * 128:(c + 1) * 128], ident,
                    [h.name, "cmat"], [pbt.name], inc=(c == NCH - 1))
            _copy(T, nc, "act", hT[:, :, col0:col0 + 128], pbt[:].rearrange("p (c t) -> p c t", c=NCH),
                  [pbt.name], ["hT"])

        def norm_pre(xsrc, r0, xt, junk, ss, gain_t, h):
            T.dma("sp", xt[:], xsrc[r0:r0 + 128, :], writes=[xt.name])
            _act(T, nc, junk[:], xt[:], AF.Square, [xt.name], [junk.name, ss.name], accum_out=ss[:, 0:1])
            _act(T, nc, ss[:, 1:2], ss[:, 0:1], AF.Sqrt, [ss.name], [ss.name], bias=EPS, scale=1.0 / DM)
            T.op("dve", lambda: nc.vector.reciprocal(out=ss[:, 2:3], in_=ss[:, 1:2]), reads=[ss.name], writes=[ss.name])
            _stt(T, nc, h[:], xt[:], ss[:, 2:3], gain_t[:], ALU.mult, ALU.mult, [xt.name, ss.name, "gain"], [h.name])

        def norm_post(h, hT, col0, pbt):
            for c in range(NCH):
                _tr(T, nc, pbt[:, c * 128:(c + 1) * 128], h[:, c * 128:(c + 1) * 128], ident,
                    [h.name, "cmat"], [pbt.name], inc=(c == NCH - 1))
            _copy(T, nc, "act", hT[:, :, col0:col0 + 128], pbt[:].rearrange("p (c t) -> p c t", c=NCH),
                  [pbt.name], ["hT"])

        def qk_proj(W, wkey, wcol, hT, pz):
            for c in range(NCH):
                _mm(T, nc, pz[:], W[:, c, wcol:wcol + 128], hT[:, c, :], c == 0, c == NCH - 1,
                    [wkey, "hT"], [pz.name], inc=(c == NCH - 1))

        def qk_norm(gcol, pz, pm, sq, rs, outt, outkey):
            _act(T, nc, sq[:], pz[:], AF.Square, [pz.name], [sq.name])
            _mm(T, nc, pm[:], onesm, sq[:], True, True, ["cmat", sq.name], [pm.name], True)
            _act(T, nc, rs[:], pm[:], AF.Ln, [pm.name], [rs.name], bias=EPS, scale=1.0)
            _act(T, nc, rs[:], rs[:], AF.Exp, [rs.name], [rs.name], scale=-0.5)
            _stt(T, nc, outt, pz[:], gcol, rs[:], ALU.mult, ALU.mult, [pz.name, rs.name, "qkg", "qg_s"], [outkey])

        def qk_head(W, wkey, wcol, hT, gcol, pz, pm, sq, rs, outt, outkey):
            qk_proj(W, wkey, wcol, hT, pz)
            qk_norm(gcol, pz, pm, sq, rs, outt, outkey)

        with contextlib.ExitStack() as pa:
            def sa_(name, shape, dtype):
                return pa.enter_context(nc.sbuf_tensor("t_" + name, shape, dtype))
            alloc_psum(pa)
            Wa = sa_("Wa", [128, NCH, 5 * 1024], BF16)
            gain_t = sa_("gain_t", [128, DM], F32)
            xts = [sa_("xt%d" % k, [128, DM], F32) for k in range(2)]
            junk = sa_("junk", [128, DM], BF16)
            sss = [sa_("ss%d" % k, [128, 4], F32) for k in range(2)]
            hs = [sa_("h%d" % k, [128, DM], BF16) for k in range(4)]
            hT = sa_("hT", [128, NCH, TS], BF16)
            sqs = [sa_("sq%d" % k, [128, TS], BF16) for k in range(2)]
            rss = [sa_("rs%d" % k, [128, TS], F32) for k in range(2)]
            kout = [sa_("kout%d" % k, [128, TS], BF16) for k in range(2)]
            vout = [sa_("vout%d" % k, [128, DM], BF16) for k in range(2)]
            cs = sa_("cs", [128, 2, TS], F32)
            ra = [sa_("ra%d" % k, [128, TS], F32) for k in range(2)]
            rb = [sa_("rb%d" % k, [128, TS], F32) for k in range(2)]
            qrT = [sa_("qrT%d" % k, [128, 2, TS], BF16) for k in range(2)]
            krT = [sa_("krT%d" % k, [128, 2, TS], BF16) for k in range(2)]
            qdT = [sa_("qdT%d" % k, [128, 2, TS], BF16) for k in range(2)]
            kdt = [sa_("kdt%d" % k, [128, 4, 256], BF16) for k in range(2)]
            rv = sa_("rv", [128, 4, DM], BF16)
            sT = [sa_("sT%d" % k, [128, 4, 128], BF16) for k in range(2)]
            St = sa_("St", [128, 4, 2, 256], F32)
            Sb0 = sa_("Sb0", [128, 4, 2, 256], BF16)
            Sbt = [sa_("Sbt%d" % k, [128, 3, 2, 256], BF16) for k in range(2)]
            dmask = sa_("dmask", [128, 4, 128], F32)
            qdec = sa_("qdec", [128, 4, 128], F32)
            kdect = sa_("kdect", [128, 4], F32)
            rout = [sa_("rout%d" % k, [128, 256], F32) for k in range(2)]
            pb0f, pb1f = pb[0][:].bitcast(F32), pb[1][:].bitcast(F32)

            T.dma("sp", gain_t[:], gain_rep, writes=["gain"])
            T.dma("sp", dmask[:].rearrange("p h q -> p (h q)"), dmask_d, writes=["dmask"])
            T.dma("sp", qdec[:].rearrange("p h q -> p (h q)"), qdec_d, writes=["qdec"])
            T.dma("sp", kdect[:], kdect_d, writes=["kdect"])
            load_w(Wa[:, :, 0:1024], C_SBK, 1024, "Wa")
            load_w(Wa[:, :, 1024:2048], C_SBV, 1024, "Wa")
            load_w(Wa[:, :, 2048:3072], C_RQ, 1024, "Wa")
            load_w(Wa[:, :, 3072:4096], C_RK, 1024, "Wa")
            load_w(Wa[:, :, 4096:5120], C_RV, 1024, "Wa")
            T.op("dve", lambda: nc.vector.memset(St[:].rearrange("p a b c -> p (a b c)"), 0.0), writes=["St"])
            T.op("dve", lambda: nc.vector.memset(Sb0[:].rearrange("p a b c -> p (a b c)"), 0.0), writes=["Sb0"])
            G128 = [g ** 128 for g in GAMMA]

            def ret_proj(hd, t0):
                par = hd % 2
                for which in range(2):
                    wc = (2048 if which == 0 else 3072) + hd * 256
                    p1, p2 = (ps[0], ps[1]) if which == 0 else (ps[2], ps[3])
                    for half, pz in enumerate((p1, p2)):
                        for c in range(NCH):
                            _mm(T, nc, pz[:], Wa[:, c, wc + half * 128: wc + (half + 1) * 128], hT[:, c, :],
                                c == 0, c == NCH - 1, ["Wa", "hT"], [pz.name], inc=(c == NCH - 1))
                    dst = qrT[par] if which == 0 else krT[par]
                    ct, st_ = cs[:, 0, :], cs[:, 1, :]
                    _tt(T, nc, "dve", ra[0][:], p1[:], ct, ALU.mult, [p1.name, "cs"], [ra[0].name])
                    _tt(T, nc, "dve", rb[0][:], p2[:], st_, ALU.mult, [p2.name, "cs"], [rb[0].name])
                    _tt(T, nc, "dve", ra[1][:], p1[:], st_, ALU.mult, [p1.name, "cs"], [ra[1].name])
                    _tt(T, nc, "dve", rb[1][:], p2[:], ct, ALU.mult, [p2.name, "cs"], [rb[1].name])
                    for half in range(2):
                        _tt(T, nc, "pool", ra[half][:], ra[half][:], rb[half][:], ALU.subtract if half == 0 else ALU.add,
                            [ra[half].name, rb[half].name], [ra[half].name])
                        if which == 0:
                            _copy(T, nc, "act", dst[:, half, :], ra[half][:], [ra[half].name], [dst.name])
                            for blk in range(4):
                                _tt(T, nc, "pool", qdT[par][:, half, blk * 128:(blk + 1) * 128], ra[half][:, blk * 128:(blk + 1) * 128],
                                    qdec[:, hd, :], ALU.mult, [ra[half].name, "qdec"], [qdT[par].name])
                        else:
                            T.op("act", (lambda o=dst[:, half, :], i_=ra[half][:]: nc.scalar.mul(out=o, in_=i_, mul=1.0 / 16.0)),
                                 reads=[ra[half].name], writes=[dst.name])

            def ret_small(hd, t0):
                par = hd % 2
                kr, qr, qd, kd, st = krT[par], qrT[par], qdT[par], kdt[par], sT[par]
                for blk in range(4):
                    for half in range(2):
                        _tr(T, nc, pb[1][:, blk * 256 + half * 128:blk * 256 + (half + 1) * 128], kr[:, half, blk * 128:(blk + 1) * 128], ident,
                            [kr.name, "cmat"], [pb[1].name], inc=(blk == 3 and half == 1))
                _ts(T, nc, "dve", kd[:].rearrange("p a b -> p (a b)"), pb[1][:], kdect[:, hd:hd + 1], None, ALU.mult, None,
                    [pb[1].name, "kdect"], [kd.name])
                for blk in range(4):
                    kv = ps[4 + blk % 2]
                    for half in range(2):
                        _mm(T, nc, kv[:, half * 256:(half + 1) * 256], kd[:, blk, half * 128:(half + 1) * 128],
                            rv[:, blk, hd * 256:(hd + 1) * 256], True, True, [kd.name, "rv"], [kv.name], inc=(half == 1))
                    sflat = St[:, hd, :, :].rearrange("p a b -> p (a b)")
                    _stt(T, nc, sflat, sflat, float(G128[hd]), kv[:], ALU.mult, ALU.add, ["St", kv.name], ["St"])
                    if blk < 3:
                        _copy(T, nc, "act", Sbt[par][:, blk, :, :].rearrange("p a b -> p (a b)"), sflat, ["St"], [Sbt[par].name + str(blk)])
                for blk in range(4):
                    b0 = blk * 128
                    for half in range(2):
                        _mm(T, nc, pb0f[:, blk * 128:(blk + 1) * 128], kr[:, half, b0:b0 + 128], qr[:, half, b0:b0 + 128],
                            half == 0, half == 1, [kr.name, qr.name], [pb[0].name], inc=(half == 1))
                for blk in range(4):
                    _tt(T, nc, "dve", st[:, blk, :], pb0f[:, blk * 128:(blk + 1) * 128], dmask[:, hd, :], ALU.mult,
                        [pb[0].name, "dmask"], [st.name])
                for blk in range(4):
                    b0 = blk * 128
                    pot = ps[blk % 2]
                    po = pot[:]
                    _mm(T, nc, po[:, 0:256], st[:, blk, :], rv[:, blk, hd * 256:(hd + 1) * 256], True, False,
                        [st.name, "rv"], [pot.name], inc=True)
                    for half in range(2):
                        sb_ap = Sb0[:, hd, half, :] if blk == 0 else Sbt[par][:, blk - 1, half, :]
                        skey = "Sb0" if blk == 0 else Sbt[par].name + str(blk - 1)
                        _mm(T, nc, po[:, 0:256], qd[:, half, b0:b0 + 128], sb_ap, False, half == 1,
                            [qd.name, skey], [pot.name], inc=(half == 1))
                    ro = rout[blk % 2]
                    _copy(T, nc, "act" if blk % 2 == 0 else "dve", ro[:], po[:, 0:256], [pot.name], [ro.name])
                    T.dma("sp", ret_d[t0 + b0:t0 + b0 + 128, hd * 256:(hd + 1) * 256], ro[:], reads=[ro.name], writes=["ret_d"])
                _copy(T, nc, "act", Sb0[:, hd, :, :].rearrange("p a b -> p (a b)"),
                      St[:, hd, :, :].rearrange("p a b -> p (a b)"), ["St"], ["Sb0"])

            for t in range(nt_a if "A" in phases else 0):
                t0 = t * TS
                for k, tab in enumerate((cosq, sinq)):
                    T.dma("sp", cs[:, k, :], tab[:, t0:t0 + TS], writes=["cs"])
                if t == 0:
                    for blk in range(4):
                        norm_pre(xa, blk * 128, xts[blk % 2], junk, sss[blk % 2], gain_t, hs[blk])
                for blk in range(4):
                    norm_post(hs[blk], hT, blk * 128, pb[blk % 2])
                pzk = [ps[0], ps[1], ps[4]]
                qk_proj(Wa, "Wa", 0, hT, pzk[0])
                qk_proj(Wa, "Wa", 128, hT, pzk[1])
                for hd in range(8):
                    if hd + 2 < 8:
                        qk_proj(Wa, "Wa", (hd + 2) * 128, hT, pzk[(hd + 2) % 3])
                    ko = kout[hd % 2]
                    qk_norm(qkg_t[:, 1:2], pzk[hd % 3], ps[2 + hd % 2], sqs[hd % 2], rss[hd % 2], ko[:], ko.name)
                    T.dma("sp", kT_d[hd, :, t0:t0 + TS], ko[:], reads=[ko.name], writes=["kT_d"])
                nb = 0
                for blk in range(4):
                    vo = vout[blk % 2]
                    for g in range(2):
                        pz = ps[(4 + nb) % 6]
                        nb += 1
                        for c in range(NCH):
                            _mm(T, nc, pz[:], hT[:, c, blk * 128:(blk + 1) * 128], Wa[:, c, 1024 + g * 512:1024 + (g + 1) * 512],
                                c == 0, c == NCH - 1, ["hT", "Wa"], [pz.name], inc=(c == NCH - 1))
                        _copy(T, nc, "act" if g == 0 else "dve", vo[:, g * 512:(g + 1) * 512], pz[:], [pz.name], [vo.name])
                    T.dma("sp", v_d[t0 + blk * 128:t0 + (blk + 1) * 128, :], vo[:], reads=[vo.name], writes=["v_d"])
                    for g in range(2):
                        pz = ps[(4 + nb) % 6]
                        nb += 1
                        for c in range(NCH):
                            _mm(T, nc, pz[:], hT[:, c, blk * 128:(blk + 1) * 128], Wa[:, c, 4096 + g * 512:4096 + (g + 1) * 512],
                                c == 0, c == NCH - 1, ["hT", "Wa"], [pz.name], inc=(c == NCH - 1))
                        _copy(T, nc, "act" if g == 0 else "dve", rv[:, blk, g * 512:(g + 1) * 512], pz[:], [pz.name], ["rv"])
                if t + 1 < (nt_a if "A" in phases else 0):
                    for blk in range(4):
                        norm_pre(xa, t0 + TS + blk * 128, xts[blk % 2], junk, sss[blk % 2], gain_t, hs[blk])
                ret_proj(0, t0)
                for hd in range(4):
                    if hd + 1 < 4:
                        ret_proj(hd + 1, t0)
                    ret_small(hd, t0)
            T.barrier()
        build_rest(nc, T, locals())
        T.finish()
    return nc


def build_rest(nc, T, L):
    CFG = L["CFG"]
    phases = CFG["phases"]
    ps, pb = L["ps"], L["pb"]
    ident, onesm, trineg, carryneg = L["ident"], L["onesm"], L["trineg"], L["carryneg"]
    qg_s, bmg_t, blend_t = L["qg_s"], L["bmg_t"], L["blend_t"]
    xo, y = L["xo"], L["y"]
    kT_d, v_d, qT_d, ga_d, gb_d, sa_d, sb_d, ret_d, oaT_d = (L[k] for k in
        ("kT_d", "v_d", "qT_d", "ga_d", "gb_d", "sa_d", "sb_d", "ret_d", "oaT_d"))
    load_w, norm_block, qk_head = L["load_w"], L["norm_block"], L["qk_head"]
    wbs, wbr, wout = L["wbs"], L["wbr"], L["wout"]

    with contextlib.ExitStack() as pbs:
        def sa_(name, shape, dtype):
            return pbs.enter_context(nc.sbuf_tensor("t_" + name, shape, dtype))
        L["alloc_psum"](pbs)
        Wb = sa_("Wb", [128, NCH, 5 * 1024], BF16)
        gain_t = sa_("gain_tb", [128, DM], F32)
        xt = sa_("xtb", [128, DM], F32)
        junk = sa_("junkb", [128, DM], F32)
        ss = sa_("ssb", [128, 4], F32)
        h = sa_("hb", [128, DM], BF16)
        hT = sa_("hTb", [128, NCH, TS], BF16)
        sq = sa_("sqb", [128, TS], BF16)
        rs = sa_("rsb", [128, TS], F32)
        qout = [sa_("qout%d" % k, [128, TS], BF16) for k in range(2)]
        gout = [sa_("gout%d" % k, [128, TS], BF16) for k in range(4)]
        T.dma("sp", gain_t[:], L["gain_rep"], writes=["gain"])
        for k, col in enumerate((C_SBQ, C_SBG, C_RG, C_MSB, C_MRET)):
            load_w(Wb[:, :, k * 1024:(k + 1) * 1024], col, 1024, "Wb")

        for i in range(CFG["no_b"] if "B" in phases else 0):
            t0 = i * TS
            for blk in range(4):
                norm_block(xo, t0 + blk * 128, xt, junk, ss, gain_t, h, hT, blk * 128)
            for hd in range(8):
                qo = qout[hd % 2]
                qk_head(Wb, "Wb", hd * 128, hT, qg_s[:, 0:1], ps[hd % 2], ps[2], sq, rs, qo[:], qo.name)
                T.dma("sp", qT_d[hd, :, t0:t0 + TS], qo[:], reads=[qo.name], writes=["qT_d"])
            n = 0
            for k, (dst, func) in enumerate(((ga_d, AF.Silu), (gb_d, AF.Silu), (sa_d, AF.Sigmoid), (sb_d, AF.Sigmoid))):
                for c in range(NCH):
                    pz = ps[3 + (n % 3)]
                    go = gout[n % 4]
                    n += 1
                    wc = (k + 1) * 1024 + c * 128
                    for cc in range(NCH):
                        _mm(T, nc, pz[:], Wb[:, cc, wc:wc + 128], hT[:, cc, :], cc == 0, cc == NCH - 1,
                            ["Wb", "hT"], [pz.name], inc=(cc == NCH - 1))
                    if func == AF.Silu:
                        _act(T, nc, go[:], pz[:], func, [pz.name], [go.name])
                    else:
                        bcol = (k - 2) * 8 + c
                        _act(T, nc, go[:], pz[:], func, [pz.name, "bmg"], [go.name], bias=bmg_t[:, bcol:bcol + 1], scale=1.0)
                    T.dma("sp", dst[c, :, t0:t0 + TS], go[:], reads=[go.name], writes=["gates_d"])
        T.barrier()

    with contextlib.ExitStack() as pcs:
        def sa_(name, shape, dtype):
            return pcs.enter_context(nc.sbuf_tensor("t_" + name, shape, dtype))
        KT = [sa_("KT%d" % k, [128, S], BF16) for k in range(2)]
        VV = [sa_("VV%d" % k, [128, S // 128, 128], BF16) for k in range(2)]
        QT = [sa_("QT%d" % k, [128, SO], BF16) for k in range(2)]
        amask = sa_("amask", [128, 8, TS], BF16)
        E = [sa_("E%d" % k, [128, 2, TS], F32) for k in range(2)]
        Lp = [sa_("Lp%d" % k, [128, 2, TS], BF16) for k in range(2)]
        G = [sa_("G%d" % k, [128, 2, TS], F32) for k in range(2)]
        W = [sa_("W%d" % k, [128, 2, TS], BF16) for k in range(2)]
        osb = [sa_("osb%d" % k, [128, TS], F32) for k in range(2)]
        T.dma("sp", amask[:].rearrange("p r q -> p (r q)"), L["amask_d"], writes=["amask"])
        Zall = pcs.enter_context(nc.psum_tensor("Zall", [128, 2, 2, TS], F32))
        Call = pcs.enter_context(nc.psum_tensor("Call", [128, 2, TS], F32))
        Oall = pcs.enter_context(nc.psum_tensor("Oall", [128, 2, TS], F32))
        seqs = ([0, 3, 4, 7], [1, 2, 5, 6])
        steps = [[(i, kb) for i in seq for kb in range(8 * i + 7, -1, -1)] for seq in seqs]
        NS = len(steps[0])
        assert NS == len(steps[1])
        if CFG["no_c"] < NO:
            seqs = ([0], [0]) if CFG["no_c"] == 1 else seqs
            steps = [[(i, kb) for i in seq for kb in range(8 * i + 7, -1, -1)] for seq in seqs]
            NS = len(steps[0])
        nosb = [0]
        for hd in range(CFG["heads_c"] if "C" in phases else 0):
            sl = hd % 2
            kt, vv, qt = KT[sl], VV[sl], QT[sl]
            for part in range(4):
                T.dma("sp", kt[:, part * 2048:(part + 1) * 2048], kT_d[hd, :, part * 2048:(part + 1) * 2048],
                      reads=["kT_d"], writes=[kt.name])
            for part in range(4):
                T.dma("sp", vv[:, part * 16:(part + 1) * 16, :],
                      v_d[part * 2048:(part + 1) * 2048, hd * 128:(hd + 1) * 128].rearrange("(b p) d -> p b d", p=128),
                      reads=["v_d"], writes=[vv.name])
            T.dma("sp", qt[:], qT_d[hd, :, :], reads=["qT_d"], writes=[qt.name])

            def pe1(st, t):
                i, kb = steps[st][t]
                par = t % 2
                Z = Zall[:, par, st, :]
                masked = kb >= 8 * i
                _mm(T, nc, Z, kt[:, kb * 128:(kb + 1) * 128], qt[:, i * TS:(i + 1) * TS], True, not masked,
                    [kt.name, qt.name], ["Z%d" % par], inc=(not masked))
                if masked:
                    _mm(T, nc, Z, ident, amask[:, kb - 8 * i, :], False, True, ["cmat", "amask"], ["Z%d" % par], True)

            def a1(t):
                par = t % 2
                _act(T, nc, E[par][:], Zall[:, par, :, :], AF.Exp, ["Z%d" % par], [E[par].name])

            def a2(t):
                par = t % 2
                _act(T, nc, Lp[par][:], E[par][:], AF.Ln, [E[par].name], [Lp[par].name], bias=1.0, scale=1.0)

            def pe2(st, t):
                i, kb = steps[st][t]
                l_ = Lp[t % 2]
                fst = (kb == 8 * i + 7)
                T.op("pe", (lambda o=Call[:, st, :], r_=l_[:, st, :], f_=fst: nc.tensor.matmul(o, lhsT=trineg, rhs=r_, start=f_, stop=True, skip_group_check=(not f_))),
                     reads=["cmat", l_.name], writes=["Call"], inc=True)

            def a3(t):
                par = t % 2
                _act(T, nc, G[par][:], Call[:], AF.Exp, ["Call"], [G[par].name])

            def pe3(st, t):
                l_ = Lp[t % 2]
                T.op("pe", (lambda o=Call[:, st, :], r_=l_[:, st, :]: nc.tensor.matmul(o, lhsT=carryneg, rhs=r_, start=False, stop=True, skip_group_check=True)),
                     reads=["cmat", l_.name], writes=["Call"], inc=True)

            def v1(t):
                par = t % 2
                _tt(T, nc, "dve", W[par][:], E[par][:], G[par][:], ALU.mult, [E[par].name, G[par].name], [W[par].name])

            def pe4(st, t):
                i, kb = steps[st][t]
                w_ = W[t % 2]
                _mm(T, nc, Oall[:, st, :], vv[:, kb, :], w_[:, st, :], kb == 8 * i + 7, kb == 0, [vv.name, w_.name], ["O%d" % st], True)
                if kb == 0:
                    ob = osb[nosb[0] % 2]
                    nosb[0] += 1
                    _copy(T, nc, "dve", ob[:], Oall[:, st, :], ["O%d" % st], [ob.name])
                    T.dma("sp", oaT_d[hd, :, i * TS:(i + 1) * TS], ob[:], reads=[ob.name], writes=["oaT_d"])

            for st in range(2):
                pe1(st, 0)
            if NS > 1:
                for st in range(2):
                    pe1(st, 1)
            a1(0)
            a2(0)
            for t in range(NS):
                if t + 1 < NS:
                    a1(t + 1)
                for st in range(2):
                    pe2(st, t)
                if t + 2 < NS:
                    for st in range(2):
                        pe1(st, t + 2)
                if t > 0:
                    for st in range(2):
                        pe4(st, t - 1)
                a3(t)
                for st in range(2):
                    pe3(st, t)
                v1(t)
                if t + 1 < NS:
                    a2(t + 1)
            for st in range(2):
                pe4(st, NS - 1)
        T.barrier()

    with contextlib.ExitStack() as pds:
        def sa_(name, shape, dtype):
            return pds.enter_context(nc.sbuf_tensor("t_" + name, shape, dtype))
        L["alloc_psum"](pds)
        Wd = sa_("Wd", [128, NCH, 3 * 1024], BF16)
        rgain_t = sa_("rgain_t", [128, DM], F32)
        rA = [sa_("rA%d" % k, [128, DM], F32) for k in range(2)]
        rB = [sa_("rB%d" % k, [128, DM], F32) for k in range(2)]
        rt = sa_("rt", [128, DM], F32)
        junk = sa_("junkd", [128, 256], F32)
        ss = sa_("ssd", [128, 12], F32)
        rn = sa_("rn", [128, DM], BF16)
        RNT = sa_("RNT", [128, NCH, TS], BF16)
        oa = [sa_("oa%d" % k, [128, TS], F32) for k in range(2)]
        ga = [sa_("ga%d" % k, [128, TS], BF16) for k in range(2)]
        gb = [sa_("gb%d" % k, [128, TS], BF16) for k in range(2)]
        sga = [sa_("sga%d" % k, [128, TS], BF16) for k in range(2)]
        sgb = [sa_("sgb%d" % k, [128, TS], BF16) for k in range(2)]
        OAG = sa_("OAG", [128, NCH, TS], BF16)
        OBG = sa_("OBG", [128, NCH, TS], BF16)
        MG = sa_("MG", [128, NCH, TS], BF16)
        t1 = sa_("t1", [128, TS], F32)
        t2 = sa_("t2", [128, TS], F32)
        xr = [sa_("xr%d" % k, [128, DM], F32) for k in range(2)]
        yt = [sa_("yt%d" % k, [128, DM], F32) for k in range(2)]
        T.dma("sp", rgain_t[:], L["rgain_rep"], writes=["rgain"])
        load_w(Wd[:, :, 0:1024], 0, 1024, "Wd", src=wbs)
        load_w(Wd[:, :, 1024:2048], 0, 1024, "Wd", src=wbr)
        load_w(Wd[:, :, 2048:3072], 0, 1024, "Wd", src=wout)
        n = 0
        for i in range(CFG["no_d"] if "D" in phases else 0):
            t0 = i * TS
            for blk in range(4):
                a_, b_ = rA[blk % 2], rB[blk % 2]
                T.dma("sp", a_[:], ret_d[(2 * i) * TS + blk * 128:(2 * i) * TS + (blk + 1) * 128, :], reads=["ret_d"], writes=[a_.name])
                T.dma("sp", b_[:], ret_d[(2 * i + 1) * TS + blk * 128:(2 * i + 1) * TS + (blk + 1) * 128, :], reads=["ret_d"], writes=[b_.name])
                _ts(T, nc, "dve", rt[:], a_[:], blend_t[:, 0:1], None, ALU.mult, None, [a_.name, "blend"], ["rt"])
                _stt(T, nc, rt[:], b_[:], blend_t[:, 1:2], rt[:], ALU.mult, ALU.add, [b_.name, "blend", "rt"], ["rt"])
                for hd in range(4):
                    _act(T, nc, junk[:], rt[:, hd * 256:(hd + 1) * 256], AF.Square, ["rt"], ["junkd", "ssd"],
                         accum_out=ss[:, hd:hd + 1])
                _act(T, nc, ss[:, 4:8], ss[:, 0:4], AF.Sqrt, ["ssd"], ["ssd"], bias=EPS, scale=1.0 / 256)
                T.op("dve", lambda: nc.vector.reciprocal(out=ss[:, 8:12], in_=ss[:, 4:8]), reads=["ssd"], writes=["ssd"])
                for hd in range(4):
                    _stt(T, nc, rn[:, hd * 256:(hd + 1) * 256], rt[:, hd * 256:(hd + 1) * 256], ss[:, 8 + hd:9 + hd],
                         rgain_t[:, hd * 256:(hd + 1) * 256], ALU.mult, ALU.mult, ["rt", "ssd", "rgain"], ["rn"])
                for c in range(NCH):
                    _tr(T, nc, pb[0][:, c * 128:(c + 1) * 128], rn[:, c * 128:(c + 1) * 128], ident,
                        ["rn", "cmat"], ["pb0"], inc=(c == NCH - 1))
                _copy(T, nc, "act", RNT[:, :, blk * 128:(blk + 1) * 128], pb[0][:].rearrange("p (c t) -> p c t", c=NCH),
                      ["pb0"], ["RNT"])
            for c in range(NCH):
                o_, ga_, gb_ = oa[c % 2], ga[c % 2], gb[c % 2]
                T.dma("sp", o_[:], oaT_d[c, :, t0:t0 + TS], reads=["oaT_d"], writes=[o_.name])
                T.dma("sp", ga_[:], ga_d[c, :, t0:t0 + TS], reads=["gates_d"], writes=[ga_.name])
                T.dma("sp", gb_[:], gb_d[c, :, t0:t0 + TS], reads=["gates_d"], writes=[gb_.name])
                _tt(T, nc, "dve", OAG[:, c, :], o_[:], ga_[:], ALU.mult, [o_.name, ga_.name], ["OAG"])
                _tt(T, nc, "dve", OBG[:, c, :], RNT[:, c, :], gb_[:], ALU.mult, ["RNT", gb_.name], ["OBG"])
            for oc in range(NCH):
                sa_t, sb_t = sga[oc % 2], sgb[oc % 2]
                T.dma("sp", sa_t[:], sa_d[oc, :, t0:t0 + TS], reads=["gates_d"], writes=[sa_t.name])
                T.dma("sp", sb_t[:], sb_d[oc, :, t0:t0 + TS], reads=["gates_d"], writes=[sb_t.name])
                for c in range(NCH):
                    _mm(T, nc, ps[0][:], Wd[:, c, oc * 128:(oc + 1) * 128], OAG[:, c, :], c == 0, c == NCH - 1,
                        ["Wd", "OAG"], ["ps0"], inc=(c == NCH - 1))
                for c in range(NCH):
                    _mm(T, nc, ps[1][:], Wd[:, c, 1024 + oc * 128:1024 + (oc + 1) * 128], OBG[:, c, :], c == 0, c == NCH - 1,
                        ["Wd", "OBG"], ["ps1"], inc=(c == NCH - 1))
                _tt(T, nc, "dve", t1[:], ps[0][:], sa_t[:], ALU.mult, ["ps0", sa_t.name], ["t1"])
                _tt(T, nc, "dve", t2[:], ps[1][:], sb_t[:], ALU.mult, ["ps1", sb_t.name], ["t2"])
                _tt(T, nc, "dve", MG[:, oc, :], t1[:], t2[:], ALU.add, ["t1", "t2"], ["MG"])
            for blk in range(4):
                x_, y_ = xr[blk % 2], yt[blk % 2]
                T.dma("sp", x_[:], xo[t0 + blk * 128:t0 + (blk + 1) * 128, :], writes=[x_.name])
                for g in range(2):
                    pz = ps[2 + g]
                    for oc in range(NCH):
                        _mm(T, nc, pz[:], MG[:, oc, blk * 128:(blk + 1) * 128], Wd[:, oc, 2048 + g * 512:2048 + (g + 1) * 512],
                            oc == 0, oc == NCH - 1, ["MG", "Wd"], [pz.name], inc=(oc == NCH - 1))
                    _tt(T, nc, "dve", y_[:, g * 512:(g + 1) * 512], pz[:], x_[:, g * 512:(g + 1) * 512], ALU.add,
                        [pz.name, x_.name], [y_.name])
                T.dma("sp", y[t0 + blk * 128:t0 + (blk + 1) * 128, :], y_[:], reads=[y_.name], writes=["y"])


_CONST_CACHE = {}


def _const_tables():
    if _CONST_CACHE:
        return _CONST_CACHE
    bf = ml_dtypes.bfloat16
    d = 256
    inv_freq = (np.float32(10000.0) ** (-np.arange(0, d, 2, dtype=np.float32) / np.float32(d))).astype(np.float32)
    pos = np.arange(S, dtype=np.float32)
    ang = (pos[None, :] * inv_freq[:, None]).astype(np.float32)
    c64, s64 = np.cos(ang.astype(np.float64)), np.sin(ang.astype(np.float64))
    _CONST_CACHE["cosq"] = c64.astype(np.float32)
    _CONST_CACHE["sinq"] = s64.astype(np.float32)
    lg = np.log1p(-np.exp2(-5.0 - np.arange(4, dtype=np.float64)))
    idx = np.arange(128)
    same = (idx[:, None] // 64) == (idx[None, :] // 64)
    k_first = (idx[:, None] // 64) < (idx[None, :] // 64)
    dm = np.zeros((128, 4, 128), np.float64)
    for hh in range(4):
        dm[:, hh, :] = np.where(same | k_first, np.exp(lg[hh] * np.abs(idx[:, None] - idx[None, :])), 0.0)
    _CONST_CACHE["dmask"] = dm.reshape(128, 512).astype(np.float32)
    qd = np.zeros((128, 4, 128), np.float64)
    for hh in range(4):
        qd[:, hh, :] = np.exp(lg[hh] * (idx + 1.0))[None, :]
    _CONST_CACHE["qdec"] = qd.reshape(128, 4 * 128).astype(np.float32)
    kd = np.zeros((128, 4), np.float64)
    for hh in range(4):
        kd[:, hh] = np.exp(lg[hh] * (127.0 - idx))
    _CONST_CACHE["kdect"] = kd.astype(np.float32)
    cm = np.zeros((128, 4, 128), np.float32)
    cm[:, 0, :] = np.eye(128)
    cm[:, 1, :] = 1.0 / 128.0
    cm[:, 2, :] = np.where(idx[:, None] >= idx[None, :], -1.0, 0.0)
    cm[:, 3, :] = np.where(idx[:, None] < idx[None, :], -1.0, 0.0)
    _CONST_CACHE["cmat"] = cm.reshape(128, 512).astype(bf)
    q = np.arange(TS)
    diag = np.zeros((4, 128, TS), np.float32)
    for r in range(4):
        diag[r] = np.where((128 * r + idx[:, None]) < q[None, :], 0.0, NEG)
    am0 = np.full((128, 8, TS), NEG, np.float32)
    am1 = np.zeros((128, 8, TS), np.float32)
    for r in range(4):
        am0[:, r, :] = diag[r]
        am1[:, 4 + r, :] = diag[r]
    _CONST_CACHE["amask0"] = am0.reshape(128, 8 * TS).astype(bf)
    _CONST_CACHE["amask1"] = am1.reshape(128, 8 * TS).astype(bf)
    return _CONST_CACHE


_NC_CACHE = {}


def kernel(x, norm_gain, w_in, b_merge, sb_q_gain, sb_k_gain, ret_out_gain, w_branch_sb, w_branch_ret, w_out):
    x = np.asarray(x, np.float32)
    C = _const_tables()
    if "nc" not in _NC_CACHE:
        _NC_CACHE["nc"] = build_program()
    nc = _NC_CACHE["nc"]
    f = lambda a: np.ascontiguousarray(np.asarray(a, np.float32))
    w_in0, wbs0, wbr0, wout0 = f(w_in[0]), f(w_branch_sb[0]), f(w_branch_ret[0]), f(w_out[0])
    gain_rep = np.ascontiguousarray(np.broadcast_to(f(norm_gain[0])[None, :], (128, DM)))
    rgain_rep = np.ascontiguousarray(np.broadcast_to(f(ret_out_gain[0]).reshape(1, DM), (128, DM)))
    qkg = np.ascontiguousarray(np.stack([f(sb_q_gain[0]), f(sb_k_gain[0])], axis=1))
    bmg = np.ascontiguousarray(f(b_merge[0]).reshape(2, 8, 128).transpose(2, 0, 1).reshape(128, 16))
    in_maps = []
    for c in range(8):
        b, p = c // 2, c % 2
        xb = x[b]
        xown = np.ascontiguousarray(xb.reshape(NT, TS, DM)[p::2].reshape(SO, DM))
        blend = np.zeros((128, 2), np.float32)
        blend[:, 0] = 1.0 - p
        blend[:, 1] = float(p)
        in_maps.append({
            "xa": np.ascontiguousarray(xb), "xo": xown, "w_in": w_in0, "wbs": wbs0, "wbr": wbr0, "wout": wout0,
            "gain_rep": gain_rep, "rgain_rep": rgain_rep, "qkg": qkg, "bmg": bmg,
            "cosq": C["cosq"], "sinq": C["sinq"],
            "dmask": C["dmask"], "qdec": C["qdec"], "kdect": C["kdect"], "cmat": C["cmat"],
            "amask": C["amask%d" % p], "blend": blend,
        })
    res = run_bass_kernel_spmd(nc, in_maps, core_ids=list(range(8)))
    _NC_CACHE["last"] = res
    out = np.empty((4, S, DM), np.float32)
    for c in range(8):
        b, p = c // 2, c % 2
        out[b].reshape(NT, TS, DM)[p::2] = np.asarray(res.results[c]["y"], np.float32).reshape(NO, TS, DM)
    return out
```

```python
import contextlib
import numpy as np
import ml_dtypes
import concourse.bass as bass
import concourse.mybir as mybir
from concourse.bass_utils import run_bass_kernel_spmd

F32 = mybir.dt.float32
BF16 = mybir.dt.bfloat16
AF = mybir.ActivationFunctionType
ALU = mybir.AluOpType
AX = mybir.AxisListType


class Tracker:
    ENG = ("pe", "act", "dve", "pool", "sp")

    def __init__(self, nc, n_dma_sems=6):
        self.nc = nc
        self.n_dma = n_dma_sems
        self.stack = contextlib.ExitStack()
        self.streams = {e: [] for e in self.ENG}
        self.count = {}
        self.known = {e: {} for e in self.ENG}
        self.last_write = {}
        self.readers = {}
        self.n_ops = 0

    def __enter__(self):
        nc = self.nc
        self.stack.__enter__()
        self.sem = {}
        for e in self.ENG:
            self.sem[e] = self.stack.enter_context(nc.semaphore("s_" + e))
            self.count[e] = 0
        self.dma_ring = {}
        self.dma_next = {}
        for q in ("sp", "pool", "act"):
            ring = []
            for k in range(self.n_dma):
                name = "d_%s%d" % (q, k)
                self.sem[name] = self.stack.enter_context(nc.semaphore(name))
                self.count[name] = 0
                ring.append(name)
            self.dma_ring[q] = ring
            self.dma_next[q] = 0
        return self

    def __exit__(self, *a):
        return self.stack.__exit__(*a)

    def _deps(self, reads, writes):
        deps = {}

        def add(s, v):
            if v > deps.get(s, 0):
                deps[s] = v

        for b in reads:
            lw = self.last_write.get(b)
            if lw:
                add(*lw)
        for b in writes:
            lw = self.last_write.get(b)
            if lw:
                add(*lw)
            for s, v in self.readers.get(b, {}).items():
                add(s, v)
        return deps

    def _emit_waits(self, e, deps, skip_self=False):
        for s, v in deps.items():
            if skip_self and s == e:
                continue
            if self.known[e].get(s, 0) >= v:
                continue
            self.known[e][s] = v
            sem = self.sem[s]
            self.streams[e].append(("wait", sem, v))

    def _record(self, key, val, reads, writes):
        for b in reads:
            self.readers.setdefault(b, {})[key] = val
        for b in writes:
            self.last_write[b] = (key, val)
            self.readers[b] = {}

    def op(self, e, fn, reads=(), writes=(), inc=True):
        deps = self._deps(reads, writes)
        self._emit_waits(e, deps, skip_self=(e == "pe"))
        if inc:
            self.count[e] += 1
            val = self.count[e]
            self.streams[e].append(("op", fn, self.sem[e], 1))
        else:
            val = self.count[e] + 1
            self.streams[e].append(("op", fn, None, 0))
        self._record(e, val, reads, writes)
        self.n_ops += 1

    def dma(self, q, out, in_, reads=(), writes=(), **kw):
        deps = self._deps(reads, writes)
        name = self.dma_ring[q][self.dma_next[q]]
        self.dma_next[q] = (self.dma_next[q] + 1) % self.n_dma
        deps[name] = max(deps.get(name, 0), self.count[name])
        self._emit_waits(q, deps)
        self.count[name] += 16
        val = self.count[name]
        self.known[q][name] = max(self.known[q].get(name, 0), 0)
        eng = {"sp": self.nc.sync, "pool": self.nc.gpsimd, "act": self.nc.scalar}[q]
        self.streams[q].append(("op", (lambda: eng.dma_start(out=out, in_=in_, **kw)), self.sem[name], 16))
        self._record(name, val, reads, writes)
        self.n_ops += 1

    def barrier(self):
        for e in self.ENG:
            deps = {s: c for s, c in self.count.items() if c > 0 and s != e}
            self._emit_waits(e, deps)

    def finish(self):
        nc = self.nc
        deps = {s: c for s, c in self.count.items() if c > 0 and s != "sp"}
        self._emit_waits("sp", deps)
        streams = self.streams

        def replay(e, eng):
            for item in streams[e]:
                if item[0] == "wait":
                    eng.wait_ge(item[1], item[2])
                else:
                    ins = item[1]()
                    if item[2] is not None:
                        ins.then_inc(item[2], item[3])

        with nc.Block() as block:
            @block.sync
            def _(eng):
                replay("sp", eng)

            @block.scalar
            def _(eng):
                replay("act", eng)

            @block.vector
            def _(eng):
                replay("dve", eng)

            @block.gpsimd
            def _(eng):
                replay("pool", eng)

            @block.tensor
            def _(eng):
                replay("pe", eng)


S = 8192
DM = 1024
TS = 512
NT = S // TS
NO = NT // 2
SO = NO * TS
NCH = DM // 128
EPS = 1e-6
NEG = -30000.0
C_SBQ, C_SBK, C_SBV, C_SBG, C_RQ, C_RK, C_RV, C_RG, C_MSB, C_MRET = [i * 1024 for i in range(10)]
GAMMA = [1.0 - 2.0 ** (-5.0 - h) for h in range(4)]
G64 = [g ** 64 for g in GAMMA]
DEBUG = False


def _mm(T, nc, out, lhsT, rhs, start, stop, reads, writes, inc):
    T.op("pe", lambda: nc.tensor.matmul(out, lhsT=lhsT, rhs=rhs, start=start, stop=stop),
         reads=reads, writes=writes, inc=inc)


def _tr(T, nc, out, in_, ident, reads, writes, inc):
    T.op("pe", lambda: nc.tensor.transpose(out, in_, ident), reads=reads, writes=writes, inc=inc)


def _act(T, nc, out, in_, func, reads, writes, **kw):
    T.op("act", lambda: nc.scalar.activation(out=out, in_=in_, func=func, **kw), reads=reads, writes=writes)


def _ts(T, nc, eng, out, in0, s1, s2, op0, op1, reads, writes):
    e = nc.vector if eng == "dve" else nc.gpsimd
    if op1 is None:
        T.op(eng, lambda: e.tensor_scalar(out=out, in0=in0, scalar1=s1, scalar2=None, op0=op0), reads=reads, writes=writes)
    else:
        T.op(eng, lambda: e.tensor_scalar(out=out, in0=in0, scalar1=s1, scalar2=s2, op0=op0, op1=op1), reads=reads, writes=writes)


def _tt(T, nc, eng, out, in0, in1, op, reads, writes):
    e = nc.vector if eng == "dve" else nc.gpsimd
    T.op(eng, lambda: e.tensor_tensor(out=out, in0=in0, in1=in1, op=op), reads=reads, writes=writes)


def _stt(T, nc, out, in0, scalar, in1, op0, op1, reads, writes):
    T.op("dve", lambda: nc.vector.scalar_tensor_tensor(out=out, in0=in0, scalar=scalar, in1=in1, op0=op0, op1=op1),
         reads=reads, writes=writes)


def _copy(T, nc, eng, out, in_, reads, writes):
    if eng == "act":
        T.op("act", lambda: nc.scalar.copy(out=out, in_=in_), reads=reads, writes=writes)
    else:
        e = nc.vector if eng == "dve" else nc.gpsimd
        T.op(eng, lambda: e.tensor_copy(out=out, in_=in_), reads=reads, writes=writes)


def build_program(phases="ABCD", nt_a=NT, no_b=NO, heads_c=8, no_c=NO, no_d=NO):
    CFG = dict(phases=phases, nt_a=nt_a, no_b=no_b, heads_c=heads_c, no_c=no_c, no_d=no_d)
    nc = bass.Bass("TRN2", target_bir_lowering=False)
    dt = nc.dram_tensor
    xa = dt("xa", [S, DM], F32, kind="ExternalInput").ap()
    xo = dt("xo", [SO, DM], F32, kind="ExternalInput").ap()
    w_in = dt("w_in", [DM, 10 * 1024], F32, kind="ExternalInput").ap()
    wbs = dt("wbs", [DM, DM], F32, kind="ExternalInput").ap()
    wbr = dt("wbr", [DM, DM], F32, kind="ExternalInput").ap()
    wout = dt("wout", [DM, DM], F32, kind="ExternalInput").ap()
    gain_rep = dt("gain_rep", [128, DM], F32, kind="ExternalInput").ap()
    rgain_rep = dt("rgain_rep", [128, DM], F32, kind="ExternalInput").ap()
    qkg = dt("qkg", [128, 2], F32, kind="ExternalInput").ap()
    bmg = dt("bmg", [128, 16], F32, kind="ExternalInput").ap()
    cosq = dt("cosq", [128, S], F32, kind="ExternalInput").ap()
    sinq = dt("sinq", [128, S], F32, kind="ExternalInput").ap()
    dmask_d = dt("dmask", [128, 4 * 128], F32, kind="ExternalInput").ap()
    qdec_d = dt("qdec", [128, 4 * 128], F32, kind="ExternalInput").ap()
    kdect_d = dt("kdect", [128, 4], F32, kind="ExternalInput").ap()
    cmat_d = dt("cmat", [128, 4 * 128], BF16, kind="ExternalInput").ap()
    amask_d = dt("amask", [128, 8 * TS], BF16, kind="ExternalInput").ap()
    blend_d = dt("blend", [128, 2], F32, kind="ExternalInput").ap()
    y = dt("y", [SO, DM], F32, kind="ExternalOutput").ap()
    sk = "ExternalOutput" if DEBUG else "Internal"
    kT_d = dt("kT_d", [8, 128, S], BF16, kind=sk).ap()
    v_d = dt("v_d", [S, DM], BF16, kind=sk).ap()
    qT_d = dt("qT_d", [8, 128, SO], BF16, kind=sk).ap()
    ga_d = dt("ga_d", [8, 128, SO], BF16, kind=sk).ap()
    gb_d = dt("gb_d", [8, 128, SO], BF16, kind=sk).ap()
    sa_d = dt("sa_d", [8, 128, SO], BF16, kind=sk).ap()
    sb_d = dt("sb_d", [8, 128, SO], BF16, kind=sk).ap()
    ret_d = dt("ret_d", [S, DM], F32, kind=sk).ap()
    oaT_d = dt("oaT_d", [8, 128, SO], F32, kind=sk).ap()
    if DEBUG:
        dbg_bf = dt("dbg_bf", [128, 8192], BF16, kind="ExternalOutput").ap()
        dbg_f = dt("dbg_f", [128, 4096], F32, kind="ExternalOutput").ap()

    es = contextlib.ExitStack()
    with es:
        def sb(name, shape, dtype):
            return es.enter_context(nc.sbuf_tensor("t_" + name, shape, dtype))

        cmat = sb("cmat", [128, 4 * 128], BF16)
        ident = cmat[:, 0:128]
        onesm = cmat[:, 128:256]
        trineg = cmat[:, 256:384]
        carryneg = cmat[:, 384:512]
        qkg_t = sb("qkg_t", [128, 2], F32)
        qg_s = sb("qg_s", [128, 1], F32)
        bmg_t = sb("bmg_t", [128, 16], F32)
        blend_t = sb("blend_t", [128, 2], F32)
        ps, pb = [], []
        psum_ctr = [0]

        def alloc_psum(stack):
            tag = "abcdefgh"[psum_ctr[0]]
            psum_ctr[0] += 1
            ps[:] = [stack.enter_context(nc.psum_tensor("ps%d%s" % (k, tag), [128, 512], F32)) for k in range(6)]
            pb[:] = [stack.enter_context(nc.psum_tensor("pb%d%s" % (k, tag), [128, 1024], BF16)) for k in range(2)]
        T = es.enter_context(Tracker(nc))

        T.dma("sp", cmat[:], cmat_d, writes=["cmat"])
        T.dma("sp", qkg_t[:], qkg, writes=["qkg"])
        T.dma("sp", bmg_t[:], bmg, writes=["bmg"])
        T.dma("sp", blend_t[:], blend_d, writes=["blend"])
        _ts(T, nc, "dve", qg_s[:], qkg_t[:, 0:1], float(128 ** -0.5), None, ALU.mult, None, ["qkg"], ["qg_s"])

        def load_w(wt, col0, ncols, key, src=w_in):
            for c0 in range(0, ncols, 512):
                T.dma("pool", wt[:, :, c0:c0 + 512],
                      src[:, col0 + c0: col0 + c0 + 512].rearrange("(c p) n -> p c n", p=128),
                      writes=[key])

        def norm_block(xsrc, r0, xt, junk, ss, gain_t, h, hT, col0, pbt=None):
            pbt = pbt if pbt is not None else pb[0]
            T.dma("sp", xt[:], xsrc[r0:r0 + 128, :], writes=[xt.name])
            _act(T, nc, junk[:], xt[:], AF.Square, [xt.name], [junk.name, ss.name], accum_out=ss[:, 0:1])
            _act(T, nc, ss[:, 1:2], ss[:, 0:1], AF.Sqrt, [ss.name], [ss.name], bias=EPS, scale=1.0 / DM)
            T.op("dve", lambda: nc.vector.reciprocal(out=ss[:, 2:3], in_=ss[:, 1:2]), reads=[ss.name], writes=[ss.name])
            _stt(T, nc, h[:], xt[:], ss[:, 2:3], gain_t[:], ALU.mult, ALU.mult, [xt.name, ss.name, "gain"], [h.name])
            for c in range(NCH):
                _tr(T, nc, pbt[:, c * 128:(c + 1) * 128], h[:, c * 128:(c + 1) * 128], ident,
                    [h.name, "cmat"], [pbt.name], inc=(c == NCH - 1))
            _copy(T, nc, "act", hT[:, :, col0:col0 + 128], pbt[:].rearrange("p (c t) -> p c t", c=NCH),
                  [pbt.name], ["hT"])

        def norm_pre(xsrc, r0, xt, junk, ss, gain_t, h):
            T.dma("sp", xt[:], xsrc[r0:r0 + 128, :], writes=[xt.name])
            _act(T, nc, junk[:], xt[:], AF.Square, [xt.name], [junk.name, ss.name], accum_out=ss[:, 0:1])
            _act(T, nc, ss[:, 1:2], ss[:, 0:1], AF.Sqrt, [ss.name], [ss.name], bias=EPS, scale=1.0 / DM)
            T.op("dve", lambda: nc.vector.reciprocal(out=ss[:, 2:3], in_=ss[:, 1:2]), reads=[ss.name], writes=[ss.name])
            _stt(T, nc, h[:], xt[:], ss[:, 2:3], gain_t[:], ALU.mult, ALU.mult, [xt.name, ss.name, "gain"], [h.name])

        def norm_post(h, hT, col0, pbt):
            for c in range(NCH):
                _tr(T, nc, pbt[:, c * 128:(c + 1) * 128], h[:, c * 128:(c + 1) * 128], ident,
                    [h.name, "cmat"], [pbt.name], inc=(c == NCH - 1))
            _copy(T, nc, "act", hT[:, :, col0:col0 + 128], pbt[:].rearrange("p (c t) -> p c t", c=NCH),
                  [pbt.name], ["hT"])

        def qk_proj(W, wkey, wcol, hT, pz):
            for c in range(NCH):
                _mm(T, nc, pz[:], W[:, c, wcol:wcol + 128], hT[:, c, :], c == 0, c == NCH - 1,
                    [wkey, "hT"], [pz.name], inc=(c == NCH - 1))

        def qk_square(pz, sq):
            _act(T, nc, sq[:], pz[:], AF.Square, [pz.name], [sq.name])

        def qk_norm(gcol, pz, pm, sq, rs, outt, outkey, do_square=True):
            if do_square:
                qk_square(pz, sq)
            _mm(T, nc, pm[:], onesm, sq[:], True, True, ["cmat", sq.name], [pm.name], True)
            _act(T, nc, rs[:], pm[:], AF.Ln, [pm.name], [rs.name], bias=EPS, scale=1.0)
            _act(T, nc, rs[:], rs[:], AF.Exp, [rs.name], [rs.name], scale=-0.5)
            _stt(T, nc, outt, pz[:], gcol, rs[:], ALU.mult, ALU.mult, [pz.name, rs.name, "qkg", "qg_s"], [outkey])

        def qk_head(W, wkey, wcol, hT, gcol, pz, pm, sq, rs, outt, outkey):
            qk_proj(W, wkey, wcol, hT, pz)
            qk_norm(gcol, pz, pm, sq, rs, outt, outkey)

        with contextlib.ExitStack() as pa:
            def sa_(name, shape, dtype):
                return pa.enter_context(nc.sbuf_tensor("t_" + name, shape, dtype))
            alloc_psum(pa)
            Wa = sa_("Wa", [128, NCH, 5 * 1024], BF16)
            gain_t = sa_("gain_t", [128, DM], F32)
            xts = [sa_("xt%d" % k, [128, DM], F32) for k in range(2)]
            junk = sa_("junk", [128, DM], BF16)
            sss = [sa_("ss%d" % k, [128, 4], F32) for k in range(2)]
            hs = [sa_("h%d" % k, [128, DM], BF16) for k in range(4)]
            hT = sa_("hT", [128, NCH, TS], BF16)
            sqs = [sa_("sq%d" % k, [128, TS], BF16) for k in range(3)]
            rss = [sa_("rs%d" % k, [128, TS], F32) for k in range(2)]
            kout = [sa_("kout%d" % k, [128, TS], BF16) for k in range(2)]
            vout = [sa_("vout%d" % k, [128, DM], BF16) for k in range(2)]
            cs = sa_("cs", [128, 2, TS], F32)
            ra = [sa_("ra%d" % k, [128, TS], F32) for k in range(2)]
            rb = [sa_("rb%d" % k, [128, TS], F32) for k in range(2)]
            qrT = [sa_("qrT%d" % k, [128, 2, TS], BF16) for k in range(2)]
            krT = [sa_("krT%d" % k, [128, 2, TS], BF16) for k in range(2)]
            qdT = [sa_("qdT%d" % k, [128, 2, TS], BF16) for k in range(3)]
            kdt = [sa_("kdt%d" % k, [128, 4, 256], BF16) for k in range(2)]
            rv = sa_("rv", [128, 4, DM], BF16)
            sT = [sa_("sT%d" % k, [128, 4, 128], BF16) for k in range(2)]
            St = sa_("St", [128, 4, 2, 256], F32)
            Sb0 = sa_("Sb0", [128, 4, 2, 256], BF16)
            Sbt = [sa_("Sbt%d" % k, [128, 3, 2, 256], BF16) for k in range(2)]
            dmask = sa_("dmask", [128, 4, 128], F32)
            qdec = sa_("qdec", [128, 4, 128], F32)
            kdect = sa_("kdect", [128, 4], F32)
            rout = [sa_("rout%d" % k, [128, 256], F32) for k in range(2)]
            pb0f, pb1f = pb[0][:].bitcast(F32), pb[1][:].bitcast(F32)

            T.dma("sp", gain_t[:], gain_rep, writes=["gain"])
            T.dma("sp", dmask[:].rearrange("p h q -> p (h q)"), dmask_d, writes=["dmask"])
            T.dma("sp", qdec[:].rearrange("p h q -> p (h q)"), qdec_d, writes=["qdec"])
            T.dma("sp", kdect[:], kdect_d, writes=["kdect"])
            load_w(Wa[:, :, 0:1024], C_SBK, 1024, "Wa")
            load_w(Wa[:, :, 1024:2048], C_SBV, 1024, "Wa")
            load_w(Wa[:, :, 2048:3072], C_RQ, 1024, "Wa")
            load_w(Wa[:, :, 3072:4096], C_RK, 1024, "Wa")
            load_w(Wa[:, :, 4096:5120], C_RV, 1024, "Wa")
            T.op("dve", lambda: nc.vector.memset(St[:].rearrange("p a b c -> p (a b c)"), 0.0), writes=["St"])
            T.op("dve", lambda: nc.vector.memset(Sb0[:].rearrange("p a b c -> p (a b c)"), 0.0), writes=["Sb0"])
            G128 = [g ** 128 for g in GAMMA]

            def ret_proj(hd, t0):
                par = hd % 2
                for which in range(2):
                    wc = (2048 if which == 0 else 3072) + hd * 256
                    p1, p2 = (ps[0], ps[1]) if which == 0 else (ps[2], ps[3])
                    for half, pz in enumerate((p1, p2)):
                        for c in range(NCH):
                            _mm(T, nc, pz[:], Wa[:, c, wc + half * 128: wc + (half + 1) * 128], hT[:, c, :],
                                c == 0, c == NCH - 1, ["Wa", "hT"], [pz.name], inc=(c == NCH - 1))
                    dst = qrT[par] if which == 0 else krT[par]
                    ct, st_ = cs[:, 0, :], cs[:, 1, :]
                    _tt(T, nc, "dve", ra[0][:], p1[:], ct, ALU.mult, [p1.name, "cs"], [ra[0].name])
                    _tt(T, nc, "dve", rb[0][:], p2[:], st_, ALU.mult, [p2.name, "cs"], [rb[0].name])
                    _tt(T, nc, "dve", ra[1][:], p1[:], st_, ALU.mult, [p1.name, "cs"], [ra[1].name])
                    _tt(T, nc, "dve", rb[1][:], p2[:], ct, ALU.mult, [p2.name, "cs"], [rb[1].name])
                    for half in range(2):
                        _tt(T, nc, "pool", ra[half][:], ra[half][:], rb[half][:], ALU.subtract if half == 0 else ALU.add,
                            [ra[half].name, rb[half].name], [ra[half].name])
                        if which == 0:
                            _copy(T, nc, "act", dst[:, half, :], ra[half][:], [ra[half].name], [dst.name])
                            for blk in range(4):
                                _tt(T, nc, "pool", qdT[hd % 3][:, half, blk * 128:(blk + 1) * 128], ra[half][:, blk * 128:(blk + 1) * 128],
                                    qdec[:, hd, :], ALU.mult, [ra[half].name, "qdec"], [qdT[hd % 3].name])
                        else:
                            T.op("act", (lambda o=dst[:, half, :], i_=ra[half][:]: nc.scalar.mul(out=o, in_=i_, mul=1.0 / 16.0)),
                                 reads=[ra[half].name], writes=[dst.name])

            def ret_small_a(hd, t0):
                par = hd % 2
                kr, qr, kd, st = krT[par], qrT[par], kdt[par], sT[par]
                for blk in range(4):
                    for half in range(2):
                        _tr(T, nc, pb[1][:, blk * 256 + half * 128:blk * 256 + (half + 1) * 128], kr[:, half, blk * 128:(blk + 1) * 128], ident,
                            [kr.name, "cmat"], [pb[1].name], inc=(blk == 3 and half == 1))
                _ts(T, nc, "dve", kd[:].rearrange("p a b -> p (a b)"), pb[1][:], kdect[:, hd:hd + 1], None, ALU.mult, None,
                    [pb[1].name, "kdect"], [kd.name])
                for blk in range(4):
                    kv = ps[4 + blk % 2]
                    for half in range(2):
                        _mm(T, nc, kv[:, half * 256:(half + 1) * 256], kd[:, blk, half * 128:(half + 1) * 128],
                            rv[:, blk, hd * 256:(hd + 1) * 256], True, True, [kd.name, "rv"], [kv.name], inc=(half == 1))
                    sflat = St[:, hd, :, :].rearrange("p a b -> p (a b)")
                    _stt(T, nc, sflat, sflat, float(G128[hd]), kv[:], ALU.mult, ALU.add, ["St", kv.name], ["St"])
                    if blk < 3:
                        _copy(T, nc, "act", Sbt[par][:, blk, :, :].rearrange("p a b -> p (a b)"), sflat, ["St"], [Sbt[par].name + str(blk)])
                for blk in range(4):
                    b0 = blk * 128
                    for half in range(2):
                        _mm(T, nc, pb0f[:, blk * 128:(blk + 1) * 128], kr[:, half, b0:b0 + 128], qr[:, half, b0:b0 + 128],
                            half == 0, half == 1, [kr.name, qr.name], [pb[0].name], inc=(half == 1))
                for blk in range(4):
                    _tt(T, nc, "dve", st[:, blk, :], pb0f[:, blk * 128:(blk + 1) * 128], dmask[:, hd, :], ALU.mult,
                        [pb[0].name, "dmask"], [st.name])

            def ret_small_b(hd, t0):
                par = hd % 2
                qd, st = qdT[hd % 3], sT[par]
                for blk in range(4):
                    b0 = blk * 128
                    pot = ps[blk % 2]
                    po = pot[:]
                    _mm(T, nc, po[:, 0:256], st[:, blk, :], rv[:, blk, hd * 256:(hd + 1) * 256], True, False,
                        [st.name, "rv"], [pot.name], inc=True)
                    for half in range(2):
                        sb_ap = Sb0[:, hd, half, :] if blk == 0 else Sbt[par][:, blk - 1, half, :]
                        skey = "Sb0" if blk == 0 else Sbt[par].name + str(blk - 1)
                        _mm(T, nc, po[:, 0:256], qd[:, half, b0:b0 + 128], sb_ap, False, half == 1,
                            [qd.name, skey], [pot.name], inc=(half == 1))
                    ro = rout[blk % 2]
                    _copy(T, nc, "act" if blk % 2 == 0 else "dve", ro[:], po[:, 0:256], [pot.name], [ro.name])
                    T.dma("sp", ret_d[t0 + b0:t0 + b0 + 128, hd * 256:(hd + 1) * 256], ro[:], reads=[ro.name], writes=["ret_d"])
                _copy(T, nc, "act", Sb0[:, hd, :, :].rearrange("p a b -> p (a b)"),
                      St[:, hd, :, :].rearrange("p a b -> p (a b)"), ["St"], ["Sb0"])

            for t in range(nt_a if "A" in phases else 0):
                t0 = t * TS
                for k, tab in enumerate((cosq, sinq)):
                    T.dma("sp", cs[:, k, :], tab[:, t0:t0 + TS], writes=["cs"])
                if t == 0:
                    for blk in range(4):
                        norm_pre(xa, blk * 128, xts[blk % 2], junk, sss[blk % 2], gain_t, hs[blk])
                for blk in range(4):
                    norm_post(hs[blk], hT, blk * 128, pb[blk % 2])
                pzk = [ps[0], ps[1], ps[4]]
                for h0 in range(2):
                    qk_proj(Wa, "Wa", h0 * 128, hT, pzk[h0])
                    qk_square(pzk[h0], sqs[h0])
                for hd in range(8):
                    if hd + 2 < 8:
                        qk_proj(Wa, "Wa", (hd + 2) * 128, hT, pzk[(hd + 2) % 3])
                        qk_square(pzk[(hd + 2) % 3], sqs[(hd + 2) % 3])
                    ko = kout[hd % 2]
                    qk_norm(qkg_t[:, 1:2], pzk[hd % 3], ps[2 + hd % 2], sqs[hd % 3], rss[hd % 2], ko[:], ko.name, do_square=False)
                    T.dma("sp", kT_d[hd, :, t0:t0 + TS], ko[:], reads=[ko.name], writes=["kT_d"])
                nb = 0
                for blk in range(4):
                    vo = vout[blk % 2]
                    for g in range(2):
                        pz = ps[(4 + nb) % 6]
                        nb += 1
                        for c in range(NCH):
                            _mm(T, nc, pz[:], hT[:, c, blk * 128:(blk + 1) * 128], Wa[:, c, 1024 + g * 512:1024 + (g + 1) * 512],
                                c == 0, c == NCH - 1, ["hT", "Wa"], [pz.name], inc=(c == NCH - 1))
                        _copy(T, nc, "act" if g == 0 else "dve", vo[:, g * 512:(g + 1) * 512], pz[:], [pz.name], [vo.name])
                    T.dma("sp", v_d[t0 + blk * 128:t0 + (blk + 1) * 128, :], vo[:], reads=[vo.name], writes=["v_d"])
                    for g in range(2):
                        pz = ps[(4 + nb) % 6]
                        nb += 1
                        for c in range(NCH):
                            _mm(T, nc, pz[:], hT[:, c, blk * 128:(blk + 1) * 128], Wa[:, c, 4096 + g * 512:4096 + (g + 1) * 512],
                                c == 0, c == NCH - 1, ["hT", "Wa"], [pz.name], inc=(c == NCH - 1))
                        _copy(T, nc, "act" if g == 0 else "dve", rv[:, blk, g * 512:(g + 1) * 512], pz[:], [pz.name], ["rv"])
                if t + 1 < (nt_a if "A" in phases else 0):
                    for blk in range(4):
                        norm_pre(xa, t0 + TS + blk * 128, xts[blk % 2], junk, sss[blk % 2], gain_t, hs[blk])
                ret_proj(0, t0)
                for hd in range(4):
                    if hd + 1 < 4:
                        ret_proj(hd + 1, t0)
                    ret_small_a(hd, t0)
                    if hd > 0:
                        ret_small_b(hd - 1, t0)
                ret_small_b(3, t0)
            T.barrier()
        build_rest(nc, T, locals())
        T.finish()
    return nc


def build_rest(nc, T, L):
    CFG = L["CFG"]
    phases = CFG["phases"]
    ps, pb = L["ps"], L["pb"]
    ident, onesm, trineg, carryneg = L["ident"], L["onesm"], L["trineg"], L["carryneg"]
    qg_s, bmg_t, blend_t = L["qg_s"], L["bmg_t"], L["blend_t"]
    xo, y = L["xo"], L["y"]
    kT_d, v_d, qT_d, ga_d, gb_d, sa_d, sb_d, ret_d, oaT_d = (L[k] for k in
        ("kT_d", "v_d", "qT_d", "ga_d", "gb_d", "sa_d", "sb_d", "ret_d", "oaT_d"))
    load_w, norm_block, qk_head = L["load_w"], L["norm_block"], L["qk_head"]
    wbs, wbr, wout = L["wbs"], L["wbr"], L["wout"]

    with contextlib.ExitStack() as pbs:
        def sa_(name, shape, dtype):
            return pbs.enter_context(nc.sbuf_tensor("t_" + name, shape, dtype))
        L["alloc_psum"](pbs)
        Wb = sa_("Wb", [128, NCH, 5 * 1024], BF16)
        gain_t = sa_("gain_tb", [128, DM], F32)
        xts = [sa_("xtb%d" % k, [128, DM], F32) for k in range(2)]
        junk = sa_("junkb", [128, DM], BF16)
        sss = [sa_("ssb%d" % k, [128, 4], F32) for k in range(2)]
        hs = [sa_("hb%d" % k, [128, DM], BF16) for k in range(4)]
        hT = sa_("hTb", [128, NCH, TS], BF16)
        sqs = [sa_("sqb%d" % k, [128, TS], BF16) for k in range(3)]
        rss = [sa_("rsb%d" % k, [128, TS], F32) for k in range(2)]
        qout = [sa_("qout%d" % k, [128, TS], BF16) for k in range(2)]
        gout = [sa_("gout%d" % k, [128, TS], BF16) for k in range(4)]
        T.dma("sp", gain_t[:], L["gain_rep"], writes=["gain"])
        for k, col in enumerate((C_SBQ, C_SBG, C_RG, C_MSB, C_MRET)):
            load_w(Wb[:, :, k * 1024:(k + 1) * 1024], col, 1024, "Wb")
        norm_pre, norm_post, qk_proj, qk_square, qk_norm = (L[k] for k in ("norm_pre", "norm_post", "qk_proj", "qk_square", "qk_norm"))
        n_b = CFG["no_b"] if "B" in phases else 0
        for i in range(n_b):
            t0 = i * TS
            if i == 0:
                for blk in range(4):
                    norm_pre(xo, blk * 128, xts[blk % 2], junk, sss[blk % 2], gain_t, hs[blk])
            for blk in range(4):
                norm_post(hs[blk], hT, blk * 128, pb[blk % 2])
            pzq = [ps[0], ps[1], ps[2]]
            for h0 in range(2):
                qk_proj(Wb, "Wb", h0 * 128, hT, pzq[h0])
                qk_square(pzq[h0], sqs[h0])
            for hd in range(8):
                if hd + 2 < 8:
                    qk_proj(Wb, "Wb", (hd + 2) * 128, hT, pzq[(hd + 2) % 3])
                    qk_square(pzq[(hd + 2) % 3], sqs[(hd + 2) % 3])
                qo = qout[hd % 2]
                qk_norm(qg_s[:, 0:1], pzq[hd % 3], ps[3 + hd % 2], sqs[hd % 3], rss[hd % 2], qo[:], qo.name, do_square=False)
                T.dma("sp", qT_d[hd, :, t0:t0 + TS], qo[:], reads=[qo.name], writes=["qT_d"])
            if i + 1 < n_b:
                for blk in range(4):
                    norm_pre(xo, t0 + TS + blk * 128, xts[blk % 2], junk, sss[blk % 2], gain_t, hs[blk])
            n = 0
            for k, (dst, func) in enumerate(((ga_d, AF.Silu), (gb_d, AF.Silu), (sa_d, AF.Sigmoid), (sb_d, AF.Sigmoid))):
                for c in range(NCH):
                    pz = ps[(5 + n) % 6]
                    go = gout[n % 4]
                    n += 1
                    wc = (k + 1) * 1024 + c * 128
                    for cc in range(NCH):
                        _mm(T, nc, pz[:], Wb[:, cc, wc:wc + 128], hT[:, cc, :], cc == 0, cc == NCH - 1,
                            ["Wb", "hT"], [pz.name], inc=(cc == NCH - 1))
                    if func == AF.Silu:
                        _act(T, nc, go[:], pz[:], func, [pz.name], [go.name])
                    else:
                        bcol = (k - 2) * 8 + c
                        _act(T, nc, go[:], pz[:], func, [pz.name, "bmg"], [go.name], bias=bmg_t[:, bcol:bcol + 1], scale=1.0)
                    T.dma("sp", dst[c, :, t0:t0 + TS], go[:], reads=[go.name], writes=["gates_d"])
        T.barrier()

    with contextlib.ExitStack() as pcs:
        def sa_(name, shape, dtype):
            return pcs.enter_context(nc.sbuf_tensor("t_" + name, shape, dtype))
        KT = [sa_("KT%d" % k, [128, S], BF16) for k in range(2)]
        VV = [sa_("VV%d" % k, [128, S // 128, 128], BF16) for k in range(2)]
        QT = [sa_("QT%d" % k, [128, SO], BF16) for k in range(2)]
        amask = sa_("amask", [128, 8, TS], BF16)
        E = [sa_("E%d" % k, [128, 2, TS], F32) for k in range(2)]
        Lp = [sa_("Lp%d" % k, [128, 2, TS], BF16) for k in range(2)]
        G = [sa_("G%d" % k, [128, 2, TS], F32) for k in range(2)]
        W = [sa_("W%d" % k, [128, 2, TS], BF16) for k in range(2)]
        osb = [sa_("osb%d" % k, [128, TS], F32) for k in range(2)]
        T.dma("sp", amask[:].rearrange("p r q -> p (r q)"), L["amask_d"], writes=["amask"])
        Zall = pcs.enter_context(nc.psum_tensor("Zall", [128, 2, 2, TS], F32))
        Call = pcs.enter_context(nc.psum_tensor("Call", [128, 2, TS], F32))
        Oall = pcs.enter_context(nc.psum_tensor("Oall", [128, 2, TS], F32))
        seqs = ([0, 3, 4, 7], [1, 2, 5, 6])
        steps = [[(i, kb) for i in seq for kb in range(8 * i + 7, -1, -1)] for seq in seqs]
        NS = len(steps[0])
        assert NS == len(steps[1])
        if CFG["no_c"] < NO:
            seqs = ([0], [0]) if CFG["no_c"] == 1 else seqs
            steps = [[(i, kb) for i in seq for kb in range(8 * i + 7, -1, -1)] for seq in seqs]
            NS = len(steps[0])
        nosb = [0]
        for hd in range(CFG["heads_c"] if "C" in phases else 0):
            sl = hd % 2
            kt, vv, qt = KT[sl], VV[sl], QT[sl]
            for part in range(4):
                T.dma("sp", kt[:, part * 2048:(part + 1) * 2048], kT_d[hd, :, part * 2048:(part + 1) * 2048],
                      reads=["kT_d"], writes=[kt.name])
            for part in range(4):
                T.dma("sp", vv[:, part * 16:(part + 1) * 16, :],
                      v_d[part * 2048:(part + 1) * 2048, hd * 128:(hd + 1) * 128].rearrange("(b p) d -> p b d", p=128),
                      reads=["v_d"], writes=[vv.name])
            T.dma("sp", qt[:], qT_d[hd, :, :], reads=["qT_d"], writes=[qt.name])

            def pe1(st, t):
                i, kb = steps[st][t]
                par = t % 2
                Z = Zall[:, par, st, :]
                masked = kb >= 8 * i
                _mm(T, nc, Z, kt[:, kb * 128:(kb + 1) * 128], qt[:, i * TS:(i + 1) * TS], True, not masked,
                    [kt.name, qt.name], ["Z%d" % par], inc=(not masked))
                if masked:
                    _mm(T, nc, Z, ident, amask[:, kb - 8 * i, :], False, True, ["cmat", "amask"], ["Z%d" % par], True)

            def a1(t):
                par = t % 2
                _act(T, nc, E[par][:], Zall[:, par, :, :], AF.Exp, ["Z%d" % par], [E[par].name])

            def a2(t):
                par = t % 2
                _act(T, nc, Lp[par][:], E[par][:], AF.Ln, [E[par].name], [Lp[par].name], bias=1.0, scale=1.0)

            def pe2(st, t):
                i, kb = steps[st][t]
                l_ = Lp[t % 2]
                fst = (kb == 8 * i + 7)
                T.op("pe", (lambda o=Call[:, st, :], r_=l_[:, st, :], f_=fst: nc.tensor.matmul(o, lhsT=trineg, rhs=r_, start=f_, stop=True, skip_group_check=(not f_))),
                     reads=["cmat", l_.name], writes=["Call"], inc=True)

            def a3(t):
                par = t % 2
                _act(T, nc, G[par][:], Call[:], AF.Exp, ["Call"], [G[par].name])

            def pe3(st, t):
                l_ = Lp[t % 2]
                T.op("pe", (lambda o=Call[:, st, :], r_=l_[:, st, :]: nc.tensor.matmul(o, lhsT=carryneg, rhs=r_, start=False, stop=True, skip_group_check=True)),
                     reads=["cmat", l_.name], writes=["Call"], inc=True)

            def v1(t):
                par = t % 2
                _tt(T, nc, "dve", W[par][:], E[par][:], G[par][:], ALU.mult, [E[par].name, G[par].name], [W[par].name])

            def pe4(st, t):
                i, kb = steps[st][t]
                w_ = W[t % 2]
                _mm(T, nc, Oall[:, st, :], vv[:, kb, :], w_[:, st, :], kb == 8 * i + 7, kb == 0, [vv.name, w_.name], ["O%d" % st], True)
                if kb == 0:
                    ob = osb[nosb[0] % 2]
                    nosb[0] += 1
                    _copy(T, nc, "dve", ob[:], Oall[:, st, :], ["O%d" % st], [ob.name])
                    T.dma("sp", oaT_d[hd, :, i * TS:(i + 1) * TS], ob[:], reads=[ob.name], writes=["oaT_d"])

            for st in range(2):
                pe1(st, 0)
            if NS > 1:
                for st in range(2):
                    pe1(st, 1)
            a1(0)
            a2(0)
            for t in range(NS):
                if t + 1 < NS:
                    a1(t + 1)
                for st in range(2):
                    pe2(st, t)
                if t + 2 < NS:
                    for st in range(2):
                        pe1(st, t + 2)
                if t > 0:
                    for st in range(2):
                        pe4(st, t - 1)
                a3(t)
                for st in range(2):
                    pe3(st, t)
                v1(t)
                if t + 1 < NS:
                    a2(t + 1)
            for st in range(2):
                pe4(st, NS - 1)
        T.barrier()

    with contextlib.ExitStack() as pds:
        def sa_(name, shape, dtype):
            return pds.enter_context(nc.sbuf_tensor("t_" + name, shape, dtype))
        L["alloc_psum"](pds)
        Wd = sa_("Wd", [128, NCH, 3 * 1024], BF16)
        rgain_t = sa_("rgain_t", [128, DM], F32)
        rA = [sa_("rA%d" % k, [128, DM], F32) for k in range(2)]
        rB = [sa_("rB%d" % k, [128, DM], F32) for k in range(2)]
        rt = sa_("rt", [128, DM], F32)
        junk = sa_("junkd", [128, 256], F32)
        ss = sa_("ssd", [128, 12], F32)
        rn = sa_("rn", [128, DM], BF16)
        RNT = sa_("RNT", [128, NCH, TS], BF16)
        oa = [sa_("oa%d" % k, [128, TS], F32) for k in range(2)]
        ga = [sa_("ga%d" % k, [128, TS], BF16) for k in range(2)]
        gb = [sa_("gb%d" % k, [128, TS], BF16) for k in range(2)]
        sga = [sa_("sga%d" % k, [128, TS], BF16) for k in range(2)]
        sgb = [sa_("sgb%d" % k, [128, TS], BF16) for k in range(2)]
        OAG = sa_("OAG", [128, NCH, TS], BF16)
        OBG = sa_("OBG", [128, NCH, TS], BF16)
        MG = sa_("MG", [128, NCH, TS], BF16)
        t1 = sa_("t1", [128, TS], F32)
        t2 = sa_("t2", [128, TS], F32)
        xr = [sa_("xr%d" % k, [128, DM], F32) for k in range(2)]
        yt = [sa_("yt%d" % k, [128, DM], F32) for k in range(2)]
        T.dma("sp", rgain_t[:], L["rgain_rep"], writes=["rgain"])
        load_w(Wd[:, :, 0:1024], 0, 1024, "Wd", src=wbs)
        load_w(Wd[:, :, 1024:2048], 0, 1024, "Wd", src=wbr)
        load_w(Wd[:, :, 2048:3072], 0, 1024, "Wd", src=wout)
        n = 0
        for i in range(CFG["no_d"] if "D" in phases else 0):
            t0 = i * TS
            for blk in range(4):
                a_, b_ = rA[blk % 2], rB[blk % 2]
                T.dma("sp", a_[:], ret_d[(2 * i) * TS + blk * 128:(2 * i) * TS + (blk + 1) * 128, :], reads=["ret_d"], writes=[a_.name])
                T.dma("sp", b_[:], ret_d[(2 * i + 1) * TS + blk * 128:(2 * i + 1) * TS + (blk + 1) * 128, :], reads=["ret_d"], writes=[b_.name])
                _ts(T, nc, "dve", rt[:], a_[:], blend_t[:, 0:1], None, ALU.mult, None, [a_.name, "blend"], ["rt"])
                _stt(T, nc, rt[:], b_[:], blend_t[:, 1:2], rt[:], ALU.mult, ALU.add, [b_.name, "blend", "rt"], ["rt"])
                for hd in range(4):
                    _act(T, nc, junk[:], rt[:, hd * 256:(hd + 1) * 256], AF.Square, ["rt"], ["junkd", "ssd"],
                         accum_out=ss[:, hd:hd + 1])
                _act(T, nc, ss[:, 4:8], ss[:, 0:4], AF.Sqrt, ["ssd"], ["ssd"], bias=EPS, scale=1.0 / 256)
                T.op("dve", lambda: nc.vector.reciprocal(out=ss[:, 8:12], in_=ss[:, 4:8]), reads=["ssd"], writes=["ssd"])
                for hd in range(4):
                    _stt(T, nc, rn[:, hd * 256:(hd + 1) * 256], rt[:, hd * 256:(hd + 1) * 256], ss[:, 8 + hd:9 + hd],
                         rgain_t[:, hd * 256:(hd + 1) * 256], ALU.mult, ALU.mult, ["rt", "ssd", "rgain"], ["rn"])
                for c in range(NCH):
                    _tr(T, nc, pb[0][:, c * 128:(c + 1) * 128], rn[:, c * 128:(c + 1) * 128], ident,
                        ["rn", "cmat"], ["pb0"], inc=(c == NCH - 1))
                _copy(T, nc, "act", RNT[:, :, blk * 128:(blk + 1) * 128], pb[0][:].rearrange("p (c t) -> p c t", c=NCH),
                      ["pb0"], ["RNT"])
            for c in range(NCH):
                o_, ga_, gb_ = oa[c % 2], ga[c % 2], gb[c % 2]
                T.dma("sp", o_[:], oaT_d[c, :, t0:t0 + TS], reads=["oaT_d"], writes=[o_.name])
                T.dma("sp", ga_[:], ga_d[c, :, t0:t0 + TS], reads=["gates_d"], writes=[ga_.name])
                T.dma("sp", gb_[:], gb_d[c, :, t0:t0 + TS], reads=["gates_d"], writes=[gb_.name])
                _tt(T, nc, "dve", OAG[:, c, :], o_[:], ga_[:], ALU.mult, [o_.name, ga_.name], ["OAG"])
                _tt(T, nc, "dve", OBG[:, c, :], RNT[:, c, :], gb_[:], ALU.mult, ["RNT", gb_.name], ["OBG"])
            for oc in range(NCH):
                sa_t, sb_t = sga[oc % 2], sgb[oc % 2]
                T.dma("sp", sa_t[:], sa_d[oc, :, t0:t0 + TS], reads=["gates_d"], writes=[sa_t.name])
                T.dma("sp", sb_t[:], sb_d[oc, :, t0:t0 + TS], reads=["gates_d"], writes=[sb_t.name])
                for c in range(NCH):
                    _mm(T, nc, ps[0][:], Wd[:, c, oc * 128:(oc + 1) * 128], OAG[:, c, :], c == 0, c == NCH - 1,
                        ["Wd", "OAG"], ["ps0"], inc=(c == NCH - 1))
                for c in range(NCH):
                    _mm(T, nc, ps[1][:], Wd[:, c, 1024 + oc * 128:1024 + (oc + 1) * 128], OBG[:, c, :], c == 0, c == NCH - 1,
                        ["Wd", "OBG"], ["ps1"], inc=(c == NCH - 1))
                _tt(T, nc, "dve", t1[:], ps[0][:], sa_t[:], ALU.mult, ["ps0", sa_t.name], ["t1"])
                _tt(T, nc, "dve", t2[:], ps[1][:], sb_t[:], ALU.mult, ["ps1", sb_t.name], ["t2"])
                _tt(T, nc, "dve", MG[:, oc, :], t1[:], t2[:], ALU.add, ["t1", "t2"], ["MG"])
            for blk in range(4):
                x_, y_ = xr[blk % 2], yt[blk % 2]
                T.dma("sp", x_[:], xo[t0 + blk * 128:t0 + (blk + 1) * 128, :], writes=[x_.name])
                for g in range(2):
                    pz = ps[2 + g]
                    for oc in range(NCH):
                        _mm(T, nc, pz[:], MG[:, oc, blk * 128:(blk + 1) * 128], Wd[:, oc, 2048 + g * 512:2048 + (g + 1) * 512],
                            oc == 0, oc == NCH - 1, ["MG", "Wd"], [pz.name], inc=(oc == NCH - 1))
                    _tt(T, nc, "dve", y_[:, g * 512:(g + 1) * 512], pz[:], x_[:, g * 512:(g + 1) * 512], ALU.add,
                        [pz.name, x_.name], [y_.name])
                T.dma("sp", y[t0 + blk * 128:t0 + (blk + 1) * 128, :], y_[:], reads=[y_.name], writes=["y"])


_CONST_CACHE = {}


def _const_tables():
    if _CONST_CACHE:
        return _CONST_CACHE
    bf = ml_dtypes.bfloat16
    d = 256
    inv_freq = (np.float32(10000.0) ** (-np.arange(0, d, 2, dtype=np.float32) / np.float32(d))).astype(np.float32)
    pos = np.arange(S, dtype=np.float32)
    ang = (pos[None, :] * inv_freq[:, None]).astype(np.float32)
    c64, s64 = np.cos(ang.astype(np.float64)), np.sin(ang.astype(np.float64))
    _CONST_CACHE["cosq"] = c64.astype(np.float32)
    _CONST_CACHE["sinq"] = s64.astype(np.float32)
    lg = np.log1p(-np.exp2(-5.0 - np.arange(4, dtype=np.float64)))
    idx = np.arange(128)
    same = (idx[:, None] // 64) == (idx[None, :] // 64)
    k_first = (idx[:, None] // 64) < (idx[None, :] // 64)
    dm = np.zeros((128, 4, 128), np.float64)
    for hh in range(4):
        dm[:, hh, :] = np.where(same | k_first, np.exp(lg[hh] * np.abs(idx[:, None] - idx[None, :])), 0.0)
    _CONST_CACHE["dmask"] = dm.reshape(128, 512).astype(np.float32)
    qd = np.zeros((128, 4, 128), np.float64)
    for hh in range(4):
        qd[:, hh, :] = np.exp(lg[hh] * (idx + 1.0))[None, :]
    _CONST_CACHE["qdec"] = qd.reshape(128, 4 * 128).astype(np.float32)
    kd = np.zeros((128, 4), np.float64)
    for hh in range(4):
        kd[:, hh] = np.exp(lg[hh] * (127.0 - idx))
    _CONST_CACHE["kdect"] = kd.astype(np.float32)
    cm = np.zeros((128, 4, 128), np.float32)
    cm[:, 0, :] = np.eye(128)
    cm[:, 1, :] = 1.0 / 128.0
    cm[:, 2, :] = np.where(idx[:, None] >= idx[None, :], -1.0, 0.0)
    cm[:, 3, :] = np.where(idx[:, None] < idx[None, :], -1.0, 0.0)
    _CONST_CACHE["cmat"] = cm.reshape(128, 512).astype(bf)
    q = np.arange(TS)
    diag = np.zeros((4, 128, TS), np.float32)
    for r in range(4):
        diag[r] = np.where((128 * r + idx[:, None]) < q[None, :], 0.0, NEG)
    am0 = np.full((128, 8, TS), NEG, np.float32)
    am1 = np.zeros((128, 8, TS), np.float32)
    for r in range(4):
        am0[:, r, :] = diag[r]
        am1[:, 4 + r, :] = diag[r]
    _CONST_CACHE["amask0"] = am0.reshape(128, 8 * TS).astype(bf)
    _CONST_CACHE["amask1"] = am1.reshape(128, 8 * TS).astype(bf)
    return _CONST_CACHE


_NC_CACHE = {}


def kernel(x, norm_gain, w_in, b_merge, sb_q_gain, sb_k_gain, ret_out_gain, w_branch_sb, w_branch_ret, w_out):
    x = np.asarray(x, np.float32)
    C = _const_tables()
    if "nc" not in _NC_CACHE:
        _NC_CACHE["nc"] = build_program()
    nc = _NC_CACHE["nc"]
    f = lambda a: np.ascontiguousarray(np.asarray(a, np.float32))
    w_in0, wbs0, wbr0, wout0 = f(w_in[0]), f(w_branch_sb[0]), f(w_branch_ret[0]), f(w_out[0])
    gain_rep = np.ascontiguousarray(np.broadcast_to(f(norm_gain[0])[None, :], (128, DM)))
    rgain_rep = np.ascontiguousarray(np.broadcast_to(f(ret_out_gain[0]).reshape(1, DM), (128, DM)))
    qkg = np.ascontiguousarray(np.stack([f(sb_q_gain[0]), f(sb_k_gain[0])], axis=1))
    bmg = np.ascontiguousarray(f(b_merge[0]).reshape(2, 8, 128).transpose(2, 0, 1).reshape(128, 16))
    in_maps = []
    for c in range(8):
        b, p = c // 2, c % 2
        xb = x[b]
        xown = np.ascontiguousarray(xb.reshape(NT, TS, DM)[p::2].reshape(SO, DM))
        blend = np.zeros((128, 2), np.float32)
        blend[:, 0] = 1.0 - p
        blend[:, 1] = float(p)
        in_maps.append({
            "xa": np.ascontiguousarray(xb), "xo": xown, "w_in": w_in0, "wbs": wbs0, "wbr": wbr0, "wout": wout0,
            "gain_rep": gain_rep, "rgain_rep": rgain_rep, "qkg": qkg, "bmg": bmg,
            "cosq": C["cosq"], "sinq": C["sinq"],
            "dmask": C["dmask"], "qdec": C["qdec"], "kdect": C["kdect"], "cmat": C["cmat"],
            "amask": C["amask%d" % p], "blend": blend,
        })
    res = run_bass_kernel_spmd(nc, in_maps, core_ids=list(range(8)))
    _NC_CACHE["last"] = res
    out = np.empty((4, S, DM), np.float32)
    for c in range(8):
        b, p = c // 2, c % 2
        out[b].reshape(NT, TS, DM)[p::2] = np.asarray(res.results[c]["y"], np.float32).reshape(NO, TS, DM)
    return out
```

```python
import contextlib
import numpy as np
import ml_dtypes
import concourse.bass as bass
import concourse.mybir as mybir
from concourse.bass_utils import run_bass_kernel_spmd

F32 = mybir.dt.float32
BF16 = mybir.dt.bfloat16
AF = mybir.ActivationFunctionType
ALU = mybir.AluOpType
AX = mybir.AxisListType


class Tracker:
    ENG = ("pe", "act", "dve", "pool", "sp")

    def __init__(self, nc, n_dma_sems=6):
        self.nc = nc
        self.n_dma = n_dma_sems
        self.stack = contextlib.ExitStack()
        self.streams = {e: [] for e in self.ENG}
        self.count = {}
        self.known = {e: {} for e in self.ENG}
        self.last_write = {}
        self.readers = {}
        self.n_ops = 0

    def __enter__(self):
        nc = self.nc
        self.stack.__enter__()
        self.sem = {}
        for e in self.ENG:
            self.sem[e] = self.stack.enter_context(nc.semaphore("s_" + e))
            self.count[e] = 0
        self.dma_ring = {}
        self.dma_next = {}
        for q in ("sp", "pool", "act"):
            ring = []
            for k in range(self.n_dma):
                name = "d_%s%d" % (q, k)
                self.sem[name] = self.stack.enter_context(nc.semaphore(name))
                self.count[name] = 0
                ring.append(name)
            self.dma_ring[q] = ring
            self.dma_next[q] = 0
        return self

    def __exit__(self, *a):
        return self.stack.__exit__(*a)

    def _deps(self, reads, writes):
        deps = {}

        def add(s, v):
            if v > deps.get(s, 0):
                deps[s] = v

        for b in reads:
            lw = self.last_write.get(b)
            if lw:
                add(*lw)
        for b in writes:
            lw = self.last_write.get(b)
            if lw:
                add(*lw)
            for s, v in self.readers.get(b, {}).items():
                add(s, v)
        return deps

    def _emit_waits(self, e, deps, skip_self=False):
        for s, v in deps.items():
            if skip_self and s == e:
                continue
            if self.known[e].get(s, 0) >= v:
                continue
            self.known[e][s] = v
            sem = self.sem[s]
            self.streams[e].append(("wait", sem, v))

    def _record(self, key, val, reads, writes):
        for b in reads:
            self.readers.setdefault(b, {})[key] = val
        for b in writes:
            self.last_write[b] = (key, val)
            self.readers[b] = {}

    def op(self, e, fn, reads=(), writes=(), inc=True):
        deps = self._deps(reads, writes)
        self._emit_waits(e, deps, skip_self=(e == "pe"))
        if inc:
            self.count[e] += 1
            val = self.count[e]
            self.streams[e].append(("op", fn, self.sem[e], 1))
        else:
            val = self.count[e] + 1
            self.streams[e].append(("op", fn, None, 0))
        self._record(e, val, reads, writes)
        self.n_ops += 1

    def dma(self, q, out, in_, reads=(), writes=(), **kw):
        deps = self._deps(reads, writes)
        name = self.dma_ring[q][self.dma_next[q]]
        self.dma_next[q] = (self.dma_next[q] + 1) % self.n_dma
        deps[name] = max(deps.get(name, 0), self.count[name])
        self._emit_waits(q, deps)
        self.count[name] += 16
        val = self.count[name]
        self.known[q][name] = max(self.known[q].get(name, 0), 0)
        eng = {"sp": self.nc.sync, "pool": self.nc.gpsimd, "act": self.nc.scalar}[q]
        self.streams[q].append(("op", (lambda: eng.dma_start(out=out, in_=in_, **kw)), self.sem[name], 16))
        self._record(name, val, reads, writes)
        self.n_ops += 1

    def barrier(self):
        for e in self.ENG:
            deps = {s: c for s, c in self.count.items() if c > 0 and s != e}
            self._emit_waits(e, deps)

    def finish(self):
        nc = self.nc
        deps = {s: c for s, c in self.count.items() if c > 0 and s != "sp"}
        self._emit_waits("sp", deps)
        streams = self.streams

        def replay(e, eng):
            for item in streams[e]:
                if item[0] == "wait":
                    eng.wait_ge(item[1], item[2])
                else:
                    ins = item[1]()
                    if item[2] is not None:
                        ins.then_inc(item[2], item[3])

        with nc.Block() as block:
            @block.sync
            def _(eng):
                replay("sp", eng)

            @block.scalar
            def _(eng):
                replay("act", eng)

            @block.vector
            def _(eng):
                replay("dve", eng)

            @block.gpsimd
            def _(eng):
                replay("pool", eng)

            @block.tensor
            def _(eng):
                replay("pe", eng)


S = 8192
DM = 1024
TS = 512
NT = S // TS
NO = NT // 2
SO = NO * TS
NCH = DM // 128
EPS = 1e-6
NEG = -30000.0
C_SBQ, C_SBK, C_SBV, C_SBG, C_RQ, C_RK, C_RV, C_RG, C_MSB, C_MRET = [i * 1024 for i in range(10)]
GAMMA = [1.0 - 2.0 ** (-5.0 - h) for h in range(4)]
G64 = [g ** 64 for g in GAMMA]
DEBUG = False


def _mm(T, nc, out, lhsT, rhs, start, stop, reads, writes, inc):
    T.op("pe", lambda: nc.tensor.matmul(out, lhsT=lhsT, rhs=rhs, start=start, stop=stop),
         reads=reads, writes=writes, inc=inc)


def _tr(T, nc, out, in_, ident, reads, writes, inc):
    T.op("pe", lambda: nc.tensor.transpose(out, in_, ident), reads=reads, writes=writes, inc=inc)


def _act(T, nc, out, in_, func, reads, writes, **kw):
    T.op("act", lambda: nc.scalar.activation(out=out, in_=in_, func=func, **kw), reads=reads, writes=writes)


def _ts(T, nc, eng, out, in0, s1, s2, op0, op1, reads, writes):
    e = nc.vector if eng == "dve" else nc.gpsimd
    if op1 is None:
        T.op(eng, lambda: e.tensor_scalar(out=out, in0=in0, scalar1=s1, scalar2=None, op0=op0), reads=reads, writes=writes)
    else:
        T.op(eng, lambda: e.tensor_scalar(out=out, in0=in0, scalar1=s1, scalar2=s2, op0=op0, op1=op1), reads=reads, writes=writes)


def _tt(T, nc, eng, out, in0, in1, op, reads, writes):
    e = nc.vector if eng == "dve" else nc.gpsimd
    T.op(eng, lambda: e.tensor_tensor(out=out, in0=in0, in1=in1, op=op), reads=reads, writes=writes)


def _stt(T, nc, out, in0, scalar, in1, op0, op1, reads, writes):
    T.op("dve", lambda: nc.vector.scalar_tensor_tensor(out=out, in0=in0, scalar=scalar, in1=in1, op0=op0, op1=op1),
         reads=reads, writes=writes)


def _copy(T, nc, eng, out, in_, reads, writes):
    if eng == "act":
        T.op("act", lambda: nc.scalar.copy(out=out, in_=in_), reads=reads, writes=writes)
    else:
        e = nc.vector if eng == "dve" else nc.gpsimd
        T.op(eng, lambda: e.tensor_copy(out=out, in_=in_), reads=reads, writes=writes)


def build_program(phases="ABCD", nt_a=NT, no_b=NO, heads_c=8, no_c=NO, no_d=NO):
    CFG = dict(phases=phases, nt_a=nt_a, no_b=no_b, heads_c=heads_c, no_c=no_c, no_d=no_d)
    nc = bass.Bass("TRN2", target_bir_lowering=False)
    dt = nc.dram_tensor
    xa = dt("xa", [S, DM], F32, kind="ExternalInput").ap()
    xo = dt("xo", [SO, DM], F32, kind="ExternalInput").ap()
    w_in = dt("w_in", [DM, 10 * 1024], F32, kind="ExternalInput").ap()
    wbs = dt("wbs", [DM, DM], F32, kind="ExternalInput").ap()
    wbr = dt("wbr", [DM, DM], F32, kind="ExternalInput").ap()
    wout = dt("wout", [DM, DM], F32, kind="ExternalInput").ap()
    gain_rep = dt("gain_rep", [128, DM], F32, kind="ExternalInput").ap()
    rgain_rep = dt("rgain_rep", [128, DM], F32, kind="ExternalInput").ap()
    qkg = dt("qkg", [128, 2], F32, kind="ExternalInput").ap()
    bmg = dt("bmg", [128, 16], F32, kind="ExternalInput").ap()
    cosq = dt("cosq", [128, S], F32, kind="ExternalInput").ap()
    sinq = dt("sinq", [128, S], F32, kind="ExternalInput").ap()
    dmask_d = dt("dmask", [128, 4 * 128], F32, kind="ExternalInput").ap()
    qdec_d = dt("qdec", [128, 4 * 128], F32, kind="ExternalInput").ap()
    kdect_d = dt("kdect", [128, 4], F32, kind="ExternalInput").ap()
    cmat_d = dt("cmat", [128, 4 * 128], BF16, kind="ExternalInput").ap()
    amask_d = dt("amask", [128, 8 * TS], BF16, kind="ExternalInput").ap()
    blend_d = dt("blend", [128, 2], F32, kind="ExternalInput").ap()
    y = dt("y", [SO, DM], F32, kind="ExternalOutput").ap()
    sk = "ExternalOutput" if DEBUG else "Internal"
    kT_d = dt("kT_d", [8, 128, S], BF16, kind=sk).ap()
    v_d = dt("v_d", [S, DM], BF16, kind=sk).ap()
    qT_d = dt("qT_d", [8, 128, SO], BF16, kind=sk).ap()
    ga_d = dt("ga_d", [8, 128, SO], BF16, kind=sk).ap()
    gb_d = dt("gb_d", [8, 128, SO], BF16, kind=sk).ap()
    sa_d = dt("sa_d", [8, 128, SO], BF16, kind=sk).ap()
    sb_d = dt("sb_d", [8, 128, SO], BF16, kind=sk).ap()
    ret_d = dt("ret_d", [S, DM], F32, kind=sk).ap()
    oaT_d = dt("oaT_d", [8, 128, SO], F32, kind=sk).ap()
    if DEBUG:
        dbg_bf = dt("dbg_bf", [128, 8192], BF16, kind="ExternalOutput").ap()
        dbg_f = dt("dbg_f", [128, 4096], F32, kind="ExternalOutput").ap()

    es = contextlib.ExitStack()
    with es:
        def sb(name, shape, dtype):
            return es.enter_context(nc.sbuf_tensor("t_" + name, shape, dtype))

        cmat = sb("cmat", [128, 4 * 128], BF16)
        ident = cmat[:, 0:128]
        onesm = cmat[:, 128:256]
        trineg = cmat[:, 256:384]
        carryneg = cmat[:, 384:512]
        qkg_t = sb("qkg_t", [128, 2], F32)
        qg_s = sb("qg_s", [128, 1], F32)
        bmg_t = sb("bmg_t", [128, 16], F32)
        blend_t = sb("blend_t", [128, 2], F32)
        ps, pb = [], []
        psum_ctr = [0]

        def alloc_psum(stack):
            tag = "abcdefgh"[psum_ctr[0]]
            psum_ctr[0] += 1
            ps[:] = [stack.enter_context(nc.psum_tensor("ps%d%s" % (k, tag), [128, 512], F32)) for k in range(6)]
            pb[:] = [stack.enter_context(nc.psum_tensor("pb%d%s" % (k, tag), [128, 1024], BF16)) for k in range(2)]
        T = es.enter_context(Tracker(nc))

        T.dma("sp", cmat[:], cmat_d, writes=["cmat"])
        T.dma("sp", qkg_t[:], qkg, writes=["qkg"])
        T.dma("sp", bmg_t[:], bmg, writes=["bmg"])
        T.dma("sp", blend_t[:], blend_d, writes=["blend"])
        _ts(T, nc, "dve", qg_s[:], qkg_t[:, 0:1], float(128 ** -0.5), None, ALU.mult, None, ["qkg"], ["qg_s"])

        def load_w(wt, col0, ncols, key, src=w_in):
            for c0 in range(0, ncols, 512):
                T.dma("pool", wt[:, :, c0:c0 + 512],
                      src[:, col0 + c0: col0 + c0 + 512].rearrange("(c p) n -> p c n", p=128),
                      writes=[key])

        def norm_block(xsrc, r0, xt, junk, ss, gain_t, h, hT, col0, pbt=None):
            pbt = pbt if pbt is not None else pb[0]
            T.dma("sp", xt[:], xsrc[r0:r0 + 128, :], writes=[xt.name])
            _act(T, nc, junk[:], xt[:], AF.Square, [xt.name], [junk.name, ss.name], accum_out=ss[:, 0:1])
            _act(T, nc, ss[:, 1:2], ss[:, 0:1], AF.Sqrt, [ss.name], [ss.name], bias=EPS, scale=1.0 / DM)
            T.op("dve", lambda: nc.vector.reciprocal(out=ss[:, 2:3], in_=ss[:, 1:2]), reads=[ss.name], writes=[ss.name])
            _stt(T, nc, h[:], xt[:], ss[:, 2:3], gain_t[:], ALU.mult, ALU.mult, [xt.name, ss.name, "gain"], [h.name])
            for c in range(NCH):
                _tr(T, nc, pbt[:, c * 128:(c + 1) * 128], h[:, c * 128:(c + 1) * 128], ident,
                    [h.name, "cmat"], [pbt.name], inc=(c == NCH - 1))
            _copy(T, nc, "act", hT[:, :, col0:col0 + 128], pbt[:].rearrange("p (c t) -> p c t", c=NCH),
                  [pbt.name], ["hT"])

        def norm_load(xsrc, r0, xt):
            T.dma("pool", xt[:], xsrc[r0:r0 + 128, :], writes=[xt.name])

        def norm_pre(xsrc, r0, xt, junk, ss, gain_t, h, load=True):
            if load:
                T.dma("sp", xt[:], xsrc[r0:r0 + 128, :], writes=[xt.name])
            _act(T, nc, junk[:], xt[:], AF.Square, [xt.name], [junk.name, ss.name], accum_out=ss[:, 0:1])
            _act(T, nc, ss[:, 1:2], ss[:, 0:1], AF.Sqrt, [ss.name], [ss.name], bias=EPS, scale=1.0 / DM)
            T.op("dve", lambda: nc.vector.reciprocal(out=ss[:, 2:3], in_=ss[:, 1:2]), reads=[ss.name], writes=[ss.name])
            _stt(T, nc, h[:], xt[:], ss[:, 2:3], gain_t[:], ALU.mult, ALU.mult, [xt.name, ss.name, "gain"], [h.name])

        def norm_post(h, hT, col0, pbt):
            for c in range(NCH):
                _tr(T, nc, pbt[:, c * 128:(c + 1) * 128], h[:, c * 128:(c + 1) * 128], ident,
                    [h.name, "cmat"], [pbt.name], inc=(c == NCH - 1))
            _copy(T, nc, "act", hT[:, :, col0:col0 + 128], pbt[:].rearrange("p (c t) -> p c t", c=NCH),
                  [pbt.name], ["hT"])

        def qk_proj(W, wkey, wcol, hT, pz):
            for c in range(NCH):
                _mm(T, nc, pz[:], W[:, c, wcol:wcol + 128], hT[:, c, :], c == 0, c == NCH - 1,
                    [wkey, "hT"], [pz.name], inc=(c == NCH - 1))

        def qk_square(pz, sq, zf=None):
            _act(T, nc, sq[:], pz[:], AF.Square, [pz.name], [sq.name])
            if zf is not None:
                _copy(T, nc, "act", zf[:], pz[:], [pz.name], [zf.name])

        def qk_norm(gcol, pz, pm, sq, rs, outt, outkey, do_square=True, zf=None):
            if do_square:
                qk_square(pz, sq)
            _mm(T, nc, pm[:], onesm, sq[:], True, True, ["cmat", sq.name], [pm.name], True)
            _act(T, nc, rs[:], pm[:], AF.Ln, [pm.name], [rs.name], bias=EPS, scale=1.0)
            _act(T, nc, rs[:], rs[:], AF.Exp, [rs.name], [rs.name], scale=-0.5)
            src, skey = (pz[:], pz.name) if zf is None else (zf[:], zf.name)
            _stt(T, nc, outt, src, gcol, rs[:], ALU.mult, ALU.mult, [skey, rs.name, "qkg", "qg_s"], [outkey])

        def qk_head(W, wkey, wcol, hT, gcol, pz, pm, sq, rs, outt, outkey):
            qk_proj(W, wkey, wcol, hT, pz)
            qk_norm(gcol, pz, pm, sq, rs, outt, outkey)

        with contextlib.ExitStack() as pa:
            def sa_(name, shape, dtype):
                return pa.enter_context(nc.sbuf_tensor("t_" + name, shape, dtype))
            alloc_psum(pa)
            Wa = sa_("Wa", [128, NCH, 5 * 1024], BF16)
            gain_t = sa_("gain_t", [128, DM], F32)
            xts = [sa_("xt%d" % k, [128, DM], F32) for k in range(4)]
            junk = sa_("junk", [128, DM], BF16)
            sss = [sa_("ss%d" % k, [128, 4], F32) for k in range(4)]
            hs = [sa_("h%d" % k, [128, DM], BF16) for k in range(4)]
            hT = sa_("hT", [128, NCH, TS], BF16)
            sqs = [sa_("sq%d" % k, [128, TS], BF16) for k in range(3)]
            rss = [sa_("rs%d" % k, [128, TS], F32) for k in range(2)]
            kout = [sa_("kout%d" % k, [128, TS], BF16) for k in range(2)]
            vout = [sa_("vout%d" % k, [128, DM], BF16) for k in range(2)]
            cs = sa_("cs", [128, 2, TS], F32)
            ra = [sa_("ra%d" % k, [128, TS], F32) for k in range(2)]
            rb = [sa_("rb%d" % k, [128, TS], F32) for k in range(2)]
            qrT = [sa_("qrT%d" % k, [128, 2, TS], BF16) for k in range(2)]
            krT = [sa_("krT%d" % k, [128, 2, TS], BF16) for k in range(2)]
            qdT = [sa_("qdT%d" % k, [128, 2, TS], BF16) for k in range(3)]
            kdt = [sa_("kdt%d" % k, [128, 4, 256], BF16) for k in range(2)]
            rv = sa_("rv", [128, 4, DM], BF16)
            sT = [sa_("sT%d" % k, [128, 4, 128], BF16) for k in range(2)]
            St = sa_("St", [128, 4, 2, 256], F32)
            Sb0 = sa_("Sb0", [128, 4, 2, 256], BF16)
            Sbt = [sa_("Sbt%d" % k, [128, 3, 2, 256], BF16) for k in range(2)]
            dmask = sa_("dmask", [128, 4, 128], F32)
            qdec = sa_("qdec", [128, 4, 128], F32)
            kdect = sa_("kdect", [128, 4], F32)
            rout = [sa_("rout%d" % k, [128, 256], F32) for k in range(2)]
            pb0f, pb1f = pb[0][:].bitcast(F32), pb[1][:].bitcast(F32)
            zfs = [ra[0], ra[1], rb[0]]

            T.dma("sp", gain_t[:], gain_rep, writes=["gain"])
            T.dma("sp", dmask[:].rearrange("p h q -> p (h q)"), dmask_d, writes=["dmask"])
            T.dma("sp", qdec[:].rearrange("p h q -> p (h q)"), qdec_d, writes=["qdec"])
            T.dma("sp", kdect[:], kdect_d, writes=["kdect"])
            load_w(Wa[:, :, 0:1024], C_SBK, 1024, "Wa")
            load_w(Wa[:, :, 1024:2048], C_SBV, 1024, "Wa")
            load_w(Wa[:, :, 2048:3072], C_RQ, 1024, "Wa")
            load_w(Wa[:, :, 3072:4096], C_RK, 1024, "Wa")
            load_w(Wa[:, :, 4096:5120], C_RV, 1024, "Wa")
            T.op("dve", lambda: nc.vector.memset(St[:].rearrange("p a b c -> p (a b c)"), 0.0), writes=["St"])
            T.op("dve", lambda: nc.vector.memset(Sb0[:].rearrange("p a b c -> p (a b c)"), 0.0), writes=["Sb0"])
            G128 = [g ** 128 for g in GAMMA]

            def ret_proj(hd, t0):
                par = hd % 2
                for which in range(2):
                    wc = (2048 if which == 0 else 3072) + hd * 256
                    p1, p2 = (ps[0], ps[1]) if which == 0 else (ps[2], ps[3])
                    for half, pz in enumerate((p1, p2)):
                        for c in range(NCH):
                            _mm(T, nc, pz[:], Wa[:, c, wc + half * 128: wc + (half + 1) * 128], hT[:, c, :],
                                c == 0, c == NCH - 1, ["Wa", "hT"], [pz.name], inc=(c == NCH - 1))
                    dst = qrT[par] if which == 0 else krT[par]
                    ct, st_ = cs[:, 0, :], cs[:, 1, :]
                    _tt(T, nc, "dve", ra[0][:], p1[:], ct, ALU.mult, [p1.name, "cs"], [ra[0].name])
                    _tt(T, nc, "dve", rb[0][:], p2[:], st_, ALU.mult, [p2.name, "cs"], [rb[0].name])
                    _tt(T, nc, "dve", ra[1][:], p1[:], st_, ALU.mult, [p1.name, "cs"], [ra[1].name])
                    _tt(T, nc, "dve", rb[1][:], p2[:], ct, ALU.mult, [p2.name, "cs"], [rb[1].name])
                    for half in range(2):
                        _tt(T, nc, "pool", ra[half][:], ra[half][:], rb[half][:], ALU.subtract if half == 0 else ALU.add,
                            [ra[half].name, rb[half].name], [ra[half].name])
                        if which == 0:
                            _copy(T, nc, "act", dst[:, half, :], ra[half][:], [ra[half].name], [dst.name])
                            for blk in range(4):
                                _tt(T, nc, "pool", qdT[hd % 3][:, half, blk * 128:(blk + 1) * 128], ra[half][:, blk * 128:(blk + 1) * 128],
                                    qdec[:, hd, :], ALU.mult, [ra[half].name, "qdec"], [qdT[hd % 3].name])
                        else:
                            T.op("act", (lambda o=dst[:, half, :], i_=ra[half][:]: nc.scalar.mul(out=o, in_=i_, mul=1.0 / 16.0)),
                                 reads=[ra[half].name], writes=[dst.name])

            def ret_small_a(hd, t0):
                par = hd % 2
                kr, qr, kd, st = krT[par], qrT[par], kdt[par], sT[par]
                for blk in range(4):
                    for half in range(2):
                        _tr(T, nc, pb[1][:, blk * 256 + half * 128:blk * 256 + (half + 1) * 128], kr[:, half, blk * 128:(blk + 1) * 128], ident,
                            [kr.name, "cmat"], [pb[1].name], inc=(blk == 3 and half == 1))
                _ts(T, nc, "dve", kd[:].rearrange("p a b -> p (a b)"), pb[1][:], kdect[:, hd:hd + 1], None, ALU.mult, None,
                    [pb[1].name, "kdect"], [kd.name])
                for blk in range(4):
                    kv = ps[4 + blk % 2]
                    for half in range(2):
                        _mm(T, nc, kv[:, half * 256:(half + 1) * 256], kd[:, blk, half * 128:(half + 1) * 128],
                            rv[:, blk, hd * 256:(hd + 1) * 256], True, True, [kd.name, "rv"], [kv.name], inc=(half == 1))
                    sflat = St[:, hd, :, :].rearrange("p a b -> p (a b)")
                    _stt(T, nc, sflat, sflat, float(G128[hd]), kv[:], ALU.mult, ALU.add, ["St", kv.name], ["St"])
                    if blk < 3:
                        _copy(T, nc, "act", Sbt[par][:, blk, :, :].rearrange("p a b -> p (a b)"), sflat, ["St"], [Sbt[par].name + str(blk)])
                for blk in range(4):
                    b0 = blk * 128
                    for half in range(2):
                        _mm(T, nc, pb0f[:, blk * 128:(blk + 1) * 128], kr[:, half, b0:b0 + 128], qr[:, half, b0:b0 + 128],
                            half == 0, half == 1, [kr.name, qr.name], [pb[0].name], inc=(half == 1))
                for blk in range(4):
                    _tt(T, nc, "dve", st[:, blk, :], pb0f[:, blk * 128:(blk + 1) * 128], dmask[:, hd, :], ALU.mult,
                        [pb[0].name, "dmask"], [st.name])

            def ret_small_b(hd, t0):
                par = hd % 2
                qd, st = qdT[hd % 3], sT[par]
                for blk in range(4):
                    b0 = blk * 128
                    pot = ps[blk % 2]
                    po = pot[:]
                    _mm(T, nc, po[:, 0:256], st[:, blk, :], rv[:, blk, hd * 256:(hd + 1) * 256], True, False,
                        [st.name, "rv"], [pot.name], inc=True)
                    for half in range(2):
                        sb_ap = Sb0[:, hd, half, :] if blk == 0 else Sbt[par][:, blk - 1, half, :]
                        skey = "Sb0" if blk == 0 else Sbt[par].name + str(blk - 1)
                        _mm(T, nc, po[:, 0:256], qd[:, half, b0:b0 + 128], sb_ap, False, half == 1,
                            [qd.name, skey], [pot.name], inc=(half == 1))
                    ro = rout[blk % 2]
                    _copy(T, nc, "act" if blk % 2 == 0 else "dve", ro[:], po[:, 0:256], [pot.name], [ro.name])
                    T.dma("sp", ret_d[t0 + b0:t0 + b0 + 128, hd * 256:(hd + 1) * 256], ro[:], reads=[ro.name], writes=["ret_d"])
                _copy(T, nc, "act", Sb0[:, hd, :, :].rearrange("p a b -> p (a b)"),
                      St[:, hd, :, :].rearrange("p a b -> p (a b)"), ["St"], ["Sb0"])

            for t in range(nt_a if "A" in phases else 0):
                t0 = t * TS
                for k, tab in enumerate((cosq, sinq)):
                    T.dma("sp", cs[:, k, :], tab[:, t0:t0 + TS], writes=["cs"])
                n_a = nt_a if "A" in phases else 0
                if t == 0:
                    for blk in range(4):
                        norm_pre(xa, blk * 128, xts[blk], junk, sss[blk], gain_t, hs[blk])
                for blk in range(4):
                    norm_post(hs[blk], hT, blk * 128, pb[blk % 2])
                if t + 1 < n_a:
                    for blk in range(4):
                        norm_load(xa, t0 + TS + blk * 128, xts[blk])
                pzk = [ps[0], ps[1], ps[4]]
                for h0 in range(2):
                    qk_proj(Wa, "Wa", h0 * 128, hT, pzk[h0])
                    qk_square(pzk[h0], sqs[h0], zfs[h0])
                for hd in range(8):
                    if hd + 2 < 8:
                        qk_proj(Wa, "Wa", (hd + 2) * 128, hT, pzk[(hd + 2) % 3])
                        qk_square(pzk[(hd + 2) % 3], sqs[(hd + 2) % 3], zfs[(hd + 2) % 3])
                    ko = kout[hd % 2]
                    qk_norm(qkg_t[:, 1:2], pzk[hd % 3], ps[2 + hd % 2], sqs[hd % 3], rss[hd % 2], ko[:], ko.name, do_square=False,
                            zf=zfs[hd % 3])
                    T.dma("sp", kT_d[hd, :, t0:t0 + TS], ko[:], reads=[ko.name], writes=["kT_d"])
                nb = 0
                for blk in range(4):
                    vo = vout[blk % 2]
                    for g in range(2):
                        pz = ps[(4 + nb) % 6]
                        nb += 1
                        for c in range(NCH):
                            _mm(T, nc, pz[:], hT[:, c, blk * 128:(blk + 1) * 128], Wa[:, c, 1024 + g * 512:1024 + (g + 1) * 512],
                                c == 0, c == NCH - 1, ["hT", "Wa"], [pz.name], inc=(c == NCH - 1))
                        _copy(T, nc, "act" if g == 0 else "dve", vo[:, g * 512:(g + 1) * 512], pz[:], [pz.name], [vo.name])
                    T.dma("sp", v_d[t0 + blk * 128:t0 + (blk + 1) * 128, :], vo[:], reads=[vo.name], writes=["v_d"])
                    for g in range(2):
                        pz = ps[(4 + nb) % 6]
                        nb += 1
                        for c in range(NCH):
                            _mm(T, nc, pz[:], hT[:, c, blk * 128:(blk + 1) * 128], Wa[:, c, 4096 + g * 512:4096 + (g + 1) * 512],
                                c == 0, c == NCH - 1, ["hT", "Wa"], [pz.name], inc=(c == NCH - 1))
                        _copy(T, nc, "act" if g == 0 else "dve", rv[:, blk, g * 512:(g + 1) * 512], pz[:], [pz.name], ["rv"])
                ret_proj(0, t0)
                for hd in range(4):
                    if hd + 1 < 4:
                        ret_proj(hd + 1, t0)
                    ret_small_a(hd, t0)
                    if t + 1 < n_a:
                        norm_pre(xa, 0, xts[hd], junk, sss[hd], gain_t, hs[hd], load=False)
                    if hd > 0:
                        ret_small_b(hd - 1, t0)
                ret_small_b(3, t0)
            T.barrier()
        build_rest(nc, T, locals())
        T.finish()
    return nc


def build_rest(nc, T, L):
    CFG = L["CFG"]
    phases = CFG["phases"]
    ps, pb = L["ps"], L["pb"]
    ident, onesm, trineg, carryneg = L["ident"], L["onesm"], L["trineg"], L["carryneg"]
    qg_s, bmg_t, blend_t = L["qg_s"], L["bmg_t"], L["blend_t"]
    xo, y = L["xo"], L["y"]
    kT_d, v_d, qT_d, ga_d, gb_d, sa_d, sb_d, ret_d, oaT_d = (L[k] for k in
        ("kT_d", "v_d", "qT_d", "ga_d", "gb_d", "sa_d", "sb_d", "ret_d", "oaT_d"))
    load_w, norm_block, qk_head = L["load_w"], L["norm_block"], L["qk_head"]
    wbs, wbr, wout = L["wbs"], L["wbr"], L["wout"]

    with contextlib.ExitStack() as pbs:
        def sa_(name, shape, dtype):
            return pbs.enter_context(nc.sbuf_tensor("t_" + name, shape, dtype))
        L["alloc_psum"](pbs)
        Wb = sa_("Wb", [128, NCH, 5 * 1024], BF16)
        gain_t = sa_("gain_tb", [128, DM], F32)
        xts = [sa_("xtb%d" % k, [128, DM], F32) for k in range(2)]
        junk = sa_("junkb", [128, DM], BF16)
        sss = [sa_("ssb%d" % k, [128, 4], F32) for k in range(2)]
        hs = [sa_("hb%d" % k, [128, DM], BF16) for k in range(4)]
        hT = sa_("hTb", [128, NCH, TS], BF16)
        sqs = [sa_("sqb%d" % k, [128, TS], BF16) for k in range(3)]
        rss = [sa_("rsb%d" % k, [128, TS], F32) for k in range(2)]
        qout = [sa_("qout%d" % k, [128, TS], BF16) for k in range(2)]
        zfs = [sa_("zfb%d" % k, [128, TS], F32) for k in range(3)]
        gout = [sa_("gout%d" % k, [128, TS], BF16) for k in range(4)]
        T.dma("sp", gain_t[:], L["gain_rep"], writes=["gain"])
        for k, col in enumerate((C_SBQ, C_SBG, C_RG, C_MSB, C_MRET)):
            load_w(Wb[:, :, k * 1024:(k + 1) * 1024], col, 1024, "Wb")
        norm_pre, norm_post, qk_proj, qk_square, qk_norm = (L[k] for k in ("norm_pre", "norm_post", "qk_proj", "qk_square", "qk_norm"))
        n_b = CFG["no_b"] if "B" in phases else 0
        for i in range(n_b):
            t0 = i * TS
            if i == 0:
                for blk in range(4):
                    norm_pre(xo, blk * 128, xts[blk % 2], junk, sss[blk % 2], gain_t, hs[blk])
            for blk in range(4):
                norm_post(hs[blk], hT, blk * 128, pb[blk % 2])
            pzq = [ps[0], ps[1], ps[2]]
            for h0 in range(2):
                qk_proj(Wb, "Wb", h0 * 128, hT, pzq[h0])
                qk_square(pzq[h0], sqs[h0], zfs[h0])
            for hd in range(8):
                if hd + 2 < 8:
                    qk_proj(Wb, "Wb", (hd + 2) * 128, hT, pzq[(hd + 2) % 3])
                    qk_square(pzq[(hd + 2) % 3], sqs[(hd + 2) % 3], zfs[(hd + 2) % 3])
                qo = qout[hd % 2]
                qk_norm(qg_s[:, 0:1], pzq[hd % 3], ps[3 + hd % 2], sqs[hd % 3], rss[hd % 2], qo[:], qo.name, do_square=False,
                        zf=zfs[hd % 3])
                T.dma("sp", qT_d[hd, :, t0:t0 + TS], qo[:], reads=[qo.name], writes=["qT_d"])
            if i + 1 < n_b:
                for blk in range(4):
                    norm_pre(xo, t0 + TS + blk * 128, xts[blk % 2], junk, sss[blk % 2], gain_t, hs[blk])
            n = 0
            for k, (dst, func) in enumerate(((ga_d, AF.Silu), (gb_d, AF.Silu), (sa_d, AF.Sigmoid), (sb_d, AF.Sigmoid))):
                for c in range(NCH):
                    pz = ps[(5 + n) % 6]
                    go = gout[n % 4]
                    n += 1
                    wc = (k + 1) * 1024 + c * 128
                    for cc in range(NCH):
                        _mm(T, nc, pz[:], Wb[:, cc, wc:wc + 128], hT[:, cc, :], cc == 0, cc == NCH - 1,
                            ["Wb", "hT"], [pz.name], inc=(cc == NCH - 1))
                    if func == AF.Silu:
                        _act(T, nc, go[:], pz[:], func, [pz.name], [go.name])
                    else:
                        bcol = (k - 2) * 8 + c
                        _act(T, nc, go[:], pz[:], func, [pz.name, "bmg"], [go.name], bias=bmg_t[:, bcol:bcol + 1], scale=1.0)
                    T.dma("sp", dst[c, :, t0:t0 + TS], go[:], reads=[go.name], writes=["gates_d"])
        T.barrier()

    with contextlib.ExitStack() as pcs:
        def sa_(name, shape, dtype):
            return pcs.enter_context(nc.sbuf_tensor("t_" + name, shape, dtype))
        KT = [sa_("KT%d" % k, [128, S], BF16) for k in range(2)]
        VV = [sa_("VV%d" % k, [128, S // 128, 128], BF16) for k in range(2)]
        QT = [sa_("QT%d" % k, [128, SO], BF16) for k in range(2)]
        amask = sa_("amask", [128, 8, TS], BF16)
        E = [sa_("E%d" % k, [128, 2, TS], F32) for k in range(2)]
        Lp = [sa_("Lp%d" % k, [128, 2, TS], BF16) for k in range(2)]
        G = [sa_("G%d" % k, [128, 2, TS], F32) for k in range(2)]
        W = [sa_("W%d" % k, [128, 2, TS], BF16) for k in range(2)]
        osb = [sa_("osb%d" % k, [128, TS], F32) for k in range(2)]
        T.dma("sp", amask[:].rearrange("p r q -> p (r q)"), L["amask_d"], writes=["amask"])
        Zall = pcs.enter_context(nc.psum_tensor("Zall", [128, 2, 2, TS], F32))
        Call = pcs.enter_context(nc.psum_tensor("Call", [128, 2, TS], F32))
        Oall = pcs.enter_context(nc.psum_tensor("Oall", [128, 2, TS], F32))
        seqs = ([0, 3, 4, 7], [1, 2, 5, 6])
        steps = [[(i, kb) for i in seq for kb in range(8 * i + 7, -1, -1)] for seq in seqs]
        NS = len(steps[0])
        assert NS == len(steps[1])
        if CFG["no_c"] < NO:
            seqs = ([0], [0]) if CFG["no_c"] == 1 else seqs
            steps = [[(i, kb) for i in seq for kb in range(8 * i + 7, -1, -1)] for seq in seqs]
            NS = len(steps[0])
        nosb = [0]
        for hd in range(CFG["heads_c"] if "C" in phases else 0):
            sl = hd % 2
            kt, vv, qt = KT[sl], VV[sl], QT[sl]
            for part in range(4):
                T.dma("sp", kt[:, part * 2048:(part + 1) * 2048], kT_d[hd, :, part * 2048:(part + 1) * 2048],
                      reads=["kT_d"], writes=[kt.name])
            for part in range(4):
                T.dma("sp", vv[:, part * 16:(part + 1) * 16, :],
                      v_d[part * 2048:(part + 1) * 2048, hd * 128:(hd + 1) * 128].rearrange("(b p) d -> p b d", p=128),
                      reads=["v_d"], writes=[vv.name])
            T.dma("sp", qt[:], qT_d[hd, :, :], reads=["qT_d"], writes=[qt.name])

            def pe1(st, t):
                i, kb = steps[st][t]
                par = t % 2
                Z = Zall[:, par, st, :]
                masked = kb >= 8 * i
                _mm(T, nc, Z, kt[:, kb * 128:(kb + 1) * 128], qt[:, i * TS:(i + 1) * TS], True, not masked,
                    [kt.name, qt.name], ["Z%d" % par], inc=(not masked))
                if masked:
                    _mm(T, nc, Z, ident, amask[:, kb - 8 * i, :], False, True, ["cmat", "amask"], ["Z%d" % par], True)

            def a1(t):
                par = t % 2
                _act(T, nc, E[par][:], Zall[:, par, :, :], AF.Exp, ["Z%d" % par], [E[par].name])

            def a2(t):
                par = t % 2
                _act(T, nc, Lp[par][:], E[par][:], AF.Ln, [E[par].name], [Lp[par].name], bias=1.0, scale=1.0)

            def pe2(st, t):
                i, kb = steps[st][t]
                l_ = Lp[t % 2]
                fst = (kb == 8 * i + 7)
                T.op("pe", (lambda o=Call[:, st, :], r_=l_[:, st, :], f_=fst: nc.tensor.matmul(o, lhsT=trineg, rhs=r_, start=f_, stop=True, skip_group_check=(not f_))),
                     reads=["cmat", l_.name], writes=["Call"], inc=True)

            def a3(t):
                par = t % 2
                _act(T, nc, G[par][:], Call[:], AF.Exp, ["Call"], [G[par].name])

            def pe3(st, t):
                l_ = Lp[t % 2]
                T.op("pe", (lambda o=Call[:, st, :], r_=l_[:, st, :]: nc.tensor.matmul(o, lhsT=carryneg, rhs=r_, start=False, stop=True, skip_group_check=True)),
                     reads=["cmat", l_.name], writes=["Call"], inc=True)

            def v1(t):
                par = t % 2
                _tt(T, nc, "dve", W[par][:], E[par][:], G[par][:], ALU.mult, [E[par].name, G[par].name], [W[par].name])

            def pe4(st, t):
                i, kb = steps[st][t]
                w_ = W[t % 2]
                _mm(T, nc, Oall[:, st, :], vv[:, kb, :], w_[:, st, :], kb == 8 * i + 7, kb == 0, [vv.name, w_.name], ["O%d" % st], True)
                if kb == 0:
                    ob = osb[nosb[0] % 2]
                    nosb[0] += 1
                    _copy(T, nc, "dve", ob[:], Oall[:, st, :], ["O%d" % st], [ob.name])
                    T.dma("sp", oaT_d[hd, :, i * TS:(i + 1) * TS], ob[:], reads=[ob.name], writes=["oaT_d"])

            for st in range(2):
                pe1(st, 0)
            if NS > 1:
                for st in range(2):
                    pe1(st, 1)
            a1(0)
            a2(0)
            for t in range(NS):
                if t + 1 < NS:
                    a1(t + 1)
                for st in range(2):
                    pe2(st, t)
                if t + 2 < NS:
                    for st in range(2):
                        pe1(st, t + 2)
                if t > 0:
                    for st in range(2):
                        pe4(st, t - 1)
                a3(t)
                for st in range(2):
                    pe3(st, t)
                v1(t)
                if t + 1 < NS:
                    a2(t + 1)
            for st in range(2):
                pe4(st, NS - 1)
        T.barrier()

    with contextlib.ExitStack() as pds:
        def sa_(name, shape, dtype):
            return pds.enter_context(nc.sbuf_tensor("t_" + name, shape, dtype))
        L["alloc_psum"](pds)
        Wd = sa_("Wd", [128, NCH, 3 * 1024], BF16)
        rgain_t = sa_("rgain_t", [128, DM], F32)
        rA = [sa_("rA%d" % k, [128, DM], F32) for k in range(2)]
        rB = [sa_("rB%d" % k, [128, DM], F32) for k in range(2)]
        rt = sa_("rt", [128, DM], F32)
        junk = sa_("junkd", [128, 256], F32)
        ss = sa_("ssd", [128, 12], F32)
        rn = sa_("rn", [128, DM], BF16)
        RNT = sa_("RNT", [128, NCH, TS], BF16)
        oa = [sa_("oa%d" % k, [128, TS], F32) for k in range(2)]
        ga = [sa_("ga%d" % k, [128, TS], BF16) for k in range(2)]
        gb = [sa_("gb%d" % k, [128, TS], BF16) for k in range(2)]
        sga = [sa_("sga%d" % k, [128, TS], BF16) for k in range(2)]
        sgb = [sa_("sgb%d" % k, [128, TS], BF16) for k in range(2)]
        OAG = sa_("OAG", [128, NCH, TS], BF16)
        OBG = sa_("OBG", [128, NCH, TS], BF16)
        MG = sa_("MG", [128, NCH, TS], BF16)
        t1 = sa_("t1", [128, TS], F32)
        t2 = sa_("t2", [128, TS], F32)
        xr = [sa_("xr%d" % k, [128, DM], F32) for k in range(2)]
        yt = [sa_("yt%d" % k, [128, DM], F32) for k in range(2)]
        T.dma("sp", rgain_t[:], L["rgain_rep"], writes=["rgain"])
        load_w(Wd[:, :, 0:1024], 0, 1024, "Wd", src=wbs)
        load_w(Wd[:, :, 1024:2048], 0, 1024, "Wd", src=wbr)
        load_w(Wd[:, :, 2048:3072], 0, 1024, "Wd", src=wout)
        n = 0
        for i in range(CFG["no_d"] if "D" in phases else 0):
            t0 = i * TS
            for blk in range(4):
                a_, b_ = rA[blk % 2], rB[blk % 2]
                T.dma("pool", a_[:], ret_d[(2 * i) * TS + blk * 128:(2 * i) * TS + (blk + 1) * 128, :], reads=["ret_d"], writes=[a_.name])
                T.dma("pool", b_[:], ret_d[(2 * i + 1) * TS + blk * 128:(2 * i + 1) * TS + (blk + 1) * 128, :], reads=["ret_d"], writes=[b_.name])
                _ts(T, nc, "dve", rt[:], a_[:], blend_t[:, 0:1], None, ALU.mult, None, [a_.name, "blend"], ["rt"])
                _stt(T, nc, rt[:], b_[:], blend_t[:, 1:2], rt[:], ALU.mult, ALU.add, [b_.name, "blend", "rt"], ["rt"])
                for hd in range(4):
                    _act(T, nc, junk[:], rt[:, hd * 256:(hd + 1) * 256], AF.Square, ["rt"], ["junkd", "ssd"],
                         accum_out=ss[:, hd:hd + 1])
                _act(T, nc, ss[:, 4:8], ss[:, 0:4], AF.Sqrt, ["ssd"], ["ssd"], bias=EPS, scale=1.0 / 256)
                T.op("dve", lambda: nc.vector.reciprocal(out=ss[:, 8:12], in_=ss[:, 4:8]), reads=["ssd"], writes=["ssd"])
                for hd in range(4):
                    _stt(T, nc, rn[:, hd * 256:(hd + 1) * 256], rt[:, hd * 256:(hd + 1) * 256], ss[:, 8 + hd:9 + hd],
                         rgain_t[:, hd * 256:(hd + 1) * 256], ALU.mult, ALU.mult, ["rt", "ssd", "rgain"], ["rn"])
                for c in range(NCH):
                    _tr(T, nc, pb[0][:, c * 128:(c + 1) * 128], rn[:, c * 128:(c + 1) * 128], ident,
                        ["rn", "cmat"], ["pb0"], inc=(c == NCH - 1))
                _copy(T, nc, "act", RNT[:, :, blk * 128:(blk + 1) * 128], pb[0][:].rearrange("p (c t) -> p c t", c=NCH),
                      ["pb0"], ["RNT"])
            for c in range(NCH):
                o_, ga_, gb_ = oa[c % 2], ga[c % 2], gb[c % 2]
                T.dma("pool", o_[:], oaT_d[c, :, t0:t0 + TS], reads=["oaT_d"], writes=[o_.name])
                T.dma("pool", ga_[:], ga_d[c, :, t0:t0 + TS], reads=["gates_d"], writes=[ga_.name])
                T.dma("pool", gb_[:], gb_d[c, :, t0:t0 + TS], reads=["gates_d"], writes=[gb_.name])
                _tt(T, nc, "dve", OAG[:, c, :], o_[:], ga_[:], ALU.mult, [o_.name, ga_.name], ["OAG"])
                _tt(T, nc, "dve", OBG[:, c, :], RNT[:, c, :], gb_[:], ALU.mult, ["RNT", gb_.name], ["OBG"])
            for oc in range(NCH):
                sa_t, sb_t = sga[oc % 2], sgb[oc % 2]
                T.dma("pool", sa_t[:], sa_d[oc, :, t0:t0 + TS], reads=["gates_d"], writes=[sa_t.name])
                T.dma("pool", sb_t[:], sb_d[oc, :, t0:t0 + TS], reads=["gates_d"], writes=[sb_t.name])
                for c in range(NCH):
                    _mm(T, nc, ps[0][:], Wd[:, c, oc * 128:(oc + 1) * 128], OAG[:, c, :], c == 0, c == NCH - 1,
                        ["Wd", "OAG"], ["ps0"], inc=(c == NCH - 1))
                for c in range(NCH):
                    _mm(T, nc, ps[1][:], Wd[:, c, 1024 + oc * 128:1024 + (oc + 1) * 128], OBG[:, c, :], c == 0, c == NCH - 1,
                        ["Wd", "OBG"], ["ps1"], inc=(c == NCH - 1))
                _tt(T, nc, "dve", t1[:], ps[0][:], sa_t[:], ALU.mult, ["ps0", sa_t.name], ["t1"])
                _tt(T, nc, "dve", t2[:], ps[1][:], sb_t[:], ALU.mult, ["ps1", sb_t.name], ["t2"])
                _tt(T, nc, "dve", MG[:, oc, :], t1[:], t2[:], ALU.add, ["t1", "t2"], ["MG"])
            for blk in range(4):
                x_, y_ = xr[blk % 2], yt[blk % 2]
                T.dma("pool", x_[:], xo[t0 + blk * 128:t0 + (blk + 1) * 128, :], writes=[x_.name])
                for g in range(2):
                    pz = ps[2 + g]
                    for oc in range(NCH):
                        _mm(T, nc, pz[:], MG[:, oc, blk * 128:(blk + 1) * 128], Wd[:, oc, 2048 + g * 512:2048 + (g + 1) * 512],
                            oc == 0, oc == NCH - 1, ["MG", "Wd"], [pz.name], inc=(oc == NCH - 1))
                    _tt(T, nc, "dve", y_[:, g * 512:(g + 1) * 512], pz[:], x_[:, g * 512:(g + 1) * 512], ALU.add,
                        [pz.name, x_.name], [y_.name])
                T.dma("sp", y[t0 + blk * 128:t0 + (blk + 1) * 128, :], y_[:], reads=[y_.name], writes=["y"])


_CONST_CACHE = {}


def _const_tables():
    if _CONST_CACHE:
        return _CONST_CACHE
    bf = ml_dtypes.bfloat16
    d = 256
    inv_freq = (np.float32(10000.0) ** (-np.arange(0, d, 2, dtype=np.float32) / np.float32(d))).astype(np.float32)
    pos = np.arange(S, dtype=np.float32)
    ang = (pos[None, :] * inv_freq[:, None]).astype(np.float32)
    c64, s64 = np.cos(ang.astype(np.float64)), np.sin(ang.astype(np.float64))
    _CONST_CACHE["cosq"] = c64.astype(np.float32)
    _CONST_CACHE["sinq"] = s64.astype(np.float32)
    lg = np.log1p(-np.exp2(-5.0 - np.arange(4, dtype=np.float64)))
    idx = np.arange(128)
    same = (idx[:, None] // 64) == (idx[None, :] // 64)
    k_first = (idx[:, None] // 64) < (idx[None, :] // 64)
    dm = np.zeros((128, 4, 128), np.float64)
    for hh in range(4):
        dm[:, hh, :] = np.where(same | k_first, np.exp(lg[hh] * np.abs(idx[:, None] - idx[None, :])), 0.0)
    _CONST_CACHE["dmask"] = dm.reshape(128, 512).astype(np.float32)
    qd = np.zeros((128, 4, 128), np.float64)
    for hh in range(4):
        qd[:, hh, :] = np.exp(lg[hh] * (idx + 1.0))[None, :]
    _CONST_CACHE["qdec"] = qd.reshape(128, 4 * 128).astype(np.float32)
    kd = np.zeros((128, 4), np.float64)
    for hh in range(4):
        kd[:, hh] = np.exp(lg[hh] * (127.0 - idx))
    _CONST_CACHE["kdect"] = kd.astype(np.float32)
    cm = np.zeros((128, 4, 128), np.float32)
    cm[:, 0, :] = np.eye(128)
    cm[:, 1, :] = 1.0 / 128.0
    cm[:, 2, :] = np.where(idx[:, None] >= idx[None, :], -1.0, 0.0)
    cm[:, 3, :] = np.where(idx[:, None] < idx[None, :], -1.0, 0.0)
    _CONST_CACHE["cmat"] = cm.reshape(128, 512).astype(bf)
    q = np.arange(TS)
    diag = np.zeros((4, 128, TS), np.float32)
    for r in range(4):
        diag[r] = np.where((128 * r + idx[:, None]) < q[None, :], 0.0, NEG)
    am0 = np.full((128, 8, TS), NEG, np.float32)
    am1 = np.zeros((128, 8, TS), np.float32)
    for r in range(4):
        am0[:, r, :] = diag[r]
        am1[:, 4 + r, :] = diag[r]
    _CONST_CACHE["amask0"] = am0.reshape(128, 8 * TS).astype(bf)
    _CONST_CACHE["amask1"] = am1.reshape(128, 8 * TS).astype(bf)
    return _CONST_CACHE


_NC_CACHE = {}


def kernel(x, norm_gain, w_in, b_merge, sb_q_gain, sb_k_gain, ret_out_gain, w_branch_sb, w_branch_ret, w_out):
    x = np.asarray(x, np.float32)
    C = _const_tables()
    if "nc" not in _NC_CACHE:
        _NC_CACHE["nc"] = build_program()
    nc = _NC_CACHE["nc"]
    f = lambda a: np.ascontiguousarray(np.asarray(a, np.float32))
    w_in0, wbs0, wbr0, wout0 = f(w_in[0]), f(w_branch_sb[0]), f(w_branch_ret[0]), f(w_out[0])
    gain_rep = np.ascontiguousarray(np.broadcast_to(f(norm_gain[0])[None, :], (128, DM)))
    rgain_rep = np.ascontiguousarray(np.broadcast_to(f(ret_out_gain[0]).reshape(1, DM), (128, DM)))
    qkg = np.ascontiguousarray(np.stack([f(sb_q_gain[0]), f(sb_k_gain[0])], axis=1))
    bmg = np.ascontiguousarray(f(b_merge[0]).reshape(2, 8, 128).transpose(2, 0, 1).reshape(128, 16))
    in_maps = []
    for c in range(8):
        b, p = c // 2, c % 2
        xb = x[b]
        xown = np.ascontiguousarray(xb.reshape(NT, TS, DM)[p::2].reshape(SO, DM))
        blend = np.zeros((128, 2), np.float32)
        blend[:, 0] = 1.0 - p
        blend[:, 1] = float(p)
        in_maps.append({
            "xa": np.ascontiguousarray(xb), "xo": xown, "w_in": w_in0, "wbs": wbs0, "wbr": wbr0, "wout": wout0,
            "gain_rep": gain_rep, "rgain_rep": rgain_rep, "qkg": qkg, "bmg": bmg,
            "cosq": C["cosq"], "sinq": C["sinq"],
            "dmask": C["dmask"], "qdec": C["qdec"], "kdect": C["kdect"], "cmat": C["cmat"],
            "amask": C["amask%d" % p], "blend": blend,
        })
    res = run_bass_kernel_spmd(nc, in_maps, core_ids=list(range(8)))
    _NC_CACHE["last"] = res
    out = np.empty((4, S, DM), np.float32)
    for c in range(8):
        b, p = c // 2, c % 2
        out[b].reshape(NT, TS, DM)[p::2] = np.asarray(res.results[c]["y"], np.float32).reshape(NO, TS, DM)
    return out
```

```python
import contextlib
import numpy as np
import ml_dtypes
import concourse.bass as bass
import concourse.mybir as mybir
from concourse.bass_utils import run_bass_kernel_spmd

F32 = mybir.dt.float32
BF16 = mybir.dt.bfloat16
AF = mybir.ActivationFunctionType
ALU = mybir.AluOpType
AX = mybir.AxisListType


class Tracker:
    ENG = ("pe", "act", "dve", "pool", "sp")

    def __init__(self, nc, n_dma_sems=6):
        self.nc = nc
        self.n_dma = n_dma_sems
        self.stack = contextlib.ExitStack()
        self.streams = {e: [] for e in self.ENG}
        self.count = {}
        self.known = {e: {} for e in self.ENG}
        self.last_write = {}
        self.readers = {}
        self.n_ops = 0

    def __enter__(self):
        nc = self.nc
        self.stack.__enter__()
        self.sem = {}
        for e in self.ENG:
            self.sem[e] = self.stack.enter_context(nc.semaphore("s_" + e))
            self.count[e] = 0
        self.dma_ring = {}
        self.dma_next = {}
        for q in ("sp", "pool", "act"):
            ring = []
            for k in range(self.n_dma):
                name = "d_%s%d" % (q, k)
                self.sem[name] = self.stack.enter_context(nc.semaphore(name))
                self.count[name] = 0
                ring.append(name)
            self.dma_ring[q] = ring
            self.dma_next[q] = 0
        return self

    def __exit__(self, *a):
        return self.stack.__exit__(*a)

    def _deps(self, reads, writes):
        deps = {}

        def add(s, v):
            if v > deps.get(s, 0):
                deps[s] = v

        for b in reads:
            lw = self.last_write.get(b)
            if lw:
                add(*lw)
        for b in writes:
            lw = self.last_write.get(b)
            if lw:
                add(*lw)
            for s, v in self.readers.get(b, {}).items():
                add(s, v)
        return deps

    def _emit_waits(self, e, deps, skip_self=False):
        for s, v in deps.items():
            if skip_self and s == e:
                continue
            if self.known[e].get(s, 0) >= v:
                continue
            self.known[e][s] = v
            sem = self.sem[s]
            self.streams[e].append(("wait", sem, v))

    def _record(self, key, val, reads, writes):
        for b in reads:
            self.readers.setdefault(b, {})[key] = val
        for b in writes:
            self.last_write[b] = (key, val)
            self.readers[b] = {}

    def op(self, e, fn, reads=(), writes=(), inc=True):
        deps = self._deps(reads, writes)
        self._emit_waits(e, deps, skip_self=(e == "pe"))
        if inc:
            self.count[e] += 1
            val = self.count[e]
            self.streams[e].append(("op", fn, self.sem[e], 1))
        else:
            val = self.count[e] + 1
            self.streams[e].append(("op", fn, None, 0))
        self._record(e, val, reads, writes)
        self.n_ops += 1

    def dma(self, q, out, in_, reads=(), writes=(), **kw):
        deps = self._deps(reads, writes)
        name = self.dma_ring[q][self.dma_next[q]]
        self.dma_next[q] = (self.dma_next[q] + 1) % self.n_dma
        deps[name] = max(deps.get(name, 0), self.count[name])
        self._emit_waits(q, deps)
        self.count[name] += 16
        val = self.count[name]
        self.known[q][name] = max(self.known[q].get(name, 0), 0)
        eng = {"sp": self.nc.sync, "pool": self.nc.gpsimd, "act": self.nc.scalar}[q]
        self.streams[q].append(("op", (lambda: eng.dma_start(out=out, in_=in_, **kw)), self.sem[name], 16))
        self._record(name, val, reads, writes)
        self.n_ops += 1

    def barrier(self):
        for e in self.ENG:
            deps = {s: c for s, c in self.count.items() if c > 0 and s != e}
            self._emit_waits(e, deps)

    def finish(self):
        nc = self.nc
        deps = {s: c for s, c in self.count.items() if c > 0 and s != "sp"}
        self._emit_waits("sp", deps)
        streams = self.streams

        def replay(e, eng):
            for item in streams[e]:
                if item[0] == "wait":
                    eng.wait_ge(item[1], item[2])
                else:
                    ins = item[1]()
                    if item[2] is not None:
                        ins.then_inc(item[2], item[3])

        with nc.Block() as block:
            @block.sync
            def _(eng):
                replay("sp", eng)

            @block.scalar
            def _(eng):
                replay("act", eng)

            @block.vector
            def _(eng):
                replay("dve", eng)

            @block.gpsimd
            def _(eng):
                replay("pool", eng)

            @block.tensor
            def _(eng):
                replay("pe", eng)


S = 8192
DM = 1024
TS = 512
NT = S // TS
NO = NT // 2
SO = NO * TS
NCH = DM // 128
EPS = 1e-6
NEG = -30000.0
C_SBQ, C_SBK, C_SBV, C_SBG, C_RQ, C_RK, C_RV, C_RG, C_MSB, C_MRET = [i * 1024 for i in range(10)]
GAMMA = [1.0 - 2.0 ** (-5.0 - h) for h in range(4)]
G64 = [g ** 64 for g in GAMMA]
DEBUG = False


def _mm(T, nc, out, lhsT, rhs, start, stop, reads, writes, inc):
    T.op("pe", lambda: nc.tensor.matmul(out, lhsT=lhsT, rhs=rhs, start=start, stop=stop),
         reads=reads, writes=writes, inc=inc)


def _tr(T, nc, out, in_, ident, reads, writes, inc):
    T.op("pe", lambda: nc.tensor.transpose(out, in_, ident), reads=reads, writes=writes, inc=inc)


def _act(T, nc, out, in_, func, reads, writes, **kw):
    T.op("act", lambda: nc.scalar.activation(out=out, in_=in_, func=func, **kw), reads=reads, writes=writes)


def _ts(T, nc, eng, out, in0, s1, s2, op0, op1, reads, writes):
    e = nc.vector if eng == "dve" else nc.gpsimd
    if op1 is None:
        T.op(eng, lambda: e.tensor_scalar(out=out, in0=in0, scalar1=s1, scalar2=None, op0=op0), reads=reads, writes=writes)
    else:
        T.op(eng, lambda: e.tensor_scalar(out=out, in0=in0, scalar1=s1, scalar2=s2, op0=op0, op1=op1), reads=reads, writes=writes)


def _tt(T, nc, eng, out, in0, in1, op, reads, writes):
    e = nc.vector if eng == "dve" else nc.gpsimd
    T.op(eng, lambda: e.tensor_tensor(out=out, in0=in0, in1=in1, op=op), reads=reads, writes=writes)


def _stt(T, nc, out, in0, scalar, in1, op0, op1, reads, writes):
    T.op("dve", lambda: nc.vector.scalar_tensor_tensor(out=out, in0=in0, scalar=scalar, in1=in1, op0=op0, op1=op1),
         reads=reads, writes=writes)


def _copy(T, nc, eng, out, in_, reads, writes):
    if eng == "act":
        T.op("act", lambda: nc.scalar.copy(out=out, in_=in_), reads=reads, writes=writes)
    else:
        e = nc.vector if eng == "dve" else nc.gpsimd
        T.op(eng, lambda: e.tensor_copy(out=out, in_=in_), reads=reads, writes=writes)


def build_program(phases="ABCD", nt_a=NT, no_b=NO, heads_c=8, no_c=NO, no_d=NO):
    CFG = dict(phases=phases, nt_a=nt_a, no_b=no_b, heads_c=heads_c, no_c=no_c, no_d=no_d)
    nc = bass.Bass("TRN2", target_bir_lowering=False)
    dt = nc.dram_tensor
    xa = dt("xa", [S, DM], F32, kind="ExternalInput").ap()
    xo = dt("xo", [SO, DM], F32, kind="ExternalInput").ap()
    w_in = dt("w_in", [DM, 10 * 1024], F32, kind="ExternalInput").ap()
    wbs = dt("wbs", [DM, DM], F32, kind="ExternalInput").ap()
    wbr = dt("wbr", [DM, DM], F32, kind="ExternalInput").ap()
    wout = dt("wout", [DM, DM], F32, kind="ExternalInput").ap()
    gain_rep = dt("gain_rep", [128, DM], F32, kind="ExternalInput").ap()
    rgain_rep = dt("rgain_rep", [128, DM], F32, kind="ExternalInput").ap()
    qkg = dt("qkg", [128, 2], F32, kind="ExternalInput").ap()
    bmg = dt("bmg", [128, 16], F32, kind="ExternalInput").ap()
    cosq = dt("cosq", [128, S], F32, kind="ExternalInput").ap()
    sinq = dt("sinq", [128, S], F32, kind="ExternalInput").ap()
    dmask_d = dt("dmask", [128, 4 * 128], F32, kind="ExternalInput").ap()
    qdec_d = dt("qdec", [128, 4], F32, kind="ExternalInput").ap()
    kdect_d = dt("kdect", [128, 4], F32, kind="ExternalInput").ap()
    cmat_d = dt("cmat", [128, 4 * 128], BF16, kind="ExternalInput").ap()
    amask_d = dt("amask", [128, 8 * TS], BF16, kind="ExternalInput").ap()
    blend_d = dt("blend", [128, 2], F32, kind="ExternalInput").ap()
    y = dt("y", [SO, DM], F32, kind="ExternalOutput").ap()
    sk = "ExternalOutput" if DEBUG else "Internal"
    kT_d = dt("kT_d", [8, 128, S], BF16, kind=sk).ap()
    v_d = dt("v_d", [S, DM], BF16, kind=sk).ap()
    qT_d = dt("qT_d", [8, 128, SO], BF16, kind=sk).ap()
    ga_d = dt("ga_d", [8, 128, SO], BF16, kind=sk).ap()
    gb_d = dt("gb_d", [8, 128, SO], BF16, kind=sk).ap()
    sa_d = dt("sa_d", [8, 128, SO], BF16, kind=sk).ap()
    sb_d = dt("sb_d", [8, 128, SO], BF16, kind=sk).ap()
    ret_d = dt("ret_d", [S, DM], F32, kind=sk).ap()
    oaT_d = dt("oaT_d", [8, 128, SO], F32, kind=sk).ap()
    if DEBUG:
        dbg_bf = dt("dbg_bf", [128, 8192], BF16, kind="ExternalOutput").ap()
        dbg_f = dt("dbg_f", [128, 4096], F32, kind="ExternalOutput").ap()

    es = contextlib.ExitStack()
    with es:
        def sb(name, shape, dtype):
            return es.enter_context(nc.sbuf_tensor("t_" + name, shape, dtype))

        cmat = sb("cmat", [128, 4 * 128], BF16)
        ident = cmat[:, 0:128]
        onesm = cmat[:, 128:256]
        trineg = cmat[:, 256:384]
        carryneg = cmat[:, 384:512]
        qkg_t = sb("qkg_t", [128, 2], F32)
        qg_s = sb("qg_s", [128, 1], F32)
        bmg_t = sb("bmg_t", [128, 16], F32)
        blend_t = sb("blend_t", [128, 2], F32)
        ps, pb = [], []
        psum_ctr = [0]

        def alloc_psum(stack):
            tag = "abcdefgh"[psum_ctr[0]]
            psum_ctr[0] += 1
            ps[:] = [stack.enter_context(nc.psum_tensor("ps%d%s" % (k, tag), [128, 512], F32)) for k in range(6)]
            pb[:] = [stack.enter_context(nc.psum_tensor("pb%d%s" % (k, tag), [128, 1024], BF16)) for k in range(2)]
        T = es.enter_context(Tracker(nc))

        T.dma("sp", cmat[:], cmat_d, writes=["cmat"])
        T.dma("sp", qkg_t[:], qkg, writes=["qkg"])
        T.dma("sp", bmg_t[:], bmg, writes=["bmg"])
        T.dma("sp", blend_t[:], blend_d, writes=["blend"])
        _ts(T, nc, "dve", qg_s[:], qkg_t[:, 0:1], float(128 ** -0.5), None, ALU.mult, None, ["qkg"], ["qg_s"])

        def load_w(wt, col0, ncols, key, src=w_in):
            for c0 in range(0, ncols, 512):
                T.dma("pool", wt[:, :, c0:c0 + 512],
                      src[:, col0 + c0: col0 + c0 + 512].rearrange("(c p) n -> p c n", p=128),
                      writes=[key])

        def norm_block(xsrc, r0, xt, junk, ss, gain_t, h, hT, col0, pbt=None):
            pbt = pbt if pbt is not None else pb[0]
            T.dma("sp", xt[:], xsrc[r0:r0 + 128, :], writes=[xt.name])
            _act(T, nc, junk[:], xt[:], AF.Square, [xt.name], [junk.name, ss.name], accum_out=ss[:, 0:1])
            _act(T, nc, ss[:, 1:2], ss[:, 0:1], AF.Sqrt, [ss.name], [ss.name], bias=EPS, scale=1.0 / DM)
            T.op("dve", lambda: nc.vector.reciprocal(out=ss[:, 2:3], in_=ss[:, 1:2]), reads=[ss.name], writes=[ss.name])
            _stt(T, nc, h[:], xt[:], ss[:, 2:3], gain_t[:], ALU.mult, ALU.mult, [xt.name, ss.name, "gain"], [h.name])
            for c in range(NCH):
                _tr(T, nc, pbt[:, c * 128:(c + 1) * 128], h[:, c * 128:(c + 1) * 128], ident,
                    [h.name, "cmat"], [pbt.name], inc=(c == NCH - 1))
            _copy(T, nc, "act", hT[:, :, col0:col0 + 128], pbt[:].rearrange("p (c t) -> p c t", c=NCH),
                  [pbt.name], ["hT"])

        def norm_load(xsrc, r0, xt):
            T.dma("pool", xt[:], xsrc[r0:r0 + 128, :], writes=[xt.name])

        def norm_pre(xsrc, r0, xt, junk, ss, gain_t, h, load=True):
            if load:
                T.dma("sp", xt[:], xsrc[r0:r0 + 128, :], writes=[xt.name])
            _act(T, nc, junk[:], xt[:], AF.Square, [xt.name], [junk.name, ss.name], accum_out=ss[:, 0:1])
            _act(T, nc, ss[:, 1:2], ss[:, 0:1], AF.Sqrt, [ss.name], [ss.name], bias=EPS, scale=1.0 / DM)
            T.op("dve", lambda: nc.vector.reciprocal(out=ss[:, 2:3], in_=ss[:, 1:2]), reads=[ss.name], writes=[ss.name])
            _stt(T, nc, h[:], xt[:], ss[:, 2:3], gain_t[:], ALU.mult, ALU.mult, [xt.name, ss.name, "gain"], [h.name])

        def norm_post(h, hT, col0, pbt):
            for c in range(NCH):
                _tr(T, nc, pbt[:, c * 128:(c + 1) * 128], h[:, c * 128:(c + 1) * 128], ident,
                    [h.name, "cmat"], [pbt.name], inc=(c == NCH - 1))
            _copy(T, nc, "act", hT[:, :, col0:col0 + 128], pbt[:].rearrange("p (c t) -> p c t", c=NCH),
                  [pbt.name], ["hT"])

        def qk_proj(W, wkey, wcol, hT, pz):
            for c in range(NCH):
                _mm(T, nc, pz[:], W[:, c, wcol:wcol + 128], hT[:, c, :], c == 0, c == NCH - 1,
                    [wkey, "hT"], [pz.name], inc=(c == NCH - 1))

        def qk_square(pz, sq, zf=None):
            _act(T, nc, sq[:], pz[:], AF.Square, [pz.name], [sq.name])
            if zf is not None:
                _copy(T, nc, "act", zf[:], pz[:], [pz.name], [zf.name])

        def qk_norm(gcol, pz, pm, sq, rs, outt, outkey, do_square=True, zf=None):
            if do_square:
                qk_square(pz, sq)
            _mm(T, nc, pm[:], onesm, sq[:], True, True, ["cmat", sq.name], [pm.name], True)
            _act(T, nc, rs[:], pm[:], AF.Ln, [pm.name], [rs.name], bias=EPS, scale=1.0)
            _act(T, nc, rs[:], rs[:], AF.Exp, [rs.name], [rs.name], scale=-0.5)
            src, skey = (pz[:], pz.name) if zf is None else (zf[:], zf.name)
            _stt(T, nc, outt, src, gcol, rs[:], ALU.mult, ALU.mult, [skey, rs.name, "qkg", "qg_s"], [outkey])

        def qk_head(W, wkey, wcol, hT, gcol, pz, pm, sq, rs, outt, outkey):
            qk_proj(W, wkey, wcol, hT, pz)
            qk_norm(gcol, pz, pm, sq, rs, outt, outkey)

        with contextlib.ExitStack() as pa:
            def sa_(name, shape, dtype):
                return pa.enter_context(nc.sbuf_tensor("t_" + name, shape, dtype))
            alloc_psum(pa)
            Wa = sa_("Wa", [128, NCH, 5 * 1024], BF16)
            gain_t = sa_("gain_t", [128, DM], F32)
            xts = [sa_("xt%d" % k, [128, DM], F32) for k in range(4)]
            junk = sa_("junk", [128, DM], BF16)
            sss = [sa_("ss%d" % k, [128, 4], F32) for k in range(4)]
            hs = [sa_("h%d" % k, [128, DM], BF16) for k in range(4)]
            hT = sa_("hT", [128, NCH, TS], BF16)
            sqs = [sa_("sq%d" % k, [128, TS], BF16) for k in range(3)]
            rss = [sa_("rs%d" % k, [128, TS], F32) for k in range(2)]
            kout = [sa_("kout%d" % k, [128, TS], BF16) for k in range(2)]
            vout = [sa_("vout%d" % k, [128, DM], BF16) for k in range(2)]
            cs = sa_("cs", [128, 2, TS], F32)
            ra = [sa_("ra%d" % k, [128, TS], F32) for k in range(2)]
            rb = [sa_("rb%d" % k, [128, TS], F32) for k in range(2)]
            qrT = [sa_("qrT%d" % k, [128, 2, TS], BF16) for k in range(3)]
            rak = [sa_("rak%d" % k, [128, TS], F32) for k in range(2)]
            rbk = [sa_("rbk%d" % k, [128, TS], F32) for k in range(2)]
            krT = [sa_("krT%d" % k, [128, 2, TS], BF16) for k in range(2)]
            kdt = [sa_("kdt%d" % k, [128, 4, 256], BF16) for k in range(2)]
            rv = sa_("rv", [128, 4, DM], BF16)
            sT = [sa_("sT%d" % k, [128, 4, 128], BF16) for k in range(2)]
            St = sa_("St", [128, 4, 2, 256], F32)
            Sb0 = sa_("Sb0", [128, 4, 2, 256], BF16)
            Sbt = [sa_("Sbt%d" % k, [128, 3, 2, 256], BF16) for k in range(2)]
            dmask = sa_("dmask", [128, 4, 128], F32)
            qdect = sa_("qdect", [128, 4], F32)
            kdect = sa_("kdect", [128, 4], F32)
            rout = [sa_("rout%d" % k, [128, 256], F32) for k in range(2)]
            pb0f, pb1f = pb[0][:].bitcast(F32), pb[1][:].bitcast(F32)
            zfs = [ra[0], ra[1], rb[0]]

            T.dma("sp", gain_t[:], gain_rep, writes=["gain"])
            T.dma("sp", dmask[:].rearrange("p h q -> p (h q)"), dmask_d, writes=["dmask"])
            T.dma("sp", qdect[:], qdec_d, writes=["qdect"])
            T.dma("sp", kdect[:], kdect_d, writes=["kdect"])
            load_w(Wa[:, :, 0:1024], C_SBK, 1024, "Wa")
            load_w(Wa[:, :, 1024:2048], C_SBV, 1024, "Wa")
            load_w(Wa[:, :, 2048:3072], C_RQ, 1024, "Wa")
            load_w(Wa[:, :, 3072:4096], C_RK, 1024, "Wa")
            load_w(Wa[:, :, 4096:5120], C_RV, 1024, "Wa")
            T.op("dve", lambda: nc.vector.memset(St[:].rearrange("p a b c -> p (a b c)"), 0.0), writes=["St"])
            T.op("dve", lambda: nc.vector.memset(Sb0[:].rearrange("p a b c -> p (a b c)"), 0.0), writes=["Sb0"])
            G128 = [g ** 128 for g in GAMMA]

            def ret_proj(hd, t0):
                par = hd % 2
                for which in range(2):
                    wc = (2048 if which == 0 else 3072) + hd * 256
                    p1, p2 = (ps[0], ps[1]) if which == 0 else (ps[2], ps[3])
                    for half, pz in enumerate((p1, p2)):
                        for c in range(NCH):
                            _mm(T, nc, pz[:], Wa[:, c, wc + half * 128: wc + (half + 1) * 128], hT[:, c, :],
                                c == 0, c == NCH - 1, ["Wa", "hT"], [pz.name], inc=(c == NCH - 1))
                    dst = qrT[hd % 3] if which == 0 else krT[par]
                    A_, B_ = (ra, rb) if which == 0 else (rak, rbk)
                    ct, st_ = cs[:, 0, :], cs[:, 1, :]
                    _tt(T, nc, "dve", A_[0][:], p1[:], ct, ALU.mult, [p1.name, "cs"], [A_[0].name])
                    _tt(T, nc, "dve", B_[0][:], p2[:], st_, ALU.mult, [p2.name, "cs"], [B_[0].name])
                    _tt(T, nc, "dve", A_[1][:], p1[:], st_, ALU.mult, [p1.name, "cs"], [A_[1].name])
                    _tt(T, nc, "dve", B_[1][:], p2[:], ct, ALU.mult, [p2.name, "cs"], [B_[1].name])
                    for half in range(2):
                        _tt(T, nc, "pool", A_[half][:], A_[half][:], B_[half][:], ALU.subtract if half == 0 else ALU.add,
                            [A_[half].name, B_[half].name], [A_[half].name])
                        if which == 0:
                            _copy(T, nc, "act", dst[:, half, :], A_[half][:], [A_[half].name], [dst.name])
                        else:
                            T.op("act", (lambda o=dst[:, half, :], i_=A_[half][:]: nc.scalar.mul(out=o, in_=i_, mul=1.0 / 16.0)),
                                 reads=[A_[half].name], writes=[dst.name])

            def ret_small_a(hd, t0):
                par = hd % 2
                kr, qr, kd, st = krT[par], qrT[hd % 3], kdt[par], sT[par]
                for blk in range(4):
                    for half in range(2):
                        _tr(T, nc, pb[1][:, blk * 256 + half * 128:blk * 256 + (half + 1) * 128], kr[:, half, blk * 128:(blk + 1) * 128], ident,
                            [kr.name, "cmat"], [pb[1].name], inc=(blk == 3 and half == 1))
                _ts(T, nc, "dve", kd[:].rearrange("p a b -> p (a b)"), pb[1][:], kdect[:, hd:hd + 1], None, ALU.mult, None,
                    [pb[1].name, "kdect"], [kd.name])
                for blk in range(4):
                    kv = ps[4 + blk % 2]
                    for half in range(2):
                        _mm(T, nc, kv[:, half * 256:(half + 1) * 256], kd[:, blk, half * 128:(half + 1) * 128],
                            rv[:, blk, hd * 256:(hd + 1) * 256], True, True, [kd.name, "rv"], [kv.name], inc=(half == 1))
                    sflat = St[:, hd, :, :].rearrange("p a b -> p (a b)")
                    _stt(T, nc, sflat, sflat, float(G128[hd]), kv[:], ALU.mult, ALU.add, ["St", kv.name], ["St"])
                    if blk < 3:
                        _copy(T, nc, "act", Sbt[par][:, blk, :, :].rearrange("p a b -> p (a b)"), sflat, ["St"], [Sbt[par].name + str(blk)])
                for blk in range(4):
                    b0 = blk * 128
                    for half in range(2):
                        _mm(T, nc, pb0f[:, blk * 128:(blk + 1) * 128], kr[:, half, b0:b0 + 128], qr[:, half, b0:b0 + 128],
                            half == 0, half == 1, [kr.name, qr.name], [pb[0].name], inc=(half == 1))
                for blk in range(4):
                    _tt(T, nc, "dve", st[:, blk, :], pb0f[:, blk * 128:(blk + 1) * 128], dmask[:, hd, :], ALU.mult,
                        [pb[0].name, "dmask"], [st.name])

            def ret_small_b(hd, t0):
                par = hd % 2
                qd, st = qrT[hd % 3], sT[par]
                for blk in range(4):
                    b0 = blk * 128
                    pot = ps[4 + blk % 2]
                    po = pot[:]
                    _mm(T, nc, po[:, 0:256], st[:, blk, :], rv[:, blk, hd * 256:(hd + 1) * 256], True, False,
                        [st.name, "rv"], [pot.name], inc=True)
                    for half in range(2):
                        sb_ap = Sb0[:, hd, half, :] if blk == 0 else Sbt[par][:, blk - 1, half, :]
                        skey = "Sb0" if blk == 0 else Sbt[par].name + str(blk - 1)
                        _mm(T, nc, po[:, 0:256], qd[:, half, b0:b0 + 128], sb_ap, False, half == 1,
                            [qd.name, skey], [pot.name], inc=(half == 1))
                    ro = rout[blk % 2]
                    if blk % 2 == 0:
                        T.op("act", (lambda o=ro[:], i_=po[:, 0:256], sc=qdect[:, hd:hd + 1]: nc.scalar.activation(out=o, in_=i_, func=AF.Copy, scale=sc)),
                             reads=[pot.name, "qdect"], writes=[ro.name])
                    else:
                        _ts(T, nc, "dve", ro[:], po[:, 0:256], qdect[:, hd:hd + 1], None, ALU.mult, None, [pot.name, "qdect"], [ro.name])
                    T.dma("sp", ret_d[t0 + b0:t0 + b0 + 128, hd * 256:(hd + 1) * 256], ro[:], reads=[ro.name], writes=["ret_d"])
                _copy(T, nc, "act", Sb0[:, hd, :, :].rearrange("p a b -> p (a b)"),
                      St[:, hd, :, :].rearrange("p a b -> p (a b)"), ["St"], ["Sb0"])

            for t in range(nt_a if "A" in phases else 0):
                t0 = t * TS
                for k, tab in enumerate((cosq, sinq)):
                    T.dma("sp", cs[:, k, :], tab[:, t0:t0 + TS], writes=["cs"])
                n_a = nt_a if "A" in phases else 0
                if t == 0:
                    for blk in range(4):
                        norm_pre(xa, blk * 128, xts[blk], junk, sss[blk], gain_t, hs[blk])
                for blk in range(4):
                    norm_post(hs[blk], hT, blk * 128, pb[blk % 2])
                if t + 1 < n_a:
                    for blk in range(4):
                        norm_load(xa, t0 + TS + blk * 128, xts[blk])
                pzk = [ps[0], ps[1], ps[4]]
                for h0 in range(2):
                    qk_proj(Wa, "Wa", h0 * 128, hT, pzk[h0])
                    qk_square(pzk[h0], sqs[h0], zfs[h0])
                for hd in range(8):
                    if hd + 2 < 8:
                        qk_proj(Wa, "Wa", (hd + 2) * 128, hT, pzk[(hd + 2) % 3])
                        qk_square(pzk[(hd + 2) % 3], sqs[(hd + 2) % 3], zfs[(hd + 2) % 3])
                    ko = kout[hd % 2]
                    qk_norm(qkg_t[:, 1:2], pzk[hd % 3], ps[2 + hd % 2], sqs[hd % 3], rss[hd % 2], ko[:], ko.name, do_square=False,
                            zf=zfs[hd % 3])
                    T.dma("sp", kT_d[hd, :, t0:t0 + TS], ko[:], reads=[ko.name], writes=["kT_d"])
                nb = 0
                for blk in range(4):
                    vo = vout[blk % 2]
                    for g in range(2):
                        pz = ps[(4 + nb) % 6]
                        nb += 1
                        for c in range(NCH):
                            _mm(T, nc, pz[:], hT[:, c, blk * 128:(blk + 1) * 128], Wa[:, c, 1024 + g * 512:1024 + (g + 1) * 512],
                                c == 0, c == NCH - 1, ["hT", "Wa"], [pz.name], inc=(c == NCH - 1))
                        _copy(T, nc, "act" if g == 0 else "dve", vo[:, g * 512:(g + 1) * 512], pz[:], [pz.name], [vo.name])
                    T.dma("sp", v_d[t0 + blk * 128:t0 + (blk + 1) * 128, :], vo[:], reads=[vo.name], writes=["v_d"])
                    for g in range(2):
                        pz = ps[(4 + nb) % 6]
                        nb += 1
                        for c in range(NCH):
                            _mm(T, nc, pz[:], hT[:, c, blk * 128:(blk + 1) * 128], Wa[:, c, 4096 + g * 512:4096 + (g + 1) * 512],
                                c == 0, c == NCH - 1, ["hT", "Wa"], [pz.name], inc=(c == NCH - 1))
                        _copy(T, nc, "act" if g == 0 else "dve", rv[:, blk, g * 512:(g + 1) * 512], pz[:], [pz.name], ["rv"])
                ret_proj(0, t0)
                for hd in range(4):
                    if hd + 1 < 4:
                        ret_proj(hd + 1, t0)
                    ret_small_a(hd, t0)
                    if t + 1 < n_a:
                        norm_pre(xa, 0, xts[hd], junk, sss[hd], gain_t, hs[hd], load=False)
                    if hd > 0:
                        ret_small_b(hd - 1, t0)
                ret_small_b(3, t0)
            T.barrier()
        build_rest(nc, T, locals())
        T.finish()
    return nc


def build_rest(nc, T, L):
    CFG = L["CFG"]
    phases = CFG["phases"]
    ps, pb = L["ps"], L["pb"]
    ident, onesm, trineg, carryneg = L["ident"], L["onesm"], L["trineg"], L["carryneg"]
    qg_s, bmg_t, blend_t = L["qg_s"], L["bmg_t"], L["blend_t"]
    xo, y = L["xo"], L["y"]
    kT_d, v_d, qT_d, ga_d, gb_d, sa_d, sb_d, ret_d, oaT_d = (L[k] for k in
        ("kT_d", "v_d", "qT_d", "ga_d", "gb_d", "sa_d", "sb_d", "ret_d", "oaT_d"))
    load_w, norm_block, qk_head = L["load_w"], L["norm_block"], L["qk_head"]
    wbs, wbr, wout = L["wbs"], L["wbr"], L["wout"]

    with contextlib.ExitStack() as pbs:
        def sa_(name, shape, dtype):
            return pbs.enter_context(nc.sbuf_tensor("t_" + name, shape, dtype))
        L["alloc_psum"](pbs)
        Wb = sa_("Wb", [128, NCH, 5 * 1024], BF16)
        gain_t = sa_("gain_tb", [128, DM], F32)
        xts = [sa_("xtb%d" % k, [128, DM], F32) for k in range(2)]
        junk = sa_("junkb", [128, DM], BF16)
        sss = [sa_("ssb%d" % k, [128, 4], F32) for k in range(2)]
        hs = [sa_("hb%d" % k, [128, DM], BF16) for k in range(4)]
        hT = sa_("hTb", [128, NCH, TS], BF16)
        sqs = [sa_("sqb%d" % k, [128, TS], BF16) for k in range(3)]
        rss = [sa_("rsb%d" % k, [128, TS], F32) for k in range(2)]
        qout = [sa_("qout%d" % k, [128, TS], BF16) for k in range(2)]
        zfs = [sa_("zfb%d" % k, [128, TS], F32) for k in range(3)]
        gout = [sa_("gout%d" % k, [128, TS], BF16) for k in range(4)]
        T.dma("sp", gain_t[:], L["gain_rep"], writes=["gain"])
        for k, col in enumerate((C_SBQ, C_SBG, C_RG, C_MSB, C_MRET)):
            load_w(Wb[:, :, k * 1024:(k + 1) * 1024], col, 1024, "Wb")
        norm_pre, norm_post, qk_proj, qk_square, qk_norm = (L[k] for k in ("norm_pre", "norm_post", "qk_proj", "qk_square", "qk_norm"))
        n_b = CFG["no_b"] if "B" in phases else 0
        for i in range(n_b):
            t0 = i * TS
            if i == 0:
                for blk in range(4):
                    norm_pre(xo, blk * 128, xts[blk % 2], junk, sss[blk % 2], gain_t, hs[blk])
            for blk in range(4):
                norm_post(hs[blk], hT, blk * 128, pb[blk % 2])
            pzq = [ps[0], ps[1], ps[2]]
            for h0 in range(2):
                qk_proj(Wb, "Wb", h0 * 128, hT, pzq[h0])
                qk_square(pzq[h0], sqs[h0], zfs[h0])
            for hd in range(8):
                if hd + 2 < 8:
                    qk_proj(Wb, "Wb", (hd + 2) * 128, hT, pzq[(hd + 2) % 3])
                    qk_square(pzq[(hd + 2) % 3], sqs[(hd + 2) % 3], zfs[(hd + 2) % 3])
                qo = qout[hd % 2]
                qk_norm(qg_s[:, 0:1], pzq[hd % 3], ps[3 + hd % 2], sqs[hd % 3], rss[hd % 2], qo[:], qo.name, do_square=False,
                        zf=zfs[hd % 3])
                T.dma("sp", qT_d[hd, :, t0:t0 + TS], qo[:], reads=[qo.name], writes=["qT_d"])
            if i + 1 < n_b:
                for blk in range(4):
                    norm_pre(xo, t0 + TS + blk * 128, xts[blk % 2], junk, sss[blk % 2], gain_t, hs[blk])
            n = 0
            for k, (dst, func) in enumerate(((ga_d, AF.Silu), (gb_d, AF.Silu), (sa_d, AF.Sigmoid), (sb_d, AF.Sigmoid))):
                for c in range(NCH):
                    pz = ps[(5 + n) % 6]
                    go = gout[n % 4]
                    n += 1
                    wc = (k + 1) * 1024 + c * 128
                    for cc in range(NCH):
                        _mm(T, nc, pz[:], Wb[:, cc, wc:wc + 128], hT[:, cc, :], cc == 0, cc == NCH - 1,
                            ["Wb", "hT"], [pz.name], inc=(cc == NCH - 1))
                    if func == AF.Silu:
                        _act(T, nc, go[:], pz[:], func, [pz.name], [go.name])
                    else:
                        bcol = (k - 2) * 8 + c
                        _act(T, nc, go[:], pz[:], func, [pz.name, "bmg"], [go.name], bias=bmg_t[:, bcol:bcol + 1], scale=1.0)
                    T.dma("sp", dst[c, :, t0:t0 + TS], go[:], reads=[go.name], writes=["gates_d"])
        T.barrier()

    with contextlib.ExitStack() as pcs:
        def sa_(name, shape, dtype):
            return pcs.enter_context(nc.sbuf_tensor("t_" + name, shape, dtype))
        KT = [sa_("KT%d" % k, [128, S], BF16) for k in range(2)]
        VV = [sa_("VV%d" % k, [128, S // 128, 128], BF16) for k in range(2)]
        QT = [sa_("QT%d" % k, [128, SO], BF16) for k in range(2)]
        amask = sa_("amask", [128, 8, TS], BF16)
        E = [sa_("E%d" % k, [128, 2, TS], F32) for k in range(2)]
        Lp = [sa_("Lp%d" % k, [128, 2, TS], BF16) for k in range(2)]
        G = [sa_("G%d" % k, [128, 2, TS], F32) for k in range(2)]
        W = [sa_("W%d" % k, [128, 2, TS], BF16) for k in range(2)]
        osb = [sa_("osb%d" % k, [128, TS], F32) for k in range(2)]
        T.dma("sp", amask[:].rearrange("p r q -> p (r q)"), L["amask_d"], writes=["amask"])
        Zall = pcs.enter_context(nc.psum_tensor("Zall", [128, 2, 2, TS], F32))
        Call = pcs.enter_context(nc.psum_tensor("Call", [128, 2, TS], F32))
        Oall = pcs.enter_context(nc.psum_tensor("Oall", [128, 2, TS], F32))
        seqs = ([0, 3, 4, 7], [1, 2, 5, 6])
        steps = [[(i, kb) for i in seq for kb in range(8 * i + 7, -1, -1)] for seq in seqs]
        NS = len(steps[0])
        assert NS == len(steps[1])
        if CFG["no_c"] < NO:
            seqs = ([0], [0]) if CFG["no_c"] == 1 else seqs
            steps = [[(i, kb) for i in seq for kb in range(8 * i + 7, -1, -1)] for seq in seqs]
            NS = len(steps[0])
        nosb = [0]
        for hd in range(CFG["heads_c"] if "C" in phases else 0):
            sl = hd % 2
            kt, vv, qt = KT[sl], VV[sl], QT[sl]
            for part in range(4):
                T.dma("sp", kt[:, part * 2048:(part + 1) * 2048], kT_d[hd, :, part * 2048:(part + 1) * 2048],
                      reads=["kT_d"], writes=[kt.name])
            for part in range(4):
                T.dma("sp", vv[:, part * 16:(part + 1) * 16, :],
                      v_d[part * 2048:(part + 1) * 2048, hd * 128:(hd + 1) * 128].rearrange("(b p) d -> p b d", p=128),
                      reads=["v_d"], writes=[vv.name])
            T.dma("sp", qt[:], qT_d[hd, :, :], reads=["qT_d"], writes=[qt.name])

            def pe1(st, t):
                i, kb = steps[st][t]
                par = t % 2
                Z = Zall[:, par, st, :]
                masked = kb >= 8 * i
                _mm(T, nc, Z, kt[:, kb * 128:(kb + 1) * 128], qt[:, i * TS:(i + 1) * TS], True, not masked,
                    [kt.name, qt.name], ["Z%d" % par], inc=(not masked))
                if masked:
                    _mm(T, nc, Z, ident, amask[:, kb - 8 * i, :], False, True, ["cmat", "amask"], ["Z%d" % par], True)

            def a1(t):
                par = t % 2
                _act(T, nc, E[par][:], Zall[:, par, :, :], AF.Exp, ["Z%d" % par], [E[par].name])

            def a2(t):
                par = t % 2
                _act(T, nc, Lp[par][:], E[par][:], AF.Ln, [E[par].name], [Lp[par].name], bias=1.0, scale=1.0)

            def pe2(st, t):
                i, kb = steps[st][t]
                l_ = Lp[t % 2]
                fst = (kb == 8 * i + 7)
                T.op("pe", (lambda o=Call[:, st, :], r_=l_[:, st, :], f_=fst: nc.tensor.matmul(o, lhsT=trineg, rhs=r_, start=f_, stop=True, skip_group_check=(not f_))),
                     reads=["cmat", l_.name], writes=["Call"], inc=True)

            def a3(t):
                par = t % 2
                _act(T, nc, G[par][:], Call[:], AF.Exp, ["Call"], [G[par].name])

            def pe3(st, t):
                l_ = Lp[t % 2]
                T.op("pe", (lambda o=Call[:, st, :], r_=l_[:, st, :]: nc.tensor.matmul(o, lhsT=carryneg, rhs=r_, start=False, stop=True, skip_group_check=True)),
                     reads=["cmat", l_.name], writes=["Call"], inc=True)

            def v1(t):
                par = t % 2
                _tt(T, nc, "dve", W[par][:], E[par][:], G[par][:], ALU.mult, [E[par].name, G[par].name], [W[par].name])

            def pe4(st, t):
                i, kb = steps[st][t]
                w_ = W[t % 2]
                _mm(T, nc, Oall[:, st, :], vv[:, kb, :], w_[:, st, :], kb == 8 * i + 7, kb == 0, [vv.name, w_.name], ["O%d" % st], True)
                if kb == 0:
                    ob = osb[nosb[0] % 2]
                    nosb[0] += 1
                    _copy(T, nc, "dve", ob[:], Oall[:, st, :], ["O%d" % st], [ob.name])
                    T.dma("sp", oaT_d[hd, :, i * TS:(i + 1) * TS], ob[:], reads=[ob.name], writes=["oaT_d"])

            for st in range(2):
                pe1(st, 0)
            if NS > 1:
                for st in range(2):
                    pe1(st, 1)
            a1(0)
            a2(0)
            for t in range(NS):
                if t + 1 < NS:
                    a1(t + 1)
                for st in range(2):
                    pe2(st, t)
                if t + 2 < NS:
                    for st in range(2):
                        pe1(st, t + 2)
                if t > 0:
                    for st in range(2):
                        pe4(st, t - 1)
                a3(t)
                for st in range(2):
                    pe3(st, t)
                v1(t)
                if t + 1 < NS:
                    a2(t + 1)
            for st in range(2):
                pe4(st, NS - 1)
        T.barrier()

    with contextlib.ExitStack() as pds:
        def sa_(name, shape, dtype):
            return pds.enter_context(nc.sbuf_tensor("t_" + name, shape, dtype))
        L["alloc_psum"](pds)
        Wd = sa_("Wd", [128, NCH, 3 * 1024], BF16)
        rgain_t = sa_("rgain_t", [128, DM], F32)
        rA = [sa_("rA%d" % k, [128, DM], F32) for k in range(2)]
        rB = [sa_("rB%d" % k, [128, DM], F32) for k in range(2)]
        rt = sa_("rt", [128, DM], F32)
        junk = sa_("junkd", [128, 256], F32)
        ss = sa_("ssd", [128, 12], F32)
        rn = sa_("rn", [128, DM], BF16)
        RNT = sa_("RNT", [128, NCH, TS], BF16)
        oa = [sa_("oa%d" % k, [128, TS], F32) for k in range(2)]
        ga = [sa_("ga%d" % k, [128, TS], BF16) for k in range(2)]
        gb = [sa_("gb%d" % k, [128, TS], BF16) for k in range(2)]
        sga = [sa_("sga%d" % k, [128, TS], BF16) for k in range(2)]
        sgb = [sa_("sgb%d" % k, [128, TS], BF16) for k in range(2)]
        OAG = sa_("OAG", [128, NCH, TS], BF16)
        OBG = sa_("OBG", [128, NCH, TS], BF16)
        MG = sa_("MG", [128, NCH, TS], BF16)
        t1 = sa_("t1", [128, TS], F32)
        t2 = sa_("t2", [128, TS], F32)
        xr = [sa_("xr%d" % k, [128, DM], F32) for k in range(2)]
        yt = [sa_("yt%d" % k, [128, DM], F32) for k in range(2)]
        T.dma("sp", rgain_t[:], L["rgain_rep"], writes=["rgain"])
        load_w(Wd[:, :, 0:1024], 0, 1024, "Wd", src=wbs)
        load_w(Wd[:, :, 1024:2048], 0, 1024, "Wd", src=wbr)
        load_w(Wd[:, :, 2048:3072], 0, 1024, "Wd", src=wout)
        n = 0
        for i in range(CFG["no_d"] if "D" in phases else 0):
            t0 = i * TS
            for blk in range(4):
                a_, b_ = rA[blk % 2], rB[blk % 2]
                T.dma("pool", a_[:], ret_d[(2 * i) * TS + blk * 128:(2 * i) * TS + (blk + 1) * 128, :], reads=["ret_d"], writes=[a_.name])
                T.dma("pool", b_[:], ret_d[(2 * i + 1) * TS + blk * 128:(2 * i + 1) * TS + (blk + 1) * 128, :], reads=["ret_d"], writes=[b_.name])
                _ts(T, nc, "dve", rt[:], a_[:], blend_t[:, 0:1], None, ALU.mult, None, [a_.name, "blend"], ["rt"])
                _stt(T, nc, rt[:], b_[:], blend_t[:, 1:2], rt[:], ALU.mult, ALU.add, [b_.name, "blend", "rt"], ["rt"])
                for hd in range(4):
                    _act(T, nc, junk[:], rt[:, hd * 256:(hd + 1) * 256], AF.Square, ["rt"], ["junkd", "ssd"],
                         accum_out=ss[:, hd:hd + 1])
                _act(T, nc, ss[:, 4:8], ss[:, 0:4], AF.Sqrt, ["ssd"], ["ssd"], bias=EPS, scale=1.0 / 256)
                T.op("dve", lambda: nc.vector.reciprocal(out=ss[:, 8:12], in_=ss[:, 4:8]), reads=["ssd"], writes=["ssd"])
                for hd in range(4):
                    _stt(T, nc, rn[:, hd * 256:(hd + 1) * 256], rt[:, hd * 256:(hd + 1) * 256], ss[:, 8 + hd:9 + hd],
                         rgain_t[:, hd * 256:(hd + 1) * 256], ALU.mult, ALU.mult, ["rt", "ssd", "rgain"], ["rn"])
                for c in range(NCH):
                    _tr(T, nc, pb[0][:, c * 128:(c + 1) * 128], rn[:, c * 128:(c + 1) * 128], ident,
                        ["rn", "cmat"], ["pb0"], inc=(c == NCH - 1))
                _copy(T, nc, "act", RNT[:, :, blk * 128:(blk + 1) * 128], pb[0][:].rearrange("p (c t) -> p c t", c=NCH),
                      ["pb0"], ["RNT"])
            for c in range(NCH):
                o_, ga_, gb_ = oa[c % 2], ga[c % 2], gb[c % 2]
                T.dma("pool", o_[:], oaT_d[c, :, t0:t0 + TS], reads=["oaT_d"], writes=[o_.name])
                T.dma("pool", ga_[:], ga_d[c, :, t0:t0 + TS], reads=["gates_d"], writes=[ga_.name])
                T.dma("pool", gb_[:], gb_d[c, :, t0:t0 + TS], reads=["gates_d"], writes=[gb_.name])
                _tt(T, nc, "dve", OAG[:, c, :], o_[:], ga_[:], ALU.mult, [o_.name, ga_.name], ["OAG"])
                _tt(T, nc, "dve", OBG[:, c, :], RNT[:, c, :], gb_[:], ALU.mult, ["RNT", gb_.name], ["OBG"])
            for oc in range(NCH):
                sa_t, sb_t = sga[oc % 2], sgb[oc % 2]
                T.dma("pool", sa_t[:], sa_d[oc, :, t0:t0 + TS], reads=["gates_d"], writes=[sa_t.name])
                T.dma("pool", sb_t[:], sb_d[oc, :, t0:t0 + TS], reads=["gates_d"], writes=[sb_t.name])
                for c in range(NCH):
                    _mm(T, nc, ps[0][:], Wd[:, c, oc * 128:(oc + 1) * 128], OAG[:, c, :], c == 0, c == NCH - 1,
                        ["Wd", "OAG"], ["ps0"], inc=(c == NCH - 1))
                for c in range(NCH):
                    _mm(T, nc, ps[1][:], Wd[:, c, 1024 + oc * 128:1024 + (oc + 1) * 128], OBG[:, c, :], c == 0, c == NCH - 1,
                        ["Wd", "OBG"], ["ps1"], inc=(c == NCH - 1))
                _tt(T, nc, "dve", t1[:], ps[0][:], sa_t[:], ALU.mult, ["ps0", sa_t.name], ["t1"])
                _tt(T, nc, "dve", t2[:], ps[1][:], sb_t[:], ALU.mult, ["ps1", sb_t.name], ["t2"])
                _tt(T, nc, "dve", MG[:, oc, :], t1[:], t2[:], ALU.add, ["t1", "t2"], ["MG"])
            for blk in range(4):
                x_, y_ = xr[blk % 2], yt[blk % 2]
                T.dma("pool", x_[:], xo[t0 + blk * 128:t0 + (blk + 1) * 128, :], writes=[x_.name])
                for g in range(2):
                    pz = ps[2 + g]
                    for oc in range(NCH):
                        _mm(T, nc, pz[:], MG[:, oc, blk * 128:(blk + 1) * 128], Wd[:, oc, 2048 + g * 512:2048 + (g + 1) * 512],
                            oc == 0, oc == NCH - 1, ["MG", "Wd"], [pz.name], inc=(oc == NCH - 1))
                    _tt(T, nc, "dve", y_[:, g * 512:(g + 1) * 512], pz[:], x_[:, g * 512:(g + 1) * 512], ALU.add,
                        [pz.name, x_.name], [y_.name])
                T.dma("sp", y[t0 + blk * 128:t0 + (blk + 1) * 128, :], y_[:], reads=[y_.name], writes=["y"])


_CONST_CACHE = {}


def _const_tables():
    if _CONST_CACHE:
        return _CONST_CACHE
    bf = ml_dtypes.bfloat16
    d = 256
    inv_freq = (np.float32(10000.0) ** (-np.arange(0, d, 2, dtype=np.float32) / np.float32(d))).astype(np.float32)
    pos = np.arange(S, dtype=np.float32)
    ang = (pos[None, :] * inv_freq[:, None]).astype(np.float32)
    c64, s64 = np.cos(ang.astype(np.float64)), np.sin(ang.astype(np.float64))
    _CONST_CACHE["cosq"] = c64.astype(np.float32)
    _CONST_CACHE["sinq"] = s64.astype(np.float32)
    lg = np.log1p(-np.exp2(-5.0 - np.arange(4, dtype=np.float64)))
    idx = np.arange(128)
    same = (idx[:, None] // 64) == (idx[None, :] // 64)
    k_first = (idx[:, None] // 64) < (idx[None, :] // 64)
    dm = np.zeros((128, 4, 128), np.float64)
    for hh in range(4):
        dm[:, hh, :] = np.where(same | k_first, np.exp(lg[hh] * (np.abs(idx[:, None] - idx[None, :]) - (idx[None, :] + 1.0))), 0.0)
    _CONST_CACHE["dmask"] = dm.reshape(128, 512).astype(np.float32)
    qd = np.zeros((128, 4), np.float64)
    for hh in range(4):
        qd[:, hh] = np.exp(lg[hh] * (idx + 1.0))
    _CONST_CACHE["qdec"] = qd.astype(np.float32)
    kd = np.zeros((128, 4), np.float64)
    for hh in range(4):
        kd[:, hh] = np.exp(lg[hh] * (127.0 - idx))
    _CONST_CACHE["kdect"] = kd.astype(np.float32)
    cm = np.zeros((128, 4, 128), np.float32)
    cm[:, 0, :] = np.eye(128)
    cm[:, 1, :] = 1.0 / 128.0
    cm[:, 2, :] = np.where(idx[:, None] >= idx[None, :], -1.0, 0.0)
    cm[:, 3, :] = np.where(idx[:, None] < idx[None, :], -1.0, 0.0)
    _CONST_CACHE["cmat"] = cm.reshape(128, 512).astype(bf)
    q = np.arange(TS)
    diag = np.zeros((4, 128, TS), np.float32)
    for r in range(4):
        diag[r] = np.where((128 * r + idx[:, None]) < q[None, :], 0.0, NEG)
    am0 = np.full((128, 8, TS), NEG, np.float32)
    am1 = np.zeros((128, 8, TS), np.float32)
    for r in range(4):
        am0[:, r, :] = diag[r]
        am1[:, 4 + r, :] = diag[r]
    _CONST_CACHE["amask0"] = am0.reshape(128, 8 * TS).astype(bf)
    _CONST_CACHE["amask1"] = am1.reshape(128, 8 * TS).astype(bf)
    return _CONST_CACHE


_NC_CACHE = {}


def kernel(x, norm_gain, w_in, b_merge, sb_q_gain, sb_k_gain, ret_out_gain, w_branch_sb, w_branch_ret, w_out):
    x = np.asarray(x, np.float32)
    C = _const_tables()
    if "nc" not in _NC_CACHE:
        _NC_CACHE["nc"] = build_program()
    nc = _NC_CACHE["nc"]
    f = lambda a: np.ascontiguousarray(np.asarray(a, np.float32))
    w_in0, wbs0, wbr0, wout0 = f(w_in[0]), f(w_branch_sb[0]), f(w_branch_ret[0]), f(w_out[0])
    gain_rep = np.ascontiguousarray(np.broadcast_to(f(norm_gain[0])[None, :], (128, DM)))
    rgain_rep = np.ascontiguousarray(np.broadcast_to(f(ret_out_gain[0]).reshape(1, DM), (128, DM)))
    qkg = np.ascontiguousarray(np.stack([f(sb_q_gain[0]), f(sb_k_gain[0])], axis=1))
    bmg = np.ascontiguousarray(f(b_merge[0]).reshape(2, 8, 128).transpose(2, 0, 1).reshape(128, 16))
    in_maps = []
    for c in range(8):
        b, p = c // 2, c % 2
        xb = x[b]
        xown = np.ascontiguousarray(xb.reshape(NT, TS, DM)[p::2].reshape(SO, DM))
        blend = np.zeros((128, 2), np.float32)
        blend[:, 0] = 1.0 - p
        blend[:, 1] = float(p)
        in_maps.append({
            "xa": np.ascontiguousarray(xb), "xo": xown, "w_in": w_in0, "wbs": wbs0, "wbr": wbr0, "wout": wout0,
            "gain_rep": gain_rep, "rgain_rep": rgain_rep, "qkg": qkg, "bmg": bmg,
            "cosq": C["cosq"], "sinq": C["sinq"],
            "dmask": C["dmask"], "qdec": C["qdec"], "kdect": C["kdect"], "cmat": C["cmat"],
            "amask": C["amask%d" % p], "blend": blend,
        })
    res = run_bass_kernel_spmd(nc, in_maps, core_ids=list(range(8)))
    _NC_CACHE["last"] = res
    out = np.empty((4, S, DM), np.float32)
    for c in range(8):
        b, p = c // 2, c % 2
        out[b].reshape(NT, TS, DM)[p::2] = np.asarray(res.results[c]["y"], np.float32).reshape(NO, TS, DM)
    return out
```

```python
import contextlib
import numpy as np
import ml_dtypes
import concourse.bass as bass
import concourse.mybir as mybir
from concourse.bass_utils import run_bass_kernel_spmd

F32 = mybir.dt.float32
BF16 = mybir.dt.bfloat16
AF = mybir.ActivationFunctionType
ALU = mybir.AluOpType
AX = mybir.AxisListType


class Tracker:
    ENG = ("pe", "act", "dve", "pool", "sp")

    def __init__(self, nc, n_dma_sems=6):
        self.nc = nc
        self.n_dma = n_dma_sems
        self.stack = contextlib.ExitStack()
        self.streams = {e: [] for e in self.ENG}
        self.count = {}
        self.known = {e: {} for e in self.ENG}
        self.last_write = {}
        self.readers = {}
        self.n_ops = 0

    def __enter__(self):
        nc = self.nc
        self.stack.__enter__()
        self.sem = {}
        for e in self.ENG:
            self.sem[e] = self.stack.enter_context(nc.semaphore("s_" + e))
            self.count[e] = 0
        self.dma_ring = {}
        self.dma_next = {}
        for q in ("sp", "pool", "act"):
            ring = []
            for k in range(self.n_dma):
                name = "d_%s%d" % (q, k)
                self.sem[name] = self.stack.enter_context(nc.semaphore(name))
                self.count[name] = 0
                ring.append(name)
            self.dma_ring[q] = ring
            self.dma_next[q] = 0
        return self

    def __exit__(self, *a):
        return self.stack.__exit__(*a)

    def _deps(self, reads, writes):
        deps = {}

        def add(s, v):
            if v > deps.get(s, 0):
                deps[s] = v

        for b in reads:
            lw = self.last_write.get(b)
            if lw:
                add(*lw)
        for b in writes:
            lw = self.last_write.get(b)
            if lw:
                add(*lw)
            for s, v in self.readers.get(b, {}).items():
                add(s, v)
        return deps

    def _emit_waits(self, e, deps, skip_self=False):
        for s, v in deps.items():
            if skip_self and s == e:
                continue
            if self.known[e].get(s, 0) >= v:
                continue
            self.known[e][s] = v
            sem = self.sem[s]
            self.streams[e].append(("wait", sem, v))

    def _record(self, key, val, reads, writes):
        for b in reads:
            self.readers.setdefault(b, {})[key] = val
        for b in writes:
            self.last_write[b] = (key, val)
            self.readers[b] = {}

    def op(self, e, fn, reads=(), writes=(), inc=True):
        deps = self._deps(reads, writes)
        self._emit_waits(e, deps, skip_self=(e == "pe"))
        if inc:
            self.count[e] += 1
            val = self.count[e]
            self.streams[e].append(("op", fn, self.sem[e], 1))
        else:
            val = self.count[e] + 1
            self.streams[e].append(("op", fn, None, 0))
        self._record(e, val, reads, writes)
        self.n_ops += 1

    def dma(self, q, out, in_, reads=(), writes=(), **kw):
        deps = self._deps(reads, writes)
        name = self.dma_ring[q][self.dma_next[q]]
        self.dma_next[q] = (self.dma_next[q] + 1) % self.n_dma
        deps[name] = max(deps.get(name, 0), self.count[name])
        self._emit_waits(q, deps)
        self.count[name] += 16
        val = self.count[name]
        self.known[q][name] = max(self.known[q].get(name, 0), 0)
        eng = {"sp": self.nc.sync, "pool": self.nc.gpsimd, "act": self.nc.scalar}[q]
        self.streams[q].append(("op", (lambda: eng.dma_start(out=out, in_=in_, **kw)), self.sem[name], 16))
        self._record(name, val, reads, writes)
        self.n_ops += 1

    def barrier(self):
        for e in self.ENG:
            deps = {s: c for s, c in self.count.items() if c > 0 and s != e}
            self._emit_waits(e, deps)

    def finish(self):
        nc = self.nc
        deps = {s: c for s, c in self.count.items() if c > 0 and s != "sp"}
        self._emit_waits("sp", deps)
        streams = self.streams

        def replay(e, eng):
            for item in streams[e]:
                if item[0] == "wait":
                    eng.wait_ge(item[1], item[2])
                else:
                    ins = item[1]()
                    if item[2] is not None:
                        ins.then_inc(item[2], item[3])

        with nc.Block() as block:
            @block.sync
            def _(eng):
                replay("sp", eng)

            @block.scalar
            def _(eng):
                replay("act", eng)

            @block.vector
            def _(eng):
                replay("dve", eng)

            @block.gpsimd
            def _(eng):
                replay("pool", eng)

            @block.tensor
            def _(eng):
                replay("pe", eng)


S = 8192
DM = 1024
TS = 512
NT = S // TS
NO = NT // 2
SO = NO * TS
NCH = DM // 128
EPS = 1e-6
NEG = -30000.0
C_SBQ, C_SBK, C_SBV, C_SBG, C_RQ, C_RK, C_RV, C_RG, C_MSB, C_MRET = [i * 1024 for i in range(10)]
GAMMA = [1.0 - 2.0 ** (-5.0 - h) for h in range(4)]
G64 = [g ** 64 for g in GAMMA]
DEBUG = False


def _mm(T, nc, out, lhsT, rhs, start, stop, reads, writes, inc):
    T.op("pe", lambda: nc.tensor.matmul(out, lhsT=lhsT, rhs=rhs, start=start, stop=stop),
         reads=reads, writes=writes, inc=inc)


def _tr(T, nc, out, in_, ident, reads, writes, inc):
    T.op("pe", lambda: nc.tensor.transpose(out, in_, ident), reads=reads, writes=writes, inc=inc)


def _act(T, nc, out, in_, func, reads, writes, **kw):
    T.op("act", lambda: nc.scalar.activation(out=out, in_=in_, func=func, **kw), reads=reads, writes=writes)


def _ts(T, nc, eng, out, in0, s1, s2, op0, op1, reads, writes):
    e = nc.vector if eng == "dve" else nc.gpsimd
    if op1 is None:
        T.op(eng, lambda: e.tensor_scalar(out=out, in0=in0, scalar1=s1, scalar2=None, op0=op0), reads=reads, writes=writes)
    else:
        T.op(eng, lambda: e.tensor_scalar(out=out, in0=in0, scalar1=s1, scalar2=s2, op0=op0, op1=op1), reads=reads, writes=writes)


def _tt(T, nc, eng, out, in0, in1, op, reads, writes):
    e = nc.vector if eng == "dve" else nc.gpsimd
    T.op(eng, lambda: e.tensor_tensor(out=out, in0=in0, in1=in1, op=op), reads=reads, writes=writes)


def _stt(T, nc, out, in0, scalar, in1, op0, op1, reads, writes):
    T.op("dve", lambda: nc.vector.scalar_tensor_tensor(out=out, in0=in0, scalar=scalar, in1=in1, op0=op0, op1=op1),
         reads=reads, writes=writes)


def _copy(T, nc, eng, out, in_, reads, writes):
    if eng == "act":
        T.op("act", lambda: nc.scalar.copy(out=out, in_=in_), reads=reads, writes=writes)
    else:
        e = nc.vector if eng == "dve" else nc.gpsimd
        T.op(eng, lambda: e.tensor_copy(out=out, in_=in_), reads=reads, writes=writes)


def build_program(phases="ABCD", nt_a=NT, no_b=NO, heads_c=8, no_c=NO, no_d=NO):
    CFG = dict(phases=phases, nt_a=nt_a, no_b=no_b, heads_c=heads_c, no_c=no_c, no_d=no_d)
    nc = bass.Bass("TRN2", target_bir_lowering=False)
    dt = nc.dram_tensor
    xa = dt("xa", [S, DM], F32, kind="ExternalInput").ap()
    xo = dt("xo", [SO, DM], F32, kind="ExternalInput").ap()
    w_in = dt("w_in", [DM, 10 * 1024], F32, kind="ExternalInput").ap()
    wbs = dt("wbs", [DM, DM], F32, kind="ExternalInput").ap()
    wbr = dt("wbr", [DM, DM], F32, kind="ExternalInput").ap()
    wout = dt("wout", [DM, DM], F32, kind="ExternalInput").ap()
    gain_rep = dt("gain_rep", [128, DM], F32, kind="ExternalInput").ap()
    rgain_rep = dt("rgain_rep", [128, DM], F32, kind="ExternalInput").ap()
    qkg = dt("qkg", [128, 2], F32, kind="ExternalInput").ap()
    bmg = dt("bmg", [128, 16], F32, kind="ExternalInput").ap()
    cosq = dt("cosq", [128, S], F32, kind="ExternalInput").ap()
    sinq = dt("sinq", [128, S], F32, kind="ExternalInput").ap()
    dmask_d = dt("dmask", [128, 4 * 128], F32, kind="ExternalInput").ap()
    qdec_d = dt("qdec", [128, 4], F32, kind="ExternalInput").ap()
    kdect_d = dt("kdect", [128, 4], F32, kind="ExternalInput").ap()
    cmat_d = dt("cmat", [128, 4 * 128], BF16, kind="ExternalInput").ap()
    amask_d = dt("amask", [128, 8 * TS], BF16, kind="ExternalInput").ap()
    blend_d = dt("blend", [128, 2], F32, kind="ExternalInput").ap()
    y = dt("y", [SO, DM], F32, kind="ExternalOutput").ap()
    sk = "ExternalOutput" if DEBUG else "Internal"
    kT_d = dt("kT_d", [8, 128, S], BF16, kind=sk).ap()
    v_d = dt("v_d", [S, DM], BF16, kind=sk).ap()
    qT_d = dt("qT_d", [8, 128, SO], BF16, kind=sk).ap()
    ga_d = dt("ga_d", [8, 128, SO], BF16, kind=sk).ap()
    gb_d = dt("gb_d", [8, 128, SO], BF16, kind=sk).ap()
    sa_d = dt("sa_d", [8, 128, SO], BF16, kind=sk).ap()
    sb_d = dt("sb_d", [8, 128, SO], BF16, kind=sk).ap()
    ret_d = dt("ret_d", [S, DM], F32, kind=sk).ap()
    oaT_d = dt("oaT_d", [8, 128, SO], F32, kind=sk).ap()
    if DEBUG:
        dbg_bf = dt("dbg_bf", [128, 8192], BF16, kind="ExternalOutput").ap()
        dbg_f = dt("dbg_f", [128, 4096], F32, kind="ExternalOutput").ap()

    es = contextlib.ExitStack()
    with es:
        def sb(name, shape, dtype):
            return es.enter_context(nc.sbuf_tensor("t_" + name, shape, dtype))

        cmat = sb("cmat", [128, 4 * 128], BF16)
        ident = cmat[:, 0:128]
        onesm = cmat[:, 128:256]
        trineg = cmat[:, 256:384]
        carryneg = cmat[:, 384:512]
        qkg_t = sb("qkg_t", [128, 2], F32)
        qg_s = sb("qg_s", [128, 1], F32)
        bmg_t = sb("bmg_t", [128, 16], F32)
        blend_t = sb("blend_t", [128, 2], F32)
        ps, pb = [], []
        psum_ctr = [0]

        def alloc_psum(stack):
            tag = "abcdefgh"[psum_ctr[0]]
            psum_ctr[0] += 1
            ps[:] = [stack.enter_context(nc.psum_tensor("ps%d%s" % (k, tag), [128, 512], F32)) for k in range(6)]
            pb[:] = [stack.enter_context(nc.psum_tensor("pb%d%s" % (k, tag), [128, 1024], BF16)) for k in range(2)]
        T = es.enter_context(Tracker(nc))

        T.dma("sp", cmat[:], cmat_d, writes=["cmat"])
        T.dma("sp", qkg_t[:], qkg, writes=["qkg"])
        T.dma("sp", bmg_t[:], bmg, writes=["bmg"])
        T.dma("sp", blend_t[:], blend_d, writes=["blend"])
        _ts(T, nc, "dve", qg_s[:], qkg_t[:, 0:1], float(128 ** -0.5), None, ALU.mult, None, ["qkg"], ["qg_s"])

        def load_w(wt, col0, ncols, key, src=w_in):
            for c0 in range(0, ncols, 512):
                T.dma("pool", wt[:, :, c0:c0 + 512],
                      src[:, col0 + c0: col0 + c0 + 512].rearrange("(c p) n -> p c n", p=128),
                      writes=[key])

        def norm_block(xsrc, r0, xt, junk, ss, gain_t, h, hT, col0, pbt=None):
            pbt = pbt if pbt is not None else pb[0]
            T.dma("sp", xt[:], xsrc[r0:r0 + 128, :], writes=[xt.name])
            _act(T, nc, junk[:], xt[:], AF.Square, [xt.name], [junk.name, ss.name], accum_out=ss[:, 0:1])
            _act(T, nc, ss[:, 1:2], ss[:, 0:1], AF.Sqrt, [ss.name], [ss.name], bias=EPS, scale=1.0 / DM)
            T.op("dve", lambda: nc.vector.reciprocal(out=ss[:, 2:3], in_=ss[:, 1:2]), reads=[ss.name], writes=[ss.name])
            _stt(T, nc, h[:], xt[:], ss[:, 2:3], gain_t[:], ALU.mult, ALU.mult, [xt.name, ss.name, "gain"], [h.name])
            for c in range(NCH):
                _tr(T, nc, pbt[:, c * 128:(c + 1) * 128], h[:, c * 128:(c + 1) * 128], ident,
                    [h.name, "cmat"], [pbt.name], inc=(c == NCH - 1))
            _copy(T, nc, "act", hT[:, :, col0:col0 + 128], pbt[:].rearrange("p (c t) -> p c t", c=NCH),
                  [pbt.name], ["hT"])

        def norm_load(xsrc, r0, xt):
            T.dma("pool", xt[:], xsrc[r0:r0 + 128, :], writes=[xt.name])

        def norm_pre(xsrc, r0, xt, junk, ss, gain_t, h, load=True):
            if load:
                T.dma("sp", xt[:], xsrc[r0:r0 + 128, :], writes=[xt.name])
            _act(T, nc, junk[:], xt[:], AF.Square, [xt.name], [junk.name, ss.name], accum_out=ss[:, 0:1])
            _act(T, nc, ss[:, 1:2], ss[:, 0:1], AF.Sqrt, [ss.name], [ss.name], bias=EPS, scale=1.0 / DM)
            T.op("dve", lambda: nc.vector.reciprocal(out=ss[:, 2:3], in_=ss[:, 1:2]), reads=[ss.name], writes=[ss.name])
            _stt(T, nc, h[:], xt[:], ss[:, 2:3], gain_t[:], ALU.mult, ALU.mult, [xt.name, ss.name, "gain"], [h.name])

        def norm_post(h, hT, col0, pbt):
            for c in range(NCH):
                _tr(T, nc, pbt[:, c * 128:(c + 1) * 128], h[:, c * 128:(c + 1) * 128], ident,
                    [h.name, "cmat"], [pbt.name], inc=(c == NCH - 1))
            _copy(T, nc, "act", hT[:, :, col0:col0 + 128], pbt[:].rearrange("p (c t) -> p c t", c=NCH),
                  [pbt.name], ["hT"])

        def qk_proj(W, wkey, wcol, hT, pz):
            for c in range(NCH):
                _mm(T, nc, pz[:], W[:, c, wcol:wcol + 128], hT[:, c, :], c == 0, c == NCH - 1,
                    [wkey, "hT"], [pz.name], inc=(c == NCH - 1))

        def qk_square(pz, sq, zf=None):
            _act(T, nc, sq[:], pz[:], AF.Square, [pz.name], [sq.name])
            if zf is not None:
                _copy(T, nc, "act", zf[:], pz[:], [pz.name], [zf.name])

        def qk_norm(gcol, pz, pm, sq, rs, outt, outkey, do_square=True, zf=None):
            if do_square:
                qk_square(pz, sq)
            _mm(T, nc, pm[:], onesm, sq[:], True, True, ["cmat", sq.name], [pm.name], True)
            _act(T, nc, rs[:], pm[:], AF.Ln, [pm.name], [rs.name], bias=EPS, scale=1.0)
            _act(T, nc, rs[:], rs[:], AF.Exp, [rs.name], [rs.name], scale=-0.5)
            src, skey = (pz[:], pz.name) if zf is None else (zf[:], zf.name)
            _stt(T, nc, outt, src, gcol, rs[:], ALU.mult, ALU.mult, [skey, rs.name, "qkg", "qg_s"], [outkey])

        def qk_head(W, wkey, wcol, hT, gcol, pz, pm, sq, rs, outt, outkey):
            qk_proj(W, wkey, wcol, hT, pz)
            qk_norm(gcol, pz, pm, sq, rs, outt, outkey)

        with contextlib.ExitStack() as pa:
            def sa_(name, shape, dtype):
                return pa.enter_context(nc.sbuf_tensor("t_" + name, shape, dtype))
            alloc_psum(pa)
            Wa = sa_("Wa", [128, NCH, 5 * 1024], BF16)
            gain_t = sa_("gain_t", [128, DM], F32)
            xts = [sa_("xt%d" % k, [128, DM], F32) for k in range(4)]
            junk = sa_("junk", [128, DM], BF16)
            sss = [sa_("ss%d" % k, [128, 4], F32) for k in range(4)]
            hs = [sa_("h%d" % k, [128, DM], BF16) for k in range(4)]
            hT = sa_("hT", [128, NCH, TS], BF16)
            sqs = [sa_("sq%d" % k, [128, TS], BF16) for k in range(3)]
            rss = [sa_("rs%d" % k, [128, TS], F32) for k in range(2)]
            kout = [sa_("kout%d" % k, [128, TS], BF16) for k in range(2)]
            vout = [sa_("vout%d" % k, [128, DM], BF16) for k in range(2)]
            cs = sa_("cs", [128, 2, TS], F32)
            ra = [sa_("ra%d" % k, [128, TS], F32) for k in range(2)]
            rb = [sa_("rb%d" % k, [128, TS], F32) for k in range(2)]
            qrT = [sa_("qrT%d" % k, [128, 2, TS], BF16) for k in range(3)]
            rak = [sa_("rak%d" % k, [128, TS], F32) for k in range(2)]
            rbk = [sa_("rbk%d" % k, [128, TS], F32) for k in range(2)]
            krT = [sa_("krT%d" % k, [128, 2, TS], BF16) for k in range(2)]
            kdt = [sa_("kdt%d" % k, [128, 4, 256], BF16) for k in range(2)]
            rv = sa_("rv", [128, 4, DM], BF16)
            sT = [sa_("sT%d" % k, [128, 4, 128], BF16) for k in range(2)]
            St = sa_("St", [128, 4, 2, 256], F32)
            Sb0 = sa_("Sb0", [128, 4, 2, 256], BF16)
            Sbt = [sa_("Sbt%d" % k, [128, 3, 2, 256], BF16) for k in range(2)]
            dmask = sa_("dmask", [128, 4, 128], F32)
            qdect = sa_("qdect", [128, 4], F32)
            kdect = sa_("kdect", [128, 4], F32)
            rout = [sa_("rout%d" % k, [128, 256], F32) for k in range(2)]
            pb0f, pb1f = pb[0][:].bitcast(F32), pb[1][:].bitcast(F32)
            zfs = [ra[0], ra[1], rb[0]]

            T.dma("sp", gain_t[:], gain_rep, writes=["gain"])
            T.dma("sp", dmask[:].rearrange("p h q -> p (h q)"), dmask_d, writes=["dmask"])
            T.dma("sp", qdect[:], qdec_d, writes=["qdect"])
            T.dma("sp", kdect[:], kdect_d, writes=["kdect"])
            load_w(Wa[:, :, 0:1024], C_SBK, 1024, "Wa")
            load_w(Wa[:, :, 1024:2048], C_SBV, 1024, "Wa")
            load_w(Wa[:, :, 2048:3072], C_RQ, 1024, "Wa")
            load_w(Wa[:, :, 3072:4096], C_RK, 1024, "Wa")
            load_w(Wa[:, :, 4096:5120], C_RV, 1024, "Wa")
            T.op("dve", lambda: nc.vector.memset(St[:].rearrange("p a b c -> p (a b c)"), 0.0), writes=["St"])
            T.op("dve", lambda: nc.vector.memset(Sb0[:].rearrange("p a b c -> p (a b c)"), 0.0), writes=["Sb0"])
            G128 = [g ** 128 for g in GAMMA]

            def ret_proj(hd, t0):
                par = hd % 2
                for which in range(2):
                    wc = (2048 if which == 0 else 3072) + hd * 256
                    p1, p2 = (ps[0], ps[1]) if which == 0 else (ps[2], ps[3])
                    for half, pz in enumerate((p1, p2)):
                        for c in range(NCH):
                            _mm(T, nc, pz[:], Wa[:, c, wc + half * 128: wc + (half + 1) * 128], hT[:, c, :],
                                c == 0, c == NCH - 1, ["Wa", "hT"], [pz.name], inc=(c == NCH - 1))
                    dst = qrT[hd % 3] if which == 0 else krT[par]
                    A_, B_ = (ra, rb) if which == 0 else (rak, rbk)
                    ct, st_ = cs[:, 0, :], cs[:, 1, :]
                    _tt(T, nc, "dve", A_[0][:], p1[:], ct, ALU.mult, [p1.name, "cs"], [A_[0].name])
                    _tt(T, nc, "dve", B_[0][:], p2[:], st_, ALU.mult, [p2.name, "cs"], [B_[0].name])
                    _tt(T, nc, "dve", A_[1][:], p1[:], st_, ALU.mult, [p1.name, "cs"], [A_[1].name])
                    _tt(T, nc, "dve", B_[1][:], p2[:], ct, ALU.mult, [p2.name, "cs"], [B_[1].name])
                    for half in range(2):
                        _tt(T, nc, "pool", A_[half][:], A_[half][:], B_[half][:], ALU.subtract if half == 0 else ALU.add,
                            [A_[half].name, B_[half].name], [A_[half].name])
                        if which == 0:
                            _copy(T, nc, "act", dst[:, half, :], A_[half][:], [A_[half].name], [dst.name])
                        else:
                            T.op("act", (lambda o=dst[:, half, :], i_=A_[half][:]: nc.scalar.mul(out=o, in_=i_, mul=1.0 / 16.0)),
                                 reads=[A_[half].name], writes=[dst.name])

            def ret_small_a(hd, t0):
                par = hd % 2
                kr, qr, kd, st = krT[par], qrT[hd % 3], kdt[par], sT[par]
                for blk in range(4):
                    for half in range(2):
                        _tr(T, nc, pb[1][:, blk * 256 + half * 128:blk * 256 + (half + 1) * 128], kr[:, half, blk * 128:(blk + 1) * 128], ident,
                            [kr.name, "cmat"], [pb[1].name], inc=(blk == 3 and half == 1))
                _ts(T, nc, "dve", kd[:].rearrange("p a b -> p (a b)"), pb[1][:], kdect[:, hd:hd + 1], None, ALU.mult, None,
                    [pb[1].name, "kdect"], [kd.name])
                for blk in range(4):
                    kv = ps[4 + blk % 2]
                    for half in range(2):
                        _mm(T, nc, kv[:, half * 256:(half + 1) * 256], kd[:, blk, half * 128:(half + 1) * 128],
                            rv[:, blk, hd * 256:(hd + 1) * 256], True, True, [kd.name, "rv"], [kv.name], inc=(half == 1))
                    sflat = St[:, hd, :, :].rearrange("p a b -> p (a b)")
                    _stt(T, nc, sflat, sflat, float(G128[hd]), kv[:], ALU.mult, ALU.add, ["St", kv.name], ["St"])
                    if blk < 3:
                        _copy(T, nc, "act", Sbt[par][:, blk, :, :].rearrange("p a b -> p (a b)"), sflat, ["St"], [Sbt[par].name + str(blk)])
                for blk in range(4):
                    b0 = blk * 128
                    for half in range(2):
                        _mm(T, nc, pb0f[:, blk * 128:(blk + 1) * 128], kr[:, half, b0:b0 + 128], qr[:, half, b0:b0 + 128],
                            half == 0, half == 1, [kr.name, qr.name], [pb[0].name], inc=(half == 1))
                for blk in range(4):
                    _tt(T, nc, "dve", st[:, blk, :], pb0f[:, blk * 128:(blk + 1) * 128], dmask[:, hd, :], ALU.mult,
                        [pb[0].name, "dmask"], [st.name])

            def ret_small_b(hd, t0):
                par = hd % 2
                qd, st = qrT[hd % 3], sT[par]
                for blk in range(4):
                    b0 = blk * 128
                    pot = ps[4 + blk % 2]
                    po = pot[:]
                    _mm(T, nc, po[:, 0:256], st[:, blk, :], rv[:, blk, hd * 256:(hd + 1) * 256], True, False,
                        [st.name, "rv"], [pot.name], inc=True)
                    for half in range(2):
                        sb_ap = Sb0[:, hd, half, :] if blk == 0 else Sbt[par][:, blk - 1, half, :]
                        skey = "Sb0" if blk == 0 else Sbt[par].name + str(blk - 1)
                        _mm(T, nc, po[:, 0:256], qd[:, half, b0:b0 + 128], sb_ap, False, half == 1,
                            [qd.name, skey], [pot.name], inc=(half == 1))
                    ro = rout[blk % 2]
                    if blk % 2 == 0:
                        T.op("act", (lambda o=ro[:], i_=po[:, 0:256], sc=qdect[:, hd:hd + 1]: nc.scalar.activation(out=o, in_=i_, func=AF.Copy, scale=sc)),
                             reads=[pot.name, "qdect"], writes=[ro.name])
                    else:
                        _ts(T, nc, "dve", ro[:], po[:, 0:256], qdect[:, hd:hd + 1], None, ALU.mult, None, [pot.name, "qdect"], [ro.name])
                    T.dma("sp", ret_d[t0 + b0:t0 + b0 + 128, hd * 256:(hd + 1) * 256], ro[:], reads=[ro.name], writes=["ret_d"])
                _copy(T, nc, "act", Sb0[:, hd, :, :].rearrange("p a b -> p (a b)"),
                      St[:, hd, :, :].rearrange("p a b -> p (a b)"), ["St"], ["Sb0"])

            for t in range(nt_a if "A" in phases else 0):
                t0 = t * TS
                for k, tab in enumerate((cosq, sinq)):
                    T.dma("sp", cs[:, k, :], tab[:, t0:t0 + TS], writes=["cs"])
                n_a = nt_a if "A" in phases else 0
                if t == 0:
                    for blk in range(4):
                        norm_pre(xa, blk * 128, xts[blk], junk, sss[blk], gain_t, hs[blk])
                for blk in range(4):
                    norm_post(hs[blk], hT, blk * 128, pb[blk % 2])
                if t + 1 < n_a:
                    for blk in range(4):
                        norm_load(xa, t0 + TS + blk * 128, xts[blk])
                pzk = [ps[0], ps[1], ps[4]]
                for h0 in range(2):
                    qk_proj(Wa, "Wa", h0 * 128, hT, pzk[h0])
                    qk_square(pzk[h0], sqs[h0], zfs[h0])
                for hd in range(8):
                    if hd + 2 < 8:
                        qk_proj(Wa, "Wa", (hd + 2) * 128, hT, pzk[(hd + 2) % 3])
                        qk_square(pzk[(hd + 2) % 3], sqs[(hd + 2) % 3], zfs[(hd + 2) % 3])
                    ko = kout[hd % 2]
                    qk_norm(qkg_t[:, 1:2], pzk[hd % 3], ps[2 + hd % 2], sqs[hd % 3], rss[hd % 2], ko[:], ko.name, do_square=False,
                            zf=zfs[hd % 3])
                    T.dma("sp", kT_d[hd, :, t0:t0 + TS], ko[:], reads=[ko.name], writes=["kT_d"])
                nb = 0
                for blk in range(4):
                    vo = vout[blk % 2]
                    for g in range(2):
                        pz = ps[(4 + nb) % 6]
                        nb += 1
                        for c in range(NCH):
                            _mm(T, nc, pz[:], hT[:, c, blk * 128:(blk + 1) * 128], Wa[:, c, 1024 + g * 512:1024 + (g + 1) * 512],
                                c == 0, c == NCH - 1, ["hT", "Wa"], [pz.name], inc=(c == NCH - 1))
                        _copy(T, nc, "act" if g == 0 else "dve", vo[:, g * 512:(g + 1) * 512], pz[:], [pz.name], [vo.name])
                    T.dma("sp", v_d[t0 + blk * 128:t0 + (blk + 1) * 128, :], vo[:], reads=[vo.name], writes=["v_d"])
                    for g in range(2):
                        pz = ps[(4 + nb) % 6]
                        nb += 1
                        for c in range(NCH):
                            _mm(T, nc, pz[:], hT[:, c, blk * 128:(blk + 1) * 128], Wa[:, c, 4096 + g * 512:4096 + (g + 1) * 512],
                                c == 0, c == NCH - 1, ["hT", "Wa"], [pz.name], inc=(c == NCH - 1))
                        _copy(T, nc, "act" if g == 0 else "dve", rv[:, blk, g * 512:(g + 1) * 512], pz[:], [pz.name], ["rv"])
                ret_proj(0, t0)
                for hd in range(4):
                    if hd + 1 < 4:
                        ret_proj(hd + 1, t0)
                    ret_small_a(hd, t0)
                    if t + 1 < n_a:
                        norm_pre(xa, 0, xts[hd], junk, sss[hd], gain_t, hs[hd], load=False)
                    if hd > 0:
                        ret_small_b(hd - 1, t0)
                ret_small_b(3, t0)
            T.barrier()
        build_rest(nc, T, locals())
        T.finish()
    return nc


def build_rest(nc, T, L):
    CFG = L["CFG"]
    phases = CFG["phases"]
    ps, pb = L["ps"], L["pb"]
    ident, onesm, trineg, carryneg = L["ident"], L["onesm"], L["trineg"], L["carryneg"]
    qg_s, bmg_t, blend_t = L["qg_s"], L["bmg_t"], L["blend_t"]
    xo, y = L["xo"], L["y"]
    kT_d, v_d, qT_d, ga_d, gb_d, sa_d, sb_d, ret_d, oaT_d = (L[k] for k in
        ("kT_d", "v_d", "qT_d", "ga_d", "gb_d", "sa_d", "sb_d", "ret_d", "oaT_d"))
    load_w, norm_block, qk_head = L["load_w"], L["norm_block"], L["qk_head"]
    wbs, wbr, wout = L["wbs"], L["wbr"], L["wout"]

    with contextlib.ExitStack() as pbs:
        def sa_(name, shape, dtype):
            return pbs.enter_context(nc.sbuf_tensor("t_" + name, shape, dtype))
        L["alloc_psum"](pbs)
        Wb = sa_("Wb", [128, NCH, 5 * 1024], BF16)
        gain_t = sa_("gain_tb", [128, DM], F32)
        xts = [sa_("xtb%d" % k, [128, DM], F32) for k in range(4)]
        junk = sa_("junkb", [128, DM], BF16)
        sss = [sa_("ssb%d" % k, [128, 4], F32) for k in range(4)]
        hs = [sa_("hb%d" % k, [128, DM], BF16) for k in range(4)]
        hT = sa_("hTb", [128, NCH, TS], BF16)
        sqs = [sa_("sqb%d" % k, [128, TS], BF16) for k in range(3)]
        rss = [sa_("rsb%d" % k, [128, TS], F32) for k in range(2)]
        qout = [sa_("qout%d" % k, [128, TS], BF16) for k in range(2)]
        zfs = [sa_("zfb%d" % k, [128, TS], F32) for k in range(3)]
        gout = [sa_("gout%d" % k, [128, TS], BF16) for k in range(4)]
        T.dma("sp", gain_t[:], L["gain_rep"], writes=["gain"])
        for k, col in enumerate((C_SBQ, C_SBG, C_RG, C_MSB, C_MRET)):
            load_w(Wb[:, :, k * 1024:(k + 1) * 1024], col, 1024, "Wb")
        norm_pre, norm_post, qk_proj, qk_square, qk_norm = (L[k] for k in ("norm_pre", "norm_post", "qk_proj", "qk_square", "qk_norm"))
        n_b = CFG["no_b"] if "B" in phases else 0
        for i in range(n_b):
            t0 = i * TS
            if i == 0:
                for blk in range(4):
                    norm_pre(xo, blk * 128, xts[blk], junk, sss[blk], gain_t, hs[blk])
            for blk in range(4):
                norm_post(hs[blk], hT, blk * 128, pb[blk % 2])
            if i + 1 < n_b:
                for blk in range(4):
                    L["norm_load"](xo, t0 + TS + blk * 128, xts[blk])
            pzq = [ps[0], ps[1], ps[2]]
            for h0 in range(2):
                qk_proj(Wb, "Wb", h0 * 128, hT, pzq[h0])
                qk_square(pzq[h0], sqs[h0], zfs[h0])
            for hd in range(8):
                if hd + 2 < 8:
                    qk_proj(Wb, "Wb", (hd + 2) * 128, hT, pzq[(hd + 2) % 3])
                    qk_square(pzq[(hd + 2) % 3], sqs[(hd + 2) % 3], zfs[(hd + 2) % 3])
                qo = qout[hd % 2]
                qk_norm(qg_s[:, 0:1], pzq[hd % 3], ps[3 + hd % 2], sqs[hd % 3], rss[hd % 2], qo[:], qo.name, do_square=False,
                        zf=zfs[hd % 3])
                T.dma("sp", qT_d[hd, :, t0:t0 + TS], qo[:], reads=[qo.name], writes=["qT_d"])
            n = 0
            for k, (dst, func) in enumerate(((ga_d, AF.Silu), (gb_d, AF.Silu), (sa_d, AF.Sigmoid), (sb_d, AF.Sigmoid))):
                for c in range(NCH):
                    pz = ps[(5 + n) % 6]
                    go = gout[n % 4]
                    n += 1
                    wc = (k + 1) * 1024 + c * 128
                    for cc in range(NCH):
                        _mm(T, nc, pz[:], Wb[:, cc, wc:wc + 128], hT[:, cc, :], cc == 0, cc == NCH - 1,
                            ["Wb", "hT"], [pz.name], inc=(cc == NCH - 1))
                    if func == AF.Silu:
                        _act(T, nc, go[:], pz[:], func, [pz.name], [go.name])
                    else:
                        bcol = (k - 2) * 8 + c
                        _act(T, nc, go[:], pz[:], func, [pz.name, "bmg"], [go.name], bias=bmg_t[:, bcol:bcol + 1], scale=1.0)
                    T.dma("sp", dst[c, :, t0:t0 + TS], go[:], reads=[go.name], writes=["gates_d"])
                    if i + 1 < n_b and n % 8 == 0:
                        bq = n // 8 - 1
                        norm_pre(xo, 0, xts[bq], junk, sss[bq], gain_t, hs[bq], load=False)
        T.barrier()

    with contextlib.ExitStack() as pcs:
        def sa_(name, shape, dtype):
            return pcs.enter_context(nc.sbuf_tensor("t_" + name, shape, dtype))
        KT = [sa_("KT%d" % k, [128, S], BF16) for k in range(2)]
        VV = [sa_("VV%d" % k, [128, S // 128, 128], BF16) for k in range(2)]
        QT = [sa_("QT%d" % k, [128, SO], BF16) for k in range(2)]
        amask = sa_("amask", [128, 8, TS], BF16)
        E = [sa_("E%d" % k, [128, 2, TS], F32) for k in range(3)]
        Lp = [sa_("Lp%d" % k, [128, 2, TS], BF16) for k in range(2)]
        G = [sa_("G%d" % k, [128, 2, TS], F32) for k in range(2)]
        W = [sa_("W%d" % k, [128, 2, TS], BF16) for k in range(2)]
        osb = [sa_("osb%d" % k, [128, TS], F32) for k in range(2)]
        T.dma("sp", amask[:].rearrange("p r q -> p (r q)"), L["amask_d"], writes=["amask"])
        Zall = pcs.enter_context(nc.psum_tensor("Zall", [128, 2, 2, TS], F32))
        Call = pcs.enter_context(nc.psum_tensor("Call", [128, 2, TS], F32))
        Oall = pcs.enter_context(nc.psum_tensor("Oall", [128, 2, TS], F32))
        seqs = ([0, 3, 4, 7], [1, 2, 5, 6])
        steps = [[(i, kb) for i in seq for kb in range(8 * i + 7, -1, -1)] for seq in seqs]
        NS = len(steps[0])
        assert NS == len(steps[1])
        if CFG["no_c"] < NO:
            seqs = ([0], [0]) if CFG["no_c"] == 1 else seqs
            steps = [[(i, kb) for i in seq for kb in range(8 * i + 7, -1, -1)] for seq in seqs]
            NS = len(steps[0])
        nosb = [0]
        for hd in range(CFG["heads_c"] if "C" in phases else 0):
            sl = hd % 2
            kt, vv, qt = KT[sl], VV[sl], QT[sl]
            for part in range(4):
                T.dma("sp", kt[:, part * 2048:(part + 1) * 2048], kT_d[hd, :, part * 2048:(part + 1) * 2048],
                      reads=["kT_d"], writes=[kt.name])
            for part in range(4):
                T.dma("sp", vv[:, part * 16:(part + 1) * 16, :],
                      v_d[part * 2048:(part + 1) * 2048, hd * 128:(hd + 1) * 128].rearrange("(b p) d -> p b d", p=128),
                      reads=["v_d"], writes=[vv.name])
            T.dma("sp", qt[:], qT_d[hd, :, :], reads=["qT_d"], writes=[qt.name])

            def pe1(st, t):
                i, kb = steps[st][t]
                par = t % 2
                Z = Zall[:, par, st, :]
                masked = kb >= 8 * i
                _mm(T, nc, Z, kt[:, kb * 128:(kb + 1) * 128], qt[:, i * TS:(i + 1) * TS], True, not masked,
                    [kt.name, qt.name], ["Z%d" % par], inc=(not masked))
                if masked:
                    _mm(T, nc, Z, ident, amask[:, kb - 8 * i, :], False, True, ["cmat", "amask"], ["Z%d" % par], True)

            def a1(t):
                par = t % 2
                _act(T, nc, E[t % 3][:], Zall[:, par, :, :], AF.Exp, ["Z%d" % par], [E[t % 3].name])

            def a2(t):
                par = t % 2
                _act(T, nc, Lp[par][:], E[t % 3][:], AF.Ln, [E[t % 3].name], [Lp[par].name], bias=1.0, scale=1.0)

            def pe2(st, t):
                i, kb = steps[st][t]
                l_ = Lp[t % 2]
                fst = (kb == 8 * i + 7)
                T.op("pe", (lambda o=Call[:, st, :], r_=l_[:, st, :], f_=fst: nc.tensor.matmul(o, lhsT=trineg, rhs=r_, start=f_, stop=True, skip_group_check=(not f_))),
                     reads=["cmat", l_.name], writes=["Call"], inc=True)

            def a3(t):
                par = t % 2
                _act(T, nc, G[par][:], Call[:], AF.Exp, ["Call"], [G[par].name])

            def pe3(st, t):
                l_ = Lp[t % 2]
                T.op("pe", (lambda o=Call[:, st, :], r_=l_[:, st, :]: nc.tensor.matmul(o, lhsT=carryneg, rhs=r_, start=False, stop=True, skip_group_check=True)),
                     reads=["cmat", l_.name], writes=["Call"], inc=True)

            def v1(t):
                par = t % 2
                _tt(T, nc, "dve", W[par][:], E[t % 3][:], G[par][:], ALU.mult, [E[t % 3].name, G[par].name], [W[par].name])

            def pe4(st, t):
                i, kb = steps[st][t]
                w_ = W[t % 2]
                _mm(T, nc, Oall[:, st, :], vv[:, kb, :], w_[:, st, :], kb == 8 * i + 7, kb == 0, [vv.name, w_.name], ["O%d" % st], True)
                if kb == 0:
                    ob = osb[nosb[0] % 2]
                    nosb[0] += 1
                    _copy(T, nc, "dve", ob[:], Oall[:, st, :], ["O%d" % st], [ob.name])
                    T.dma("sp", oaT_d[hd, :, i * TS:(i + 1) * TS], ob[:], reads=[ob.name], writes=["oaT_d"])

            for st in range(2):
                pe1(st, 0)
            if NS > 1:
                for st in range(2):
                    pe1(st, 1)
            a1(0)
            a2(0)
            for t in range(NS):
                if t + 1 < NS:
                    a1(t + 1)
                for st in range(2):
                    pe2(st, t)
                if t + 2 < NS:
                    for st in range(2):
                        pe1(st, t + 2)
                if t > 0:
                    for st in range(2):
                        pe4(st, t - 1)
                a3(t)
                for st in range(2):
                    pe3(st, t)
                v1(t)
                if t + 1 < NS:
                    a2(t + 1)
            for st in range(2):
                pe4(st, NS - 1)
        T.barrier()

    with contextlib.ExitStack() as pds:
        def sa_(name, shape, dtype):
            return pds.enter_context(nc.sbuf_tensor("t_" + name, shape, dtype))
        L["alloc_psum"](pds)
        Wd = sa_("Wd", [128, NCH, 3 * 1024], BF16)
        rgain_t = sa_("rgain_t", [128, DM], F32)
        rA = [sa_("rA%d" % k, [128, DM], F32) for k in range(2)]
        rB = [sa_("rB%d" % k, [128, DM], F32) for k in range(2)]
        rts = [sa_("rt%d" % k, [128, DM], F32) for k in range(2)]
        junk = sa_("junkd", [128, 256], F32)
        sss = [sa_("ssd%d" % k, [128, 12], F32) for k in range(2)]
        rns = [sa_("rn%d" % k, [128, DM], BF16) for k in range(4)]
        RNT = sa_("RNT", [128, NCH, TS], BF16)
        oa = [sa_("oa%d" % k, [128, TS], F32) for k in range(4)]
        ga = [sa_("ga%d" % k, [128, TS], BF16) for k in range(4)]
        gb = [sa_("gb%d" % k, [128, TS], BF16) for k in range(4)]
        sga = [sa_("sga%d" % k, [128, TS], BF16) for k in range(4)]
        sgb = [sa_("sgb%d" % k, [128, TS], BF16) for k in range(4)]
        OAGs = [sa_("OAG%d" % k, [128, NCH, TS], BF16) for k in range(2)]
        OBG = sa_("OBG", [128, NCH, TS], BF16)
        MG = sa_("MG", [128, NCH, TS], BF16)
        t1 = sa_("t1", [128, TS], F32)
        t2 = sa_("t2", [128, TS], F32)
        xr = [sa_("xr%d" % k, [128, DM], F32) for k in range(2)]
        yt = [sa_("yt%d" % k, [128, DM], F32) for k in range(2)]
        T.dma("sp", rgain_t[:], L["rgain_rep"], writes=["rgain"])
        load_w(Wd[:, :, 0:1024], 0, 1024, "Wd", src=wbs)
        load_w(Wd[:, :, 1024:2048], 0, 1024, "Wd", src=wbr)
        load_w(Wd[:, :, 2048:3072], 0, 1024, "Wd", src=wout)
        n_d = CFG["no_d"] if "D" in phases else 0

        def ret_prep(i, blk):
            a_, b_, rt, ss, rn = rA[blk % 2], rB[blk % 2], rts[blk % 2], sss[blk % 2], rns[blk]
            T.dma("pool", a_[:], ret_d[(2 * i) * TS + blk * 128:(2 * i) * TS + (blk + 1) * 128, :], reads=["ret_d"], writes=[a_.name])
            T.dma("pool", b_[:], ret_d[(2 * i + 1) * TS + blk * 128:(2 * i + 1) * TS + (blk + 1) * 128, :], reads=["ret_d"], writes=[b_.name])
            _ts(T, nc, "dve", rt[:], a_[:], blend_t[:, 0:1], None, ALU.mult, None, [a_.name, "blend"], [rt.name])
            _stt(T, nc, rt[:], b_[:], blend_t[:, 1:2], rt[:], ALU.mult, ALU.add, [b_.name, "blend", rt.name], [rt.name])
            for hd in range(4):
                _act(T, nc, junk[:], rt[:, hd * 256:(hd + 1) * 256], AF.Square, [rt.name], ["junkd", ss.name],
                     accum_out=ss[:, hd:hd + 1])
            _act(T, nc, ss[:, 4:8], ss[:, 0:4], AF.Sqrt, [ss.name], [ss.name], bias=EPS, scale=1.0 / 256)
            T.op("dve", lambda: nc.vector.reciprocal(out=ss[:, 8:12], in_=ss[:, 4:8]), reads=[ss.name], writes=[ss.name])
            for hd in range(4):
                _stt(T, nc, rn[:, hd * 256:(hd + 1) * 256], rt[:, hd * 256:(hd + 1) * 256], ss[:, 8 + hd:9 + hd],
                     rgain_t[:, hd * 256:(hd + 1) * 256], ALU.mult, ALU.mult, [rt.name, ss.name, "rgain"], [rn.name])

        def oag_prep(i, c):
            o_, ga_ = oa[c % 4], ga[c % 4]
            T.dma("pool", o_[:], oaT_d[c, :, i * TS:(i + 1) * TS], reads=["oaT_d"], writes=[o_.name])
            T.dma("pool", ga_[:], ga_d[c, :, i * TS:(i + 1) * TS], reads=["gates_d"], writes=[ga_.name])
            _tt(T, nc, "dve", OAGs[i % 2][:, c, :], o_[:], ga_[:], ALU.mult, [o_.name, ga_.name], [OAGs[i % 2].name])

        for i in range(n_d):
            t0 = i * TS
            OAG = OAGs[i % 2]
            if i == 0:
                for blk in range(4):
                    ret_prep(0, blk)
                for c in range(NCH):
                    oag_prep(0, c)
            for blk in range(4):
                rn = rns[blk]
                for c in range(NCH):
                    _tr(T, nc, pb[blk % 2][:, c * 128:(c + 1) * 128], rn[:, c * 128:(c + 1) * 128], ident,
                        [rn.name, "cmat"], [pb[blk % 2].name], inc=(c == NCH - 1))
                _copy(T, nc, "act", RNT[:, :, blk * 128:(blk + 1) * 128], pb[blk % 2][:].rearrange("p (c t) -> p c t", c=NCH),
                      [pb[blk % 2].name], ["RNT"])
            for c in range(NCH):
                gb_ = gb[c % 4]
                T.dma("act", gb_[:], gb_d[c, :, t0:t0 + TS], reads=["gates_d"], writes=[gb_.name])
                _tt(T, nc, "dve", OBG[:, c, :], RNT[:, c, :], gb_[:], ALU.mult, ["RNT", gb_.name], ["OBG"])
            for oc in range(NCH):
                sa_t, sb_t = sga[oc % 4], sgb[oc % 4]
                T.dma("act", sa_t[:], sa_d[oc, :, t0:t0 + TS], reads=["gates_d"], writes=[sa_t.name])
                T.dma("act", sb_t[:], sb_d[oc, :, t0:t0 + TS], reads=["gates_d"], writes=[sb_t.name])
                for c in range(NCH):
                    _mm(T, nc, ps[0][:], Wd[:, c, oc * 128:(oc + 1) * 128], OAG[:, c, :], c == 0, c == NCH - 1,
                        ["Wd", OAG.name], ["ps0"], inc=(c == NCH - 1))
                for c in range(NCH):
                    _mm(T, nc, ps[1][:], Wd[:, c, 1024 + oc * 128:1024 + (oc + 1) * 128], OBG[:, c, :], c == 0, c == NCH - 1,
                        ["Wd", "OBG"], ["ps1"], inc=(c == NCH - 1))
                _tt(T, nc, "dve", t1[:], ps[0][:], sa_t[:], ALU.mult, ["ps0", sa_t.name], ["t1"])
                _tt(T, nc, "dve", t2[:], ps[1][:], sb_t[:], ALU.mult, ["ps1", sb_t.name], ["t2"])
                _tt(T, nc, "dve", MG[:, oc, :], t1[:], t2[:], ALU.add, ["t1", "t2"], ["MG"])
                if i + 1 < n_d:
                    if oc < 4:
                        ret_prep(i + 1, oc)
                    oag_prep(i + 1, oc)
            for blk in range(4):
                x_, y_ = xr[blk % 2], yt[blk % 2]
                T.dma("act", x_[:], xo[t0 + blk * 128:t0 + (blk + 1) * 128, :], writes=[x_.name])
                for g in range(2):
                    pz = ps[2 + g]
                    for oc in range(NCH):
                        _mm(T, nc, pz[:], MG[:, oc, blk * 128:(blk + 1) * 128], Wd[:, oc, 2048 + g * 512:2048 + (g + 1) * 512],
                            oc == 0, oc == NCH - 1, ["MG", "Wd"], [pz.name], inc=(oc == NCH - 1))
                    _tt(T, nc, "dve", y_[:, g * 512:(g + 1) * 512], pz[:], x_[:, g * 512:(g + 1) * 512], ALU.add,
                        [pz.name, x_.name], [y_.name])
                T.dma("sp", y[t0 + blk * 128:t0 + (blk + 1) * 128, :], y_[:], reads=[y_.name], writes=["y"])


_CONST_CACHE = {}


def _const_tables():
    if _CONST_CACHE:
        return _CONST_CACHE
    bf = ml_dtypes.bfloat16
    d = 256
    inv_freq = (np.float32(10000.0) ** (-np.arange(0, d, 2, dtype=np.float32) / np.float32(d))).astype(np.float32)
    pos = np.arange(S, dtype=np.float32)
    ang = (pos[None, :] * inv_freq[:, None]).astype(np.float32)
    c64, s64 = np.cos(ang.astype(np.float64)), np.sin(ang.astype(np.float64))
    _CONST_CACHE["cosq"] = c64.astype(np.float32)
    _CONST_CACHE["sinq"] = s64.astype(np.float32)
    lg = np.log1p(-np.exp2(-5.0 - np.arange(4, dtype=np.float64)))
    idx = np.arange(128)
    same = (idx[:, None] // 64) == (idx[None, :] // 64)
    k_first = (idx[:, None] // 64) < (idx[None, :] // 64)
    dm = np.zeros((128, 4, 128), np.float64)
    for hh in range(4):
        dm[:, hh, :] = np.where(same | k_first, np.exp(lg[hh] * (np.abs(idx[:, None] - idx[None, :]) - (idx[None, :] + 1.0))), 0.0)
    _CONST_CACHE["dmask"] = dm.reshape(128, 512).astype(np.float32)
    qd = np.zeros((128, 4), np.float64)
    for hh in range(4):
        qd[:, hh] = np.exp(lg[hh] * (idx + 1.0))
    _CONST_CACHE["qdec"] = qd.astype(np.float32)
    kd = np.zeros((128, 4), np.float64)
    for hh in range(4):
        kd[:, hh] = np.exp(lg[hh] * (127.0 - idx))
    _CONST_CACHE["kdect"] = kd.astype(np.float32)
    cm = np.zeros((128, 4, 128), np.float32)
    cm[:, 0, :] = np.eye(128)
    cm[:, 1, :] = 1.0 / 128.0
    cm[:, 2, :] = np.where(idx[:, None] >= idx[None, :], -1.0, 0.0)
    cm[:, 3, :] = np.where(idx[:, None] < idx[None, :], -1.0, 0.0)
    _CONST_CACHE["cmat"] = cm.reshape(128, 512).astype(bf)
    q = np.arange(TS)
    diag = np.zeros((4, 128, TS), np.float32)
    for r in range(4):
        diag[r] = np.where((128 * r + idx[:, None]) < q[None, :], 0.0, NEG)
    am0 = np.full((128, 8, TS), NEG, np.float32)
    am1 = np.zeros((128, 8, TS), np.float32)
    for r in range(4):
        am0[:, r, :] = diag[r]
        am1[:, 4 + r, :] = diag[r]
    _CONST_CACHE["amask0"] = am0.reshape(128, 8 * TS).astype(bf)
    _CONST_CACHE["amask1"] = am1.reshape(128, 8 * TS).astype(bf)
    return _CONST_CACHE


_NC_CACHE = {}


def kernel(x, norm_gain, w_in, b_merge, sb_q_gain, sb_k_gain, ret_out_gain, w_branch_sb, w_branch_ret, w_out):
    x = np.asarray(x, np.float32)
    C = _const_tables()
    if "nc" not in _NC_CACHE:
        _NC_CACHE["nc"] = build_program()
    nc = _NC_CACHE["nc"]
    f = lambda a: np.ascontiguousarray(np.asarray(a, np.float32))
    w_in0, wbs0, wbr0, wout0 = f(w_in[0]), f(w_branch_sb[0]), f(w_branch_ret[0]), f(w_out[0])
    gain_rep = np.ascontiguousarray(np.broadcast_to(f(norm_gain[0])[None, :], (128, DM)))
    rgain_rep = np.ascontiguousarray(np.broadcast_to(f(ret_out_gain[0]).reshape(1, DM), (128, DM)))
    qkg = np.ascontiguousarray(np.stack([f(sb_q_gain[0]), f(sb_k_gain[0])], axis=1))
    bmg = np.ascontiguousarray(f(b_merge[0]).reshape(2, 8, 128).transpose(2, 0, 1).reshape(128, 16))
    in_maps = []
    for c in range(8):
        b, p = c // 2, c % 2
        xb = x[b]
        xown = np.ascontiguousarray(xb.reshape(NT, TS, DM)[p::2].reshape(SO, DM))
        blend = np.zeros((128, 2), np.float32)
        blend[:, 0] = 1.0 - p
        blend[:, 1] = float(p)
        in_maps.append({
            "xa": np.ascontiguousarray(xb), "xo": xown, "w_in": w_in0, "wbs": wbs0, "wbr": wbr0, "wout": wout0,
            "gain_rep": gain_rep, "rgain_rep": rgain_rep, "qkg": qkg, "bmg": bmg,
            "cosq": C["cosq"], "sinq": C["sinq"],
            "dmask": C["dmask"], "qdec": C["qdec"], "kdect": C["kdect"], "cmat": C["cmat"],
            "amask": C["amask%d" % p], "blend": blend,
        })
    res = run_bass_kernel_spmd(nc, in_maps, core_ids=list(range(8)))
    _NC_CACHE["last"] = res
    out = np.empty((4, S, DM), np.float32)
    for c in range(8):
        b, p = c // 2, c % 2
        out[b].reshape(NT, TS, DM)[p::2] = np.asarray(res.results[c]["y"], np.float32).reshape(NO, TS, DM)
    return out
```
